# Optimizing a Trainium2 kernel written in Bass

```python
import math
import jax, jax.numpy as jnp
from jax import lax
import numpy as np

D_MODEL = 1024
BATCH = 32
SEQ = 2048
DEPTH = 2
DEC_BATCH = 32
DEC_SEQ = 16
PAST_LEN = 2048

CHUNK = 64
QBLK = 128
ROPE_THETA = 500000.0
NORM_EPS = 1e-6
A_HEADS = 4
A_QK_DIM = 64
A_V_DIM = 2 * A_QK_DIM
A_ROT = A_QK_DIM // 4
A_WIDTH = A_HEADS * A_V_DIM
B_HEADS = 8
B_HD = 64
B_WIDTH = B_HEADS * B_HD
B_LEFT_CHUNKS = 8
B_REL_CLIP = 128
C_HEADS = 8
C_NOPE = 64
C_ROPE = 32
C_V = 64
C_Q_LORA = 768
C_KV_LORA = 256
C_WIDTH = C_HEADS * C_V
D_HEADS = 8
D_HD = 64
D_WIDTH = D_HEADS * D_HD
EVEN_SPLITS = (A_HEADS * 2 * A_QK_DIM, A_HEADS * 2 * A_QK_DIM, A_WIDTH, B_WIDTH, B_WIDTH, B_WIDTH, A_WIDTH + B_WIDTH)
ODD_SPLITS = (C_Q_LORA, C_KV_LORA, C_ROPE, D_WIDTH, D_WIDTH, D_WIDTH, C_WIDTH + D_WIDTH)
EVEN_IN = 4 * A_HEADS * A_QK_DIM + A_WIDTH + 3 * B_WIDTH + A_WIDTH + B_WIDTH
ODD_IN = C_Q_LORA + C_KV_LORA + C_ROPE + 3 * D_WIDTH + C_WIDTH + D_WIDTH
MIX_EVEN = A_WIDTH + B_WIDTH
MIX_ODD = C_WIDTH + D_WIDTH

kernel_name = 'hybrid_chunk_streaming_encoder_step'


def rmsnorm(x, g):
    xf = x.astype(jnp.float32)
    y = xf * lax.rsqrt(jnp.mean(xf * xf, axis=-1, keepdims=True) + NORM_EPS)
    return (y * g.astype(jnp.float32)).astype(x.dtype)


def split_cols(z, sizes):
    idx = np.cumsum(np.array(sizes))[:-1].tolist()
    return jnp.split(z, idx, axis=-1)


def rope(x, pos, rot_dim):
    half = rot_dim // 2
    inv_freq = ROPE_THETA ** (-jnp.arange(half, dtype=jnp.float32) / half)
    ang = pos.astype(jnp.float32)[:, None] * inv_freq[None, :]
    bshape = (pos.shape[0],) + (1,) * (x.ndim - 3) + (half,)
    cos = jnp.cos(ang).reshape(bshape).astype(x.dtype)
    sin = jnp.sin(ang).reshape(bshape).astype(x.dtype)
    x1, x2, rest = x[..., :half], x[..., half:rot_dim], x[..., rot_dim:]
    return jnp.concatenate([x1 * cos - x2 * sin, x2 * cos + x1 * sin, rest], axis=-1)


def chunk_mask(q_pos, k_pos):
    return (k_pos[None, :] // CHUNK) <= (q_pos[:, None] // CHUNK)


def sweep_queries(fn, *q_args):
    t = q_args[0].shape[1]
    if t <= QBLK or t % QBLK != 0:
        return fn(*q_args)
    nb = t // QBLK
    blocked = tuple(a.reshape((a.shape[0], nb, QBLK) + a.shape[2:]).swapaxes(0, 1) for a in q_args)
    out = lax.map(lambda args: fn(*args), blocked)
    out = out.swapaxes(0, 1)
    return out.reshape((out.shape[0], t) + out.shape[3:])


def diff_attention(q, k, v, q_pos, k_pos, lam, lam_init, g_sub):
    scale = A_QK_DIM ** -0.5

    def block(qb, pb):
        s = jnp.einsum('bthmd,bshmd->bhmts', qb, k).astype(jnp.float32) * scale
        s = jnp.where(chunk_mask(pb[0], k_pos)[None, None, None], s, -jnp.inf)
        p = jax.nn.softmax(s, axis=-1)
        w = p[:, :, 0] - lam * p[:, :, 1]
        return jnp.einsum('bhts,bshd->bthd', w.astype(v.dtype), v)

    o = sweep_queries(block, q, q_pos[None])
    o = rmsnorm(o, g_sub) * (1.0 - lam_init)
    return o.reshape(o.shape[0], o.shape[1], A_WIDTH)


def band_attend(q, k, v, q_pos, k_pos, rel_bias, valid):
    s = jnp.einsum('bthd,bshd->bhts', q, k).astype(jnp.float32) * (B_HD ** -0.5)
    rel = jnp.clip(q_pos[:, None] - k_pos[None, :], -B_REL_CLIP, B_REL_CLIP) + B_REL_CLIP
    s = s + rel_bias[:, rel].astype(jnp.float32)[None]
    s = jnp.where(valid[None, None, None, :], s, -jnp.inf)
    p = jax.nn.softmax(s, axis=-1)
    return jnp.einsum('bhts,bshd->bthd', p.astype(v.dtype), v)


def chunk_band_prompt(q, k, v, rel_bias):
    bn, s, h, d = q.shape
    n_chunks = s // CHUNK
    pad = B_LEFT_CHUNKS * CHUNK
    band = pad + CHUNK
    kp = jnp.pad(k, ((0, 0), (pad, 0), (0, 0), (0, 0)))
    vp = jnp.pad(v, ((0, 0), (pad, 0), (0, 0), (0, 0)))
    qc = q.reshape(bn, n_chunks, CHUNK, h, d).swapaxes(0, 1)

    def one(args):
        c, qb = args
        start = c * CHUNK
        kb = lax.dynamic_slice_in_dim(kp, start, band, axis=1)
        vb = lax.dynamic_slice_in_dim(vp, start, band, axis=1)
        q_pos = start + jnp.arange(CHUNK)
        k_pos = start - pad + jnp.arange(band)
        return band_attend(qb, kb, vb, q_pos, k_pos, rel_bias, k_pos >= 0)

    o = lax.map(one, (jnp.arange(n_chunks), qc))
    return o.swapaxes(0, 1).reshape(bn, s, h, d)


def mla_attention(q_nope, q_rope, ckv, krope, q_pos, k_pos, w_uk, w_uv):
    q_lat = jnp.einsum('bthd,chd->bthc', q_nope, w_uk)
    scale = (C_NOPE + C_ROPE) ** -0.5

    def block(ql, qr, pb):
        s = (jnp.einsum('bthc,bsc->bhts', ql, ckv) + jnp.einsum('bthr,bsr->bhts', qr, krope)).astype(jnp.float32) * scale
        s = jnp.where(chunk_mask(pb[0], k_pos)[None, None], s, -jnp.inf)
        p = jax.nn.softmax(s, axis=-1)
        return jnp.einsum('bhts,bsc->bthc', p.astype(ckv.dtype), ckv)

    o_lat = sweep_queries(block, q_lat, q_rope, q_pos[None])
    o = jnp.einsum('bthc,chd->bthd', o_lat, w_uv)
    return o.reshape(o.shape[0], o.shape[1], C_WIDTH)


def stick_breaking(q, k, v, q_pos, k_pos):
    scale = D_HD ** -0.5

    def block(qb, pb):
        z = jnp.einsum('bthd,bshd->bhts', qb, k).astype(jnp.float32) * scale
        m = (k_pos[None, :] < pb[0][:, None])[None, None]
        log_beta = jax.nn.log_sigmoid(z)
        log_1m = jnp.where(m, jax.nn.log_sigmoid(-z), 0.0)
        later = lax.cumsum(log_1m, axis=3, reverse=True) - log_1m
        a = jnp.where(m, jnp.exp(log_beta + later), 0.0)
        return jnp.einsum('bhts,bshd->bthd', a.astype(v.dtype), v)

    o = sweep_queries(block, q, q_pos[None])
    return o.reshape(o.shape[0], o.shape[1], D_WIDTH)


def layer_even(h, pos, cache, layer, g_pre, w_in, lam_q1, lam_k1, lam_q2, lam_k2, g_sub, rel_bias, w_out, g_post):
    bn, t, _ = h.shape
    u = rmsnorm(h, g_pre)
    aq, ak, av, bq, bk, bv, gate = split_cols(u @ w_in, EVEN_SPLITS)
    aq = rope(aq.reshape(bn, t, A_HEADS, 2, A_QK_DIM), pos, A_ROT)
    ak = rope(ak.reshape(bn, t, A_HEADS, 2, A_QK_DIM), pos, A_ROT)
    av = av.reshape(bn, t, A_HEADS, A_V_DIM)
    bq = bq.reshape(bn, t, B_HEADS, B_HD)
    bk = bk.reshape(bn, t, B_HEADS, B_HD)
    bv = bv.reshape(bn, t, B_HEADS, B_HD)
    lam_init = 0.8 - 0.6 * math.exp(-0.3 * layer)
    lam = (jnp.exp(jnp.sum(lam_q1.astype(jnp.float32) * lam_k1.astype(jnp.float32)))
           - jnp.exp(jnp.sum(lam_q2.astype(jnp.float32) * lam_k2.astype(jnp.float32))) + lam_init)
    if cache is None:
        ak_all, av_all, k_pos_a = ak, av, pos
        ob = chunk_band_prompt(bq, bk, bv, rel_bias)
        b_rows = min(B_LEFT_CHUNKS * CHUNK, t)
        b_k_new, b_v_new = bk[:, t - b_rows:], bv[:, t - b_rows:]
    else:
        ca_k, ca_v, cb_k, cb_v = cache
        past = ca_k.shape[1]
        ak_all = jnp.concatenate([ca_k, ak], axis=1)
        av_all = jnp.concatenate([ca_v, av], axis=1)
        k_pos_a = jnp.arange(past + t)
        band_len = cb_k.shape[1]
        k_pos_b = past - band_len + jnp.arange(band_len + t)
        ob = band_attend(bq, jnp.concatenate([cb_k, bk], axis=1), jnp.concatenate([cb_v, bv], axis=1),
                         pos, k_pos_b, rel_bias, k_pos_b >= 0)
        b_k_new, b_v_new = bk, bv
    oa = diff_attention(aq, ak_all, av_all, pos, k_pos_a, lam, lam_init, g_sub)
    o = jnp.concatenate([oa, ob.reshape(bn, t, B_WIDTH)], axis=-1) * jax.nn.silu(gate)
    h = h + rmsnorm(o @ w_out, g_post)
    return h, (ak, av, b_k_new, b_v_new)


def layer_odd(h, pos, cache, g_pre, w_in, g_cq, w_uq, g_ckv, w_uk, w_uv, w_out, g_post):
    bn, t, _ = h.shape
    u = rmsnorm(h, g_pre)
    cq, ckv, kr, dq, dk, dv, gate = split_cols(u @ w_in, ODD_SPLITS)
    qc = (rmsnorm(cq, g_cq) @ w_uq).reshape(bn, t, C_HEADS, C_NOPE + C_ROPE)
    q_nope = qc[..., :C_NOPE]
    q_rope = rope(qc[..., C_NOPE:], pos, C_ROPE)
    ckv = rmsnorm(ckv, g_ckv)
    kr = rope(kr, pos, C_ROPE)
    dq = dq.reshape(bn, t, D_HEADS, D_HD)
    dk = dk.reshape(bn, t, D_HEADS, D_HD)
    dv = dv.reshape(bn, t, D_HEADS, D_HD)
    if cache is None:
        lat_all, kr_all, dk_all, dv_all, k_pos = ckv, kr, dk, dv, pos
    else:
        c_lat, c_kr, c_dk, c_dv = cache
        lat_all = jnp.concatenate([c_lat, ckv], axis=1)
        kr_all = jnp.concatenate([c_kr, kr], axis=1)
        dk_all = jnp.concatenate([c_dk, dk], axis=1)
        dv_all = jnp.concatenate([c_dv, dv], axis=1)
        k_pos = jnp.arange(c_lat.shape[1] + t)
    oc = mla_attention(q_nope, q_rope, lat_all, kr_all, pos, k_pos, w_uk, w_uv)
    od = stick_breaking(dq, dk_all, dv_all, pos, k_pos)
    o = jnp.concatenate([oc, od], axis=-1) * jax.nn.silu(gate)
    h = h + rmsnorm(o @ w_out, g_post)
    return h, (ckv, kr, dk, dv)


def setup_inputs(seed: int = 0) -> dict:
    key = jax.random.key(seed)
    ks = jax.random.split(key, 32)

    def nrm(i, shape, scale=1.0):
        return jax.random.normal(ks[i], shape, jnp.float32) * scale

    def gain(i, n):
        return 1.0 + 0.01 * jax.random.normal(ks[i], (n,), jnp.float32)

    b_rows = min(B_LEFT_CHUNKS * CHUNK, PAST_LEN)
    return {
        'x_prompt': nrm(0, (BATCH, SEQ, D_MODEL)),
        'x_sample': nrm(1, (DEC_BATCH, DEC_SEQ, D_MODEL)),
        'cache_a_k': nrm(2, (DEC_BATCH, PAST_LEN, A_HEADS, 2, A_QK_DIM)),
        'cache_a_v': nrm(3, (DEC_BATCH, PAST_LEN, A_HEADS, A_V_DIM)),
        'cache_b_k': nrm(4, (DEC_BATCH, b_rows, B_HEADS, B_HD)),
        'cache_b_v': nrm(5, (DEC_BATCH, b_rows, B_HEADS, B_HD)),
        'cache_c_latent': nrm(6, (DEC_BATCH, PAST_LEN, C_KV_LORA)),
        'cache_c_krope': nrm(7, (DEC_BATCH, PAST_LEN, C_ROPE)),
        'cache_d_k': nrm(8, (DEC_BATCH, PAST_LEN, D_HEADS, D_HD)),
        'cache_d_v': nrm(9, (DEC_BATCH, PAST_LEN, D_HEADS, D_HD)),
        'g_pre0': gain(10, D_MODEL),
        'w_in0': nrm(11, (D_MODEL, EVEN_IN), D_MODEL ** -0.5),
        'lam_q1': nrm(12, (A_QK_DIM,), 0.1),
        'lam_k1': nrm(13, (A_QK_DIM,), 0.1),
        'lam_q2': nrm(14, (A_QK_DIM,), 0.1),
        'lam_k2': nrm(15, (A_QK_DIM,), 0.1),
        'g_sub_a': gain(16, A_V_DIM),
        'rel_bias_b': nrm(17, (B_HEADS, 2 * B_REL_CLIP + 1), 0.1),
        'w_out0': nrm(18, (MIX_EVEN, D_MODEL), MIX_EVEN ** -0.5),
        'g_post0': gain(19, D_MODEL),
        'g_pre1': gain(20, D_MODEL),
        'w_in1': nrm(21, (D_MODEL, ODD_IN), D_MODEL ** -0.5),
        'g_cq': gain(22, C_Q_LORA),
        'w_uq': nrm(23, (C_Q_LORA, C_HEADS * (C_NOPE + C_ROPE)), C_Q_LORA ** -0.5),
        'g_ckv': gain(24, C_KV_LORA),
        'w_uk': nrm(25, (C_KV_LORA, C_HEADS, C_NOPE), C_KV_LORA ** -0.5),
        'w_uv': nrm(26, (C_KV_LORA, C_HEADS, C_V), C_KV_LORA ** -0.5),
        'w_out1': nrm(27, (MIX_ODD, D_MODEL), MIX_ODD ** -0.5),
        'g_post1': gain(28, D_MODEL),
    }


def reference(x_prompt, x_sample, cache_a_k, cache_a_v, cache_b_k, cache_b_v, cache_c_latent, cache_c_krope,
              cache_d_k, cache_d_v, g_pre0, w_in0, lam_q1, lam_k1, lam_q2, lam_k2, g_sub_a, rel_bias_b, w_out0,
              g_post0, g_pre1, w_in1, g_cq, w_uq, g_ckv, w_uk, w_uv, w_out1, g_post1):
    past = cache_a_k.shape[1]
    pos_p = jnp.arange(x_prompt.shape[1])
    pos_s = past + jnp.arange(x_sample.shape[1])
    hp, hs = x_prompt, x_sample
    for layer in range(DEPTH):
        if layer % 2 == 0:
            hp, st_even_p = layer_even(hp, pos_p, None, layer, g_pre0, w_in0, lam_q1, lam_k1, lam_q2, lam_k2,
                                       g_sub_a, rel_bias_b, w_out0, g_post0)
            hs, st_even_s = layer_even(hs, pos_s, (cache_a_k, cache_a_v, cache_b_k, cache_b_v), layer,
                                       g_pre0, w_in0, lam_q1, lam_k1, lam_q2, lam_k2, g_sub_a, rel_bias_b,
                                       w_out0, g_post0)
        else:
            hp, st_odd_p = layer_odd(hp, pos_p, None, g_pre1, w_in1, g_cq, w_uq, g_ckv, w_uk, w_uv, w_out1, g_post1)
            hs, st_odd_s = layer_odd(hs, pos_s, (cache_c_latent, cache_c_krope, cache_d_k, cache_d_v),
                                     g_pre1, w_in1, g_cq, w_uq, g_ckv, w_uk, w_uv, w_out1, g_post1)
    a_k_p, a_v_p, b_k_p, b_v_p = st_even_p
    a_k_s, a_v_s, b_k_s, b_v_s = st_even_s
    c_lat_p, c_krope_p, d_k_p, d_v_p = st_odd_p
    c_lat_s, c_krope_s, d_k_s, d_v_s = st_odd_s
    return (hp, hs, a_k_p, a_v_p, b_k_p, b_v_p, c_lat_p, c_krope_p, d_k_p, d_v_p,
            a_k_s, a_v_s, b_k_s, b_v_s, c_lat_s, c_krope_s, d_k_s, d_v_s)
```

```python
import math
import numpy as np
from contextlib import ExitStack
import concourse.bass as bass
import concourse.mybir as mybir
from concourse.bass_utils import run_bass_kernel_spmd

F32 = mybir.dt.float32
BF16 = mybir.dt.bfloat16
AF = mybir.ActivationFunctionType
ALU = mybir.AluOpType
AX = mybir.AxisListType

NCORES = 8
DM = 1024
T = 2048
TS = 16
PAST = 2048
EPS = 1e-6
NEG = -30000.0
THETA = 500000.0
LAM_INIT0 = 0.8 - 0.6 * math.exp(-0.3 * 0)
W0N = 4096
W1N = 3616
VST = 520


class Buf:
    __slots__ = ("name", "lw", "rd", "dsem", "dcnt", "excl")

    def __init__(self, name):
        self.name = name
        self.excl = False
        self.lw = None
        self.rd = []
        self.dsem = None
        self.dcnt = 0


class FW:
    def __init__(self, nc, stack):
        self.nc = nc
        self.stack = stack
        self.q = {"pe": nc.tensor, "act": nc.scalar, "dve": nc.vector, "pool": nc.gpsimd, "sp": nc.sync}
        self.sem = {}
        self.cnt = {}
        self.waited = {}
        for e in self.q:
            self.sem[e] = stack.enter_context(nc.semaphore("s_" + e))
            self.cnt[e] = 0
            self.waited[e] = {}
        self.out_events = []
        self.ninst = 0
        self.nbuf = 0

    def buf(self, name=None):
        self.nbuf += 1
        return Buf(name or ("b%d" % self.nbuf))

    def sb(self, name, shape, dtype):
        return self.stack.enter_context(self.nc.sbuf_tensor(name, list(shape), dtype))

    def ps(self, name, shape, dtype):
        return self.stack.enter_context(self.nc.psum_tensor(name, list(shape), dtype))

    def _dsem(self, b):
        if b.dsem is None:
            b.dsem = self.stack.enter_context(self.nc.semaphore("d_" + b.name))
        return b.dsem

    def _deps(self, eng, reads, writes):
        best = {}

        def add(ev):
            s, v, en = ev
            if en == "pe" and eng == "pe":
                return
            k = id(s)
            if k not in best or best[k][1] < v:
                best[k] = (s, v)

        for b in reads:
            if b.lw is not None:
                add(b.lw)
            if b.excl:
                for ev in b.rd:
                    if ev[2] != eng:
                        add(ev)
        for b in writes:
            if b.lw is not None:
                add(b.lw)
            for ev in b.rd:
                add(ev)
        w = self.waited[eng]
        out = []
        for k, (s, v) in best.items():
            if w.get(k, 0) >= v:
                continue
            w[k] = v
            out.append((s, v))
        return out

    def _record(self, ev, reads, writes):
        for b in reads:
            b.rd.append(ev)
            if len(b.rd) > 16:
                best = {}
                for e in b.rd:
                    k = id(e[0])
                    if k not in best or best[k][1] < e[1]:
                        best[k] = e
                b.rd = list(best.values())
        for b in writes:
            b.lw = ev
            b.rd = []

    def op(self, eng, fns, reads=(), writes=()):
        q = self.q[eng]
        if not isinstance(fns, (list, tuple)):
            fns = [fns]
        for (s, v) in self._deps(eng, reads, writes):
            q.wait_ge(s, v)
        ins = None
        for f in fns:
            ins = f(q)
            self.ninst += 1
        self.cnt[eng] += 1
        ins.then_inc(self.sem[eng], 1)
        ev = (self.sem[eng], self.cnt[eng], eng)
        self._record(ev, reads, writes)
        return ev

    def dma(self, eng, out, in_, reads=(), writes=(), sembuf=None, is_output=False, **kw):
        q = self.q[eng]
        if sembuf is None:
            sembuf = writes[0] if writes else reads[0]
        s = self._dsem(sembuf)
        for (ws, v) in self._deps(eng, reads, writes):
            q.wait_ge(ws, v)
        q.dma_start(out=out, in_=in_, **kw).then_inc(s, 16)
        self.ninst += 1
        sembuf.dcnt += 16
        ev = (s, sembuf.dcnt, "dma")
        self._record(ev, reads, writes)
        if is_output:
            self.out_events.append(ev)
        return ev

    def wait_events(self, eng, events):
        q = self.q[eng]
        best = {}
        for (s, v, en) in events:
            k = id(s)
            if k not in best or best[k][1] < v:
                best[k] = (s, v)
        w = self.waited[eng]
        for k, (s, v) in best.items():
            if w.get(k, 0) >= v:
                continue
            w[k] = v
            q.wait_ge(s, v)

    def finish(self, eng="sp"):
        q = self.q[eng]
        best = {}
        for (s, v, en) in self.out_events:
            k = id(s)
            if k not in best or best[k][1] < v:
                best[k] = (s, v)
        for k, (s, v) in best.items():
            q.wait_ge(s, v)
        for e in ("pe", "act", "dve", "pool"):
            if self.cnt[e] > 0:
                q.wait_ge(self.sem[e], self.cnt[e])


class Seq:
    def __init__(self, kind, idx):
        self.kind = kind
        self.idx = idx
        if kind == "p":
            self.ntok = T
            self.tiles = [(i * 128, 128, i) for i in range(16)]
            self.keys = [(i, i * 128, 128, False) for i in range(16)]
            self.keys_b = self.keys
        else:
            self.ntok = TS
            self.tiles = [(0, TS, 16)]
            self.keys = [(i, i * 128, 128, True) for i in range(16)] + [(16, 2048, TS, False)]
            self.keys_b = [(i, i * 128, 128, True) for i in range(4)] + [(4, 512, TS, False)]


DEV_STOP = 99
DEV_DBG = False


def build(NP, NS, layers=(0, 1)):
    nc = bass.Bass("TRN2", target_bir_lowering=False)

    def din(name, shape):
        return nc.dram_tensor(name, list(shape), F32, kind="ExternalInput").ap()

    def dout(name, shape):
        return nc.dram_tensor(name, list(shape), F32, kind="ExternalOutput").ap()

    def dscr(name, shape, dtype):
        return nc.dram_tensor(name, list(shape), dtype).ap()

    NPa, NSa = max(NP, 1), max(NS, 1)
    I = {}
    I["x_prompt"] = din("x_prompt", [NPa, T, DM])
    I["x_sample"] = din("x_sample", [NSa, TS, DM])
    I["cache_a_k"] = din("cache_a_k", [NSa, PAST, 512])
    I["cache_a_v"] = din("cache_a_v", [NSa, PAST, 512])
    I["cache_b_k"] = din("cache_b_k", [NSa, 512, 512])
    I["cache_b_v"] = din("cache_b_v", [NSa, 512, 512])
    I["cache_c_latent"] = din("cache_c_latent", [NSa, PAST, 256])
    I["cache_c_krope"] = din("cache_c_krope", [NSa, PAST, 32])
    I["cache_d_k"] = din("cache_d_k", [NSa, PAST, 512])
    I["cache_d_v"] = din("cache_d_v", [NSa, PAST, 512])
    for nm, shp in [("g_pre0", [DM]), ("w_in0", [DM, W0N]), ("lam_q1", [64]), ("lam_k1", [64]), ("lam_q2", [64]),
                    ("lam_k2", [64]), ("g_sub_a", [128]), ("rel_bias_b", [8, 257]), ("w_out0", [DM, DM]),
                    ("g_post0", [DM]), ("g_pre1", [DM]), ("w_in1", [DM, W1N]), ("g_cq", [768]), ("w_uq", [768, 768]),
                    ("g_ckv", [256]), ("w_uk", [256, 512]), ("w_uv", [256, 512]), ("w_out1", [DM, DM]),
                    ("g_post1", [DM])]:
        I[nm] = din(nm, shp)
    I["c_ident"] = din("c_ident", [128, 128])
    I["c_negq"] = din("c_negq", [128, 128])
    I["c_negq4"] = din("c_negq4", [128, 128])
    I["c_negd"] = din("c_negd", [128, 128])
    I["c_m01d"] = din("c_m01d", [128, 128])
    I["c_nut"] = din("c_nut", [128, 128])
    I["c_rbT"] = din("c_rbT", [8, 2, 128, 128])
    I["c_ropeA"] = din("c_ropeA", [128, 2 * 17 * 8])
    I["c_ropeC"] = din("c_ropeC", [128, 2 * 17 * 16])

    O = {}
    O["y_prompt"] = dout("y_prompt", [NPa, T, DM])
    O["y_sample"] = dout("y_sample", [NSa, TS, DM])
    O["a_k_p"] = dout("a_k_p", [NPa, T, 512])
    O["a_v_p"] = dout("a_v_p", [NPa, T, 512])
    O["b_k_p"] = dout("b_k_p", [NPa, 512, 512])
    O["b_v_p"] = dout("b_v_p", [NPa, 512, 512])
    O["c_lat_p"] = dout("c_lat_p", [NPa, T, 256])
    O["c_krope_p"] = dout("c_krope_p", [NPa, T, 32])
    O["d_k_p"] = dout("d_k_p", [NPa, T, 512])
    O["d_v_p"] = dout("d_v_p", [NPa, T, 512])
    O["a_k_s"] = dout("a_k_s", [NSa, TS, 512])
    O["a_v_s"] = dout("a_v_s", [NSa, TS, 512])
    O["b_k_s"] = dout("b_k_s", [NSa, TS, 512])
    O["b_v_s"] = dout("b_v_s", [NSa, TS, 512])
    O["c_lat_s"] = dout("c_lat_s", [NSa, TS, 256])
    O["c_krope_s"] = dout("c_krope_s", [NSa, TS, 32])
    O["d_k_s"] = dout("d_k_s", [NSa, TS, 512])
    O["d_v_s"] = dout("d_v_s", [NSa, TS, 512])
    if DEV_DBG:
        O["dbg"] = dout("dbg", [T, DM])

    S_w0 = dscr("s_w0", [DM, W0N], BF16)
    S_wo0 = dscr("s_wo0", [DM, DM], BF16)
    S_w1 = dscr("s_w1", [DM, W1N], BF16)
    S_wo1 = dscr("s_wo1", [DM, DM], BF16)
    S_wuq = dscr("s_wuq", [768, 768], BF16)
    S_h1p = dscr("s_h1p", [NPa, T, DM], F32)
    S_h1s = dscr("s_h1s", [NSa, TS, DM], F32)
    S_rbp = dscr("s_rbp", [8, 512], F32)
    S_qt1 = dscr("s_qt1", [96, 4, T], BF16)
    S_kt1 = dscr("s_kt1", [64, 4, 2064], BF16)
    S_v1 = dscr("s_v1", [17, 128, 256], BF16)

    with ExitStack() as st:
        fw = FW(nc, st)
        B = fw.buf
        uT = fw.sb("uT", [128, 8, T], BF16)
        b_uT = [B("uT%d" % i) for i in range(16)]
        R = fw.sb("R", [128, 8192 + 8256 + 17 * VST], BF16)
        QT = R[:, 0:8192].rearrange("p (c t) -> p c t", c=4)
        KT = R[:, 8192:8192 + 8256].rearrange("p (c t) -> p c t", c=4)
        VR = R[:, 16448:16448 + 17 * VST].rearrange("p (k e) -> p k e", k=17)
        b_QT = [B("QT%d" % i) for i in range(16)]
        b_KT = [B("KT%d" % i) for i in range(17)]
        b_V = [B("V%d" % i) for i in range(17)]
        oall = fw.sb("oall", [128, 16, DM], BF16)
        b_o = [B("o%d" % i) for i in range(16)]
        wblk = [fw.sb("wblk%d" % i, [128, 8, 512], BF16) for i in range(2)]
        b_wblk = [B("wblk%d" % i) for i in range(2)]
        wout = fw.sb("wout", [128, 8, DM], BF16)
        b_wout = B("wout")
        xt = [fw.sb("xt%d" % i, [128, DM], F32) for i in range(2)]
        b_xt = [B("xt%d" % i) for i in range(2)]
        b_xout = [B("xout%d" % i) for i in range(2)]
        xn = fw.sb("xn", [128, DM], BF16)
        b_xn = B("xn")
        junk = fw.sb("junk", [128, DM], BF16)
        b_junk = B("junk")
        stg = [fw.sb("stg%d" % i, [128, 512], F32) for i in range(3)]
        b_stg = [B("stg%d" % i) for i in range(3)]
        tb16 = [fw.sb("tb16_%d" % i, [128, 512], BF16) for i in range(3)]
        b_tb16 = [B("tb16_%d" % i) for i in range(3)]
        PT = [fw.sb("PT%d" % i, [128, 512], BF16) for i in range(3)]
        b_PT = [B("PT%d" % i) for i in range(3)]
        Ef = [fw.sb("Ef%d" % i, [128, 512], F32) for i in range(2)]
        b_Ef = [B("Ef%d" % i) for i in range(2)]
        SPb = [fw.sb("SPb%d" % i, [128, 512], BF16) for i in range(2)]
        b_SPb = [B("SPb%d" % i) for i in range(2)]
        ogT = [fw.sb("ogT%d" % i, [128, 8, 128], BF16) for i in range(2)]
        b_ogT = [B("ogT%d" % i) for i in range(2)]
        ytmp = fw.sb("ytmp", [128, DM], F32)
        b_ytmp = B("ytmp")
        sm = fw.sb("sm", [128, 64], F32)
        b_sm = {}

        def smb(k):
            if k not in b_sm:
                b_sm[k] = B("sm%d" % k)
            return b_sm[k]

        osb = fw.sb("osb", [128, 4, 128], F32)
        b_osb = [B("osb%d" % i) for i in range(4)]
        ident = fw.sb("ident", [128, 128], BF16); b_ident = B("ident")
        negq = fw.sb("negq", [128, 128], BF16); b_negq = B("negq")
        negq4 = fw.sb("negq4", [128, 128], BF16); b_negq4 = B("negq4")
        negd = fw.sb("negd", [128, 128], BF16); b_negd = B("negd")
        m01d = fw.sb("m01d", [128, 128], BF16); b_m01d = B("m01d")
        nut = fw.sb("nut", [128, 128], BF16); b_nut = B("nut")
        onec = fw.sb("onec", [128, 2], BF16); b_onec = B("onec")
        ropeA = fw.sb("ropeA", [128, 2, 17, 8], F32); b_ropeA = B("ropeA")
        ropeC = fw.sb("ropeC", [128, 2, 17, 16], F32); b_ropeC = B("ropeC")
        gcol = fw.sb("gcol", [128, 24], F32); b_gcol = B("gcol")
        gpost = fw.sb("gpost", [128, DM], F32); b_gpost = B("gpost")
        gsub = fw.sb("gsub", [128, 128], F32); b_gsub = B("gsub")
        gckv = fw.sb("gckv", [128, 256], F32); b_gckv = B("gckv")
        lamt = ytmp[:, 520:776].rearrange("p (a b) -> p a b", a=4); b_lamt = b_ytmp
        lamv = fw.sb("lamv", [128, 8], F32); b_lamv = B("lamv")
        LR = fw.sb("LR", [128, 4608], BF16)
        Tb = LR[:, 0:2048].rearrange("p (a b) -> p a b", a=16); b_Tb = B("Tb")
        wuk = LR[:, 0:1024].rearrange("p (a b) -> p a b", a=2); b_wuk = B("wuk")
        wuv = LR[:, 1024:2048].rearrange("p (a b) -> p a b", a=2); b_wuv = B("wuv")
        latT = LR[:, 2048:2304].rearrange("p (a b) -> p a b", a=2); b_latT = B("latT")
        wqb = LR[:, 2304:4608].rearrange("p (a b) -> p a b", a=6); b_wqb = B("wqb")
        bank = [fw.ps("bank%d" % i, [128, 512], F32) for i in range(8)]
        b_bank = [B("bank%d" % i) for i in range(8)]
        tbk = [bank[6][:, :].bitcast(BF16), bank[7][:, :].bitcast(BF16)]
        b_tbk = [b_bank[6], b_bank[7]]
        for bb in b_bank:
            bb.excl = True
        pexp = fw.sb("pexp", [128, 1], F32); b_pexp = B("pexp")
        fw.op("pool", lambda q: q.memset(pexp[:], -0.5), writes=[b_pexp])
        b_dram = {}

        def dbuf(k):
            if k not in b_dram:
                b_dram[k] = B("dram_" + str(k))
            return b_dram[k]

        rr_ctr = {"stg": 0, "tb16": 0, "tbk": 0, "xt": 0, "wblk": 0, "pbank": 0, "cp": 0}

        def nxt(k, n):
            v = rr_ctr[k] % n
            rr_ctr[k] += 1
            return v

        def load_const_bf16(dst, b_dst, src):
            i = nxt("stg", 3)
            fw.dma("sp", stg[i][:, 0:128], src, writes=[b_stg[i]])
            fw.op("dve", lambda q: q.tensor_copy(out=dst[:], in_=stg[i][:, 0:128]), reads=[b_stg[i]], writes=[b_dst])

        load_const_bf16(ident, b_ident, I["c_ident"])
        load_const_bf16(negq, b_negq, I["c_negq"])
        load_const_bf16(negq4, b_negq4, I["c_negq4"])
        load_const_bf16(negd, b_negd, I["c_negd"])
        load_const_bf16(m01d, b_m01d, I["c_m01d"])
        load_const_bf16(nut, b_nut, I["c_nut"])
        fw.op("pool", lambda q: q.memset(onec[:], 1.0), writes=[b_onec])
        fw.dma("sp", ropeA[:].rearrange("p a b c -> p (a b c)"), I["c_ropeA"], writes=[b_ropeA])
        fw.dma("sp", ropeC[:].rearrange("p a b c -> p (a b c)"), I["c_ropeC"], writes=[b_ropeC])
        with nc.allow_non_contiguous_dma(reason="tiny gain vectors"):
            for (gsrc, o0, n) in ((I["g_pre0"], 0, 8), (I["g_pre1"], 8, 8), (I["g_cq"], 16, 6)):
                for c in range(n):
                    fw.dma("sp", gcol[:, o0 + c:o0 + c + 1], bass.AP(gsrc.tensor, c * 128, [[1, 128], [1, 1]]), writes=[b_gcol])

        def bcast_ap(src, n):
            return bass.AP(src.tensor, 0, [[0, 128], [1, n]])

        fw.dma("sp", gsub[:], bcast_ap(I["g_sub_a"], 128), writes=[b_gsub])
        fw.op("dve", lambda q: q.tensor_scalar(out=gsub[:], in0=gsub[:], scalar1=1.0 - LAM_INIT0, scalar2=None, op0=ALU.mult),
              reads=[b_gsub], writes=[b_gsub])
        fw.dma("sp", gckv[:], bcast_ap(I["g_ckv"], 256), writes=[b_gckv])
        for i, nm in enumerate(["lam_q1", "lam_k1", "lam_q2", "lam_k2"]):
            fw.dma("sp", lamt[:, i, :], bcast_ap(I[nm], 64), writes=[b_lamt])
        fw.op("dve", lambda q: q.tensor_tensor(out=lamt[:, 0, :], in0=lamt[:, 0, :], in1=lamt[:, 1, :], op=ALU.mult), reads=[b_lamt], writes=[b_lamt])
        fw.op("dve", lambda q: q.tensor_tensor(out=lamt[:, 2, :], in0=lamt[:, 2, :], in1=lamt[:, 3, :], op=ALU.mult), reads=[b_lamt], writes=[b_lamt])
        fw.op("dve", lambda q: q.reduce_sum(out=lamv[:, 0:1], in_=lamt[:, 0, :], axis=AX.X), reads=[b_lamt], writes=[b_lamv])
        fw.op("dve", lambda q: q.reduce_sum(out=lamv[:, 1:2], in_=lamt[:, 2, :], axis=AX.X), reads=[b_lamt], writes=[b_lamv])
        fw.op("act", lambda q: q.activation(out=lamv[:, 0:2], in_=lamv[:, 0:2], func=AF.Exp), reads=[b_lamv], writes=[b_lamv])
        fw.op("dve", lambda q: q.tensor_tensor(out=lamv[:, 2:3], in0=lamv[:, 1:2], in1=lamv[:, 0:1], op=ALU.subtract), reads=[b_lamv], writes=[b_lamv])
        fw.op("dve", lambda q: q.tensor_scalar(out=lamv[:, 3:4], in0=lamv[:, 2:3], scalar1=-LAM_INIT0, scalar2=None, op0=ALU.add), reads=[b_lamv], writes=[b_lamv])
        nlam = lamv[:, 3:4]
        cb = lamv[:, 4:8]
        cbt = fw.sb("cbt", [128, 8], F32); b_cbt = B("cbt")
        with nc.allow_non_contiguous_dma(reason="tiny"):
            for h in range(8):
                fw.dma("sp", cbt[:, h:h + 1], bass.AP(I["rel_bias_b"].tensor, 256 + 257 * h, [[0, 128], [1, 1]]), writes=[b_cbt])
        jn = nxt("stg", 3)
        fw.dma("sp", stg[jn][:, 0:128], I["c_negq"], writes=[b_stg[jn]])
        for h in range(8):
            for d in range(2):
                i = nxt("stg", 3)
                if i == jn:
                    i = nxt("stg", 3)
                fw.dma("sp", stg[i][:, 0:128], I["c_rbT"][h, d], writes=[b_stg[i]])
                fw.op("dve", lambda q: q.tensor_scalar(out=stg[i][:, 128:256], in0=stg[i][:, 0:128], scalar1=cbt[:, h:h + 1], scalar2=None, op0=ALU.subtract),
                      reads=[b_stg[i], b_cbt], writes=[b_stg[i]])
                if d == 0:
                    fw.op("dve", lambda q: q.tensor_tensor(out=Tb[:, h * 2 + d, :], in0=stg[i][:, 128:256], in1=stg[jn][:, 0:128], op=ALU.add),
                          reads=[b_stg[i], b_stg[jn]], writes=[b_Tb])
                else:
                    fw.op("dve", lambda q: q.tensor_copy(out=Tb[:, h * 2 + d, :], in_=stg[i][:, 128:256]), reads=[b_stg[i]], writes=[b_Tb])

        def prep_weight(src, dst, K, N, gofs):
            for kc in range(K // 128):
                for c0 in range(0, N, 1024):
                    ncol = min(1024, N - c0)
                    i = nxt("xt", 2)
                    fw.dma("sp", xt[i][:, 0:ncol], src[kc * 128:(kc + 1) * 128, c0:c0 + ncol], writes=[b_xt[i]])
                    if gofs is None:
                        fw.op("dve", lambda q: q.tensor_copy(out=junk[:, 0:ncol], in_=xt[i][:, 0:ncol]), reads=[b_xt[i]], writes=[b_junk])
                    else:
                        fw.op("dve", lambda q: q.tensor_scalar(out=junk[:, 0:ncol], in0=xt[i][:, 0:ncol], scalar1=gcol[:, gofs + kc:gofs + kc + 1],
                                                               scalar2=None, op0=ALU.mult), reads=[b_xt[i], b_gcol], writes=[b_junk])
                    fw.dma("pool", dst[kc * 128:(kc + 1) * 128, c0:c0 + ncol], junk[:, 0:ncol], reads=[b_junk], writes=[dbuf(dst.name)], sembuf=b_junk)

        if 0 in layers and DEV_STOP >= 1:
            prep_weight(I["w_in0"], S_w0, DM, W0N, 0)
            prep_weight(I["w_out0"], S_wo0, DM, DM, None)
        if 1 in layers and DEV_STOP >= 1:
            prep_weight(I["w_in1"], S_w1, DM, W1N, 8)
            prep_weight(I["w_uq"], S_wuq, 768, 768, 16)
            prep_weight(I["w_out1"], S_wo1, DM, DM, None)

        def set_ones(H, dv):
            v = VR[:, :, 0:H * (dv + 1)].rearrange("p k (h e) -> p k h e", h=H)[:, :, :, dv:dv + 1]
            fw.op("pool", lambda q: q.memset(v, 1.0), writes=b_V)

        def load_wblock(scr, K, c0, ncol):
            i = nxt("wblk", 2)
            kc = K // 128
            fw.dma("sp", wblk[i][:, 0:kc, 0:ncol], scr.rearrange("(c p) n -> p c n", p=128)[:, :, c0:c0 + ncol],
                   reads=[dbuf(scr.name)], writes=[b_wblk[i]])
            return wblk[i], b_wblk[i]

        def rsqrt_col(col_ap, nr, bs):
            fw.op("pool", lambda q: q.tensor_scalar(out=col_ap, in0=col_ap, scalar1=EPS, scalar2=0.0, op0=ALU.add, op1=ALU.add), reads=[bs], writes=[bs])
            fw.op("pool", lambda q: q.tensor_tensor(out=col_ap, in0=col_ap, in1=pexp[0:nr, :], op=ALU.pow), reads=[bs, b_pexp], writes=[bs])

        def rms_scale(src_ap, nr, n, b_src, key, engines="act"):
            bs = smb(key)
            aps = src_ap if isinstance(src_ap, (list, tuple)) else [src_ap]
            for k, a in enumerate(aps):
                w = a.shape[-1]
                fw.op("act", lambda q: q.activation(out=junk[0:nr, 0:w], in_=a, func=AF.Square, scale=float(n) ** -0.5, accum_out=sm[0:nr, key + k:key + k + 1]),
                      reads=b_src, writes=[b_junk, bs])
            if len(aps) == 2:
                fw.op("pool", lambda q: q.tensor_tensor(out=sm[0:nr, key:key + 1], in0=sm[0:nr, key:key + 1], in1=sm[0:nr, key + 1:key + 2], op=ALU.add),
                      reads=[bs], writes=[bs])
            rsqrt_col(sm[0:nr, key:key + 1], nr, bs)
            return sm[0:nr, key:key + 1], bs

        def transposes(src_aps, nr, widths):
            i = nxt("tbk", 2)
            offs = []
            fns = []
            o = 0
            for a, w in zip(src_aps, widths):
                offs.append(o)
                fns.append(lambda q, a=a, w=w, o=o: q.transpose(out=tbk[i][0:w, o:o + nr], in_=a, identity=ident[0:nr, 0:nr]))
                o += 128
            return i, fns, offs

        def copy_eng():
            return ("dve", "act")[nxt("cp", 2)]

        rr_ctr["ce"] = 0

        def cast_eng():
            return ("dve", "pool", "act")[nxt("ce", 3)]

        def ecopy(eng, out, in_, reads, writes, scale=None):
            if eng == "act":
                if scale is None:
                    fw.op("act", lambda q: q.activation(out=out, in_=in_, func=AF.Copy), reads=reads, writes=writes)
                else:
                    fw.op("act", lambda q: q.activation(out=out, in_=in_, func=AF.Copy, scale=scale), reads=reads, writes=writes)
            else:
                if scale is None:
                    fw.op(eng, lambda q: q.tensor_copy(out=out, in_=in_), reads=reads, writes=writes)
                else:
                    fw.op(eng, lambda q: q.tensor_scalar(out=out, in0=in_, scalar1=scale, scalar2=0.0, op0=ALU.mult, op1=ALU.add), reads=reads, writes=writes)

        def stage_U(seq, src, src_bufs=()):
            sel = {}

            def pa(ti):
                r0, nr, rt = seq.tiles[ti]
                i = nxt("xt", 2)
                fw.dma("sp", xt[i][0:nr, :], src[r0:r0 + nr, :], reads=list(src_bufs), writes=[b_xt[i]], sembuf=b_xt[i])
                rs, bs = rms_scale(xt[i][0:nr, :], nr, DM, [b_xt[i]], 2 * (ti % 2))
                sel[ti] = (i, rs, bs)

            def pb(ti):
                r0, nr, rt = seq.tiles[ti]
                i, rs, bs = sel[ti]
                fw.op("dve", lambda q: q.tensor_scalar(out=xn[0:nr, :], in0=xt[i][0:nr, :], scalar1=rs, scalar2=None, op0=ALU.mult), reads=[b_xt[i], bs], writes=[b_xn])
                k = nxt("tbk", 2)
                fw.op("pe", [lambda q, c=c: q.transpose(out=tbk[k][:, c * 128:c * 128 + nr], in_=xn[0:nr, c * 128:(c + 1) * 128], identity=ident[0:nr, 0:nr])
                             for c in range(8)], reads=[b_xn, b_ident], writes=[b_tbk[k]])
                ecopy("act", uT[:, 0:4, r0:r0 + nr], tbk[k][:, 0:512].rearrange("p (c t) -> p c t", c=4)[:, :, 0:nr], [b_tbk[k]], [b_uT[ti]])
                ecopy("dve", uT[:, 4:8, r0:r0 + nr], tbk[k][:, 512:1024].rearrange("p (c t) -> p c t", c=4)[:, :, 0:nr], [b_tbk[k]], [b_uT[ti]])

            n = len(seq.tiles)
            pa(0)
            for ti in range(n):
                if ti + 1 < n:
                    pa(ti + 1)
                pb(ti)

        def proj_block(seq, scr, c0, ncol, consume, K=DM, lhs=None):
            wt, bw = load_wblock(scr, K, c0, ncol)
            kcn = K // 128
            later = []
            later2 = []
            for ti, (r0, nr, rt) in enumerate(seq.tiles):
                pb = 4 + nxt("pbank", 2)
                if lhs is None:
                    lf = lambda c: uT[:, c, r0:r0 + nr]
                    lb = [b_uT[ti]]
                else:
                    lf, lb = lhs(ti)
                fw.op("pe", [lambda q, c=c: q.matmul(bank[pb][0:nr, 0:ncol], lhsT=lf(c), rhs=wt[:, c, 0:ncol], start=(c == 0), stop=(c == kcn - 1))
                             for c in range(kcn)], reads=lb + [bw], writes=[b_bank[pb]])
                for f in later2:
                    f()
                later2 = later
                later = consume(ti, r0, nr, rt, bank[pb], b_bank[pb]) or []
            for f in later2 + later:
                f()

        def store_out(dst, src_ap, b_src):
            fw.dma("pool", dst, src_ap, reads=[b_src], is_output=True, sembuf=b_src)

        def ingest_k(src_ap, b_src, nr, kcol0, ktile, scale=None, ncol=512, part=128, rows=None, eng="dve"):
            j = nxt("tb16", 3)
            ecopy(eng, tb16[j][0:nr, 0:ncol], src_ap, [b_src], [b_tb16[j]], scale)

            def later():
                nchunk = ncol // part
                k = nxt("tbk", 2)
                fw.op("pe", [lambda q, c=c: q.transpose(out=tbk[k][0:part, c * 128:c * 128 + nr], in_=tb16[j][0:nr, c * part:(c + 1) * part],
                                                         identity=ident[0:nr, 0:nr]) for c in range(nchunk)],
                      reads=[b_tb16[j], b_ident], writes=[b_tbk[k]])
                p0, p1 = rows if rows is not None else (0, part)
                ecopy(copy_eng(), KT[p0:p1, 0:nchunk, kcol0:kcol0 + nr] if rows is None else KT[p0:p1, 0:nchunk, kcol0:kcol0 + nr],
                      tbk[k][0:part, 0:nchunk * 128].rearrange("p (c t) -> p c t", c=nchunk)[:, :, 0:nr], [b_tbk[k]], [b_KT[ktile]])
            return later

        def ingest_q(src_ap, b_src, nr, qcol0, qtile, scale, ncol=512, part=128, eng="dve"):
            j = nxt("tb16", 3)
            ecopy(eng, tb16[j][0:nr, 0:ncol], src_ap, [b_src], [b_tb16[j]], scale)

            def later():
                nchunk = ncol // part
                k = nxt("tbk", 2)
                fw.op("pe", [lambda q, c=c: q.transpose(out=tbk[k][0:part, c * 128:c * 128 + nr], in_=tb16[j][0:nr, c * part:(c + 1) * part],
                                                         identity=ident[0:nr, 0:nr]) for c in range(nchunk)],
                      reads=[b_tb16[j], b_ident], writes=[b_tbk[k]])
                ecopy(copy_eng(), QT[0:part, 0:nchunk, qcol0:qcol0 + nr],
                      tbk[k][0:part, 0:nchunk * 128].rearrange("p (c t) -> p c t", c=nchunk)[:, :, 0:nr], [b_tbk[k]], [b_QT[qtile]])
            return later

        def ingest_v(src_ap, b_src, nr, vtile, H, dv, eng="dve", hofs=0, Hsrc=None):
            Hs = Hsrc or H
            src3 = src_ap.rearrange("p (h d) -> p h d", h=Hs)[:, hofs:hofs + H, :]
            dst3 = VR[0:nr, vtile, 0:H * (dv + 1)].rearrange("p (h e) -> p h e", h=H)[:, :, 0:dv] if dv != 64 or True else None
            ecopy(eng, dst3, src3, [b_src], [b_V[vtile]])

        def rope_ops(dst, src3, nr, rt, tab, half, nblk, b_src, b_dst, blk_stride_dst3):
            cosb = tab[0:nr, 0, rt, :].unsqueeze(1).to_broadcast([nr, nblk, half])
            sinb = tab[0:nr, 1, rt, :].unsqueeze(1).to_broadcast([nr, nblk, half])
            x1 = src3[:, :, 0:half]
            x2 = src3[:, :, half:2 * half]
            d3 = blk_stride_dst3
            t = osb[0:nr, :, :].rearrange("p a b -> p (a b)")
            n = nblk * half
            tt = [t[:, k * n:(k + 1) * n].rearrange("p (b d) -> p b d", b=nblk) for k in range(4)]
            btab = b_ropeA if tab is ropeA else b_ropeC
            fw.op("dve", lambda q: q.tensor_tensor(out=tt[0], in0=x1, in1=cosb, op=ALU.mult), reads=[b_src, btab], writes=b_osb)
            fw.op("dve", lambda q: q.tensor_tensor(out=tt[1], in0=x2, in1=sinb, op=ALU.mult), reads=[b_src, btab], writes=b_osb)
            fw.op("dve", lambda q: q.tensor_tensor(out=tt[2], in0=x2, in1=cosb, op=ALU.mult), reads=[b_src, btab], writes=b_osb)
            fw.op("dve", lambda q: q.tensor_tensor(out=tt[3], in0=x1, in1=sinb, op=ALU.mult), reads=[b_src, btab], writes=b_osb)
            fw.op("dve", lambda q: q.tensor_tensor(out=d3[:, :, 0:half], in0=tt[0], in1=tt[1], op=ALU.subtract), reads=b_osb, writes=[b_dst])
            fw.op("dve", lambda q: q.tensor_tensor(out=d3[:, :, half:2 * half], in0=tt[2], in1=tt[3], op=ALU.add), reads=b_osb, writes=[b_dst])

        def run_units(units, sbanks=((0,), (1,))):
            pend = None
            for ui, u in enumerate(units):
                sb_ = sbanks[ui % 2]
                nk = u["nk"]
                used = sorted(set(x[0] for x in u["qk"]))
                for bsel in used:
                    bk_ = sb_[bsel]
                    fns = []
                    for (bs_, c0, n, lhsT, rhs) in u["qk"]:
                        if bs_ == bsel:
                            fns.append(lambda q, c0=c0, n=n, lhsT=lhsT, rhs=rhs: q.matmul(bank[bk_][0:nk, c0:c0 + n], lhsT=lhsT, rhs=rhs, start=True, stop=False,
                                                                                          skip_group_check=True))
                    for (bs_, c0, n, rhs_t) in u["extra"]:
                        if bs_ == bsel:
                            fns.append(lambda q, c0=c0, n=n, rhs_t=rhs_t: q.matmul(bank[bk_][0:nk, c0:c0 + n], lhsT=ident[0:nk, 0:nk], rhs=rhs_t, start=False, stop=True,
                                                                                   skip_group_check=True))
                    fw.op("pe", fns, reads=u["rd"] + [b_ident, b_negq], writes=[b_bank[bk_]])
                pi = ui % 3
                for (bsel, c0, n, ptc0) in u["exps"]:
                    bk_ = sb_[bsel]
                    fw.op("act", lambda q, c0=c0, n=n, ptc0=ptc0: q.activation(out=PT[pi][0:nk, ptc0:ptc0 + n], in_=bank[bk_][0:nk, c0:c0 + n], func=AF.Exp),
                          reads=[b_bank[bk_]], writes=[b_PT[pi]])
                if pend is not None:
                    pend()

                def pvs(u=u, pi=pi, nk=nk):
                    obufs = []
                    fns = []
                    for (O_ap, pc0, nq, V_ap, start, bO) in u["pv"]:
                        fns.append(lambda q, O_ap=O_ap, pc0=pc0, nq=nq, V_ap=V_ap, start=start: q.matmul(
                            O_ap, lhsT=PT[pi][0:nk, pc0:pc0 + nq], rhs=V_ap, start=start, stop=True, skip_group_check=True))
                        if bO not in obufs:
                            obufs.append(bO)
                    if DEV_STOP >= 3.2:
                        fw.op("pe", fns, reads=[b_PT[pi]] + u["rdv"], writes=obufs)
                    if u.get("fin"):
                        u["fin"]()
                pend = pvs
            if pend is not None:
                pend()

        def stage_PA(seq, b):
            pre = "p" if seq.kind == "p" else "s"
            if DEV_STOP >= 2.1:
                set_ones(4, 128)
            if seq.kind == "s" and DEV_STOP >= 2.2:
                for (vt, kc0, nk, is_cache) in seq.keys:
                    if not is_cache:
                        continue
                    i = nxt("stg", 3)
                    fw.dma("sp", stg[i][:, :], I["cache_a_k"][b, kc0:kc0 + 128, :], writes=[b_stg[i]])
                    ingest_k(stg[i][:, :], b_stg[i], 128, kc0, vt, eng=cast_eng())()
                    i = nxt("stg", 3)
                    fw.dma("sp", stg[i][:, :], I["cache_a_v"][b, kc0:kc0 + 128, :], writes=[b_stg[i]])
                    ingest_v(stg[i][:, :], b_stg[i], 128, vt, 4, 128, eng=cast_eng())
            knew0 = 0 if seq.kind == "p" else 2048
            vnew0 = 0 if seq.kind == "p" else 16

            def cons_q(ti, r0, nr, rt, bk, bbk):
                if DEV_STOP < 2.31:
                    return []
                i = nxt("stg", 3)
                ecopy("act", stg[i][0:nr, :], bk[0:nr, :], [bbk], [b_stg[i]])
                if DEV_STOP < 2.32:
                    return []
                rope_ops(None, stg[i][0:nr, :].rearrange("p (b d) -> p b d", b=8), nr, rt, ropeA, 8, 8, b_stg[i], b_stg[i],
                         stg[i][0:nr, :].rearrange("p (b d) -> p b d", b=8))
                if DEV_STOP < 2.33:
                    return []
                return [ingest_q(stg[i][0:nr, :], b_stg[i], nr, r0, ti, 0.125, eng="dve")]

            def cons_k(ti, r0, nr, rt, bk, bbk):
                i = nxt("stg", 3)
                ecopy("act", stg[i][0:nr, :], bk[0:nr, :], [bbk], [b_stg[i]])
                rope_ops(None, stg[i][0:nr, :].rearrange("p (b d) -> p b d", b=8), nr, rt, ropeA, 8, 8, b_stg[i], b_stg[i],
                         stg[i][0:nr, :].rearrange("p (b d) -> p b d", b=8))
                store_out(O["a_k_" + pre][b, r0:r0 + nr, :], stg[i][0:nr, :], b_stg[i])
                return [ingest_k(stg[i][0:nr, :], b_stg[i], nr, knew0 + r0, vnew0 + ti, eng="dve")]

            def cons_v(ti, r0, nr, rt, bk, bbk):
                i = nxt("stg", 3)
                ecopy("act", stg[i][0:nr, :], bk[0:nr, :], [bbk], [b_stg[i]])
                store_out(O["a_v_" + pre][b, r0:r0 + nr, :], stg[i][0:nr, :], b_stg[i])
                ingest_v(stg[i][0:nr, :], b_stg[i], nr, vnew0 + ti, 4, 128, eng="dve")
                return []

            if DEV_STOP >= 2.3:
                proj_block(seq, S_w0, 0, 512, cons_q)
            if DEV_STOP >= 2.4:
                proj_block(seq, S_w0, 512, 512, cons_k)
            if DEV_STOP >= 2.5:
                proj_block(seq, S_w0, 1024, 512, cons_v)

        def stage_QA(seq):
            nt = len(seq.tiles)
            bq = 2 if seq.kind == "p" else 1
            blk = 0
            for h in range(4):
                for qb0 in range(0, nt, bq):
                    qts = list(range(qb0, min(nt, qb0 + bq)))
                    nqs = [seq.tiles[j][1] for j in qts]
                    bqn = sum(nqs)
                    qcol0 = seq.tiles[qb0][0]
                    ob = (4, 5) if blk % 2 == 0 else (6, 7)
                    blk += 1
                    units = []
                    started = [False, False]
                    for ki, (vt, kc0, nk, is_cache) in enumerate(seq.keys):
                        if seq.kind == "p":
                            valid = [j for j in qts if ki <= j]
                        else:
                            valid = qts
                        if not valid:
                            continue
                        j0 = valid[0] - qb0
                        off0 = j0 * 128
                        nv = sum(seq.tiles[j][1] for j in valid)
                        u = dict(nk=nk, qk=[], extra=[], exps=[], pv=[], rd=[b_KT[vt]] + [b_QT[j] for j in valid], rdv=[b_V[vt]])
                        for m in range(2):
                            u["qk"].append((m, off0, nv, KT[m * 64:(m + 1) * 64, h, kc0:kc0 + nk], QT[m * 64:(m + 1) * 64, h, qcol0 + off0:qcol0 + off0 + nv]))
                            for j in valid:
                                if seq.kind == "p" and ki == j:
                                    u["extra"].append((m, (j - qb0) * 128, seq.tiles[j][1], negq[0:nk, 0:seq.tiles[j][1]]))
                            u["exps"].append((m, off0, nv, m * bqn + off0))
                        for m in range(2):
                            for j in valid:
                                jj = j - qb0
                                nq = seq.tiles[j][1]
                                u["pv"].append((bank[ob[m]][0:nq, jj * 129:(jj + 1) * 129], m * bqn + jj * 128, nq,
                                                VR[0:nk, vt, h * 129:(h + 1) * 129], not started[m], b_bank[ob[m]]))
                                started[m] = True
                        units.append(u)

                    def fin(h=h, qts=qts, qb0=qb0, ob=ob):
                        for j in qts:
                            jj = j - qb0
                            nq = seq.tiles[j][1]
                            o0 = bank[ob[0]][0:nq, jj * 129:jj * 129 + 128]
                            o1 = bank[ob[1]][0:nq, jj * 129:jj * 129 + 128]
                            kb = 8 + 4 * (jj % 2)
                            bs = smb(kb)
                            fw.op("dve", lambda q: q.reciprocal(out=sm[0:nq, kb:kb + 1], in_=bank[ob[0]][0:nq, jj * 129 + 128:jj * 129 + 129]), reads=[b_bank[ob[0]]], writes=[bs])
                            fw.op("dve", lambda q: q.reciprocal(out=sm[0:nq, kb + 1:kb + 2], in_=bank[ob[1]][0:nq, jj * 129 + 128:jj * 129 + 129]), reads=[b_bank[ob[1]]], writes=[bs])
                            fw.op("dve", lambda q: q.tensor_tensor(out=sm[0:nq, kb + 1:kb + 2], in0=sm[0:nq, kb + 1:kb + 2], in1=nlam[0:nq, :], op=ALU.mult), reads=[bs, b_lamv], writes=[bs])
                            ot = osb[0:nq, 1 + (jj % 2), :]
                            bo = b_osb[1 + (jj % 2)]
                            fw.op("dve", lambda q: q.tensor_scalar(out=ot, in0=o0, scalar1=sm[0:nq, kb:kb + 1], scalar2=None, op0=ALU.mult), reads=[b_bank[ob[0]], bs], writes=[bo])
                            fw.op("dve", lambda q: q.scalar_tensor_tensor(out=ot, in0=o1, scalar=sm[0:nq, kb + 1:kb + 2], in1=ot, op0=ALU.mult, op1=ALU.add),
                                  reads=[b_bank[ob[1]], bs, bo], writes=[bo])
                            fw.op("dve", lambda q: q.scalar_tensor_tensor(out=osb[0:nq, 3, :], in0=ot, scalar=1.0 / 128, in1=ot, op0=ALU.mult, op1=ALU.mult,
                                                                          accum_out=sm[0:nq, kb + 2:kb + 3]), reads=[bo], writes=[b_osb[3], bs])
                            rsqrt_col(sm[0:nq, kb + 2:kb + 3], nq, bs)
                            fw.op("dve", lambda q: q.scalar_tensor_tensor(out=oall[0:nq, j, h * 128:(h + 1) * 128], in0=ot, scalar=sm[0:nq, kb + 2:kb + 3], in1=gsub[0:nq, :],
                                                                          op0=ALU.mult, op1=ALU.mult), reads=[bo, bs, b_gsub], writes=[b_o[j]])
                    if DEV_STOP >= 3.4:
                        units[-1]["fin"] = fin
                    if DEV_STOP < 3.3:
                        units = units[:1]
                    run_units(units, sbanks=((0, 1), (2, 3)))
                    if DEV_STOP < 3.35:
                        return

        def stage_PB(seq, b):
            pre = "p" if seq.kind == "p" else "s"
            set_ones(8, 64)
            if seq.kind == "s":
                for (vt, kc0, nk, is_cache) in seq.keys_b:
                    if not is_cache:
                        continue
                    i = nxt("stg", 3)
                    fw.dma("sp", stg[i][:, :], I["cache_b_k"][b, kc0:kc0 + 128, :], writes=[b_stg[i]])
                    ingest_k(stg[i][:, :], b_stg[i], 128, kc0, vt, eng=cast_eng())()
                    i = nxt("stg", 3)
                    fw.dma("sp", stg[i][:, :], I["cache_b_v"][b, kc0:kc0 + 128, :], writes=[b_stg[i]])
                    ingest_v(stg[i][:, :], b_stg[i], 128, vt, 8, 64, eng=cast_eng())
            knew0 = 0 if seq.kind == "p" else 512
            vnew0 = 0 if seq.kind == "p" else 4

            def cons_q(ti, r0, nr, rt, bk, bbk):
                return [ingest_q(bk[0:nr, :], bbk, nr, r0, ti, 0.125, eng="dve")]

            def cons_k(ti, r0, nr, rt, bk, bbk):
                if seq.kind == "s" or r0 >= T - 512:
                    i = nxt("stg", 3)
                    ecopy("act", stg[i][0:nr, :], bk[0:nr, :], [bbk], [b_stg[i]])
                    orow = r0 - (T - 512) if seq.kind == "p" else r0
                    store_out(O["b_k_" + pre][b, orow:orow + nr, :], stg[i][0:nr, :], b_stg[i])
                return [ingest_k(bk[0:nr, :], bbk, nr, knew0 + r0, vnew0 + ti, eng="dve")]

            def cons_v(ti, r0, nr, rt, bk, bbk):
                if seq.kind == "s" or r0 >= T - 512:
                    i = nxt("stg", 3)
                    ecopy("act", stg[i][0:nr, :], bk[0:nr, :], [bbk], [b_stg[i]])
                    orow = r0 - (T - 512) if seq.kind == "p" else r0
                    store_out(O["b_v_" + pre][b, orow:orow + nr, :], stg[i][0:nr, :], b_stg[i])
                ingest_v(bk[0:nr, :], bbk, nr, vnew0 + ti, 8, 64, eng="dve")
                return []

            proj_block(seq, S_w0, 1536, 512, cons_q)
            proj_block(seq, S_w0, 2048, 512, cons_k)
            proj_block(seq, S_w0, 2560, 512, cons_v)

        def stage_QB(seq):
            blk = 0
            for j, (r0, nq, rt) in enumerate(seq.tiles):
                ob = (2, 3) if j % 2 == 0 else (4, 5)
                pairs = []
                for h in (0, 2, 4, 6, 1, 3, 5, 7):
                    if seq.kind == "p":
                        for d in (4, 3, 2, 1, 0):
                            ki = j - d
                            if ki < 0:
                                continue
                            ex = None
                            if d == 0:
                                ex = Tb[:, h * 2 + 0, :]
                            elif d == 1:
                                ex = Tb[:, h * 2 + 1, :]
                            elif d == 4:
                                ex = negq4[:, :]
                            pairs.append((h, ki, ex))
                    else:
                        for ki in range(5):
                            ex = None
                            if ki == 3:
                                ex = Tb[:, h * 2 + 1, :]
                            elif ki == 4:
                                ex = Tb[:, h * 2 + 0, :]
                            pairs.append((h, ki, ex))
                units = []
                started = [False, False]
                slotw = 128 if seq.kind == "p" else nq
                for p0 in range(0, len(pairs), 4):
                    grp = pairs[p0:p0 + 4]
                    nkmax = max(seq.keys_b[ki][2] for (_, ki, _) in grp)
                    u = dict(nk=nkmax, qk=[], extra=[], exps=[], pv=[], rd=[b_QT[j]], rdv=[])
                    same = all(seq.keys_b[ki][2] == nkmax for (_, ki, _) in grp)
                    for s, (h, ki, ex) in enumerate(grp):
                        vt, kc0, nk, _c = seq.keys_b[ki]
                        u["qk"].append((s * slotw, nq, KT[(h % 2) * 64:(h % 2 + 1) * 64, h // 2, kc0:kc0 + nk], QT[(h % 2) * 64:(h % 2 + 1) * 64, h // 2, r0:r0 + nq], nk))
                        if ex is not None:
                            u["extra"].append((s * slotw, nq, ex[0:nk, 0:nq], nk))
                        u["pv"].append((bank[ob[h // 4]][0:nq, (h % 4) * 65:(h % 4 + 1) * 65], s * slotw, nq, VR[0:nk, vt, h * 65:(h + 1) * 65],
                                        not started[h // 4], b_bank[ob[h // 4]], nk))
                        started[h // 4] = True
                        u["rd"].append(b_KT[vt])
                        u["rdv"].append(b_V[vt])
                        u["exps"].append((s * slotw, nq, nk))
                    if same:
                        u["exps"] = [(0, (len(grp) - 1) * slotw + nq, nkmax)]
                    units.append(u)

                def fin(j=j, nq=nq, ob=ob):
                    for g in range(2):
                        kb = 16 + 4 * g
                        bs = smb(kb)
                        fw.op("dve", lambda q: q.reciprocal(out=sm[0:nq, kb:kb + 4], in_=bank[ob[g]][0:nq, 0:260].rearrange("p (h e) -> p h e", e=65)[:, :, 64]),
                              reads=[b_bank[ob[g]]], writes=[bs])
                        for hh in range(4):
                            h = g * 4 + hh
                            osl = oall[0:nq, j, 512 + h * 64:512 + (h + 1) * 64]
                            fw.op("dve", lambda q, osl=osl, hh=hh: q.scalar_tensor_tensor(out=osl, in0=bank[ob[g]][0:nq, hh * 65:hh * 65 + 64], scalar=sm[0:nq, kb + hh:kb + hh + 1],
                                                                                          in1=osl, op0=ALU.mult, op1=ALU.mult), reads=[b_bank[ob[g]], bs, b_o[j]], writes=[b_o[j]])
                units[-1]["fin"] = fin
                run_units_nk(units)

        def run_units_nk(units):
            pend = None
            for ui, u in enumerate(units):
                sbk = ui % 2
                fns = []
                exd = {c0: (n, rhs_t, nk) for (c0, n, rhs_t, nk) in u["extra"]}
                for (c0, n, lhsT, rhs, nk) in u["qk"]:
                    fns.append(lambda q, c0=c0, n=n, lhsT=lhsT, rhs=rhs, nk=nk: q.matmul(bank[sbk][0:nk, c0:c0 + n], lhsT=lhsT, rhs=rhs, start=True, stop=False,
                                                                                         skip_group_check=True))
                    if c0 in exd:
                        n2, rhs_t, nk2 = exd[c0]
                        fns.append(lambda q, c0=c0, n2=n2, rhs_t=rhs_t, nk2=nk2: q.matmul(bank[sbk][0:nk2, c0:c0 + n2], lhsT=ident[0:nk2, 0:nk2], rhs=rhs_t, start=False, stop=True,
                                                                                          skip_group_check=True))
                fw.op("pe", fns, reads=u["rd"] + [b_ident, b_negq4, b_Tb], writes=[b_bank[sbk]])
                pi = ui % 3
                for (c0, n, nk) in u["exps"]:
                    fw.op("act", lambda q, c0=c0, n=n, nk=nk: q.activation(out=PT[pi][0:nk, c0:c0 + n], in_=bank[sbk][0:nk, c0:c0 + n], func=AF.Exp),
                          reads=[b_bank[sbk]], writes=[b_PT[pi]])
                if pend is not None:
                    pend()
                if u.get("hook"):
                    u["hook"]()

                def pvs(u=u, pi=pi):
                    obufs = []
                    fns = []
                    for (O_ap, pc0, nq, V_ap, start, bO, nk) in u["pv"]:
                        fns.append(lambda q, O_ap=O_ap, pc0=pc0, nq=nq, V_ap=V_ap, start=start, nk=nk: q.matmul(
                            O_ap, lhsT=PT[pi][0:nk, pc0:pc0 + nq], rhs=V_ap, start=start, stop=True, skip_group_check=True))
                        if bO not in obufs:
                            obufs.append(bO)
                    fw.op("pe", fns, reads=[b_PT[pi]] + u["rdv"], writes=obufs)
                    if u.get("fin"):
                        u["fin"]()
                pend = pvs
            if pend is not None:
                pend()

        def stage_gate(seq, scr_w, gate_c0, half, premul):
            def cons(ti, r0, nr, rt, bk, bbk):
                k = nxt("tb16", 3)
                fw.op("act", lambda q: q.activation(out=tb16[k][0:nr, :], in_=bk[0:nr, :], func=AF.Tanh, scale=0.5), reads=[bbk], writes=[b_tb16[k]])
                fw.op("dve", lambda q: q.scalar_tensor_tensor(out=tb16[k][0:nr, :], in0=tb16[k][0:nr, :], scalar=1.0, in1=bk[0:nr, :], op0=ALU.add, op1=ALU.mult),
                      reads=[b_tb16[k], bbk], writes=[b_tb16[k]])
                osl = oall[0:nr, ti, half * 512:(half + 1) * 512]
                if premul:
                    fw.op("dve", lambda q: q.scalar_tensor_tensor(out=osl, in0=tb16[k][0:nr, :], scalar=0.5, in1=osl, op0=ALU.mult, op1=ALU.mult),
                          reads=[b_o[ti], b_tb16[k]], writes=[b_o[ti]])
                else:
                    fw.op("dve", lambda q: q.tensor_scalar(out=osl, in0=tb16[k][0:nr, :], scalar1=0.5, scalar2=None, op0=ALU.mult), reads=[b_tb16[k]], writes=[b_o[ti]])
                return []
            proj_block(seq, scr_w, gate_c0 + half * 512, 512, cons)

        def stage_GY(seq, b, resid_src, dst, dst_is_output, src_bufs=()):
            xsel = {}

            def tphase(j):
                r0, nr, rt = seq.tiles[j]
                i = nxt("xt", 2)
                xsel[j] = i
                fw.dma("sp", xt[i][0:nr, :], resid_src[r0:r0 + nr, :], reads=list(src_bufs), writes=[b_xt[i]], sembuf=b_xt[i])
                if DEV_DBG and seq.idx == 0:
                    fw.op("dve", lambda q: q.tensor_copy(out=ytmp[0:nr, :], in_=oall[0:nr, j, :]), reads=[b_o[j]], writes=[b_ytmp])
                    fw.dma("pool", O["dbg"][r0:r0 + nr, :], ytmp[0:nr, :], reads=[b_ytmp], is_output=True, sembuf=b_ytmp)
                k = j % 2
                fw.op("pe", [lambda q, c=c: q.transpose(out=tbk[k][:, c * 128:c * 128 + nr], in_=oall[0:nr, j, c * 128:(c + 1) * 128], identity=ident[0:nr, 0:nr])
                             for c in range(8)], reads=[b_o[j], b_ident], writes=[b_tbk[k]])
                g = j % 2
                ecopy("act", ogT[g][:, 0:4, 0:nr], tbk[k][:, 0:512].rearrange("p (c t) -> p c t", c=4)[:, :, 0:nr], [b_tbk[k]], [b_ogT[g]])
                ecopy("dve", ogT[g][:, 4:8, 0:nr], tbk[k][:, 512:1024].rearrange("p (c t) -> p c t", c=4)[:, :, 0:nr], [b_tbk[k]], [b_ogT[g]])

            def yphase(j):
                r0, nr, rt = seq.tiles[j]
                i = xsel[j]
                g = j % 2
                yb = ((4, 5), (2, 3))[j % 2]
                for half in range(2):
                    fw.op("pe", [lambda q, c=c: q.matmul(bank[yb[half]][0:nr, :], lhsT=ogT[g][:, c, 0:nr], rhs=wout[:, c, half * 512:(half + 1) * 512],
                                                         start=(c == 0), stop=(c == 7)) for c in range(8)],
                          reads=[b_ogT[g], b_wout], writes=[b_bank[yb[half]]])
                rs, bs = rms_scale([bank[yb[0]][0:nr, :], bank[yb[1]][0:nr, :]], nr, DM, [b_bank[yb[0]], b_bank[yb[1]]], 24 + 2 * (j % 2))
                for half in range(2):
                    fw.op("dve", lambda q: q.scalar_tensor_tensor(out=ytmp[0:nr, half * 512:(half + 1) * 512], in0=bank[yb[half]][0:nr, :], scalar=rs,
                                                                  in1=gpost[0:nr, half * 512:(half + 1) * 512], op0=ALU.mult, op1=ALU.mult),
                          reads=[b_bank[yb[half]], bs, b_gpost], writes=[b_ytmp])
                fw.op("dve", lambda q: q.tensor_tensor(out=xt[i][0:nr, 0:512], in0=ytmp[0:nr, 0:512], in1=xt[i][0:nr, 0:512], op=ALU.add), reads=[b_ytmp, b_xt[i]], writes=[b_xt[i]])
                fw.op("pool", lambda q: q.tensor_tensor(out=xt[i][0:nr, 512:1024], in0=ytmp[0:nr, 512:1024], in1=xt[i][0:nr, 512:1024], op=ALU.add), reads=[b_ytmp, b_xt[i]], writes=[b_xt[i]])
                if dst_is_output:
                    fw.dma("pool", dst[r0:r0 + nr, :], xt[i][0:nr, :], reads=[b_xt[i]], is_output=True, sembuf=b_xout[i])
                else:
                    h1_evs.setdefault((seq.kind, b), []).append(
                        fw.dma("pool", dst[r0:r0 + nr, :], xt[i][0:nr, :], reads=[b_xt[i]], writes=[dbuf(("h1", seq.kind, b))], sembuf=b_xout[i]))

            n = len(seq.tiles)
            tphase(0)
            for j in range(n):
                if j + 1 < n:
                    tphase(j + 1)
                yphase(j)

        def stage_PC(seq, b):
            pre = "p" if seq.kind == "p" else "s"
            SC = (64 + 32) ** -0.5
            set_ones(4, 64)
            knew0 = 0 if seq.kind == "p" else 2048
            vnew0 = 0 if seq.kind == "p" else 16
            spill_ev = []
            slot_ctr = {"i": 0}

            def next_slot():
                i = slot_ctr["i"] % 13
                slot_ctr["i"] += 1
                return i

            wq1 = oall[:, 13:16, :].rearrange("p a b -> p (a b)")[:, 0:2304].rearrange("p (c n) -> p c n", c=6)
            bwq1 = [b_o[13], b_o[14], b_o[15]]
            wsrc = S_wuq.rearrange("(c p) n -> p c n", p=128)
            fw.dma("sp", wqb[:, :, :], wsrc[:, :, 0:384], reads=[dbuf(S_wuq.name)], writes=[b_wqb])
            fw.dma("sp", wq1, wsrc[:, :, 384:768], reads=[dbuf(S_wuq.name)], writes=bwq1, sembuf=b_o[13])

            def lat_ingest(src_lat, b_lat, src_kr, b_kr, nr, kcol0, vt, eng):
                j = nxt("tb16", 3)
                ecopy(eng, tb16[j][0:nr, 0:256], src_lat, [b_lat], [b_tb16[j]])
                ecopy(eng, tb16[j][0:nr, 256:288], src_kr, [b_kr], [b_tb16[j]])
                k = nxt("tbk", 2)
                fw.op("pe", [lambda q, c=c: q.transpose(out=tbk[k][:, c * 128:c * 128 + nr], in_=tb16[j][0:nr, c * 128:(c + 1) * 128], identity=ident[0:nr, 0:nr]) for c in range(2)]
                      + [lambda q: q.transpose(out=tbk[k][0:32, 256:256 + nr], in_=tb16[j][0:nr, 256:288], identity=ident[0:nr, 0:nr])],
                      reads=[b_tb16[j], b_ident], writes=[b_tbk[k]])
                ecopy("dve", latT[:, :, 0:nr], tbk[k][:, 0:256].rearrange("p (c t) -> p c t", c=2)[:, :, 0:nr], [b_tbk[k]], [b_latT])
                ecopy("act", KT[64:96, 0:4, kcol0:kcol0 + nr], tbk[k][0:32, 256:256 + nr].unsqueeze(1).to_broadcast([32, 4, nr]), [b_tbk[k]], [b_KT[vt]])
                for g in range(2):
                    fns = []
                    for hh in range(4):
                        h = g * 4 + hh
                        for c in range(2):
                            fns.append(lambda q, hh=hh, h=h, c=c: q.matmul(bank[g][0:64, hh * 128:hh * 128 + nr], lhsT=wuk[:, c, h * 64:(h + 1) * 64], rhs=latT[:, c, 0:nr],
                                                                           start=(c == 0), stop=(c == 1), skip_group_check=True))
                    fw.op("pe", fns, reads=[b_latT, b_wuk], writes=[b_bank[g]])
                ecopy("dve", KT[0:64, 0:4, kcol0:kcol0 + nr], bank[0][0:64, :].rearrange("p (h t) -> p h t", h=4)[:, :, 0:nr], [b_bank[0]], [b_KT[vt]])
                sk = next_slot()
                kst = oall[0:64, sk, 0:512].rearrange("p (h t) -> p h t", h=4)[:, :, 0:nr]
                ecopy("act", kst, bank[1][0:64, :].rearrange("p (h t) -> p h t", h=4)[:, :, 0:nr], [b_bank[1]], [b_o[sk]])
                spill_ev.append(fw.dma("pool", S_kt1[:, :, kcol0:kcol0 + nr], kst, reads=[b_o[sk]], writes=[dbuf("kt1")], sembuf=b_o[sk]))
                fw.op("pe", [lambda q, c=c: q.matmul(bank[2][0:nr, 0:512], lhsT=latT[:, c, 0:nr], rhs=wuv[:, c, :], start=(c == 0), stop=(c == 1))
                             for c in range(2)], reads=[b_latT, b_wuv], writes=[b_bank[2]])
                ingest_v(bank[2][0:nr, 0:256], b_bank[2], nr, vt, 4, 64, eng="act")
                sv = next_slot()
                ecopy("dve", oall[0:nr, sv, 0:256], bank[2][0:nr, 256:512], [b_bank[2]], [b_o[sv]])
                spill_ev.append(fw.dma("pool", S_v1[vt, 0:nr, :], oall[0:nr, sv, 0:256], reads=[b_o[sv]], writes=[dbuf("v1")], sembuf=b_o[sv]))

            if seq.kind == "s":
                for (vt, kc0, nk, is_cache) in seq.keys:
                    if not is_cache:
                        continue
                    i = nxt("stg", 3)
                    fw.dma("sp", stg[i][:, 0:256], I["cache_c_latent"][b, kc0:kc0 + 128, :], writes=[b_stg[i]])
                    fw.dma("sp", stg[i][:, 256:288], I["cache_c_krope"][b, kc0:kc0 + 128, :], writes=[b_stg[i]])
                    lat_ingest(stg[i][:, 0:256], b_stg[i], stg[i][:, 256:288], b_stg[i], 128, kc0, vt, cast_eng())

            def cons_ckv(ti, r0, nr, rt, bk, bbk):
                i = nxt("stg", 3)
                rs, bs = rms_scale(bk[0:nr, 0:256], nr, 256, [bbk], 28)
                ecopy("act", stg[i][0:nr, 256:288], bk[0:nr, 256:288], [bbk], [b_stg[i]])
                fw.op("dve", lambda q: q.scalar_tensor_tensor(out=stg[i][0:nr, 0:256], in0=bk[0:nr, 0:256], scalar=rs, in1=gckv[0:nr, :], op0=ALU.mult, op1=ALU.mult),
                      reads=[bbk, bs, b_gckv], writes=[b_stg[i]])
                rope_ops(None, stg[i][0:nr, 256:288].rearrange("p (b d) -> p b d", b=1), nr, rt, ropeC, 16, 1, b_stg[i], b_stg[i],
                         stg[i][0:nr, 256:288].rearrange("p (b d) -> p b d", b=1))
                store_out(O["c_lat_" + pre][b, r0:r0 + nr, :], stg[i][0:nr, 0:256], b_stg[i])
                store_out(O["c_krope_" + pre][b, r0:r0 + nr, :], stg[i][0:nr, 256:288], b_stg[i])
                return [lambda: lat_ingest(stg[i][0:nr, 0:256], b_stg[i], stg[i][0:nr, 256:288], b_stg[i], nr, knew0 + r0, vnew0 + ti, "dve")]

            proj_block(seq, S_w1, 768, 288, cons_ckv)

            w1a = load_wblock(S_w1, DM, 0, 512)
            w1b = load_wblock(S_w1, DM, 512, 256)
            def s1(ti):
                r0, nr, rt = seq.tiles[ti]
                ba, bb = ((4, 5), (0, 1))[ti % 2]
                fw.op("pe", [lambda q, c=c: q.matmul(bank[ba][0:nr, :], lhsT=uT[:, c, r0:r0 + nr], rhs=w1a[0][:, c, :], start=(c == 0), stop=(c == 7)) for c in range(8)],
                      reads=[b_uT[ti], w1a[1]], writes=[b_bank[ba]])
                fw.op("pe", [lambda q, c=c: q.matmul(bank[bb][0:nr, 0:256], lhsT=uT[:, c, r0:r0 + nr], rhs=w1b[0][:, c, 0:256], start=(c == 0), stop=(c == 7)) for c in range(8)],
                      reads=[b_uT[ti], w1b[1]], writes=[b_bank[bb]])

            stgsel = {}

            def s2(ti):
                r0, nr, rt = seq.tiles[ti]
                ba, bb = ((4, 5), (0, 1))[ti % 2]
                rs, bs = rms_scale([bank[ba][0:nr, :], bank[bb][0:nr, 0:256]], nr, 768, [b_bank[ba], b_bank[bb]], 32 + 2 * (ti % 2))
                g = ti % 2
                cq16 = ogT[g][:, :, :].rearrange("p c t -> p (c t)")
                fw.op("act", lambda q: q.activation(out=cq16[0:nr, 0:512], in_=bank[ba][0:nr, :], func=AF.Copy, scale=rs), reads=[b_bank[ba], bs], writes=[b_ogT[g]])
                fw.op("act", lambda q: q.activation(out=cq16[0:nr, 512:768], in_=bank[bb][0:nr, 0:256], func=AF.Copy, scale=rs), reads=[b_bank[bb], bs], writes=[b_ogT[g]])
                k = nxt("tbk", 2)
                fw.op("pe", [lambda q, c=c: q.transpose(out=tbk[k][:, c * 128:c * 128 + nr], in_=cq16[0:nr, c * 128:(c + 1) * 128], identity=ident[0:nr, 0:nr]) for c in range(6)],
                      reads=[b_ogT[g], b_ident], writes=[b_tbk[k]])
                cqT = Ef[g][:, :].bitcast(BF16).rearrange("p (c t) -> p c t", c=8)
                ecopy("dve", cqT[:, 0:6, 0:nr], tbk[k][:, 0:768].rearrange("p (c t) -> p c t", c=6)[:, :, 0:nr], [b_tbk[k]], [b_Ef[g]])
                for grp in range(2):
                    qbk = (3, 2)[grp]
                    wq_t, wq_b = (wqb, [b_wqb]) if grp == 0 else (wq1, bwq1)
                    fw.op("pe", [lambda q, c=c: q.matmul(bank[qbk][0:nr, 0:384], lhsT=cqT[:, c, 0:nr], rhs=wq_t[:, c, 0:384], start=(c == 0), stop=(c == 5)) for c in range(6)],
                          reads=[b_Ef[g]] + wq_b, writes=[b_bank[qbk]])
                    i = nxt("stg", 3)
                    stgsel[(ti, grp)] = i
                    ecopy("act", stg[i][0:nr, 0:384], bank[qbk][0:nr, 0:384], [b_bank[qbk]], [b_stg[i]])

            def s3(ti):
                r0, nr, rt = seq.tiles[ti]
                for grp in range(2):
                    i = stgsel[(ti, grp)]
                    dst3 = stg[i][0:nr, 0:384].rearrange("p (h d) -> p h d", h=4)[:, :, 64:96]
                    rope_ops(None, dst3, nr, rt, ropeC, 16, 4, b_stg[i], b_stg[i], dst3)
                    if grp == 0:
                        ingest_q(stg[i][0:nr, 0:384], b_stg[i], nr, r0, ti, SC, ncol=384, part=96, eng="dve")()
                    else:
                        j = nxt("tb16", 3)
                        ecopy("dve", tb16[j][0:nr, 0:384], stg[i][0:nr, 0:384], [b_stg[i]], [b_tb16[j]], SC)
                        k2 = nxt("tbk", 2)
                        fw.op("pe", [lambda q, c=c: q.transpose(out=tbk[k2][0:96, c * 128:c * 128 + nr], in_=tb16[j][0:nr, c * 96:(c + 1) * 96], identity=ident[0:nr, 0:nr]) for c in range(4)],
                              reads=[b_tb16[j], b_ident], writes=[b_tbk[k2]])
                        sq = next_slot()
                        qst = oall[0:96, sq, 0:512].rearrange("p (h t) -> p h t", h=4)[:, :, 0:nr]
                        ecopy(copy_eng(), qst, tbk[k2][0:96, 0:512].rearrange("p (c t) -> p c t", c=4)[:, :, 0:nr], [b_tbk[k2]], [b_o[sq]])
                        spill_ev.append(fw.dma("pool", S_qt1[:, :, r0:r0 + nr], qst, reads=[b_o[sq]], writes=[dbuf("qt1")], sembuf=b_o[sq]))

            ntl = len(seq.tiles)
            s1(0)
            for ti in range(ntl):
                if ti + 1 < ntl:
                    s1(ti + 1)
                if ti >= 1:
                    s3(ti - 1)
                s2(ti)
            s3(ntl - 1)
            return spill_ev

        def reload_C(seq, spill_ev):
            fw.wait_events("sp", spill_ev)
            nq = seq.ntok
            nkc = seq.keys[-1][1] + seq.keys[-1][2]
            fw.dma("sp", QT[0:96, 0:4, 0:nq], S_qt1[:, :, 0:nq], reads=[dbuf("qt1")], writes=b_QT, sembuf=b_QT[0])
            fw.dma("sp", KT[0:64, 0:4, 0:nkc], S_kt1[:, :, 0:nkc], reads=[dbuf("kt1")], writes=b_KT, sembuf=b_KT[0])
            for (vt, kc0, nk, is_cache) in seq.keys:
                fw.dma("sp", VR[0:nk, vt, 0:260].rearrange("p (h e) -> p h e", e=65)[:, :, 0:64], S_v1[vt, 0:nk, :].rearrange("p (h d) -> p h d", h=4),
                       reads=[dbuf("v1")], writes=[b_V[vt]], sembuf=b_V[vt])

        def stage_QC(seq, grp):
            nt = len(seq.tiles)
            bq = 4 if seq.kind == "p" else 1
            blk = 0
            for hh in range(4):
                h = grp * 4 + hh
                for qb0 in range(0, nt, bq):
                    qts = list(range(qb0, min(nt, qb0 + bq)))
                    qcol0 = seq.tiles[qb0][0]
                    ob = 2 + (blk % 2)
                    blk += 1
                    units = []
                    started = False
                    for ki, (vt, kc0, nk, is_cache) in enumerate(seq.keys):
                        valid = [j for j in qts if ki <= j] if seq.kind == "p" else qts
                        if not valid:
                            continue
                        j0 = valid[0] - qb0
                        off0 = j0 * 128
                        nv = sum(seq.tiles[j][1] for j in valid)
                        u = dict(nk=nk, qk=[], extra=[], exps=[(0, off0, nv, off0)], pv=[], rd=[b_KT[vt]] + [b_QT[j] for j in valid], rdv=[b_V[vt]])
                        u["qk"].append((0, off0, nv, KT[0:96, hh, kc0:kc0 + nk], QT[0:96, hh, qcol0 + off0:qcol0 + off0 + nv]))
                        for j in valid:
                            if seq.kind == "p" and ki == j:
                                u["extra"].append((0, (j - qb0) * 128, seq.tiles[j][1], negq[0:nk, 0:seq.tiles[j][1]]))
                        for j in valid:
                            jj = j - qb0
                            nq = seq.tiles[j][1]
                            u["pv"].append((bank[ob][0:nq, jj * 65:(jj + 1) * 65], jj * 128, nq, VR[0:nk, vt, hh * 65:(hh + 1) * 65], not started, b_bank[ob]))
                            started = True
                        units.append(u)

                    def fin(h=h, qts=qts, qb0=qb0, ob=ob):
                        nq = seq.tiles[qts[0]][1]
                        kb = 36
                        bs = smb(kb)
                        nj = len(qts)
                        fw.op("dve", lambda q: q.reciprocal(out=sm[0:nq, kb:kb + nj], in_=bank[ob][0:nq, 0:nj * 65].rearrange("p (j e) -> p j e", e=65)[:, :, 64]),
                              reads=[b_bank[ob]], writes=[bs])
                        for j in qts:
                            jj = j - qb0
                            ecopy("dve", oall[0:nq, j, h * 64:(h + 1) * 64], bank[ob][0:nq, jj * 65:jj * 65 + 64], [b_bank[ob], bs], [b_o[j]], scale=sm[0:nq, kb + jj:kb + jj + 1])
                    units[-1]["fin"] = fin
                    run_units(units)

        def stage_PD(seq, b):
            pre = "p" if seq.kind == "p" else "s"
            set_ones(8, 64)
            if seq.kind == "s":
                for (vt, kc0, nk, is_cache) in seq.keys:
                    if not is_cache:
                        continue
                    i = nxt("stg", 3)
                    fw.dma("sp", stg[i][:, :], I["cache_d_k"][b, kc0:kc0 + 128, :], writes=[b_stg[i]])
                    ingest_k(stg[i][:, :], b_stg[i], 128, kc0, vt, eng=cast_eng())()
                    i = nxt("stg", 3)
                    fw.dma("sp", stg[i][:, :], I["cache_d_v"][b, kc0:kc0 + 128, :], writes=[b_stg[i]])
                    ingest_v(stg[i][:, :], b_stg[i], 128, vt, 8, 64, eng=cast_eng())
            knew0 = 0 if seq.kind == "p" else 2048
            vnew0 = 0 if seq.kind == "p" else 16

            def cons_q(ti, r0, nr, rt, bk, bbk):
                return [ingest_q(bk[0:nr, :], bbk, nr, r0, ti, 0.125, eng="dve")]

            def cons_k(ti, r0, nr, rt, bk, bbk):
                i = nxt("stg", 3)
                ecopy("act", stg[i][0:nr, :], bk[0:nr, :], [bbk], [b_stg[i]])
                store_out(O["d_k_" + pre][b, r0:r0 + nr, :], stg[i][0:nr, :], b_stg[i])
                return [ingest_k(bk[0:nr, :], bbk, nr, knew0 + r0, vnew0 + ti, eng="dve")]

            def cons_v(ti, r0, nr, rt, bk, bbk):
                i = nxt("stg", 3)
                ecopy("act", stg[i][0:nr, :], bk[0:nr, :], [bbk], [b_stg[i]])
                store_out(O["d_v_" + pre][b, r0:r0 + nr, :], stg[i][0:nr, :], b_stg[i])
                ingest_v(bk[0:nr, :], bbk, nr, vnew0 + ti, 8, 64, eng="dve")
                return []

            proj_block(seq, S_w1, 1056, 512, cons_q)
            proj_block(seq, S_w1, 1568, 512, cons_k)
            proj_block(seq, S_w1, 2080, 512, cons_v)

        def stage_QD(seq):
            nt = len(seq.tiles)
            bq = 4 if seq.kind == "p" else 1
            xbanks = (0, 1, 3)
            st_ = {"u": 0, "cs": 0}
            prev_tiles = []
            for qb0 in range(0, nt, bq):
                qts = list(range(qb0, min(nt, qb0 + bq)))
                qcol0 = seq.tiles[qb0][0]
                for h in range(8):
                    hp = (h % 2) * 64
                    fw.op("pool", lambda q: q.memset(osb[:, :, 0:64], 0.0), writes=b_osb)
                    pend_a = None
                    pend_b = None
                    for ki, (vt, kc0, nk, is_cache) in enumerate(seq.keys):
                        valid = [j for j in qts if ki <= j] if seq.kind == "p" else qts
                        if not valid:
                            continue
                        j0 = valid[0] - qb0
                        off0 = j0 * 128
                        nv = sum(seq.tiles[j][1] for j in valid)
                        ucount = st_["u"]
                        st_["u"] += 1
                        xb = xbanks[ucount % 3]
                        eb = ucount % 2
                        pi = ucount % 3
                        diag = [j for j in valid if (seq.kind == "p" and ki == j) or (seq.kind == "s" and not is_cache)]
                        fw.op("pe", lambda q: q.matmul(bank[xb][0:nk, off0:off0 + nv], lhsT=KT[hp:hp + 64, h // 2, kc0:kc0 + nk], rhs=QT[hp:hp + 64, h // 2, qcol0 + off0:qcol0 + off0 + nv],
                                                       start=True, stop=False, skip_group_check=True), reads=[b_KT[vt]] + [b_QT[j] for j in valid], writes=[b_bank[xb]])
                        fw.op("act", lambda q: q.activation(out=Ef[eb][0:nk, off0:off0 + nv], in_=bank[xb][0:nk, off0:off0 + nv], func=AF.Exp), reads=[b_bank[xb]], writes=[b_Ef[eb]])
                        fw.op("act", lambda q: q.activation(out=SPb[eb][0:nk, off0:off0 + nv], in_=Ef[eb][0:nk, off0:off0 + nv], func=AF.Ln, bias=1.0), reads=[b_Ef[eb]], writes=[b_SPb[eb]])
                        for j in diag:
                            c = (j - qb0) * 128
                            nq = seq.tiles[j][1]
                            fw.op("pool", lambda q: q.tensor_tensor(out=SPb[eb][0:nk, c:c + nq], in0=SPb[eb][0:nk, c:c + nq], in1=m01d[0:nk, 0:nq], op=ALU.mult),
                                  reads=[b_SPb[eb], b_m01d], writes=[b_SPb[eb]])

                        def stage2a(vt=vt, nk=nk, valid=valid, off0=off0, nv=nv, xb=xb, eb=eb, pi=pi, diag=diag, qts=qts, qb0=qb0):
                            fns = [lambda q: q.matmul(bank[xb][0:nk, off0:off0 + nv], lhsT=nut[0:nk, 0:nk], rhs=SPb[eb][0:nk, off0:off0 + nv], start=False, stop=False, skip_group_check=True)]
                            for j in diag:
                                c = (j - qb0) * 128
                                nq = seq.tiles[j][1]
                                fns.append(lambda q, c=c, nq=nq: q.matmul(bank[xb][0:nk, c:c + nq], lhsT=ident[0:nk, 0:nk], rhs=negd[0:nk, 0:nq], start=False, stop=True, skip_group_check=True))
                            fw.op("pe", fns, reads=[b_SPb[eb], b_nut, b_negd, b_ident], writes=[b_bank[xb]])
                            fw.op("act", lambda q: q.activation(out=PT[pi][0:nk, off0:off0 + nv], in_=bank[xb][0:nk, off0:off0 + nv], func=AF.Exp), reads=[b_bank[xb]], writes=[b_PT[pi]])

                        def stage2b(vt=vt, nk=nk, valid=valid, pi=pi, qb0=qb0, h=h, qts=qts, ucount=ucount):
                            dk = 40 + 4 * pi
                            bs = smb(dk)
                            wbk = (2, 6)[ucount % 2]
                            fns = []
                            for j in valid:
                                jj = j - qb0
                                nq = seq.tiles[j][1]
                                fns.append(lambda q, jj=jj, nq=nq: q.matmul(bank[wbk][0:nq, jj * 65:(jj + 1) * 65], lhsT=PT[pi][0:nk, jj * 128:jj * 128 + nq], rhs=VR[0:nk, vt, h * 65:(h + 1) * 65],
                                                                            start=True, stop=True, skip_group_check=True))
                            fw.op("pe", fns, reads=[b_PT[pi], b_V[vt]], writes=[b_bank[wbk]])
                            jlo = valid[0] - qb0
                            nqm = seq.tiles[valid[0]][1]
                            nj = len(qts)
                            w3 = bank[wbk][0:nqm, 0:nj * 65].rearrange("p (j e) -> p j e", e=65)
                            fw.op("dve", lambda q: q.tensor_scalar(out=sm[0:nqm, dk + jlo:dk + nj], in0=w3[:, jlo:nj, 64], scalar1=-1.0, scalar2=1.0, op0=ALU.mult, op1=ALU.add),
                                  reads=[b_bank[wbk]], writes=[bs])
                            nv_ = nj - jlo
                            dec_b = sm[0:nqm, dk + jlo:dk + nj].unsqueeze(2).to_broadcast([nqm, nv_, 64])
                            fw.op("dve", lambda q: q.tensor_tensor(out=osb[0:nqm, jlo:nj, 0:64], in0=osb[0:nqm, jlo:nj, 0:64], in1=dec_b, op=ALU.mult),
                                  reads=b_osb + [bs], writes=b_osb)
                            fw.op("dve", lambda q: q.tensor_tensor(out=osb[0:nqm, jlo:nj, 0:64], in0=w3[:, jlo:nj, 0:64], in1=osb[0:nqm, jlo:nj, 0:64], op=ALU.add),
                                  reads=b_osb + [b_bank[wbk]], writes=b_osb)

                        if pend_a is not None:
                            pend_a()
                        if pend_b is not None:
                            pend_b()
                        pend_b = None
                        if pend_a is not None:
                            pend_b = pend_a.b
                        stage2a.b = stage2b
                        pend_a = stage2a
                    if pend_a is not None:
                        pend_a()
                    if pend_b is not None:
                        pend_b()
                    if pend_a is not None:
                        pend_a.b()
                    for j in qts:
                        jj = j - qb0
                        nq = seq.tiles[j][1]
                        osl = oall[0:nq, j, 512 + h * 64:512 + (h + 1) * 64]
                        fw.op("dve", lambda q, osl=osl, jj=jj: q.tensor_tensor(out=osl, in0=osb[0:nq, jj, 0:64], in1=osl, op=ALU.mult), reads=[b_osb[jj], b_o[j]], writes=[b_o[j]])

        def load_wukv():
            for bb in (b_wuk, b_wuv, b_latT, b_wqb):
                bb.lw = b_Tb.lw
                bb.rd = list(b_Tb.rd)
            for (wsrc, wdst, bw) in ((I["w_uk"], wuk, b_wuk), (I["w_uv"], wuv, b_wuv)):
                for c in range(2):
                    i = nxt("stg", 3)
                    fw.dma("sp", stg[i][:, :], wsrc[c * 128:(c + 1) * 128, :], writes=[b_stg[i]])
                    fw.op("dve", lambda q: q.tensor_copy(out=wdst[:, c, :], in_=stg[i][:, :]), reads=[b_stg[i]], writes=[bw])

        seqs = [Seq("p", i) for i in range(NP)] + [Seq("s", i) for i in range(NS)]
        h1_evs = {}

        def load_layer_consts(layer):
            fw.dma("sp", gpost[:], bcast_ap(I["g_post0"] if layer == 0 else I["g_post1"], DM), writes=[b_gpost])
            fw.dma("sp", wout[:], (S_wo0 if layer == 0 else S_wo1).rearrange("(c p) n -> p c n", p=128), reads=[dbuf((S_wo0 if layer == 0 else S_wo1).name)], writes=[b_wout])

        if 0 in layers and DEV_STOP >= 2:
            load_layer_consts(0)
            for seq in seqs:
                b = seq.idx
                src = I["x_prompt"][b] if seq.kind == "p" else I["x_sample"][b]
                if 1 in layers:
                    dst = S_h1p[b] if seq.kind == "p" else S_h1s[b]
                    is_out = False
                else:
                    dst = O["y_prompt"][b] if seq.kind == "p" else O["y_sample"][b]
                    is_out = True
                stage_U(seq, src)
                if DEV_STOP >= 2.1:
                    stage_PA(seq, b)
                if DEV_STOP >= 3.1:
                    stage_QA(seq)
                if DEV_STOP >= 5:
                    stage_PB(seq, b)
                if DEV_STOP >= 6:
                    stage_gate(seq, S_w0, 3072, 0, True)
                    stage_gate(seq, S_w0, 3072, 1, False)
                    stage_QB(seq)
                    stage_GY(seq, b, src, dst, is_out)

        if 1 in layers:
            load_layer_consts(1)
            load_wukv()
            for seq in seqs:
                b = seq.idx
                if 0 in layers:
                    src = S_h1p[b] if seq.kind == "p" else S_h1s[b]
                else:
                    src = I["x_prompt"][b] if seq.kind == "p" else I["x_sample"][b]
                dst = O["y_prompt"][b] if seq.kind == "p" else O["y_sample"][b]
                hb = [dbuf(("h1", seq.kind, b))] if 0 in layers else []
                fw.wait_events("sp", h1_evs.get((seq.kind, b), []))
                stage_U(seq, src, hb)
                sp_ev = stage_PC(seq, b)
                stage_QC(seq, 0)
                reload_C(seq, sp_ev)
                stage_QC(seq, 1)
                stage_gate(seq, S_w1, 2592, 0, True)
                stage_PD(seq, b)
                stage_gate(seq, S_w1, 2592, 1, False)
                stage_QD(seq)
                stage_GY(seq, b, src, dst, True, hb)

        fw.finish("sp")
        print("ninst", fw.ninst, {e: fw.cnt[e] for e in fw.cnt})
    return nc


def _consts():
    c = {}
    c["c_ident"] = np.eye(128, dtype=np.float32)
    m = np.zeros((128, 128), np.float32); m[64:128, 0:64] = NEG; c["c_negq"] = m
    m = np.zeros((128, 128), np.float32); m[0:64, 64:128] = NEG; c["c_negq4"] = m
    s = np.arange(128)[:, None]; t = np.arange(128)[None, :]
    c["c_negd"] = np.where(s >= t, NEG, 0.0).astype(np.float32)
    c["c_m01d"] = (s < t).astype(np.float32)
    c["c_nut"] = np.where(s >= t, -1.0, 0.0).astype(np.float32)
    pos = np.zeros((17, 128), np.float32)
    for tt in range(16):
        pos[tt] = tt * 128 + np.arange(128)
    pos[16] = PAST + np.arange(128)
    for nm, half in (("c_ropeA", 8), ("c_ropeC", 16)):
        inv = (np.float32(THETA) ** (-np.arange(half, dtype=np.float32) / np.float32(half))).astype(np.float32)
        ang = (pos[:, :, None] * inv[None, None, :]).astype(np.float32)
        tab = np.stack([np.cos(ang), np.sin(ang)], 0).astype(np.float32)
        c[nm] = np.ascontiguousarray(tab.transpose(2, 0, 1, 3).reshape(128, 2 * 17 * half))
    return c


_CACHE = {}
_IN_SHARDED = ["x_prompt", "x_sample", "cache_a_k", "cache_a_v", "cache_b_k", "cache_b_v", "cache_c_latent", "cache_c_krope", "cache_d_k", "cache_d_v"]
_OUT_NAMES = ["y_prompt", "y_sample", "a_k_p", "a_v_p", "b_k_p", "b_v_p", "c_lat_p", "c_krope_p", "d_k_p", "d_v_p",
              "a_k_s", "a_v_s", "b_k_s", "b_v_s", "c_lat_s", "c_krope_s", "d_k_s", "d_v_s"]


def _out_shapes(nb_p, nb_s):
    return [(nb_p, T, DM), (nb_s, TS, DM), (nb_p, T, 4, 2, 64), (nb_p, T, 4, 128), (nb_p, 512, 8, 64), (nb_p, 512, 8, 64),
            (nb_p, T, 256), (nb_p, T, 32), (nb_p, T, 8, 64), (nb_p, T, 8, 64),
            (nb_s, TS, 4, 2, 64), (nb_s, TS, 4, 128), (nb_s, TS, 8, 64), (nb_s, TS, 8, 64), (nb_s, TS, 256), (nb_s, TS, 32),
            (nb_s, TS, 8, 64), (nb_s, TS, 8, 64)]


def _core_inputs(inputs, lo_p, hi_p, lo_s, hi_s):
    m = {}
    for k, v in inputs.items():
        a = np.asarray(v)
        if k in _IN_SHARDED:
            lo, hi = (lo_p, hi_p) if k == "x_prompt" else (lo_s, hi_s)
            a = a[lo:hi]
            a = a.reshape(a.shape[0], a.shape[1], -1)
        elif k in ("w_uk", "w_uv"):
            a = a.reshape(256, 512)
        m[k] = np.ascontiguousarray(a, dtype=np.float32)
    m.update(_consts())
    rb = np.asarray(inputs["rel_bias_b"], dtype=np.float32)
    s_ = np.arange(128)[:, None]; t_ = np.arange(128)[None, :]
    idx = np.stack([np.clip(t_ - s_, -128, 128) + 128, np.clip(128 + t_ - s_, -128, 128) + 128], 0)
    m["c_rbT"] = np.ascontiguousarray(rb[:, idx], dtype=np.float32)
    return m


def kernel(**inputs):
    nb = np.asarray(inputs["x_prompt"]).shape[0]
    per = nb // NCORES
    key = ("full", per)
    if key not in _CACHE:
        _CACHE[key] = build(per, per)
    nc = _CACHE[key]
    in_maps = [_core_inputs(inputs, i * per, (i + 1) * per, i * per, (i + 1) * per) for i in range(NCORES)]
    res = run_bass_kernel_spmd(nc, in_maps, core_ids=list(range(NCORES)))
    outs = []
    shapes = _out_shapes(nb, nb)
    for nm, shp in zip(_OUT_NAMES, shapes):
        full = np.concatenate([np.asarray(r[nm]) for r in res.results], axis=0)
        outs.append(np.ascontiguousarray(full.reshape(shp), dtype=np.float32))
    return tuple(outs)
```

```python
import math
import itertools
import numpy as np
from contextlib import ExitStack
import concourse.bass as bass
import concourse.mybir as mybir
from concourse.bass_utils import run_bass_kernel_spmd

F32 = mybir.dt.float32
BF16 = mybir.dt.bfloat16
AF = mybir.ActivationFunctionType
ALU = mybir.AluOpType
AX = mybir.AxisListType

NCORES = 8
DM = 1024
T = 2048
TS = 16
PAST = 2048
EPS = 1e-6
NEG = -30000.0
THETA = 500000.0
LAM_INIT0 = 0.8 - 0.6 * math.exp(-0.3 * 0)
W0N = 4096
W1N = 3616
VST = 520


class Buf:
    __slots__ = ("name", "lw", "rd", "dsem", "dcnt", "excl")

    def __init__(self, name):
        self.name = name
        self.excl = False
        self.lw = None
        self.rd = []
        self.dsem = None
        self.dcnt = 0


class FW:
    def __init__(self, nc, stack):
        self.nc = nc
        self.stack = stack
        self.q = {"pe": nc.tensor, "act": nc.scalar, "dve": nc.vector, "pool": nc.gpsimd, "sp": nc.sync}
        self.sem = {}
        self.cnt = {}
        self.waited = {}
        for e in self.q:
            self.sem[e] = stack.enter_context(nc.semaphore("s_" + e))
            self.cnt[e] = 0
            self.waited[e] = {}
        self.out_events = []
        self.ninst = 0
        self.nbuf = 0

    def buf(self, name=None):
        self.nbuf += 1
        return Buf(name or ("b%d" % self.nbuf))

    def sb(self, name, shape, dtype):
        return self.stack.enter_context(self.nc.sbuf_tensor(name, list(shape), dtype))

    def ps(self, name, shape, dtype):
        return self.stack.enter_context(self.nc.psum_tensor(name, list(shape), dtype))

    def _dsem(self, b):
        if b.dsem is None:
            b.dsem = self.stack.enter_context(self.nc.semaphore("d_" + b.name))
        return b.dsem

    def _deps(self, eng, reads, writes):
        best = {}

        def add(ev):
            s, v, en = ev
            if en == "pe" and eng == "pe":
                return
            k = id(s)
            if k not in best or best[k][1] < v:
                best[k] = (s, v)

        for b in reads:
            if b.lw is not None:
                add(b.lw)
            if b.excl:
                for ev in b.rd:
                    if ev[2] != eng:
                        add(ev)
        for b in writes:
            if b.lw is not None:
                add(b.lw)
            for ev in b.rd:
                add(ev)
        w = self.waited[eng]
        out = []
        for k, (s, v) in best.items():
            if w.get(k, 0) >= v:
                continue
            w[k] = v
            out.append((s, v))
        return out

    def _record(self, ev, reads, writes):
        for b in reads:
            b.rd.append(ev)
            if len(b.rd) > 16:
                best = {}
                for e in b.rd:
                    k = id(e[0])
                    if k not in best or best[k][1] < e[1]:
                        best[k] = e
                b.rd = list(best.values())
        for b in writes:
            b.lw = ev
            b.rd = []

    def op(self, eng, fns, reads=(), writes=()):
        q = self.q[eng]
        if not isinstance(fns, (list, tuple)):
            fns = [fns]
        for (s, v) in self._deps(eng, reads, writes):
            q.wait_ge(s, v)
        ins = None
        for f in fns:
            ins = f(q)
            self.ninst += 1
        self.cnt[eng] += 1
        ins.then_inc(self.sem[eng], 1)
        ev = (self.sem[eng], self.cnt[eng], eng)
        self._record(ev, reads, writes)
        return ev

    def dma(self, eng, out, in_, reads=(), writes=(), sembuf=None, is_output=False, **kw):
        q = self.q[eng]
        if sembuf is None:
            sembuf = writes[0] if writes else reads[0]
        s = self._dsem(sembuf)
        for (ws, v) in self._deps(eng, reads, writes):
            q.wait_ge(ws, v)
        q.dma_start(out=out, in_=in_, **kw).then_inc(s, 16)
        self.ninst += 1
        sembuf.dcnt += 16
        ev = (s, sembuf.dcnt, "dma")
        self._record(ev, reads, writes)
        if is_output:
            self.out_events.append(ev)
        return ev

    def wait_events(self, eng, events):
        q = self.q[eng]
        best = {}
        for (s, v, en) in events:
            k = id(s)
            if k not in best or best[k][1] < v:
                best[k] = (s, v)
        w = self.waited[eng]
        for k, (s, v) in best.items():
            if w.get(k, 0) >= v:
                continue
            w[k] = v
            q.wait_ge(s, v)

    def finish(self, eng="sp"):
        q = self.q[eng]
        best = {}
        for (s, v, en) in self.out_events:
            k = id(s)
            if k not in best or best[k][1] < v:
                best[k] = (s, v)
        for k, (s, v) in best.items():
            q.wait_ge(s, v)
        for e in ("pe", "act", "dve", "pool"):
            if self.cnt[e] > 0:
                q.wait_ge(self.sem[e], self.cnt[e])


class Seq:
    def __init__(self, kind, idx):
        self.kind = kind
        self.idx = idx
        if kind == "p":
            self.ntok = T
            self.tiles = [(i * 128, 128, i) for i in range(16)]
            self.keys = [(i, i * 128, 128, False) for i in range(16)]
            self.keys_b = self.keys
        else:
            self.ntok = TS
            self.tiles = [(0, TS, 16)]
            self.keys = [(i, i * 128, 128, True) for i in range(16)] + [(16, 2048, TS, False)]
            self.keys_b = [(i, i * 128, 128, True) for i in range(4)] + [(4, 512, TS, False)]


DEV_STOP = 99
DEV_DBG = False


def build(NP, NS, layers=(0, 1)):
    nc = bass.Bass("TRN2", target_bir_lowering=False)

    def din(name, shape):
        return nc.dram_tensor(name, list(shape), F32, kind="ExternalInput").ap()

    def dout(name, shape):
        return nc.dram_tensor(name, list(shape), F32, kind="ExternalOutput").ap()

    def dscr(name, shape, dtype):
        return nc.dram_tensor(name, list(shape), dtype).ap()

    NPa, NSa = max(NP, 1), max(NS, 1)
    I = {}
    I["x_prompt"] = din("x_prompt", [NPa, T, DM])
    I["x_sample"] = din("x_sample", [NSa, TS, DM])
    I["cache_a_k"] = din("cache_a_k", [NSa, PAST, 512])
    I["cache_a_v"] = din("cache_a_v", [NSa, PAST, 512])
    I["cache_b_k"] = din("cache_b_k", [NSa, 512, 512])
    I["cache_b_v"] = din("cache_b_v", [NSa, 512, 512])
    I["cache_c_latent"] = din("cache_c_latent", [NSa, PAST, 256])
    I["cache_c_krope"] = din("cache_c_krope", [NSa, PAST, 32])
    I["cache_d_k"] = din("cache_d_k", [NSa, PAST, 512])
    I["cache_d_v"] = din("cache_d_v", [NSa, PAST, 512])
    for nm, shp in [("g_pre0", [DM]), ("w_in0", [DM, W0N]), ("lam_q1", [64]), ("lam_k1", [64]), ("lam_q2", [64]),
                    ("lam_k2", [64]), ("g_sub_a", [128]), ("rel_bias_b", [8, 257]), ("w_out0", [DM, DM]),
                    ("g_post0", [DM]), ("g_pre1", [DM]), ("w_in1", [DM, W1N]), ("g_cq", [768]), ("w_uq", [768, 768]),
                    ("g_ckv", [256]), ("w_uk", [256, 512]), ("w_uv", [256, 512]), ("w_out1", [DM, DM]),
                    ("g_post1", [DM])]:
        I[nm] = din(nm, shp)
    I["c_ident"] = din("c_ident", [128, 128])
    I["c_negq"] = din("c_negq", [128, 128])
    I["c_negq4"] = din("c_negq4", [128, 128])
    I["c_negd"] = din("c_negd", [128, 128])
    I["c_m01d"] = din("c_m01d", [128, 128])
    I["c_nut"] = din("c_nut", [128, 128])
    I["c_rbT"] = din("c_rbT", [8, 2, 128, 128])
    I["c_ropeA"] = din("c_ropeA", [128, 2 * 17 * 8])
    I["c_ropeC"] = din("c_ropeC", [128, 2 * 17 * 16])

    O = {}
    O["y_prompt"] = dout("y_prompt", [NPa, T, DM])
    O["y_sample"] = dout("y_sample", [NSa, TS, DM])
    O["a_k_p"] = dout("a_k_p", [NPa, T, 512])
    O["a_v_p"] = dout("a_v_p", [NPa, T, 512])
    O["b_k_p"] = dout("b_k_p", [NPa, 512, 512])
    O["b_v_p"] = dout("b_v_p", [NPa, 512, 512])
    O["c_lat_p"] = dout("c_lat_p", [NPa, T, 256])
    O["c_krope_p"] = dout("c_krope_p", [NPa, T, 32])
    O["d_k_p"] = dout("d_k_p", [NPa, T, 512])
    O["d_v_p"] = dout("d_v_p", [NPa, T, 512])
    O["a_k_s"] = dout("a_k_s", [NSa, TS, 512])
    O["a_v_s"] = dout("a_v_s", [NSa, TS, 512])
    O["b_k_s"] = dout("b_k_s", [NSa, TS, 512])
    O["b_v_s"] = dout("b_v_s", [NSa, TS, 512])
    O["c_lat_s"] = dout("c_lat_s", [NSa, TS, 256])
    O["c_krope_s"] = dout("c_krope_s", [NSa, TS, 32])
    O["d_k_s"] = dout("d_k_s", [NSa, TS, 512])
    O["d_v_s"] = dout("d_v_s", [NSa, TS, 512])
    if DEV_DBG:
        O["dbg"] = dout("dbg", [T, DM])

    S_w0 = dscr("s_w0", [DM, W0N], BF16)
    S_wo0 = dscr("s_wo0", [DM, DM], BF16)
    S_w1 = dscr("s_w1", [DM, W1N], BF16)
    S_wo1 = dscr("s_wo1", [DM, DM], BF16)
    S_wuq = dscr("s_wuq", [768, 768], BF16)
    S_h1p = dscr("s_h1p", [NPa, T, DM], F32)
    S_h1s = dscr("s_h1s", [NSa, TS, DM], F32)
    S_rbp = dscr("s_rbp", [8, 512], F32)
    S_qt1 = dscr("s_qt1", [96, 4, T], BF16)
    S_kt1 = dscr("s_kt1", [64, 4, 2064], BF16)
    S_v1 = dscr("s_v1", [17, 128, 256], BF16)

    with ExitStack() as st:
        fw = FW(nc, st)
        B = fw.buf
        uT = fw.sb("uT", [128, 8, T], BF16)
        b_uT = [B("uT%d" % i) for i in range(16)]
        R = fw.sb("R", [128, 8192 + 8256 + 17 * VST], BF16)
        QT = R[:, 0:8192].rearrange("p (c t) -> p c t", c=4)
        KT = R[:, 8192:8192 + 8256].rearrange("p (c t) -> p c t", c=4)
        VR = R[:, 16448:16448 + 17 * VST].rearrange("p (k e) -> p k e", k=17)
        b_QT = [B("QT%d" % i) for i in range(16)]
        b_KT = [B("KT%d" % i) for i in range(17)]
        b_V = [B("V%d" % i) for i in range(17)]
        oall = fw.sb("oall", [128, 16, DM], BF16)
        b_o = [B("o%d" % i) for i in range(16)]
        wblk = [fw.sb("wblk%d" % i, [128, 8, 512], BF16) for i in range(2)]
        b_wblk = [B("wblk%d" % i) for i in range(2)]
        wout = fw.sb("wout", [128, 8, DM], BF16)
        b_wout = B("wout")
        xt = [fw.sb("xt%d" % i, [128, DM], F32) for i in range(2)]
        b_xt = [B("xt%d" % i) for i in range(2)]
        b_xout = [B("xout%d" % i) for i in range(2)]
        xn = fw.sb("xn", [128, DM], BF16)
        b_xn = B("xn")
        junk = fw.sb("junk", [128, DM], BF16)
        b_junk = B("junk")
        stg = [fw.sb("stg%d" % i, [128, 512], F32) for i in range(3)]
        b_stg = [B("stg%d" % i) for i in range(3)]
        tb16 = [fw.sb("tb16_%d" % i, [128, 512], BF16) for i in range(3)]
        b_tb16 = [B("tb16_%d" % i) for i in range(3)]
        PT = [fw.sb("PT%d" % i, [128, 512], BF16) for i in range(3)]
        b_PT = [B("PT%d" % i) for i in range(3)]
        Ef = [fw.sb("Ef%d" % i, [128, 512], F32) for i in range(2)]
        b_Ef = [B("Ef%d" % i) for i in range(2)]
        SPb = [fw.sb("SPb%d" % i, [128, 512], BF16) for i in range(2)]
        b_SPb = [B("SPb%d" % i) for i in range(2)]
        ogT = [fw.sb("ogT%d" % i, [128, 8, 128], BF16) for i in range(2)]
        b_ogT = [B("ogT%d" % i) for i in range(2)]
        ytmp = fw.sb("ytmp", [128, DM], F32)
        b_ytmp = B("ytmp")
        sm = fw.sb("sm", [128, 64], F32)
        b_sm = {}

        def smb(k):
            if k not in b_sm:
                b_sm[k] = B("sm%d" % k)
            return b_sm[k]

        osb = fw.sb("osb", [128, 4, 128], F32)
        b_osb = [B("osb%d" % i) for i in range(4)]
        ident = fw.sb("ident", [128, 128], BF16); b_ident = B("ident")
        negq = fw.sb("negq", [128, 128], BF16); b_negq = B("negq")
        negq4 = fw.sb("negq4", [128, 128], BF16); b_negq4 = B("negq4")
        negd = fw.sb("negd", [128, 128], BF16); b_negd = B("negd")
        m01d = fw.sb("m01d", [128, 128], BF16); b_m01d = B("m01d")
        nut = fw.sb("nut", [128, 128], BF16); b_nut = B("nut")
        onec = fw.sb("onec", [128, 2], BF16); b_onec = B("onec")
        ropeA = fw.sb("ropeA", [128, 2, 17, 8], F32); b_ropeA = B("ropeA")
        ropeC = fw.sb("ropeC", [128, 2, 17, 16], F32); b_ropeC = B("ropeC")
        gcol = fw.sb("gcol", [128, 24], F32); b_gcol = B("gcol")
        gpost = fw.sb("gpost", [128, DM], F32); b_gpost = B("gpost")
        gsub = fw.sb("gsub", [128, 128], F32); b_gsub = B("gsub")
        gckv = fw.sb("gckv", [128, 256], F32); b_gckv = B("gckv")
        lamt = ytmp[:, 520:776].rearrange("p (a b) -> p a b", a=4); b_lamt = b_ytmp
        lamv = fw.sb("lamv", [128, 8], F32); b_lamv = B("lamv")
        LR = fw.sb("LR", [128, 4608], BF16)
        Tb = LR[:, 0:2048].rearrange("p (a b) -> p a b", a=16); b_Tb = B("Tb")
        wuk = LR[:, 0:1024].rearrange("p (a b) -> p a b", a=2); b_wuk = B("wuk")
        wuv = LR[:, 1024:2048].rearrange("p (a b) -> p a b", a=2); b_wuv = B("wuv")
        latT = LR[:, 2048:2304].rearrange("p (a b) -> p a b", a=2); b_latT = B("latT")
        wqb = LR[:, 2304:4608].rearrange("p (a b) -> p a b", a=6); b_wqb = B("wqb")
        bank = [fw.ps("bank%d" % i, [128, 512], F32) for i in range(8)]
        b_bank = [B("bank%d" % i) for i in range(8)]
        tbk = [bank[6][:, :].bitcast(BF16), bank[7][:, :].bitcast(BF16)]
        b_tbk = [b_bank[6], b_bank[7]]
        for bb in b_bank:
            bb.excl = True
        pexp = fw.sb("pexp", [128, 1], F32); b_pexp = B("pexp")
        fw.op("pool", lambda q: q.memset(pexp[:], -0.5), writes=[b_pexp])
        b_dram = {}

        def dbuf(k):
            if k not in b_dram:
                b_dram[k] = B("dram_" + str(k))
            return b_dram[k]

        rr_ctr = {"stg": 0, "tb16": 0, "tbk": 0, "xt": 0, "wblk": 0, "pbank": 0, "cp": 0}

        def nxt(k, n):
            v = rr_ctr[k] % n
            rr_ctr[k] += 1
            return v

        def load_const_bf16(dst, b_dst, src):
            i = nxt("stg", 3)
            fw.dma("sp", stg[i][:, 0:128], src, writes=[b_stg[i]])
            fw.op("dve", lambda q: q.tensor_copy(out=dst[:], in_=stg[i][:, 0:128]), reads=[b_stg[i]], writes=[b_dst])

        load_const_bf16(ident, b_ident, I["c_ident"])
        load_const_bf16(negq, b_negq, I["c_negq"])
        load_const_bf16(negq4, b_negq4, I["c_negq4"])
        load_const_bf16(negd, b_negd, I["c_negd"])
        load_const_bf16(m01d, b_m01d, I["c_m01d"])
        load_const_bf16(nut, b_nut, I["c_nut"])
        fw.op("pool", lambda q: q.memset(onec[:], 1.0), writes=[b_onec])
        fw.dma("sp", ropeA[:].rearrange("p a b c -> p (a b c)"), I["c_ropeA"], writes=[b_ropeA])
        fw.dma("sp", ropeC[:].rearrange("p a b c -> p (a b c)"), I["c_ropeC"], writes=[b_ropeC])
        with nc.allow_non_contiguous_dma(reason="tiny gain vectors"):
            for (gsrc, o0, n) in ((I["g_pre0"], 0, 8), (I["g_pre1"], 8, 8), (I["g_cq"], 16, 6)):
                for c in range(n):
                    fw.dma("sp", gcol[:, o0 + c:o0 + c + 1], bass.AP(gsrc.tensor, c * 128, [[1, 128], [1, 1]]), writes=[b_gcol])

        def bcast_ap(src, n):
            return bass.AP(src.tensor, 0, [[0, 128], [1, n]])

        fw.dma("sp", gsub[:], bcast_ap(I["g_sub_a"], 128), writes=[b_gsub])
        fw.op("dve", lambda q: q.tensor_scalar(out=gsub[:], in0=gsub[:], scalar1=1.0 - LAM_INIT0, scalar2=None, op0=ALU.mult),
              reads=[b_gsub], writes=[b_gsub])
        fw.dma("sp", gckv[:], bcast_ap(I["g_ckv"], 256), writes=[b_gckv])
        for i, nm in enumerate(["lam_q1", "lam_k1", "lam_q2", "lam_k2"]):
            fw.dma("sp", lamt[:, i, :], bcast_ap(I[nm], 64), writes=[b_lamt])
        fw.op("dve", lambda q: q.tensor_tensor(out=lamt[:, 0, :], in0=lamt[:, 0, :], in1=lamt[:, 1, :], op=ALU.mult), reads=[b_lamt], writes=[b_lamt])
        fw.op("dve", lambda q: q.tensor_tensor(out=lamt[:, 2, :], in0=lamt[:, 2, :], in1=lamt[:, 3, :], op=ALU.mult), reads=[b_lamt], writes=[b_lamt])
        fw.op("dve", lambda q: q.reduce_sum(out=lamv[:, 0:1], in_=lamt[:, 0, :], axis=AX.X), reads=[b_lamt], writes=[b_lamv])
        fw.op("dve", lambda q: q.reduce_sum(out=lamv[:, 1:2], in_=lamt[:, 2, :], axis=AX.X), reads=[b_lamt], writes=[b_lamv])
        fw.op("act", lambda q: q.activation(out=lamv[:, 0:2], in_=lamv[:, 0:2], func=AF.Exp), reads=[b_lamv], writes=[b_lamv])
        fw.op("dve", lambda q: q.tensor_tensor(out=lamv[:, 2:3], in0=lamv[:, 1:2], in1=lamv[:, 0:1], op=ALU.subtract), reads=[b_lamv], writes=[b_lamv])
        fw.op("dve", lambda q: q.tensor_scalar(out=lamv[:, 3:4], in0=lamv[:, 2:3], scalar1=-LAM_INIT0, scalar2=None, op0=ALU.add), reads=[b_lamv], writes=[b_lamv])
        nlam = lamv[:, 3:4]
        cb = lamv[:, 4:8]
        cbt = fw.sb("cbt", [128, 8], F32); b_cbt = B("cbt")
        with nc.allow_non_contiguous_dma(reason="tiny"):
            for h in range(8):
                fw.dma("sp", cbt[:, h:h + 1], bass.AP(I["rel_bias_b"].tensor, 256 + 257 * h, [[0, 128], [1, 1]]), writes=[b_cbt])
        jn = nxt("stg", 3)
        fw.dma("sp", stg[jn][:, 0:128], I["c_negq"], writes=[b_stg[jn]])
        for h in range(8):
            for d in range(2):
                i = nxt("stg", 3)
                if i == jn:
                    i = nxt("stg", 3)
                fw.dma("sp", stg[i][:, 0:128], I["c_rbT"][h, d], writes=[b_stg[i]])
                fw.op("dve", lambda q: q.tensor_scalar(out=stg[i][:, 128:256], in0=stg[i][:, 0:128], scalar1=cbt[:, h:h + 1], scalar2=None, op0=ALU.subtract),
                      reads=[b_stg[i], b_cbt], writes=[b_stg[i]])
                if d == 0:
                    fw.op("dve", lambda q: q.tensor_tensor(out=Tb[:, h * 2 + d, :], in0=stg[i][:, 128:256], in1=stg[jn][:, 0:128], op=ALU.add),
                          reads=[b_stg[i], b_stg[jn]], writes=[b_Tb])
                else:
                    fw.op("dve", lambda q: q.tensor_copy(out=Tb[:, h * 2 + d, :], in_=stg[i][:, 128:256]), reads=[b_stg[i]], writes=[b_Tb])

        def set_ones(H, dv):
            v = VR[:, :, 0:H * (dv + 1)].rearrange("p k (h e) -> p k h e", h=H)[:, :, :, dv:dv + 1]
            fw.op("pool", lambda q: q.memset(v, 1.0), writes=b_V)

        def load_wblock(scr, K, c0, ncol):
            i = nxt("wblk", 2)
            kc = K // 128
            wait_prep(scr.name)
            fw.dma("sp", wblk[i][:, 0:kc, 0:ncol], scr.rearrange("(c p) n -> p c n", p=128)[:, :, c0:c0 + ncol],
                   reads=[dbuf(scr.name)], writes=[b_wblk[i]])
            return wblk[i], b_wblk[i]

        def rsqrt_col(col_ap, nr, bs):
            fw.op("pool", lambda q: q.tensor_scalar(out=col_ap, in0=col_ap, scalar1=EPS, scalar2=0.0, op0=ALU.add, op1=ALU.add), reads=[bs], writes=[bs])
            fw.op("pool", lambda q: q.tensor_tensor(out=col_ap, in0=col_ap, in1=pexp[0:nr, :], op=ALU.pow), reads=[bs, b_pexp], writes=[bs])

        def rms_scale(src_ap, nr, n, b_src, key, engines="act"):
            bs = smb(key)
            aps = src_ap if isinstance(src_ap, (list, tuple)) else [src_ap]
            for k, a in enumerate(aps):
                w = a.shape[-1]
                fw.op("act", lambda q: q.activation(out=junk[0:nr, 0:w], in_=a, func=AF.Square, scale=float(n) ** -0.5, accum_out=sm[0:nr, key + k:key + k + 1]),
                      reads=b_src, writes=[b_junk, bs])
            if len(aps) == 2:
                fw.op("pool", lambda q: q.tensor_tensor(out=sm[0:nr, key:key + 1], in0=sm[0:nr, key:key + 1], in1=sm[0:nr, key + 1:key + 2], op=ALU.add),
                      reads=[bs], writes=[bs])
            rsqrt_col(sm[0:nr, key:key + 1], nr, bs)
            return sm[0:nr, key:key + 1], bs

        def transposes(src_aps, nr, widths):
            i = nxt("tbk", 2)
            offs = []
            fns = []
            o = 0
            for a, w in zip(src_aps, widths):
                offs.append(o)
                fns.append(lambda q, a=a, w=w, o=o: q.transpose(out=tbk[i][0:w, o:o + nr], in_=a, identity=ident[0:nr, 0:nr]))
                o += 128
            return i, fns, offs

        def copy_eng():
            return ("dve", "act")[nxt("cp", 2)]

        rr_ctr["ce"] = 0

        def cast_eng():
            return ("dve", "pool", "act")[nxt("ce", 3)]

        def ecopy(eng, out, in_, reads, writes, scale=None):
            if eng == "act":
                if scale is None:
                    fw.op("act", lambda q: q.activation(out=out, in_=in_, func=AF.Copy), reads=reads, writes=writes)
                else:
                    fw.op("act", lambda q: q.activation(out=out, in_=in_, func=AF.Copy, scale=scale), reads=reads, writes=writes)
            else:
                if scale is None:
                    fw.op(eng, lambda q: q.tensor_copy(out=out, in_=in_), reads=reads, writes=writes)
                else:
                    fw.op(eng, lambda q: q.tensor_scalar(out=out, in0=in_, scalar1=scale, scalar2=0.0, op0=ALU.mult, op1=ALU.add), reads=reads, writes=writes)

        rr_ctr["pin"] = 0
        rr_ctr["pout"] = 0
        pin_bufs = [(xt[0], b_xt[0]), (xt[1], b_xt[1]), (ytmp, b_ytmp)]

        def prep_weight(src, dst, K, N, gofs):
            for kc in range(K // 128):
                for c0 in range(0, N, 1024):
                    ncol = min(1024, N - c0)
                    it, ib = pin_bufs[nxt("pin", 3)]
                    so = nxt("pout", 16)
                    ot = oall[:, so, :]
                    fw.dma("sp", it[:, 0:ncol], src[kc * 128:(kc + 1) * 128, c0:c0 + ncol], writes=[ib], sembuf=ib)
                    eng = ("dve", "act")[(kc + c0 // 1024) % 2]
                    if gofs is None:
                        ecopy(eng, ot[:, 0:ncol], it[:, 0:ncol], [ib], [b_o[so]])
                    else:
                        ecopy(eng, ot[:, 0:ncol], it[:, 0:ncol], [ib, b_gcol], [b_o[so]], scale=gcol[:, gofs + kc:gofs + kc + 1])
                    prep_evs.setdefault(dst.name, []).append(
                        fw.dma("pool", dst[kc * 128:(kc + 1) * 128, c0:c0 + ncol], ot[:, 0:ncol], reads=[b_o[so]], writes=[dbuf(dst.name)], sembuf=b_o[so]))
                    yield

        prep_evs = {}
        prep_late = []
        if 0 in layers and DEV_STOP >= 1:
            for _ in prep_weight(I["w_in0"], S_w0, DM, W0N, 0):
                pass
            for _ in prep_weight(I["w_out0"], S_wo0, DM, DM, None):
                pass
        if 1 in layers and DEV_STOP >= 1:
            prep_late = itertools.chain(prep_weight(I["w_in1"], S_w1, DM, W1N, 8), prep_weight(I["w_uq"], S_wuq, 768, 768, 16),
                                        prep_weight(I["w_out1"], S_wo1, DM, DM, None))
            if 0 not in layers or NP == 0:
                for _ in prep_late:
                    pass
                prep_late = []
        prep_state = {"it": iter(prep_late), "active": False}

        def prep_step(n=2):
            if prep_state["active"]:
                for _ in range(n):
                    if next(prep_state["it"], "done") == "done":
                        prep_state["active"] = False
                        break

        def prep_flush():
            for _ in prep_state["it"]:
                pass
            prep_state["active"] = False

        def wait_prep(name):
            fw.wait_events("sp", prep_evs.get(name, []))

        def stage_U(seq, src, src_bufs=()):
            sel = {}

            def pa(ti):
                r0, nr, rt = seq.tiles[ti]
                i = nxt("xt", 2)
                fw.dma("sp", xt[i][0:nr, :], src[r0:r0 + nr, :], reads=list(src_bufs), writes=[b_xt[i]], sembuf=b_xt[i])
                rs, bs = rms_scale(xt[i][0:nr, :], nr, DM, [b_xt[i]], 2 * (ti % 2))
                sel[ti] = (i, rs, bs)

            def pb(ti):
                r0, nr, rt = seq.tiles[ti]
                i, rs, bs = sel[ti]
                fw.op("dve", lambda q: q.tensor_scalar(out=xn[0:nr, :], in0=xt[i][0:nr, :], scalar1=rs, scalar2=None, op0=ALU.mult), reads=[b_xt[i], bs], writes=[b_xn])
                k = nxt("tbk", 2)
                fw.op("pe", [lambda q, c=c: q.transpose(out=tbk[k][:, c * 128:c * 128 + nr], in_=xn[0:nr, c * 128:(c + 1) * 128], identity=ident[0:nr, 0:nr])
                             for c in range(8)], reads=[b_xn, b_ident], writes=[b_tbk[k]])
                ecopy("act", uT[:, 0:4, r0:r0 + nr], tbk[k][:, 0:512].rearrange("p (c t) -> p c t", c=4)[:, :, 0:nr], [b_tbk[k]], [b_uT[ti]])
                ecopy("dve", uT[:, 4:8, r0:r0 + nr], tbk[k][:, 512:1024].rearrange("p (c t) -> p c t", c=4)[:, :, 0:nr], [b_tbk[k]], [b_uT[ti]])

            n = len(seq.tiles)
            pa(0)
            for ti in range(n):
                if ti + 1 < n:
                    pa(ti + 1)
                pb(ti)

        def proj_block(seq, scr, c0, ncol, consume, K=DM, lhs=None):
            wt, bw = load_wblock(scr, K, c0, ncol)
            kcn = K // 128
            later = []
            later2 = []
            for ti, (r0, nr, rt) in enumerate(seq.tiles):
                pb = 4 + nxt("pbank", 2)
                if lhs is None:
                    lf = lambda c: uT[:, c, r0:r0 + nr]
                    lb = [b_uT[ti]]
                else:
                    lf, lb = lhs(ti)
                fw.op("pe", [lambda q, c=c: q.matmul(bank[pb][0:nr, 0:ncol], lhsT=lf(c), rhs=wt[:, c, 0:ncol], start=(c == 0), stop=(c == kcn - 1))
                             for c in range(kcn)], reads=lb + [bw], writes=[b_bank[pb]])
                for f in later2:
                    f()
                later2 = later
                later = consume(ti, r0, nr, rt, bank[pb], b_bank[pb]) or []
                prep_step()
            for f in later2 + later:
                f()

        def store_out(dst, src_ap, b_src):
            fw.dma("pool", dst, src_ap, reads=[b_src], is_output=True, sembuf=b_src)

        def ingest_k(src_ap, b_src, nr, kcol0, ktile, scale=None, ncol=512, part=128, rows=None, eng="dve"):
            j = nxt("tb16", 3)
            ecopy(eng, tb16[j][0:nr, 0:ncol], src_ap, [b_src], [b_tb16[j]], scale)

            def later():
                nchunk = ncol // part
                k = nxt("tbk", 2)
                fw.op("pe", [lambda q, c=c: q.transpose(out=tbk[k][0:part, c * 128:c * 128 + nr], in_=tb16[j][0:nr, c * part:(c + 1) * part],
                                                         identity=ident[0:nr, 0:nr]) for c in range(nchunk)],
                      reads=[b_tb16[j], b_ident], writes=[b_tbk[k]])
                p0, p1 = rows if rows is not None else (0, part)
                ecopy(copy_eng(), KT[p0:p1, 0:nchunk, kcol0:kcol0 + nr] if rows is None else KT[p0:p1, 0:nchunk, kcol0:kcol0 + nr],
                      tbk[k][0:part, 0:nchunk * 128].rearrange("p (c t) -> p c t", c=nchunk)[:, :, 0:nr], [b_tbk[k]], [b_KT[ktile]])
            return later

        def ingest_q(src_ap, b_src, nr, qcol0, qtile, scale, ncol=512, part=128, eng="dve"):
            j = nxt("tb16", 3)
            ecopy(eng, tb16[j][0:nr, 0:ncol], src_ap, [b_src], [b_tb16[j]], scale)

            def later():
                nchunk = ncol // part
                k = nxt("tbk", 2)
                fw.op("pe", [lambda q, c=c: q.transpose(out=tbk[k][0:part, c * 128:c * 128 + nr], in_=tb16[j][0:nr, c * part:(c + 1) * part],
                                                         identity=ident[0:nr, 0:nr]) for c in range(nchunk)],
                      reads=[b_tb16[j], b_ident], writes=[b_tbk[k]])
                ecopy(copy_eng(), QT[0:part, 0:nchunk, qcol0:qcol0 + nr],
                      tbk[k][0:part, 0:nchunk * 128].rearrange("p (c t) -> p c t", c=nchunk)[:, :, 0:nr], [b_tbk[k]], [b_QT[qtile]])
            return later

        def ingest_v(src_ap, b_src, nr, vtile, H, dv, eng="dve", hofs=0, Hsrc=None):
            Hs = Hsrc or H
            src3 = src_ap.rearrange("p (h d) -> p h d", h=Hs)[:, hofs:hofs + H, :]
            dst3 = VR[0:nr, vtile, 0:H * (dv + 1)].rearrange("p (h e) -> p h e", h=H)[:, :, 0:dv] if dv != 64 or True else None
            ecopy(eng, dst3, src3, [b_src], [b_V[vtile]])

        def rope_ops(dst, src3, nr, rt, tab, half, nblk, b_src, b_dst, blk_stride_dst3):
            cosb = tab[0:nr, 0, rt, :].unsqueeze(1).to_broadcast([nr, nblk, half])
            sinb = tab[0:nr, 1, rt, :].unsqueeze(1).to_broadcast([nr, nblk, half])
            x1 = src3[:, :, 0:half]
            x2 = src3[:, :, half:2 * half]
            d3 = blk_stride_dst3
            t = osb[0:nr, :, :].rearrange("p a b -> p (a b)")
            n = nblk * half
            tt = [t[:, k * n:(k + 1) * n].rearrange("p (b d) -> p b d", b=nblk) for k in range(4)]
            btab = b_ropeA if tab is ropeA else b_ropeC
            fw.op("dve", lambda q: q.tensor_tensor(out=tt[0], in0=x1, in1=cosb, op=ALU.mult), reads=[b_src, btab], writes=b_osb)
            fw.op("dve", lambda q: q.tensor_tensor(out=tt[1], in0=x2, in1=sinb, op=ALU.mult), reads=[b_src, btab], writes=b_osb)
            fw.op("dve", lambda q: q.tensor_tensor(out=tt[2], in0=x2, in1=cosb, op=ALU.mult), reads=[b_src, btab], writes=b_osb)
            fw.op("dve", lambda q: q.tensor_tensor(out=tt[3], in0=x1, in1=sinb, op=ALU.mult), reads=[b_src, btab], writes=b_osb)
            fw.op("dve", lambda q: q.tensor_tensor(out=d3[:, :, 0:half], in0=tt[0], in1=tt[1], op=ALU.subtract), reads=b_osb, writes=[b_dst])
            fw.op("dve", lambda q: q.tensor_tensor(out=d3[:, :, half:2 * half], in0=tt[2], in1=tt[3], op=ALU.add), reads=b_osb, writes=[b_dst])

        def run_units(units, sbanks=((0,), (1,))):
            pend = None
            for ui, u in enumerate(units):
                sb_ = sbanks[ui % 2]
                nk = u["nk"]
                used = sorted(set(x[0] for x in u["qk"]))
                for bsel in used:
                    bk_ = sb_[bsel]
                    fns = []
                    for (bs_, c0, n, lhsT, rhs) in u["qk"]:
                        if bs_ == bsel:
                            fns.append(lambda q, c0=c0, n=n, lhsT=lhsT, rhs=rhs: q.matmul(bank[bk_][0:nk, c0:c0 + n], lhsT=lhsT, rhs=rhs, start=True, stop=False,
                                                                                          skip_group_check=True))
                    for (bs_, c0, n, rhs_t) in u["extra"]:
                        if bs_ == bsel:
                            fns.append(lambda q, c0=c0, n=n, rhs_t=rhs_t: q.matmul(bank[bk_][0:nk, c0:c0 + n], lhsT=ident[0:nk, 0:nk], rhs=rhs_t, start=False, stop=True,
                                                                                   skip_group_check=True))
                    fw.op("pe", fns, reads=u["rd"] + [b_ident, b_negq], writes=[b_bank[bk_]])
                pi = ui % 3
                for (bsel, c0, n, ptc0) in u["exps"]:
                    bk_ = sb_[bsel]
                    fw.op("act", lambda q, c0=c0, n=n, ptc0=ptc0: q.activation(out=PT[pi][0:nk, ptc0:ptc0 + n], in_=bank[bk_][0:nk, c0:c0 + n], func=AF.Exp),
                          reads=[b_bank[bk_]], writes=[b_PT[pi]])
                if pend is not None:
                    pend()

                def pvs(u=u, pi=pi, nk=nk):
                    obufs = []
                    fns = []
                    for (O_ap, pc0, nq, V_ap, start, bO) in u["pv"]:
                        fns.append(lambda q, O_ap=O_ap, pc0=pc0, nq=nq, V_ap=V_ap, start=start: q.matmul(
                            O_ap, lhsT=PT[pi][0:nk, pc0:pc0 + nq], rhs=V_ap, start=start, stop=True, skip_group_check=True))
                        if bO not in obufs:
                            obufs.append(bO)
                    if DEV_STOP >= 3.2:
                        fw.op("pe", fns, reads=[b_PT[pi]] + u["rdv"], writes=obufs)
                    if u.get("fin"):
                        u["fin"]()
                pend = pvs
            if pend is not None:
                pend()

        def stage_PA(seq, b):
            pre = "p" if seq.kind == "p" else "s"
            if DEV_STOP >= 2.1:
                set_ones(4, 128)
            if seq.kind == "s" and DEV_STOP >= 2.2:
                for (vt, kc0, nk, is_cache) in seq.keys:
                    if not is_cache:
                        continue
                    i = nxt("stg", 3)
                    fw.dma("sp", stg[i][:, :], I["cache_a_k"][b, kc0:kc0 + 128, :], writes=[b_stg[i]])
                    ingest_k(stg[i][:, :], b_stg[i], 128, kc0, vt, eng=cast_eng())()
                    i = nxt("stg", 3)
                    fw.dma("sp", stg[i][:, :], I["cache_a_v"][b, kc0:kc0 + 128, :], writes=[b_stg[i]])
                    ingest_v(stg[i][:, :], b_stg[i], 128, vt, 4, 128, eng=cast_eng())
            knew0 = 0 if seq.kind == "p" else 2048
            vnew0 = 0 if seq.kind == "p" else 16

            def cons_q(ti, r0, nr, rt, bk, bbk):
                if DEV_STOP < 2.31:
                    return []
                i = nxt("stg", 3)
                ecopy("act", stg[i][0:nr, :], bk[0:nr, :], [bbk], [b_stg[i]])
                if DEV_STOP < 2.32:
                    return []
                rope_ops(None, stg[i][0:nr, :].rearrange("p (b d) -> p b d", b=8), nr, rt, ropeA, 8, 8, b_stg[i], b_stg[i],
                         stg[i][0:nr, :].rearrange("p (b d) -> p b d", b=8))
                if DEV_STOP < 2.33:
                    return []
                return [ingest_q(stg[i][0:nr, :], b_stg[i], nr, r0, ti, 0.125, eng="dve")]

            def cons_k(ti, r0, nr, rt, bk, bbk):
                i = nxt("stg", 3)
                ecopy("act", stg[i][0:nr, :], bk[0:nr, :], [bbk], [b_stg[i]])
                rope_ops(None, stg[i][0:nr, :].rearrange("p (b d) -> p b d", b=8), nr, rt, ropeA, 8, 8, b_stg[i], b_stg[i],
                         stg[i][0:nr, :].rearrange("p (b d) -> p b d", b=8))
                store_out(O["a_k_" + pre][b, r0:r0 + nr, :], stg[i][0:nr, :], b_stg[i])
                return [ingest_k(stg[i][0:nr, :], b_stg[i], nr, knew0 + r0, vnew0 + ti, eng="dve")]

            def cons_v(ti, r0, nr, rt, bk, bbk):
                i = nxt("stg", 3)
                ecopy("act", stg[i][0:nr, :], bk[0:nr, :], [bbk], [b_stg[i]])
                store_out(O["a_v_" + pre][b, r0:r0 + nr, :], stg[i][0:nr, :], b_stg[i])
                ingest_v(stg[i][0:nr, :], b_stg[i], nr, vnew0 + ti, 4, 128, eng="dve")
                return []

            if DEV_STOP >= 2.3:
                proj_block(seq, S_w0, 0, 512, cons_q)
            if DEV_STOP >= 2.4:
                proj_block(seq, S_w0, 512, 512, cons_k)
            if DEV_STOP >= 2.5:
                proj_block(seq, S_w0, 1024, 512, cons_v)

        def stage_QA(seq):
            nt = len(seq.tiles)
            bq = 2 if seq.kind == "p" else 1
            blk = 0
            for h in range(4):
                for qb0 in range(0, nt, bq):
                    qts = list(range(qb0, min(nt, qb0 + bq)))
                    nqs = [seq.tiles[j][1] for j in qts]
                    bqn = sum(nqs)
                    qcol0 = seq.tiles[qb0][0]
                    ob = (4, 5) if blk % 2 == 0 else (6, 7)
                    blk += 1
                    units = []
                    started = [False, False]
                    for ki, (vt, kc0, nk, is_cache) in enumerate(seq.keys):
                        if seq.kind == "p":
                            valid = [j for j in qts if ki <= j]
                        else:
                            valid = qts
                        if not valid:
                            continue
                        j0 = valid[0] - qb0
                        off0 = j0 * 128
                        nv = sum(seq.tiles[j][1] for j in valid)
                        u = dict(nk=nk, qk=[], extra=[], exps=[], pv=[], rd=[b_KT[vt]] + [b_QT[j] for j in valid], rdv=[b_V[vt]])
                        for m in range(2):
                            u["qk"].append((m, off0, nv, KT[m * 64:(m + 1) * 64, h, kc0:kc0 + nk], QT[m * 64:(m + 1) * 64, h, qcol0 + off0:qcol0 + off0 + nv]))
                            for j in valid:
                                if seq.kind == "p" and ki == j:
                                    u["extra"].append((m, (j - qb0) * 128, seq.tiles[j][1], negq[0:nk, 0:seq.tiles[j][1]]))
                            u["exps"].append((m, off0, nv, m * bqn + off0))
                        for m in range(2):
                            for j in valid:
                                jj = j - qb0
                                nq = seq.tiles[j][1]
                                u["pv"].append((bank[ob[m]][0:nq, jj * 129:(jj + 1) * 129], m * bqn + jj * 128, nq,
                                                VR[0:nk, vt, h * 129:(h + 1) * 129], not started[m], b_bank[ob[m]]))
                                started[m] = True
                        units.append(u)

                    def fin(h=h, qts=qts, qb0=qb0, ob=ob):
                        for j in qts:
                            jj = j - qb0
                            nq = seq.tiles[j][1]
                            o0 = bank[ob[0]][0:nq, jj * 129:jj * 129 + 128]
                            o1 = bank[ob[1]][0:nq, jj * 129:jj * 129 + 128]
                            kb = 8 + 4 * (jj % 2)
                            bs = smb(kb)
                            fw.op("dve", lambda q: q.reciprocal(out=sm[0:nq, kb:kb + 1], in_=bank[ob[0]][0:nq, jj * 129 + 128:jj * 129 + 129]), reads=[b_bank[ob[0]]], writes=[bs])
                            fw.op("dve", lambda q: q.reciprocal(out=sm[0:nq, kb + 1:kb + 2], in_=bank[ob[1]][0:nq, jj * 129 + 128:jj * 129 + 129]), reads=[b_bank[ob[1]]], writes=[bs])
                            fw.op("dve", lambda q: q.tensor_tensor(out=sm[0:nq, kb + 1:kb + 2], in0=sm[0:nq, kb + 1:kb + 2], in1=nlam[0:nq, :], op=ALU.mult), reads=[bs, b_lamv], writes=[bs])
                            ot = osb[0:nq, 1 + (jj % 2), :]
                            bo = b_osb[1 + (jj % 2)]
                            fw.op("dve", lambda q: q.tensor_scalar(out=ot, in0=o0, scalar1=sm[0:nq, kb:kb + 1], scalar2=None, op0=ALU.mult), reads=[b_bank[ob[0]], bs], writes=[bo])
                            fw.op("dve", lambda q: q.scalar_tensor_tensor(out=ot, in0=o1, scalar=sm[0:nq, kb + 1:kb + 2], in1=ot, op0=ALU.mult, op1=ALU.add),
                                  reads=[b_bank[ob[1]], bs, bo], writes=[bo])
                            fw.op("dve", lambda q: q.scalar_tensor_tensor(out=osb[0:nq, 3, :], in0=ot, scalar=1.0 / 128, in1=ot, op0=ALU.mult, op1=ALU.mult,
                                                                          accum_out=sm[0:nq, kb + 2:kb + 3]), reads=[bo], writes=[b_osb[3], bs])
                            rsqrt_col(sm[0:nq, kb + 2:kb + 3], nq, bs)
                            fw.op("dve", lambda q: q.scalar_tensor_tensor(out=oall[0:nq, j, h * 128:(h + 1) * 128], in0=ot, scalar=sm[0:nq, kb + 2:kb + 3], in1=gsub[0:nq, :],
                                                                          op0=ALU.mult, op1=ALU.mult), reads=[bo, bs, b_gsub], writes=[b_o[j]])
                    if DEV_STOP >= 3.4:
                        units[-1]["fin"] = fin
                    if DEV_STOP < 3.3:
                        units = units[:1]
                    run_units(units, sbanks=((0, 1), (2, 3)))
                    if DEV_STOP < 3.35:
                        return

        def stage_PB(seq, b):
            pre = "p" if seq.kind == "p" else "s"
            set_ones(8, 64)
            if seq.kind == "s":
                for (vt, kc0, nk, is_cache) in seq.keys_b:
                    if not is_cache:
                        continue
                    i = nxt("stg", 3)
                    fw.dma("sp", stg[i][:, :], I["cache_b_k"][b, kc0:kc0 + 128, :], writes=[b_stg[i]])
                    ingest_k(stg[i][:, :], b_stg[i], 128, kc0, vt, eng=cast_eng())()
                    i = nxt("stg", 3)
                    fw.dma("sp", stg[i][:, :], I["cache_b_v"][b, kc0:kc0 + 128, :], writes=[b_stg[i]])
                    ingest_v(stg[i][:, :], b_stg[i], 128, vt, 8, 64, eng=cast_eng())
            knew0 = 0 if seq.kind == "p" else 512
            vnew0 = 0 if seq.kind == "p" else 4

            def cons_q(ti, r0, nr, rt, bk, bbk):
                return [ingest_q(bk[0:nr, :], bbk, nr, r0, ti, 0.125, eng="dve")]

            def cons_k(ti, r0, nr, rt, bk, bbk):
                if seq.kind == "s" or r0 >= T - 512:
                    i = nxt("stg", 3)
                    ecopy("act", stg[i][0:nr, :], bk[0:nr, :], [bbk], [b_stg[i]])
                    orow = r0 - (T - 512) if seq.kind == "p" else r0
                    store_out(O["b_k_" + pre][b, orow:orow + nr, :], stg[i][0:nr, :], b_stg[i])
                return [ingest_k(bk[0:nr, :], bbk, nr, knew0 + r0, vnew0 + ti, eng="dve")]

            def cons_v(ti, r0, nr, rt, bk, bbk):
                if seq.kind == "s" or r0 >= T - 512:
                    i = nxt("stg", 3)
                    ecopy("act", stg[i][0:nr, :], bk[0:nr, :], [bbk], [b_stg[i]])
                    orow = r0 - (T - 512) if seq.kind == "p" else r0
                    store_out(O["b_v_" + pre][b, orow:orow + nr, :], stg[i][0:nr, :], b_stg[i])
                ingest_v(bk[0:nr, :], bbk, nr, vnew0 + ti, 8, 64, eng="dve")
                return []

            proj_block(seq, S_w0, 1536, 512, cons_q)
            proj_block(seq, S_w0, 2048, 512, cons_k)
            proj_block(seq, S_w0, 2560, 512, cons_v)

        def stage_QB(seq):
            blk = 0
            for j, (r0, nq, rt) in enumerate(seq.tiles):
                ob = (2, 3) if j % 2 == 0 else (4, 5)
                pairs = []
                for h in (0, 2, 4, 6, 1, 3, 5, 7):
                    if seq.kind == "p":
                        for d in (4, 3, 2, 1, 0):
                            ki = j - d
                            if ki < 0:
                                continue
                            ex = None
                            if d == 0:
                                ex = Tb[:, h * 2 + 0, :]
                            elif d == 1:
                                ex = Tb[:, h * 2 + 1, :]
                            elif d == 4:
                                ex = negq4[:, :]
                            pairs.append((h, ki, ex))
                    else:
                        for ki in range(5):
                            ex = None
                            if ki == 3:
                                ex = Tb[:, h * 2 + 1, :]
                            elif ki == 4:
                                ex = Tb[:, h * 2 + 0, :]
                            pairs.append((h, ki, ex))
                units = []
                started = [False, False]
                slotw = 128 if seq.kind == "p" else nq
                for p0 in range(0, len(pairs), 4):
                    grp = pairs[p0:p0 + 4]
                    nkmax = max(seq.keys_b[ki][2] for (_, ki, _) in grp)
                    u = dict(nk=nkmax, qk=[], extra=[], exps=[], pv=[], rd=[b_QT[j]], rdv=[])
                    same = all(seq.keys_b[ki][2] == nkmax for (_, ki, _) in grp)
                    for s, (h, ki, ex) in enumerate(grp):
                        vt, kc0, nk, _c = seq.keys_b[ki]
                        u["qk"].append((s * slotw, nq, KT[(h % 2) * 64:(h % 2 + 1) * 64, h // 2, kc0:kc0 + nk], QT[(h % 2) * 64:(h % 2 + 1) * 64, h // 2, r0:r0 + nq], nk))
                        if ex is not None:
                            u["extra"].append((s * slotw, nq, ex[0:nk, 0:nq], nk))
                        u["pv"].append((bank[ob[h // 4]][0:nq, (h % 4) * 65:(h % 4 + 1) * 65], s * slotw, nq, VR[0:nk, vt, h * 65:(h + 1) * 65],
                                        not started[h // 4], b_bank[ob[h // 4]], nk))
                        started[h // 4] = True
                        u["rd"].append(b_KT[vt])
                        u["rdv"].append(b_V[vt])
                        u["exps"].append((s * slotw, nq, nk))
                    if same:
                        u["exps"] = [(0, (len(grp) - 1) * slotw + nq, nkmax)]
                    units.append(u)

                def fin(j=j, nq=nq, ob=ob):
                    for g in range(2):
                        kb = 16 + 4 * g
                        bs = smb(kb)
                        fw.op("dve", lambda q: q.reciprocal(out=sm[0:nq, kb:kb + 4], in_=bank[ob[g]][0:nq, 0:260].rearrange("p (h e) -> p h e", e=65)[:, :, 64]),
                              reads=[b_bank[ob[g]]], writes=[bs])
                        for hh in range(4):
                            h = g * 4 + hh
                            osl = oall[0:nq, j, 512 + h * 64:512 + (h + 1) * 64]
                            fw.op("dve", lambda q, osl=osl, hh=hh: q.scalar_tensor_tensor(out=osl, in0=bank[ob[g]][0:nq, hh * 65:hh * 65 + 64], scalar=sm[0:nq, kb + hh:kb + hh + 1],
                                                                                          in1=osl, op0=ALU.mult, op1=ALU.mult), reads=[b_bank[ob[g]], bs, b_o[j]], writes=[b_o[j]])
                units[-1]["fin"] = fin
                run_units_nk(units)

        def run_units_nk(units):
            pend = None
            for ui, u in enumerate(units):
                sbk = ui % 2
                fns = []
                exd = {c0: (n, rhs_t, nk) for (c0, n, rhs_t, nk) in u["extra"]}
                for (c0, n, lhsT, rhs, nk) in u["qk"]:
                    fns.append(lambda q, c0=c0, n=n, lhsT=lhsT, rhs=rhs, nk=nk: q.matmul(bank[sbk][0:nk, c0:c0 + n], lhsT=lhsT, rhs=rhs, start=True, stop=False,
                                                                                         skip_group_check=True))
                    if c0 in exd:
                        n2, rhs_t, nk2 = exd[c0]
                        fns.append(lambda q, c0=c0, n2=n2, rhs_t=rhs_t, nk2=nk2: q.matmul(bank[sbk][0:nk2, c0:c0 + n2], lhsT=ident[0:nk2, 0:nk2], rhs=rhs_t, start=False, stop=True,
                                                                                          skip_group_check=True))
                fw.op("pe", fns, reads=u["rd"] + [b_ident, b_negq4, b_Tb], writes=[b_bank[sbk]])
                pi = ui % 3
                for (c0, n, nk) in u["exps"]:
                    fw.op("act", lambda q, c0=c0, n=n, nk=nk: q.activation(out=PT[pi][0:nk, c0:c0 + n], in_=bank[sbk][0:nk, c0:c0 + n], func=AF.Exp),
                          reads=[b_bank[sbk]], writes=[b_PT[pi]])
                if pend is not None:
                    pend()
                if u.get("hook"):
                    u["hook"]()

                def pvs(u=u, pi=pi):
                    obufs = []
                    fns = []
                    for (O_ap, pc0, nq, V_ap, start, bO, nk) in u["pv"]:
                        fns.append(lambda q, O_ap=O_ap, pc0=pc0, nq=nq, V_ap=V_ap, start=start, nk=nk: q.matmul(
                            O_ap, lhsT=PT[pi][0:nk, pc0:pc0 + nq], rhs=V_ap, start=start, stop=True, skip_group_check=True))
                        if bO not in obufs:
                            obufs.append(bO)
                    fw.op("pe", fns, reads=[b_PT[pi]] + u["rdv"], writes=obufs)
                    if u.get("fin"):
                        u["fin"]()
                pend = pvs
            if pend is not None:
                pend()

        def stage_gate(seq, scr_w, gate_c0, half, premul):
            def cons(ti, r0, nr, rt, bk, bbk):
                k = nxt("tb16", 3)
                fw.op("act", lambda q: q.activation(out=tb16[k][0:nr, :], in_=bk[0:nr, :], func=AF.Tanh, scale=0.5), reads=[bbk], writes=[b_tb16[k]])
                fw.op("dve", lambda q: q.scalar_tensor_tensor(out=tb16[k][0:nr, :], in0=tb16[k][0:nr, :], scalar=1.0, in1=bk[0:nr, :], op0=ALU.add, op1=ALU.mult),
                      reads=[b_tb16[k], bbk], writes=[b_tb16[k]])
                osl = oall[0:nr, ti, half * 512:(half + 1) * 512]
                if premul:
                    fw.op("dve", lambda q: q.scalar_tensor_tensor(out=osl, in0=tb16[k][0:nr, :], scalar=0.5, in1=osl, op0=ALU.mult, op1=ALU.mult),
                          reads=[b_o[ti], b_tb16[k]], writes=[b_o[ti]])
                else:
                    fw.op("dve", lambda q: q.tensor_scalar(out=osl, in0=tb16[k][0:nr, :], scalar1=0.5, scalar2=None, op0=ALU.mult), reads=[b_tb16[k]], writes=[b_o[ti]])
                return []
            proj_block(seq, scr_w, gate_c0 + half * 512, 512, cons)

        def stage_GY(seq, b, resid_src, dst, dst_is_output, src_bufs=()):
            xsel = {}

            def tphase(j):
                r0, nr, rt = seq.tiles[j]
                i = nxt("xt", 2)
                xsel[j] = i
                fw.dma("sp", xt[i][0:nr, :], resid_src[r0:r0 + nr, :], reads=list(src_bufs), writes=[b_xt[i]], sembuf=b_xt[i])
                if DEV_DBG and seq.idx == 0:
                    fw.op("dve", lambda q: q.tensor_copy(out=ytmp[0:nr, :], in_=oall[0:nr, j, :]), reads=[b_o[j]], writes=[b_ytmp])
                    fw.dma("pool", O["dbg"][r0:r0 + nr, :], ytmp[0:nr, :], reads=[b_ytmp], is_output=True, sembuf=b_ytmp)
                k = j % 2
                fw.op("pe", [lambda q, c=c: q.transpose(out=tbk[k][:, c * 128:c * 128 + nr], in_=oall[0:nr, j, c * 128:(c + 1) * 128], identity=ident[0:nr, 0:nr])
                             for c in range(8)], reads=[b_o[j], b_ident], writes=[b_tbk[k]])
                g = j % 2
                ecopy("act", ogT[g][:, 0:4, 0:nr], tbk[k][:, 0:512].rearrange("p (c t) -> p c t", c=4)[:, :, 0:nr], [b_tbk[k]], [b_ogT[g]])
                ecopy("dve", ogT[g][:, 4:8, 0:nr], tbk[k][:, 512:1024].rearrange("p (c t) -> p c t", c=4)[:, :, 0:nr], [b_tbk[k]], [b_ogT[g]])

            def yphase(j):
                r0, nr, rt = seq.tiles[j]
                i = xsel[j]
                g = j % 2
                yb = ((4, 5), (2, 3))[j % 2]
                for half in range(2):
                    fw.op("pe", [lambda q, c=c: q.matmul(bank[yb[half]][0:nr, :], lhsT=ogT[g][:, c, 0:nr], rhs=wout[:, c, half * 512:(half + 1) * 512],
                                                         start=(c == 0), stop=(c == 7)) for c in range(8)],
                          reads=[b_ogT[g], b_wout], writes=[b_bank[yb[half]]])
                rs, bs = rms_scale([bank[yb[0]][0:nr, :], bank[yb[1]][0:nr, :]], nr, DM, [b_bank[yb[0]], b_bank[yb[1]]], 24 + 2 * (j % 2))
                for half in range(2):
                    fw.op("dve", lambda q: q.scalar_tensor_tensor(out=ytmp[0:nr, half * 512:(half + 1) * 512], in0=bank[yb[half]][0:nr, :], scalar=rs,
                                                                  in1=gpost[0:nr, half * 512:(half + 1) * 512], op0=ALU.mult, op1=ALU.mult),
                          reads=[b_bank[yb[half]], bs, b_gpost], writes=[b_ytmp])
                fw.op("dve", lambda q: q.tensor_tensor(out=xt[i][0:nr, 0:512], in0=ytmp[0:nr, 0:512], in1=xt[i][0:nr, 0:512], op=ALU.add), reads=[b_ytmp, b_xt[i]], writes=[b_xt[i]])
                fw.op("pool", lambda q: q.tensor_tensor(out=xt[i][0:nr, 512:1024], in0=ytmp[0:nr, 512:1024], in1=xt[i][0:nr, 512:1024], op=ALU.add), reads=[b_ytmp, b_xt[i]], writes=[b_xt[i]])
                if dst_is_output:
                    fw.dma("pool", dst[r0:r0 + nr, :], xt[i][0:nr, :], reads=[b_xt[i]], is_output=True, sembuf=b_xout[i])
                else:
                    h1_evs.setdefault((seq.kind, b), []).append(
                        fw.dma("pool", dst[r0:r0 + nr, :], xt[i][0:nr, :], reads=[b_xt[i]], writes=[dbuf(("h1", seq.kind, b))], sembuf=b_xout[i]))

            n = len(seq.tiles)
            tphase(0)
            for j in range(n):
                if j + 1 < n:
                    tphase(j + 1)
                yphase(j)

        def stage_PC(seq, b):
            pre = "p" if seq.kind == "p" else "s"
            SC = (64 + 32) ** -0.5
            set_ones(4, 64)
            knew0 = 0 if seq.kind == "p" else 2048
            vnew0 = 0 if seq.kind == "p" else 16
            spill_ev = []
            slot_ctr = {"i": 0}

            def next_slot():
                i = slot_ctr["i"] % 13
                slot_ctr["i"] += 1
                return i

            wq1 = oall[:, 13:16, :].rearrange("p a b -> p (a b)")[:, 0:2304].rearrange("p (c n) -> p c n", c=6)
            bwq1 = [b_o[13], b_o[14], b_o[15]]
            wsrc = S_wuq.rearrange("(c p) n -> p c n", p=128)
            wait_prep(S_wuq.name)
            fw.dma("sp", wqb[:, :, :], wsrc[:, :, 0:384], reads=[dbuf(S_wuq.name)], writes=[b_wqb])
            fw.dma("sp", wq1, wsrc[:, :, 384:768], reads=[dbuf(S_wuq.name)], writes=bwq1, sembuf=b_o[13])

            def lat_ingest(src_lat, b_lat, src_kr, b_kr, nr, kcol0, vt, eng):
                j = nxt("tb16", 3)
                ecopy(eng, tb16[j][0:nr, 0:256], src_lat, [b_lat], [b_tb16[j]])
                ecopy(eng, tb16[j][0:nr, 256:288], src_kr, [b_kr], [b_tb16[j]])
                k = nxt("tbk", 2)
                fw.op("pe", [lambda q, c=c: q.transpose(out=tbk[k][:, c * 128:c * 128 + nr], in_=tb16[j][0:nr, c * 128:(c + 1) * 128], identity=ident[0:nr, 0:nr]) for c in range(2)]
                      + [lambda q: q.transpose(out=tbk[k][0:32, 256:256 + nr], in_=tb16[j][0:nr, 256:288], identity=ident[0:nr, 0:nr])],
                      reads=[b_tb16[j], b_ident], writes=[b_tbk[k]])
                ecopy("dve", latT[:, :, 0:nr], tbk[k][:, 0:256].rearrange("p (c t) -> p c t", c=2)[:, :, 0:nr], [b_tbk[k]], [b_latT])
                ecopy("act", KT[64:96, 0:4, kcol0:kcol0 + nr], tbk[k][0:32, 256:256 + nr].unsqueeze(1).to_broadcast([32, 4, nr]), [b_tbk[k]], [b_KT[vt]])
                for g in range(2):
                    fns = []
                    for hh in range(4):
                        h = g * 4 + hh
                        for c in range(2):
                            fns.append(lambda q, hh=hh, h=h, c=c: q.matmul(bank[g][0:64, hh * 128:hh * 128 + nr], lhsT=wuk[:, c, h * 64:(h + 1) * 64], rhs=latT[:, c, 0:nr],
                                                                           start=(c == 0), stop=(c == 1), skip_group_check=True))
                    fw.op("pe", fns, reads=[b_latT, b_wuk], writes=[b_bank[g]])
                ecopy("dve", KT[0:64, 0:4, kcol0:kcol0 + nr], bank[0][0:64, :].rearrange("p (h t) -> p h t", h=4)[:, :, 0:nr], [b_bank[0]], [b_KT[vt]])
                sk = next_slot()
                kst = oall[0:64, sk, 0:512].rearrange("p (h t) -> p h t", h=4)[:, :, 0:nr]
                ecopy("act", kst, bank[1][0:64, :].rearrange("p (h t) -> p h t", h=4)[:, :, 0:nr], [b_bank[1]], [b_o[sk]])
                spill_ev.append(fw.dma("pool", S_kt1[:, :, kcol0:kcol0 + nr], kst, reads=[b_o[sk]], writes=[dbuf("kt1")], sembuf=b_o[sk]))
                fw.op("pe", [lambda q, c=c: q.matmul(bank[2][0:nr, 0:512], lhsT=latT[:, c, 0:nr], rhs=wuv[:, c, :], start=(c == 0), stop=(c == 1))
                             for c in range(2)], reads=[b_latT, b_wuv], writes=[b_bank[2]])
                ingest_v(bank[2][0:nr, 0:256], b_bank[2], nr, vt, 4, 64, eng="act")
                sv = next_slot()
                ecopy("dve", oall[0:nr, sv, 0:256], bank[2][0:nr, 256:512], [b_bank[2]], [b_o[sv]])
                spill_ev.append(fw.dma("pool", S_v1[vt, 0:nr, :], oall[0:nr, sv, 0:256], reads=[b_o[sv]], writes=[dbuf("v1")], sembuf=b_o[sv]))

            if seq.kind == "s":
                for (vt, kc0, nk, is_cache) in seq.keys:
                    if not is_cache:
                        continue
                    i = nxt("stg", 3)
                    fw.dma("sp", stg[i][:, 0:256], I["cache_c_latent"][b, kc0:kc0 + 128, :], writes=[b_stg[i]])
                    fw.dma("sp", stg[i][:, 256:288], I["cache_c_krope"][b, kc0:kc0 + 128, :], writes=[b_stg[i]])
                    lat_ingest(stg[i][:, 0:256], b_stg[i], stg[i][:, 256:288], b_stg[i], 128, kc0, vt, cast_eng())

            def cons_ckv(ti, r0, nr, rt, bk, bbk):
                i = nxt("stg", 3)
                rs, bs = rms_scale(bk[0:nr, 0:256], nr, 256, [bbk], 28)
                ecopy("act", stg[i][0:nr, 256:288], bk[0:nr, 256:288], [bbk], [b_stg[i]])
                fw.op("dve", lambda q: q.scalar_tensor_tensor(out=stg[i][0:nr, 0:256], in0=bk[0:nr, 0:256], scalar=rs, in1=gckv[0:nr, :], op0=ALU.mult, op1=ALU.mult),
                      reads=[bbk, bs, b_gckv], writes=[b_stg[i]])
                rope_ops(None, stg[i][0:nr, 256:288].rearrange("p (b d) -> p b d", b=1), nr, rt, ropeC, 16, 1, b_stg[i], b_stg[i],
                         stg[i][0:nr, 256:288].rearrange("p (b d) -> p b d", b=1))
                store_out(O["c_lat_" + pre][b, r0:r0 + nr, :], stg[i][0:nr, 0:256], b_stg[i])
                store_out(O["c_krope_" + pre][b, r0:r0 + nr, :], stg[i][0:nr, 256:288], b_stg[i])
                return [lambda: lat_ingest(stg[i][0:nr, 0:256], b_stg[i], stg[i][0:nr, 256:288], b_stg[i], nr, knew0 + r0, vnew0 + ti, "dve")]

            proj_block(seq, S_w1, 768, 288, cons_ckv)

            w1a = load_wblock(S_w1, DM, 0, 512)
            w1b = load_wblock(S_w1, DM, 512, 256)
            def s1(ti):
                r0, nr, rt = seq.tiles[ti]
                ba, bb = ((4, 5), (0, 1))[ti % 2]
                fw.op("pe", [lambda q, c=c: q.matmul(bank[ba][0:nr, :], lhsT=uT[:, c, r0:r0 + nr], rhs=w1a[0][:, c, :], start=(c == 0), stop=(c == 7)) for c in range(8)],
                      reads=[b_uT[ti], w1a[1]], writes=[b_bank[ba]])
                fw.op("pe", [lambda q, c=c: q.matmul(bank[bb][0:nr, 0:256], lhsT=uT[:, c, r0:r0 + nr], rhs=w1b[0][:, c, 0:256], start=(c == 0), stop=(c == 7)) for c in range(8)],
                      reads=[b_uT[ti], w1b[1]], writes=[b_bank[bb]])

            stgsel = {}

            def s2(ti):
                r0, nr, rt = seq.tiles[ti]
                ba, bb = ((4, 5), (0, 1))[ti % 2]
                rs, bs = rms_scale([bank[ba][0:nr, :], bank[bb][0:nr, 0:256]], nr, 768, [b_bank[ba], b_bank[bb]], 32 + 2 * (ti % 2))
                g = ti % 2
                cq16 = ogT[g][:, :, :].rearrange("p c t -> p (c t)")
                fw.op("act", lambda q: q.activation(out=cq16[0:nr, 0:512], in_=bank[ba][0:nr, :], func=AF.Copy, scale=rs), reads=[b_bank[ba], bs], writes=[b_ogT[g]])
                fw.op("act", lambda q: q.activation(out=cq16[0:nr, 512:768], in_=bank[bb][0:nr, 0:256], func=AF.Copy, scale=rs), reads=[b_bank[bb], bs], writes=[b_ogT[g]])
                k = nxt("tbk", 2)
                fw.op("pe", [lambda q, c=c: q.transpose(out=tbk[k][:, c * 128:c * 128 + nr], in_=cq16[0:nr, c * 128:(c + 1) * 128], identity=ident[0:nr, 0:nr]) for c in range(6)],
                      reads=[b_ogT[g], b_ident], writes=[b_tbk[k]])
                cqT = Ef[g][:, :].bitcast(BF16).rearrange("p (c t) -> p c t", c=8)
                ecopy("dve", cqT[:, 0:6, 0:nr], tbk[k][:, 0:768].rearrange("p (c t) -> p c t", c=6)[:, :, 0:nr], [b_tbk[k]], [b_Ef[g]])
                for grp in range(2):
                    qbk = (3, 2)[grp]
                    wq_t, wq_b = (wqb, [b_wqb]) if grp == 0 else (wq1, bwq1)
                    fw.op("pe", [lambda q, c=c: q.matmul(bank[qbk][0:nr, 0:384], lhsT=cqT[:, c, 0:nr], rhs=wq_t[:, c, 0:384], start=(c == 0), stop=(c == 5)) for c in range(6)],
                          reads=[b_Ef[g]] + wq_b, writes=[b_bank[qbk]])
                    i = nxt("stg", 3)
                    stgsel[(ti, grp)] = i
                    ecopy("act", stg[i][0:nr, 0:384], bank[qbk][0:nr, 0:384], [b_bank[qbk]], [b_stg[i]])

            def s3(ti):
                r0, nr, rt = seq.tiles[ti]
                for grp in range(2):
                    i = stgsel[(ti, grp)]
                    dst3 = stg[i][0:nr, 0:384].rearrange("p (h d) -> p h d", h=4)[:, :, 64:96]
                    rope_ops(None, dst3, nr, rt, ropeC, 16, 4, b_stg[i], b_stg[i], dst3)
                    if grp == 0:
                        ingest_q(stg[i][0:nr, 0:384], b_stg[i], nr, r0, ti, SC, ncol=384, part=96, eng="dve")()
                    else:
                        j = nxt("tb16", 3)
                        ecopy("dve", tb16[j][0:nr, 0:384], stg[i][0:nr, 0:384], [b_stg[i]], [b_tb16[j]], SC)
                        k2 = nxt("tbk", 2)
                        fw.op("pe", [lambda q, c=c: q.transpose(out=tbk[k2][0:96, c * 128:c * 128 + nr], in_=tb16[j][0:nr, c * 96:(c + 1) * 96], identity=ident[0:nr, 0:nr]) for c in range(4)],
                              reads=[b_tb16[j], b_ident], writes=[b_tbk[k2]])
                        sq = next_slot()
                        qst = oall[0:96, sq, 0:512].rearrange("p (h t) -> p h t", h=4)[:, :, 0:nr]
                        ecopy(copy_eng(), qst, tbk[k2][0:96, 0:512].rearrange("p (c t) -> p c t", c=4)[:, :, 0:nr], [b_tbk[k2]], [b_o[sq]])
                        spill_ev.append(fw.dma("pool", S_qt1[:, :, r0:r0 + nr], qst, reads=[b_o[sq]], writes=[dbuf("qt1")], sembuf=b_o[sq]))

            ntl = len(seq.tiles)
            s1(0)
            for ti in range(ntl):
                if ti + 1 < ntl:
                    s1(ti + 1)
                if ti >= 1:
                    s3(ti - 1)
                s2(ti)
            s3(ntl - 1)
            return spill_ev

        def reload_C(seq, spill_ev):
            fw.wait_events("sp", spill_ev)
            nq = seq.ntok
            nkc = seq.keys[-1][1] + seq.keys[-1][2]
            fw.dma("sp", QT[0:96, 0:4, 0:nq], S_qt1[:, :, 0:nq], reads=[dbuf("qt1")], writes=b_QT, sembuf=b_QT[0])
            fw.dma("sp", KT[0:64, 0:4, 0:nkc], S_kt1[:, :, 0:nkc], reads=[dbuf("kt1")], writes=b_KT, sembuf=b_KT[0])
            for (vt, kc0, nk, is_cache) in seq.keys:
                fw.dma("sp", VR[0:nk, vt, 0:260].rearrange("p (h e) -> p h e", e=65)[:, :, 0:64], S_v1[vt, 0:nk, :].rearrange("p (h d) -> p h d", h=4),
                       reads=[dbuf("v1")], writes=[b_V[vt]], sembuf=b_V[vt])

        def stage_QC(seq, grp):
            nt = len(seq.tiles)
            bq = 4 if seq.kind == "p" else 1
            blk = 0
            for hh in range(4):
                h = grp * 4 + hh
                for qb0 in range(0, nt, bq):
                    qts = list(range(qb0, min(nt, qb0 + bq)))
                    qcol0 = seq.tiles[qb0][0]
                    ob = 2 + (blk % 2)
                    blk += 1
                    units = []
                    started = False
                    for ki, (vt, kc0, nk, is_cache) in enumerate(seq.keys):
                        valid = [j for j in qts if ki <= j] if seq.kind == "p" else qts
                        if not valid:
                            continue
                        j0 = valid[0] - qb0
                        off0 = j0 * 128
                        nv = sum(seq.tiles[j][1] for j in valid)
                        u = dict(nk=nk, qk=[], extra=[], exps=[(0, off0, nv, off0)], pv=[], rd=[b_KT[vt]] + [b_QT[j] for j in valid], rdv=[b_V[vt]])
                        u["qk"].append((0, off0, nv, KT[0:96, hh, kc0:kc0 + nk], QT[0:96, hh, qcol0 + off0:qcol0 + off0 + nv]))
                        for j in valid:
                            if seq.kind == "p" and ki == j:
                                u["extra"].append((0, (j - qb0) * 128, seq.tiles[j][1], negq[0:nk, 0:seq.tiles[j][1]]))
                        for j in valid:
                            jj = j - qb0
                            nq = seq.tiles[j][1]
                            u["pv"].append((bank[ob][0:nq, jj * 65:(jj + 1) * 65], jj * 128, nq, VR[0:nk, vt, hh * 65:(hh + 1) * 65], not started, b_bank[ob]))
                            started = True
                        units.append(u)

                    def fin(h=h, qts=qts, qb0=qb0, ob=ob):
                        nq = seq.tiles[qts[0]][1]
                        kb = 36
                        bs = smb(kb)
                        nj = len(qts)
                        fw.op("dve", lambda q: q.reciprocal(out=sm[0:nq, kb:kb + nj], in_=bank[ob][0:nq, 0:nj * 65].rearrange("p (j e) -> p j e", e=65)[:, :, 64]),
                              reads=[b_bank[ob]], writes=[bs])
                        for j in qts:
                            jj = j - qb0
                            ecopy("dve", oall[0:nq, j, h * 64:(h + 1) * 64], bank[ob][0:nq, jj * 65:jj * 65 + 64], [b_bank[ob], bs], [b_o[j]], scale=sm[0:nq, kb + jj:kb + jj + 1])
                    units[-1]["fin"] = fin
                    run_units(units)

        def stage_PD(seq, b):
            pre = "p" if seq.kind == "p" else "s"
            set_ones(8, 64)
            if seq.kind == "s":
                for (vt, kc0, nk, is_cache) in seq.keys:
                    if not is_cache:
                        continue
                    i = nxt("stg", 3)
                    fw.dma("sp", stg[i][:, :], I["cache_d_k"][b, kc0:kc0 + 128, :], writes=[b_stg[i]])
                    ingest_k(stg[i][:, :], b_stg[i], 128, kc0, vt, eng=cast_eng())()
                    i = nxt("stg", 3)
                    fw.dma("sp", stg[i][:, :], I["cache_d_v"][b, kc0:kc0 + 128, :], writes=[b_stg[i]])
                    ingest_v(stg[i][:, :], b_stg[i], 128, vt, 8, 64, eng=cast_eng())
            knew0 = 0 if seq.kind == "p" else 2048
            vnew0 = 0 if seq.kind == "p" else 16

            def cons_q(ti, r0, nr, rt, bk, bbk):
                return [ingest_q(bk[0:nr, :], bbk, nr, r0, ti, 0.125, eng="dve")]

            def cons_k(ti, r0, nr, rt, bk, bbk):
                i = nxt("stg", 3)
                ecopy("act", stg[i][0:nr, :], bk[0:nr, :], [bbk], [b_stg[i]])
                store_out(O["d_k_" + pre][b, r0:r0 + nr, :], stg[i][0:nr, :], b_stg[i])
                return [ingest_k(bk[0:nr, :], bbk, nr, knew0 + r0, vnew0 + ti, eng="dve")]

            def cons_v(ti, r0, nr, rt, bk, bbk):
                i = nxt("stg", 3)
                ecopy("act", stg[i][0:nr, :], bk[0:nr, :], [bbk], [b_stg[i]])
                store_out(O["d_v_" + pre][b, r0:r0 + nr, :], stg[i][0:nr, :], b_stg[i])
                ingest_v(bk[0:nr, :], bbk, nr, vnew0 + ti, 8, 64, eng="dve")
                return []

            proj_block(seq, S_w1, 1056, 512, cons_q)
            proj_block(seq, S_w1, 1568, 512, cons_k)
            proj_block(seq, S_w1, 2080, 512, cons_v)

        def stage_QD(seq):
            nt = len(seq.tiles)
            bq = 4 if seq.kind == "p" else 1
            xbanks = (0, 1, 3)
            st_ = {"u": 0, "cs": 0}
            prev_tiles = []
            for qb0 in range(0, nt, bq):
                qts = list(range(qb0, min(nt, qb0 + bq)))
                qcol0 = seq.tiles[qb0][0]
                for h in range(8):
                    hp = (h % 2) * 64
                    fw.op("pool", lambda q: q.memset(osb[:, :, 0:64], 0.0), writes=b_osb)
                    pend_a = None
                    pend_b = None
                    for ki, (vt, kc0, nk, is_cache) in enumerate(seq.keys):
                        valid = [j for j in qts if ki <= j] if seq.kind == "p" else qts
                        if not valid:
                            continue
                        j0 = valid[0] - qb0
                        off0 = j0 * 128
                        nv = sum(seq.tiles[j][1] for j in valid)
                        ucount = st_["u"]
                        st_["u"] += 1
                        xb = xbanks[ucount % 3]
                        eb = ucount % 2
                        pi = ucount % 3
                        diag = [j for j in valid if (seq.kind == "p" and ki == j) or (seq.kind == "s" and not is_cache)]
                        fw.op("pe", lambda q: q.matmul(bank[xb][0:nk, off0:off0 + nv], lhsT=KT[hp:hp + 64, h // 2, kc0:kc0 + nk], rhs=QT[hp:hp + 64, h // 2, qcol0 + off0:qcol0 + off0 + nv],
                                                       start=True, stop=False, skip_group_check=True), reads=[b_KT[vt]] + [b_QT[j] for j in valid], writes=[b_bank[xb]])
                        fw.op("act", lambda q: q.activation(out=Ef[eb][0:nk, off0:off0 + nv], in_=bank[xb][0:nk, off0:off0 + nv], func=AF.Exp), reads=[b_bank[xb]], writes=[b_Ef[eb]])
                        fw.op("act", lambda q: q.activation(out=SPb[eb][0:nk, off0:off0 + nv], in_=Ef[eb][0:nk, off0:off0 + nv], func=AF.Ln, bias=1.0), reads=[b_Ef[eb]], writes=[b_SPb[eb]])
                        for j in diag:
                            c = (j - qb0) * 128
                            nq = seq.tiles[j][1]
                            fw.op("pool", lambda q: q.tensor_tensor(out=SPb[eb][0:nk, c:c + nq], in0=SPb[eb][0:nk, c:c + nq], in1=m01d[0:nk, 0:nq], op=ALU.mult),
                                  reads=[b_SPb[eb], b_m01d], writes=[b_SPb[eb]])

                        def stage2a(vt=vt, nk=nk, valid=valid, off0=off0, nv=nv, xb=xb, eb=eb, pi=pi, diag=diag, qts=qts, qb0=qb0):
                            fns = [lambda q: q.matmul(bank[xb][0:nk, off0:off0 + nv], lhsT=nut[0:nk, 0:nk], rhs=SPb[eb][0:nk, off0:off0 + nv], start=False, stop=False, skip_group_check=True)]
                            for j in diag:
                                c = (j - qb0) * 128
                                nq = seq.tiles[j][1]
                                fns.append(lambda q, c=c, nq=nq: q.matmul(bank[xb][0:nk, c:c + nq], lhsT=ident[0:nk, 0:nk], rhs=negd[0:nk, 0:nq], start=False, stop=True, skip_group_check=True))
                            fw.op("pe", fns, reads=[b_SPb[eb], b_nut, b_negd, b_ident], writes=[b_bank[xb]])
                            fw.op("act", lambda q: q.activation(out=PT[pi][0:nk, off0:off0 + nv], in_=bank[xb][0:nk, off0:off0 + nv], func=AF.Exp), reads=[b_bank[xb]], writes=[b_PT[pi]])

                        def stage2b(vt=vt, nk=nk, valid=valid, pi=pi, qb0=qb0, h=h, qts=qts, ucount=ucount):
                            dk = 40 + 4 * pi
                            bs = smb(dk)
                            wbk = (2, 4)[ucount % 2]
                            fns = []
                            for j in valid:
                                jj = j - qb0
                                nq = seq.tiles[j][1]
                                fns.append(lambda q, jj=jj, nq=nq: q.matmul(bank[wbk][0:nq, jj * 65:(jj + 1) * 65], lhsT=PT[pi][0:nk, jj * 128:jj * 128 + nq], rhs=VR[0:nk, vt, h * 65:(h + 1) * 65],
                                                                            start=True, stop=True, skip_group_check=True))
                            fw.op("pe", fns, reads=[b_PT[pi], b_V[vt]], writes=[b_bank[wbk]])
                            jlo = valid[0] - qb0
                            nqm = seq.tiles[valid[0]][1]
                            nj = len(qts)
                            w3 = bank[wbk][0:nqm, 0:nj * 65].rearrange("p (j e) -> p j e", e=65)
                            fw.op("dve", lambda q: q.tensor_scalar(out=sm[0:nqm, dk + jlo:dk + nj], in0=w3[:, jlo:nj, 64], scalar1=-1.0, scalar2=1.0, op0=ALU.mult, op1=ALU.add),
                                  reads=[b_bank[wbk]], writes=[bs])
                            nv_ = nj - jlo
                            dec_b = sm[0:nqm, dk + jlo:dk + nj].unsqueeze(2).to_broadcast([nqm, nv_, 64])
                            fw.op("dve", lambda q: q.tensor_tensor(out=osb[0:nqm, jlo:nj, 0:64], in0=osb[0:nqm, jlo:nj, 0:64], in1=dec_b, op=ALU.mult),
                                  reads=b_osb + [bs], writes=b_osb)
                            fw.op("dve", lambda q: q.tensor_tensor(out=osb[0:nqm, jlo:nj, 0:64], in0=w3[:, jlo:nj, 0:64], in1=osb[0:nqm, jlo:nj, 0:64], op=ALU.add),
                                  reads=b_osb + [b_bank[wbk]], writes=b_osb)

                        if pend_a is not None:
                            pend_a()
                        if pend_b is not None:
                            pend_b()
                        pend_b = None
                        if pend_a is not None:
                            pend_b = pend_a.b
                        stage2a.b = stage2b
                        pend_a = stage2a
                    if pend_a is not None:
                        pend_a()
                    if pend_b is not None:
                        pend_b()
                    if pend_a is not None:
                        pend_a.b()
                    for j in qts:
                        jj = j - qb0
                        nq = seq.tiles[j][1]
                        osl = oall[0:nq, j, 512 + h * 64:512 + (h + 1) * 64]
                        fw.op("dve", lambda q, osl=osl, jj=jj: q.tensor_tensor(out=osl, in0=osb[0:nq, jj, 0:64], in1=osl, op=ALU.mult), reads=[b_osb[jj], b_o[j]], writes=[b_o[j]])

        def load_wukv():
            for bb in (b_wuk, b_wuv, b_latT, b_wqb):
                bb.lw = b_Tb.lw
                bb.rd = list(b_Tb.rd)
            for (wsrc, wdst, bw) in ((I["w_uk"], wuk, b_wuk), (I["w_uv"], wuv, b_wuv)):
                for c in range(2):
                    i = nxt("stg", 3)
                    fw.dma("sp", stg[i][:, :], wsrc[c * 128:(c + 1) * 128, :], writes=[b_stg[i]])
                    fw.op("dve", lambda q: q.tensor_copy(out=wdst[:, c, :], in_=stg[i][:, :]), reads=[b_stg[i]], writes=[bw])

        seqs = [Seq("p", i) for i in range(NP)] + [Seq("s", i) for i in range(NS)]
        h1_evs = {}

        def load_layer_consts(layer):
            fw.dma("sp", gpost[:], bcast_ap(I["g_post0"] if layer == 0 else I["g_post1"], DM), writes=[b_gpost])
            wait_prep((S_wo0 if layer == 0 else S_wo1).name)
            fw.dma("sp", wout[:], (S_wo0 if layer == 0 else S_wo1).rearrange("(c p) n -> p c n", p=128), reads=[dbuf((S_wo0 if layer == 0 else S_wo1).name)], writes=[b_wout])

        u_done = set()

        def do_U(seq, layer):
            key = (layer, seq.kind, seq.idx)
            if key in u_done:
                return
            u_done.add(key)
            b = seq.idx
            if layer == 0 or 0 not in layers:
                stage_U(seq, I["x_prompt"][b] if seq.kind == "p" else I["x_sample"][b])
            else:
                fw.wait_events("sp", h1_evs.get((seq.kind, b), []))
                stage_U(seq, S_h1p[b] if seq.kind == "p" else S_h1s[b], [dbuf(("h1", seq.kind, b))])

        if 0 in layers and DEV_STOP >= 2:
            load_layer_consts(0)
            for si, seq in enumerate(seqs):
                b = seq.idx
                src = I["x_prompt"][b] if seq.kind == "p" else I["x_sample"][b]
                if 1 in layers:
                    dst = S_h1p[b] if seq.kind == "p" else S_h1s[b]
                    is_out = False
                else:
                    dst = O["y_prompt"][b] if seq.kind == "p" else O["y_sample"][b]
                    is_out = True
                do_U(seq, 0)
                if si == 0 and seq.kind == "p":
                    prep_state["active"] = True
                if DEV_STOP >= 2.1:
                    stage_PA(seq, b)
                prep_flush()
                if DEV_STOP >= 3.1:
                    stage_QA(seq)
                if DEV_STOP >= 5:
                    stage_PB(seq, b)
                if DEV_STOP >= 6:
                    stage_gate(seq, S_w0, 3072, 0, True)
                    stage_gate(seq, S_w0, 3072, 1, False)
                    if si + 1 < len(seqs):
                        do_U(seqs[si + 1], 0)
                    stage_QB(seq)
                    stage_GY(seq, b, src, dst, is_out)
        prep_flush()

        if 1 in layers:
            load_layer_consts(1)
            load_wukv()
            for si, seq in enumerate(seqs):
                b = seq.idx
                if 0 in layers:
                    src = S_h1p[b] if seq.kind == "p" else S_h1s[b]
                else:
                    src = I["x_prompt"][b] if seq.kind == "p" else I["x_sample"][b]
                dst = O["y_prompt"][b] if seq.kind == "p" else O["y_sample"][b]
                hb = [dbuf(("h1", seq.kind, b))] if 0 in layers else []
                fw.wait_events("sp", h1_evs.get((seq.kind, b), []))
                do_U(seq, 1)
                sp_ev = stage_PC(seq, b)
                stage_QC(seq, 0)
                reload_C(seq, sp_ev)
                stage_QC(seq, 1)
                stage_gate(seq, S_w1, 2592, 0, True)
                stage_PD(seq, b)
                stage_gate(seq, S_w1, 2592, 1, False)
                if si + 1 < len(seqs):
                    do_U(seqs[si + 1], 1)
                stage_QD(seq)
                stage_GY(seq, b, src, dst, True, hb)

        fw.finish("sp")
        print("ninst", fw.ninst, {e: fw.cnt[e] for e in fw.cnt})
    return nc


def _consts():
    c = {}
    c["c_ident"] = np.eye(128, dtype=np.float32)
    m = np.zeros((128, 128), np.float32); m[64:128, 0:64] = NEG; c["c_negq"] = m
    m = np.zeros((128, 128), np.float32); m[0:64, 64:128] = NEG; c["c_negq4"] = m
    s = np.arange(128)[:, None]; t = np.arange(128)[None, :]
    c["c_negd"] = np.where(s >= t, NEG, 0.0).astype(np.float32)
    c["c_m01d"] = (s < t).astype(np.float32)
    c["c_nut"] = np.where(s >= t, -1.0, 0.0).astype(np.float32)
    pos = np.zeros((17, 128), np.float32)
    for tt in range(16):
        pos[tt] = tt * 128 + np.arange(128)
    pos[16] = PAST + np.arange(128)
    for nm, half in (("c_ropeA", 8), ("c_ropeC", 16)):
        inv = (np.float32(THETA) ** (-np.arange(half, dtype=np.float32) / np.float32(half))).astype(np.float32)
        ang = (pos[:, :, None] * inv[None, None, :]).astype(np.float32)
        tab = np.stack([np.cos(ang), np.sin(ang)], 0).astype(np.float32)
        c[nm] = np.ascontiguousarray(tab.transpose(2, 0, 1, 3).reshape(128, 2 * 17 * half))
    return c


_CACHE = {}
_IN_SHARDED = ["x_prompt", "x_sample", "cache_a_k", "cache_a_v", "cache_b_k", "cache_b_v", "cache_c_latent", "cache_c_krope", "cache_d_k", "cache_d_v"]
_OUT_NAMES = ["y_prompt", "y_sample", "a_k_p", "a_v_p", "b_k_p", "b_v_p", "c_lat_p", "c_krope_p", "d_k_p", "d_v_p",
              "a_k_s", "a_v_s", "b_k_s", "b_v_s", "c_lat_s", "c_krope_s", "d_k_s", "d_v_s"]


def _out_shapes(nb_p, nb_s):
    return [(nb_p, T, DM), (nb_s, TS, DM), (nb_p, T, 4, 2, 64), (nb_p, T, 4, 128), (nb_p, 512, 8, 64), (nb_p, 512, 8, 64),
            (nb_p, T, 256), (nb_p, T, 32), (nb_p, T, 8, 64), (nb_p, T, 8, 64),
            (nb_s, TS, 4, 2, 64), (nb_s, TS, 4, 128), (nb_s, TS, 8, 64), (nb_s, TS, 8, 64), (nb_s, TS, 256), (nb_s, TS, 32),
            (nb_s, TS, 8, 64), (nb_s, TS, 8, 64)]


def _core_inputs(inputs, lo_p, hi_p, lo_s, hi_s):
    m = {}
    for k, v in inputs.items():
        a = np.asarray(v)
        if k in _IN_SHARDED:
            lo, hi = (lo_p, hi_p) if k == "x_prompt" else (lo_s, hi_s)
            a = a[lo:hi]
            a = a.reshape(a.shape[0], a.shape[1], -1)
        elif k in ("w_uk", "w_uv"):
            a = a.reshape(256, 512)
        m[k] = np.ascontiguousarray(a, dtype=np.float32)
    m.update(_consts())
    rb = np.asarray(inputs["rel_bias_b"], dtype=np.float32)
    s_ = np.arange(128)[:, None]; t_ = np.arange(128)[None, :]
    idx = np.stack([np.clip(t_ - s_, -128, 128) + 128, np.clip(128 + t_ - s_, -128, 128) + 128], 0)
    m["c_rbT"] = np.ascontiguousarray(rb[:, idx], dtype=np.float32)
    return m


def kernel(**inputs):
    nb = np.asarray(inputs["x_prompt"]).shape[0]
    per = nb // NCORES
    key = ("full", per)
    if key not in _CACHE:
        _CACHE[key] = build(per, per)
    nc = _CACHE[key]
    in_maps = [_core_inputs(inputs, i * per, (i + 1) * per, i * per, (i + 1) * per) for i in range(NCORES)]
    res = run_bass_kernel_spmd(nc, in_maps, core_ids=list(range(NCORES)))
    outs = []
    shapes = _out_shapes(nb, nb)
    for nm, shp in zip(_OUT_NAMES, shapes):
        full = np.concatenate([np.asarray(r[nm]) for r in res.results], axis=0)
        outs.append(np.ascontiguousarray(full.reshape(shp), dtype=np.float32))
    return tuple(outs)
```

```python
import math
import itertools
import numpy as np
from contextlib import ExitStack
import concourse.bass as bass
import concourse.mybir as mybir
from concourse.bass_utils import run_bass_kernel_spmd

F32 = mybir.dt.float32
BF16 = mybir.dt.bfloat16
AF = mybir.ActivationFunctionType
ALU = mybir.AluOpType
AX = mybir.AxisListType

NCORES = 8
DM = 1024
T = 2048
TS = 16
PAST = 2048
EPS = 1e-6
NEG = -30000.0
THETA = 500000.0
LAM_INIT0 = 0.8 - 0.6 * math.exp(-0.3 * 0)
W0N = 4096
W1N = 3616
VST = 520


class Buf:
    __slots__ = ("name", "lw", "rd", "dsem", "dcnt", "excl")

    def __init__(self, name):
        self.name = name
        self.excl = False
        self.lw = None
        self.rd = []
        self.dsem = None
        self.dcnt = 0


class FW:
    def __init__(self, nc, stack):
        self.nc = nc
        self.stack = stack
        self.q = {"pe": nc.tensor, "act": nc.scalar, "dve": nc.vector, "pool": nc.gpsimd, "sp": nc.sync}
        self.sem = {}
        self.cnt = {}
        self.waited = {}
        for e in self.q:
            self.sem[e] = stack.enter_context(nc.semaphore("s_" + e))
            self.cnt[e] = 0
            self.waited[e] = {}
        self.out_events = []
        self.ninst = 0
        self.nbuf = 0

    def buf(self, name=None):
        self.nbuf += 1
        return Buf(name or ("b%d" % self.nbuf))

    def sb(self, name, shape, dtype):
        return self.stack.enter_context(self.nc.sbuf_tensor(name, list(shape), dtype))

    def ps(self, name, shape, dtype):
        return self.stack.enter_context(self.nc.psum_tensor(name, list(shape), dtype))

    def _dsem(self, b):
        if b.dsem is None:
            b.dsem = self.stack.enter_context(self.nc.semaphore("d_" + b.name))
        return b.dsem

    def _deps(self, eng, reads, writes):
        best = {}

        def add(ev):
            s, v, en = ev
            if en == "pe" and eng == "pe":
                return
            k = id(s)
            if k not in best or best[k][1] < v:
                best[k] = (s, v)

        for b in reads:
            if b.lw is not None:
                add(b.lw)
            if b.excl:
                for ev in b.rd:
                    if ev[2] != eng:
                        add(ev)
        for b in writes:
            if b.lw is not None:
                add(b.lw)
            for ev in b.rd:
                add(ev)
        w = self.waited[eng]
        out = []
        for k, (s, v) in best.items():
            if w.get(k, 0) >= v:
                continue
            w[k] = v
            out.append((s, v))
        return out

    def _record(self, ev, reads, writes):
        for b in reads:
            b.rd.append(ev)
            if len(b.rd) > 16:
                best = {}
                for e in b.rd:
                    k = id(e[0])
                    if k not in best or best[k][1] < e[1]:
                        best[k] = e
                b.rd = list(best.values())
        for b in writes:
            b.lw = ev
            b.rd = []

    def op(self, eng, fns, reads=(), writes=()):
        q = self.q[eng]
        if not isinstance(fns, (list, tuple)):
            fns = [fns]
        for (s, v) in self._deps(eng, reads, writes):
            q.wait_ge(s, v)
        ins = None
        for f in fns:
            ins = f(q)
            self.ninst += 1
        self.cnt[eng] += 1
        ins.then_inc(self.sem[eng], 1)
        ev = (self.sem[eng], self.cnt[eng], eng)
        self._record(ev, reads, writes)
        return ev

    def dma(self, eng, out, in_, reads=(), writes=(), sembuf=None, is_output=False, **kw):
        q = self.q[eng]
        if sembuf is None:
            sembuf = writes[0] if writes else reads[0]
        s = self._dsem(sembuf)
        for (ws, v) in self._deps(eng, reads, writes):
            q.wait_ge(ws, v)
        q.dma_start(out=out, in_=in_, **kw).then_inc(s, 16)
        self.ninst += 1
        sembuf.dcnt += 16
        ev = (s, sembuf.dcnt, "dma")
        self._record(ev, reads, writes)
        if is_output:
            self.out_events.append(ev)
        return ev

    def wait_events(self, eng, events):
        q = self.q[eng]
        best = {}
        for (s, v, en) in events:
            k = id(s)
            if k not in best or best[k][1] < v:
                best[k] = (s, v)
        w = self.waited[eng]
        for k, (s, v) in best.items():
            if w.get(k, 0) >= v:
                continue
            w[k] = v
            q.wait_ge(s, v)

    def finish(self, eng="sp"):
        q = self.q[eng]
        best = {}
        for (s, v, en) in self.out_events:
            k = id(s)
            if k not in best or best[k][1] < v:
                best[k] = (s, v)
        for k, (s, v) in best.items():
            q.wait_ge(s, v)
        for e in ("pe", "act", "dve", "pool"):
            if self.cnt[e] > 0:
                q.wait_ge(self.sem[e], self.cnt[e])


class Seq:
    def __init__(self, kind, idx):
        self.kind = kind
        self.idx = idx
        if kind == "p":
            self.ntok = T
            self.tiles = [(i * 128, 128, i) for i in range(16)]
            self.keys = [(i, i * 128, 128, False) for i in range(16)]
            self.keys_b = self.keys
        else:
            self.ntok = TS
            self.tiles = [(0, TS, 16)]
            self.keys = [(i, i * 128, 128, True) for i in range(16)] + [(16, 2048, TS, False)]
            self.keys_b = [(i, i * 128, 128, True) for i in range(4)] + [(4, 512, TS, False)]


DEV_STOP = 99
DEV_DBG = False


def build(NP, NS, layers=(0, 1)):
    nc = bass.Bass("TRN2", target_bir_lowering=False)

    def din(name, shape):
        return nc.dram_tensor(name, list(shape), F32, kind="ExternalInput").ap()

    def dout(name, shape):
        return nc.dram_tensor(name, list(shape), F32, kind="ExternalOutput").ap()

    def dscr(name, shape, dtype):
        return nc.dram_tensor(name, list(shape), dtype).ap()

    NPa, NSa = max(NP, 1), max(NS, 1)
    I = {}
    I["x_prompt"] = din("x_prompt", [NPa, T, DM])
    I["x_sample"] = din("x_sample", [NSa, TS, DM])
    I["cache_a_k"] = din("cache_a_k", [NSa, PAST, 512])
    I["cache_a_v"] = din("cache_a_v", [NSa, PAST, 512])
    I["cache_b_k"] = din("cache_b_k", [NSa, 512, 512])
    I["cache_b_v"] = din("cache_b_v", [NSa, 512, 512])
    I["cache_c_latent"] = din("cache_c_latent", [NSa, PAST, 256])
    I["cache_c_krope"] = din("cache_c_krope", [NSa, PAST, 32])
    I["cache_d_k"] = din("cache_d_k", [NSa, PAST, 512])
    I["cache_d_v"] = din("cache_d_v", [NSa, PAST, 512])
    for nm, shp in [("g_pre0", [DM]), ("w_in0", [DM, W0N]), ("lam_q1", [64]), ("lam_k1", [64]), ("lam_q2", [64]),
                    ("lam_k2", [64]), ("g_sub_a", [128]), ("rel_bias_b", [8, 257]), ("w_out0", [DM, DM]),
                    ("g_post0", [DM]), ("g_pre1", [DM]), ("w_in1", [DM, W1N]), ("g_cq", [768]), ("w_uq", [768, 768]),
                    ("g_ckv", [256]), ("w_uk", [256, 512]), ("w_uv", [256, 512]), ("w_out1", [DM, DM]),
                    ("g_post1", [DM])]:
        I[nm] = din(nm, shp)
    I["c_ident"] = din("c_ident", [128, 128])
    I["c_negq"] = din("c_negq", [128, 128])
    I["c_negq4"] = din("c_negq4", [128, 128])
    I["c_negd"] = din("c_negd", [128, 128])
    I["c_m01d"] = din("c_m01d", [128, 128])
    I["c_nut"] = din("c_nut", [128, 128])
    I["c_rbT"] = din("c_rbT", [8, 2, 128, 128])
    I["c_ropeA"] = din("c_ropeA", [128, 2 * 17 * 8])
    I["c_ropeC"] = din("c_ropeC", [128, 2 * 17 * 16])

    O = {}
    O["y_prompt"] = dout("y_prompt", [NPa, T, DM])
    O["y_sample"] = dout("y_sample", [NSa, TS, DM])
    O["a_k_p"] = dout("a_k_p", [NPa, T, 512])
    O["a_v_p"] = dout("a_v_p", [NPa, T, 512])
    O["b_k_p"] = dout("b_k_p", [NPa, 512, 512])
    O["b_v_p"] = dout("b_v_p", [NPa, 512, 512])
    O["c_lat_p"] = dout("c_lat_p", [NPa, T, 256])
    O["c_krope_p"] = dout("c_krope_p", [NPa, T, 32])
    O["d_k_p"] = dout("d_k_p", [NPa, T, 512])
    O["d_v_p"] = dout("d_v_p", [NPa, T, 512])
    O["a_k_s"] = dout("a_k_s", [NSa, TS, 512])
    O["a_v_s"] = dout("a_v_s", [NSa, TS, 512])
    O["b_k_s"] = dout("b_k_s", [NSa, TS, 512])
    O["b_v_s"] = dout("b_v_s", [NSa, TS, 512])
    O["c_lat_s"] = dout("c_lat_s", [NSa, TS, 256])
    O["c_krope_s"] = dout("c_krope_s", [NSa, TS, 32])
    O["d_k_s"] = dout("d_k_s", [NSa, TS, 512])
    O["d_v_s"] = dout("d_v_s", [NSa, TS, 512])
    if DEV_DBG:
        O["dbg"] = dout("dbg", [T, DM])

    S_w0 = dscr("s_w0", [DM, W0N], BF16)
    S_wo0 = dscr("s_wo0", [DM, DM], BF16)
    S_w1 = dscr("s_w1", [DM, W1N], BF16)
    S_wo1 = dscr("s_wo1", [DM, DM], BF16)
    S_wuq = dscr("s_wuq", [768, 768], BF16)
    S_h1p = dscr("s_h1p", [NPa, T, DM], F32)
    S_h1s = dscr("s_h1s", [NSa, TS, DM], F32)
    S_rbp = dscr("s_rbp", [8, 512], F32)
    S_qt1 = dscr("s_qt1", [96, 4, T], BF16)
    S_kt1 = dscr("s_kt1", [64, 4, 2064], BF16)
    S_v1 = dscr("s_v1", [17, 128, 256], BF16)

    with ExitStack() as st:
        fw = FW(nc, st)
        B = fw.buf
        uT = fw.sb("uT", [128, 8, T], BF16)
        b_uT = [B("uT%d" % i) for i in range(16)]
        R = fw.sb("R", [128, 8192 + 8256 + 17 * VST], BF16)
        QT = R[:, 0:8192].rearrange("p (c t) -> p c t", c=4)
        KT = R[:, 8192:8192 + 8256].rearrange("p (c t) -> p c t", c=4)
        VR = R[:, 16448:16448 + 17 * VST].rearrange("p (k e) -> p k e", k=17)
        b_QT = [B("QT%d" % i) for i in range(16)]
        b_KT = [B("KT%d" % i) for i in range(17)]
        b_V = [B("V%d" % i) for i in range(17)]
        oall = fw.sb("oall", [128, 16, DM], BF16)
        b_o = [B("o%d" % i) for i in range(16)]
        wblk = [fw.sb("wblk%d" % i, [128, 8, 512], BF16) for i in range(2)]
        b_wblk = [B("wblk%d" % i) for i in range(2)]
        wout = fw.sb("wout", [128, 8, DM], BF16)
        b_wout = B("wout")
        xt = [fw.sb("xt%d" % i, [128, DM], F32) for i in range(2)]
        b_xt = [B("xt%d" % i) for i in range(2)]
        b_xout = [B("xout%d" % i) for i in range(2)]
        xn = fw.sb("xn", [128, DM], BF16)
        b_xn = B("xn")
        junk = fw.sb("junk", [128, DM], BF16)
        b_junk = B("junk")
        stg = [fw.sb("stg%d" % i, [128, 512], F32) for i in range(3)]
        b_stg = [B("stg%d" % i) for i in range(3)]
        tb16 = [fw.sb("tb16_%d" % i, [128, 512], BF16) for i in range(3)]
        b_tb16 = [B("tb16_%d" % i) for i in range(3)]
        PT = [fw.sb("PT%d" % i, [128, 512], BF16) for i in range(3)]
        b_PT = [B("PT%d" % i) for i in range(3)]
        Ef = [fw.sb("Ef%d" % i, [128, 512], F32) for i in range(2)]
        b_Ef = [B("Ef%d" % i) for i in range(2)]
        SPb = [fw.sb("SPb%d" % i, [128, 512], BF16) for i in range(2)]
        b_SPb = [B("SPb%d" % i) for i in range(2)]
        ogT = [fw.sb("ogT%d" % i, [128, 8, 128], BF16) for i in range(2)]
        b_ogT = [B("ogT%d" % i) for i in range(2)]
        ytmp = fw.sb("ytmp", [128, DM], F32)
        b_ytmp = B("ytmp")
        sm = fw.sb("sm", [128, 64], F32)
        b_sm = {}

        def smb(k):
            if k not in b_sm:
                b_sm[k] = B("sm%d" % k)
            return b_sm[k]

        osb = fw.sb("osb", [128, 4, 128], F32)
        b_osb = [B("osb%d" % i) for i in range(4)]
        ident = fw.sb("ident", [128, 128], BF16); b_ident = B("ident")
        negq = fw.sb("negq", [128, 128], BF16); b_negq = B("negq")
        negq4 = fw.sb("negq4", [128, 128], BF16); b_negq4 = B("negq4")
        negd = fw.sb("negd", [128, 128], BF16); b_negd = B("negd")
        m01d = fw.sb("m01d", [128, 128], BF16); b_m01d = B("m01d")
        nut = fw.sb("nut", [128, 128], BF16); b_nut = B("nut")
        onec = fw.sb("onec", [128, 2], BF16); b_onec = B("onec")
        ropeA = fw.sb("ropeA", [128, 2, 17, 8], F32); b_ropeA = B("ropeA")
        ropeC = fw.sb("ropeC", [128, 2, 17, 16], F32); b_ropeC = B("ropeC")
        gcol = fw.sb("gcol", [128, 24], F32); b_gcol = B("gcol")
        gpost = fw.sb("gpost", [128, DM], F32); b_gpost = B("gpost")
        gsub = fw.sb("gsub", [128, 128], F32); b_gsub = B("gsub")
        gckv = fw.sb("gckv", [128, 256], F32); b_gckv = B("gckv")
        lamt = ytmp[:, 520:776].rearrange("p (a b) -> p a b", a=4); b_lamt = b_ytmp
        lamv = fw.sb("lamv", [128, 8], F32); b_lamv = B("lamv")
        LR = fw.sb("LR", [128, 4608], BF16)
        Tb = LR[:, 0:2048].rearrange("p (a b) -> p a b", a=16); b_Tb = B("Tb")
        wuk = LR[:, 0:1024].rearrange("p (a b) -> p a b", a=2); b_wuk = B("wuk")
        wuv = LR[:, 1024:2048].rearrange("p (a b) -> p a b", a=2); b_wuv = B("wuv")
        latT = LR[:, 2048:2304].rearrange("p (a b) -> p a b", a=2); b_latT = B("latT")
        wqb = LR[:, 2304:4608].rearrange("p (a b) -> p a b", a=6); b_wqb = B("wqb")
        bank = [fw.ps("bank%d" % i, [128, 512], F32) for i in range(8)]
        b_bank = [B("bank%d" % i) for i in range(8)]
        tbk = [bank[6][:, :].bitcast(BF16), bank[7][:, :].bitcast(BF16)]
        b_tbk = [b_bank[6], b_bank[7]]
        for bb in b_bank:
            bb.excl = True
        pexp = fw.sb("pexp", [128, 1], F32); b_pexp = B("pexp")
        fw.op("pool", lambda q: q.memset(pexp[:], -0.5), writes=[b_pexp])
        b_dram = {}

        def dbuf(k):
            if k not in b_dram:
                b_dram[k] = B("dram_" + str(k))
            return b_dram[k]

        rr_ctr = {"stg": 0, "tb16": 0, "tbk": 0, "xt": 0, "wblk": 0, "pbank": 0, "cp": 0}

        def nxt(k, n):
            v = rr_ctr[k] % n
            rr_ctr[k] += 1
            return v

        def load_const_bf16(dst, b_dst, src):
            i = nxt("stg", 3)
            fw.dma("sp", stg[i][:, 0:128], src, writes=[b_stg[i]])
            fw.op("dve", lambda q: q.tensor_copy(out=dst[:], in_=stg[i][:, 0:128]), reads=[b_stg[i]], writes=[b_dst])

        load_const_bf16(ident, b_ident, I["c_ident"])
        load_const_bf16(negq, b_negq, I["c_negq"])
        load_const_bf16(negq4, b_negq4, I["c_negq4"])
        load_const_bf16(negd, b_negd, I["c_negd"])
        load_const_bf16(m01d, b_m01d, I["c_m01d"])
        load_const_bf16(nut, b_nut, I["c_nut"])
        fw.op("pool", lambda q: q.memset(onec[:], 1.0), writes=[b_onec])
        fw.dma("sp", ropeA[:].rearrange("p a b c -> p (a b c)"), I["c_ropeA"], writes=[b_ropeA])
        fw.dma("sp", ropeC[:].rearrange("p a b c -> p (a b c)"), I["c_ropeC"], writes=[b_ropeC])
        with nc.allow_non_contiguous_dma(reason="tiny gain vectors"):
            for (gsrc, o0, n) in ((I["g_pre0"], 0, 8), (I["g_pre1"], 8, 8), (I["g_cq"], 16, 6)):
                for c in range(n):
                    fw.dma("sp", gcol[:, o0 + c:o0 + c + 1], bass.AP(gsrc.tensor, c * 128, [[1, 128], [1, 1]]), writes=[b_gcol])

        def bcast_ap(src, n):
            return bass.AP(src.tensor, 0, [[0, 128], [1, n]])

        fw.dma("sp", gsub[:], bcast_ap(I["g_sub_a"], 128), writes=[b_gsub])
        fw.op("dve", lambda q: q.tensor_scalar(out=gsub[:], in0=gsub[:], scalar1=1.0 - LAM_INIT0, scalar2=None, op0=ALU.mult),
              reads=[b_gsub], writes=[b_gsub])
        fw.dma("sp", gckv[:], bcast_ap(I["g_ckv"], 256), writes=[b_gckv])
        for i, nm in enumerate(["lam_q1", "lam_k1", "lam_q2", "lam_k2"]):
            fw.dma("sp", lamt[:, i, :], bcast_ap(I[nm], 64), writes=[b_lamt])
        fw.op("dve", lambda q: q.tensor_tensor(out=lamt[:, 0, :], in0=lamt[:, 0, :], in1=lamt[:, 1, :], op=ALU.mult), reads=[b_lamt], writes=[b_lamt])
        fw.op("dve", lambda q: q.tensor_tensor(out=lamt[:, 2, :], in0=lamt[:, 2, :], in1=lamt[:, 3, :], op=ALU.mult), reads=[b_lamt], writes=[b_lamt])
        fw.op("dve", lambda q: q.reduce_sum(out=lamv[:, 0:1], in_=lamt[:, 0, :], axis=AX.X), reads=[b_lamt], writes=[b_lamv])
        fw.op("dve", lambda q: q.reduce_sum(out=lamv[:, 1:2], in_=lamt[:, 2, :], axis=AX.X), reads=[b_lamt], writes=[b_lamv])
        fw.op("act", lambda q: q.activation(out=lamv[:, 0:2], in_=lamv[:, 0:2], func=AF.Exp), reads=[b_lamv], writes=[b_lamv])
        fw.op("dve", lambda q: q.tensor_tensor(out=lamv[:, 2:3], in0=lamv[:, 1:2], in1=lamv[:, 0:1], op=ALU.subtract), reads=[b_lamv], writes=[b_lamv])
        fw.op("dve", lambda q: q.tensor_scalar(out=lamv[:, 3:4], in0=lamv[:, 2:3], scalar1=-LAM_INIT0, scalar2=None, op0=ALU.add), reads=[b_lamv], writes=[b_lamv])
        nlam = lamv[:, 3:4]
        cb = lamv[:, 4:8]
        cbt = fw.sb("cbt", [128, 8], F32); b_cbt = B("cbt")
        with nc.allow_non_contiguous_dma(reason="tiny"):
            for h in range(8):
                fw.dma("sp", cbt[:, h:h + 1], bass.AP(I["rel_bias_b"].tensor, 256 + 257 * h, [[0, 128], [1, 1]]), writes=[b_cbt])
        jn = nxt("stg", 3)
        fw.dma("sp", stg[jn][:, 0:128], I["c_negq"], writes=[b_stg[jn]])
        for h in range(8):
            for d in range(2):
                i = nxt("stg", 3)
                if i == jn:
                    i = nxt("stg", 3)
                fw.dma("sp", stg[i][:, 0:128], I["c_rbT"][h, d], writes=[b_stg[i]])
                fw.op("dve", lambda q: q.tensor_scalar(out=stg[i][:, 128:256], in0=stg[i][:, 0:128], scalar1=cbt[:, h:h + 1], scalar2=None, op0=ALU.subtract),
                      reads=[b_stg[i], b_cbt], writes=[b_stg[i]])
                if d == 0:
                    fw.op("dve", lambda q: q.tensor_tensor(out=Tb[:, h * 2 + d, :], in0=stg[i][:, 128:256], in1=stg[jn][:, 0:128], op=ALU.add),
                          reads=[b_stg[i], b_stg[jn]], writes=[b_Tb])
                else:
                    fw.op("dve", lambda q: q.tensor_copy(out=Tb[:, h * 2 + d, :], in_=stg[i][:, 128:256]), reads=[b_stg[i]], writes=[b_Tb])

        def set_ones(H, dv):
            v = VR[:, :, 0:H * (dv + 1)].rearrange("p k (h e) -> p k h e", h=H)[:, :, :, dv:dv + 1]
            fw.op("pool", lambda q: q.memset(v, 1.0), writes=b_V)

        def load_wblock(scr, K, c0, ncol):
            i = nxt("wblk", 2)
            kc = K // 128
            wait_prep(scr.name)
            fw.dma("sp", wblk[i][:, 0:kc, 0:ncol], scr.rearrange("(c p) n -> p c n", p=128)[:, :, c0:c0 + ncol],
                   reads=[dbuf(scr.name)], writes=[b_wblk[i]])
            return wblk[i], b_wblk[i]

        def rsqrt_col(col_ap, nr, bs):
            fw.op("pool", lambda q: q.tensor_scalar(out=col_ap, in0=col_ap, scalar1=EPS, scalar2=0.0, op0=ALU.add, op1=ALU.add), reads=[bs], writes=[bs])
            fw.op("pool", lambda q: q.tensor_tensor(out=col_ap, in0=col_ap, in1=pexp[0:nr, :], op=ALU.pow), reads=[bs, b_pexp], writes=[bs])

        def rms_scale(src_ap, nr, n, b_src, key, engines="act"):
            bs = smb(key)
            aps = src_ap if isinstance(src_ap, (list, tuple)) else [src_ap]
            for k, a in enumerate(aps):
                w = a.shape[-1]
                fw.op("act", lambda q: q.activation(out=junk[0:nr, 0:w], in_=a, func=AF.Square, scale=float(n) ** -0.5, accum_out=sm[0:nr, key + k:key + k + 1]),
                      reads=b_src, writes=[b_junk, bs])
            if len(aps) == 2:
                fw.op("pool", lambda q: q.tensor_tensor(out=sm[0:nr, key:key + 1], in0=sm[0:nr, key:key + 1], in1=sm[0:nr, key + 1:key + 2], op=ALU.add),
                      reads=[bs], writes=[bs])
            rsqrt_col(sm[0:nr, key:key + 1], nr, bs)
            return sm[0:nr, key:key + 1], bs

        def transposes(src_aps, nr, widths):
            i = nxt("tbk", 2)
            offs = []
            fns = []
            o = 0
            for a, w in zip(src_aps, widths):
                offs.append(o)
                fns.append(lambda q, a=a, w=w, o=o: q.transpose(out=tbk[i][0:w, o:o + nr], in_=a, identity=ident[0:nr, 0:nr]))
                o += 128
            return i, fns, offs

        def copy_eng():
            return ("dve", "act")[nxt("cp", 2)]

        rr_ctr["ce"] = 0

        def cast_eng():
            return ("dve", "pool", "act")[nxt("ce", 3)]

        def ecopy(eng, out, in_, reads, writes, scale=None):
            if eng == "act":
                if scale is None:
                    fw.op("act", lambda q: q.activation(out=out, in_=in_, func=AF.Copy), reads=reads, writes=writes)
                else:
                    fw.op("act", lambda q: q.activation(out=out, in_=in_, func=AF.Copy, scale=scale), reads=reads, writes=writes)
            else:
                if scale is None:
                    fw.op(eng, lambda q: q.tensor_copy(out=out, in_=in_), reads=reads, writes=writes)
                else:
                    fw.op(eng, lambda q: q.tensor_scalar(out=out, in0=in_, scalar1=scale, scalar2=0.0, op0=ALU.mult, op1=ALU.add), reads=reads, writes=writes)

        rr_ctr["pin"] = 0
        rr_ctr["pout"] = 0
        pin_bufs = [(xt[0], b_xt[0]), (xt[1], b_xt[1]), (ytmp, b_ytmp)]

        def prep_weight(src, dst, K, N, gofs, c_lo=0):
            for kc in range(K // 128):
                for c0 in range(c_lo, N, 1024):
                    ncol = min(1024, N - c0)
                    it, ib = pin_bufs[nxt("pin", 3)]
                    so = nxt("pout", 16)
                    ot = oall[:, so, :]
                    fw.dma("sp", it[:, 0:ncol], src[kc * 128:(kc + 1) * 128, c0:c0 + ncol], writes=[ib], sembuf=ib)
                    eng = ("dve", "act")[(kc + c0 // 1024) % 2]
                    if gofs is None:
                        ecopy(eng, ot[:, 0:ncol], it[:, 0:ncol], [ib], [b_o[so]])
                    else:
                        ecopy(eng, ot[:, 0:ncol], it[:, 0:ncol], [ib, b_gcol], [b_o[so]], scale=gcol[:, gofs + kc:gofs + kc + 1])
                    prep_evs.setdefault(dst.name, []).append(
                        fw.dma("pool", dst[kc * 128:(kc + 1) * 128, c0:c0 + ncol], ot[:, 0:ncol], reads=[b_o[so]], writes=[dbuf(dst.name)], sembuf=b_o[so]))
                    yield

        prep_evs = {}
        prep_late = []
        late_gens = []
        if 0 in layers and DEV_STOP >= 1:
            for _ in prep_weight(I["w_in0"], S_w0, DM, 1536, 0):
                pass
            late_gens += [prep_weight(I["w_in0"], S_w0, DM, W0N, 0, c_lo=1536), prep_weight(I["w_out0"], S_wo0, DM, DM, None)]
        if 1 in layers and DEV_STOP >= 1:
            late_gens += [prep_weight(I["w_in1"], S_w1, DM, W1N, 8), prep_weight(I["w_uq"], S_wuq, 768, 768, 16),
                          prep_weight(I["w_out1"], S_wo1, DM, DM, None)]
        prep_late = itertools.chain(*late_gens)
        if 0 not in layers or NP == 0:
            for _ in prep_late:
                pass
            prep_late = []
        prep_state = {"it": iter(prep_late), "active": False}

        def prep_step(n=2):
            if prep_state["active"]:
                for _ in range(n):
                    if next(prep_state["it"], "done") == "done":
                        prep_state["active"] = False
                        break

        def prep_flush():
            for _ in prep_state["it"]:
                pass
            prep_state["active"] = False

        def wait_prep(name):
            fw.wait_events("sp", prep_evs.get(name, []))

        def stage_U(seq, src, src_bufs=()):
            sel = {}

            def pa(ti):
                r0, nr, rt = seq.tiles[ti]
                i = nxt("xt", 2)
                fw.dma("sp", xt[i][0:nr, :], src[r0:r0 + nr, :], reads=list(src_bufs), writes=[b_xt[i]], sembuf=b_xt[i])
                rs, bs = rms_scale(xt[i][0:nr, :], nr, DM, [b_xt[i]], 2 * (ti % 2))
                sel[ti] = (i, rs, bs)

            def pb(ti):
                r0, nr, rt = seq.tiles[ti]
                i, rs, bs = sel[ti]
                fw.op("dve", lambda q: q.tensor_scalar(out=xn[0:nr, :], in0=xt[i][0:nr, :], scalar1=rs, scalar2=None, op0=ALU.mult), reads=[b_xt[i], bs], writes=[b_xn])
                k = nxt("tbk", 2)
                fw.op("pe", [lambda q, c=c: q.transpose(out=tbk[k][:, c * 128:c * 128 + nr], in_=xn[0:nr, c * 128:(c + 1) * 128], identity=ident[0:nr, 0:nr])
                             for c in range(8)], reads=[b_xn, b_ident], writes=[b_tbk[k]])
                ecopy("act", uT[:, 0:4, r0:r0 + nr], tbk[k][:, 0:512].rearrange("p (c t) -> p c t", c=4)[:, :, 0:nr], [b_tbk[k]], [b_uT[ti]])
                ecopy("dve", uT[:, 4:8, r0:r0 + nr], tbk[k][:, 512:1024].rearrange("p (c t) -> p c t", c=4)[:, :, 0:nr], [b_tbk[k]], [b_uT[ti]])

            n = len(seq.tiles)
            pa(0)
            for ti in range(n):
                if ti + 1 < n:
                    pa(ti + 1)
                pb(ti)

        def proj_block(seq, scr, c0, ncol, consume, K=DM, lhs=None):
            wt, bw = load_wblock(scr, K, c0, ncol)
            kcn = K // 128
            later = []
            later2 = []
            for ti, (r0, nr, rt) in enumerate(seq.tiles):
                pb = 4 + nxt("pbank", 2)
                if lhs is None:
                    lf = lambda c: uT[:, c, r0:r0 + nr]
                    lb = [b_uT[ti]]
                else:
                    lf, lb = lhs(ti)
                fw.op("pe", [lambda q, c=c: q.matmul(bank[pb][0:nr, 0:ncol], lhsT=lf(c), rhs=wt[:, c, 0:ncol], start=(c == 0), stop=(c == kcn - 1))
                             for c in range(kcn)], reads=lb + [bw], writes=[b_bank[pb]])
                for f in later2:
                    f()
                later2 = later
                later = consume(ti, r0, nr, rt, bank[pb], b_bank[pb]) or []
                prep_step()
            for f in later2 + later:
                f()

        def store_out(dst, src_ap, b_src):
            fw.dma("pool", dst, src_ap, reads=[b_src], is_output=True, sembuf=b_src)

        def ingest_k(src_ap, b_src, nr, kcol0, ktile, scale=None, ncol=512, part=128, rows=None, eng="dve"):
            j = nxt("tb16", 3)
            ecopy(eng, tb16[j][0:nr, 0:ncol], src_ap, [b_src], [b_tb16[j]], scale)

            def later():
                nchunk = ncol // part
                k = nxt("tbk", 2)
                fw.op("pe", [lambda q, c=c: q.transpose(out=tbk[k][0:part, c * 128:c * 128 + nr], in_=tb16[j][0:nr, c * part:(c + 1) * part],
                                                         identity=ident[0:nr, 0:nr]) for c in range(nchunk)],
                      reads=[b_tb16[j], b_ident], writes=[b_tbk[k]])
                p0, p1 = rows if rows is not None else (0, part)
                ecopy(copy_eng(), KT[p0:p1, 0:nchunk, kcol0:kcol0 + nr] if rows is None else KT[p0:p1, 0:nchunk, kcol0:kcol0 + nr],
                      tbk[k][0:part, 0:nchunk * 128].rearrange("p (c t) -> p c t", c=nchunk)[:, :, 0:nr], [b_tbk[k]], [b_KT[ktile]])
            return later

        def ingest_q(src_ap, b_src, nr, qcol0, qtile, scale, ncol=512, part=128, eng="dve"):
            j = nxt("tb16", 3)
            ecopy(eng, tb16[j][0:nr, 0:ncol], src_ap, [b_src], [b_tb16[j]], scale)

            def later():
                nchunk = ncol // part
                k = nxt("tbk", 2)
                fw.op("pe", [lambda q, c=c: q.transpose(out=tbk[k][0:part, c * 128:c * 128 + nr], in_=tb16[j][0:nr, c * part:(c + 1) * part],
                                                         identity=ident[0:nr, 0:nr]) for c in range(nchunk)],
                      reads=[b_tb16[j], b_ident], writes=[b_tbk[k]])
                ecopy(copy_eng(), QT[0:part, 0:nchunk, qcol0:qcol0 + nr],
                      tbk[k][0:part, 0:nchunk * 128].rearrange("p (c t) -> p c t", c=nchunk)[:, :, 0:nr], [b_tbk[k]], [b_QT[qtile]])
            return later

        def ingest_v(src_ap, b_src, nr, vtile, H, dv, eng="dve", hofs=0, Hsrc=None):
            Hs = Hsrc or H
            src3 = src_ap.rearrange("p (h d) -> p h d", h=Hs)[:, hofs:hofs + H, :]
            dst3 = VR[0:nr, vtile, 0:H * (dv + 1)].rearrange("p (h e) -> p h e", h=H)[:, :, 0:dv] if dv != 64 or True else None
            ecopy(eng, dst3, src3, [b_src], [b_V[vtile]])

        def rope_ops(dst, src3, nr, rt, tab, half, nblk, b_src, b_dst, blk_stride_dst3):
            cosb = tab[0:nr, 0, rt, :].unsqueeze(1).to_broadcast([nr, nblk, half])
            sinb = tab[0:nr, 1, rt, :].unsqueeze(1).to_broadcast([nr, nblk, half])
            x1 = src3[:, :, 0:half]
            x2 = src3[:, :, half:2 * half]
            d3 = blk_stride_dst3
            t = osb[0:nr, :, :].rearrange("p a b -> p (a b)")
            n = nblk * half
            tt = [t[:, k * n:(k + 1) * n].rearrange("p (b d) -> p b d", b=nblk) for k in range(4)]
            btab = b_ropeA if tab is ropeA else b_ropeC
            fw.op("dve", lambda q: q.tensor_tensor(out=tt[0], in0=x1, in1=cosb, op=ALU.mult), reads=[b_src, btab], writes=b_osb)
            fw.op("dve", lambda q: q.tensor_tensor(out=tt[1], in0=x2, in1=sinb, op=ALU.mult), reads=[b_src, btab], writes=b_osb)
            fw.op("dve", lambda q: q.tensor_tensor(out=tt[2], in0=x2, in1=cosb, op=ALU.mult), reads=[b_src, btab], writes=b_osb)
            fw.op("dve", lambda q: q.tensor_tensor(out=tt[3], in0=x1, in1=sinb, op=ALU.mult), reads=[b_src, btab], writes=b_osb)
            fw.op("dve", lambda q: q.tensor_tensor(out=d3[:, :, 0:half], in0=tt[0], in1=tt[1], op=ALU.subtract), reads=b_osb, writes=[b_dst])
            fw.op("dve", lambda q: q.tensor_tensor(out=d3[:, :, half:2 * half], in0=tt[2], in1=tt[3], op=ALU.add), reads=b_osb, writes=[b_dst])

        def run_units(units, sbanks=((0,), (1,))):
            pend = None
            for ui, u in enumerate(units):
                sb_ = sbanks[ui % 2]
                nk = u["nk"]
                used = sorted(set(x[0] for x in u["qk"]))
                for bsel in used:
                    bk_ = sb_[bsel]
                    fns = []
                    for (bs_, c0, n, lhsT, rhs) in u["qk"]:
                        if bs_ == bsel:
                            fns.append(lambda q, c0=c0, n=n, lhsT=lhsT, rhs=rhs: q.matmul(bank[bk_][0:nk, c0:c0 + n], lhsT=lhsT, rhs=rhs, start=True, stop=False,
                                                                                          skip_group_check=True))
                    for (bs_, c0, n, rhs_t) in u["extra"]:
                        if bs_ == bsel:
                            fns.append(lambda q, c0=c0, n=n, rhs_t=rhs_t: q.matmul(bank[bk_][0:nk, c0:c0 + n], lhsT=ident[0:nk, 0:nk], rhs=rhs_t, start=False, stop=True,
                                                                                   skip_group_check=True))
                    fw.op("pe", fns, reads=u["rd"] + [b_ident, b_negq], writes=[b_bank[bk_]])
                pi = ui % 3
                for (bsel, c0, n, ptc0) in u["exps"]:
                    bk_ = sb_[bsel]
                    fw.op("act", lambda q, c0=c0, n=n, ptc0=ptc0: q.activation(out=PT[pi][0:nk, ptc0:ptc0 + n], in_=bank[bk_][0:nk, c0:c0 + n], func=AF.Exp),
                          reads=[b_bank[bk_]], writes=[b_PT[pi]])
                if pend is not None:
                    pend()

                def pvs(u=u, pi=pi, nk=nk):
                    obufs = []
                    fns = []
                    for (O_ap, pc0, nq, V_ap, start, bO) in u["pv"]:
                        fns.append(lambda q, O_ap=O_ap, pc0=pc0, nq=nq, V_ap=V_ap, start=start: q.matmul(
                            O_ap, lhsT=PT[pi][0:nk, pc0:pc0 + nq], rhs=V_ap, start=start, stop=True, skip_group_check=True))
                        if bO not in obufs:
                            obufs.append(bO)
                    if DEV_STOP >= 3.2:
                        fw.op("pe", fns, reads=[b_PT[pi]] + u["rdv"], writes=obufs)
                    if u.get("fin"):
                        u["fin"]()
                pend = pvs
            if pend is not None:
                pend()

        def stage_PA(seq, b):
            pre = "p" if seq.kind == "p" else "s"
            if DEV_STOP >= 2.1:
                set_ones(4, 128)
            if seq.kind == "s" and DEV_STOP >= 2.2:
                for (vt, kc0, nk, is_cache) in seq.keys:
                    if not is_cache:
                        continue
                    i = nxt("stg", 3)
                    fw.dma("sp", stg[i][:, :], I["cache_a_k"][b, kc0:kc0 + 128, :], writes=[b_stg[i]])
                    ingest_k(stg[i][:, :], b_stg[i], 128, kc0, vt, eng=cast_eng())()
                    i = nxt("stg", 3)
                    fw.dma("sp", stg[i][:, :], I["cache_a_v"][b, kc0:kc0 + 128, :], writes=[b_stg[i]])
                    ingest_v(stg[i][:, :], b_stg[i], 128, vt, 4, 128, eng=cast_eng())
            knew0 = 0 if seq.kind == "p" else 2048
            vnew0 = 0 if seq.kind == "p" else 16

            def cons_q(ti, r0, nr, rt, bk, bbk):
                if DEV_STOP < 2.31:
                    return []
                i = nxt("stg", 3)
                ecopy("act", stg[i][0:nr, :], bk[0:nr, :], [bbk], [b_stg[i]])
                if DEV_STOP < 2.32:
                    return []
                rope_ops(None, stg[i][0:nr, :].rearrange("p (b d) -> p b d", b=8), nr, rt, ropeA, 8, 8, b_stg[i], b_stg[i],
                         stg[i][0:nr, :].rearrange("p (b d) -> p b d", b=8))
                if DEV_STOP < 2.33:
                    return []
                return [ingest_q(stg[i][0:nr, :], b_stg[i], nr, r0, ti, 0.125, eng="dve")]

            def cons_k(ti, r0, nr, rt, bk, bbk):
                i = nxt("stg", 3)
                ecopy("act", stg[i][0:nr, :], bk[0:nr, :], [bbk], [b_stg[i]])
                rope_ops(None, stg[i][0:nr, :].rearrange("p (b d) -> p b d", b=8), nr, rt, ropeA, 8, 8, b_stg[i], b_stg[i],
                         stg[i][0:nr, :].rearrange("p (b d) -> p b d", b=8))
                store_out(O["a_k_" + pre][b, r0:r0 + nr, :], stg[i][0:nr, :], b_stg[i])
                return [ingest_k(stg[i][0:nr, :], b_stg[i], nr, knew0 + r0, vnew0 + ti, eng="dve")]

            def cons_v(ti, r0, nr, rt, bk, bbk):
                i = nxt("stg", 3)
                ecopy("act", stg[i][0:nr, :], bk[0:nr, :], [bbk], [b_stg[i]])
                store_out(O["a_v_" + pre][b, r0:r0 + nr, :], stg[i][0:nr, :], b_stg[i])
                ingest_v(stg[i][0:nr, :], b_stg[i], nr, vnew0 + ti, 4, 128, eng="dve")
                return []

            if DEV_STOP >= 2.3:
                proj_block(seq, S_w0, 0, 512, cons_q)
            if DEV_STOP >= 2.4:
                proj_block(seq, S_w0, 512, 512, cons_k)
            if DEV_STOP >= 2.5:
                proj_block(seq, S_w0, 1024, 512, cons_v)

        def fin_A(seq, h, qts, qb0, ob):
            for j in qts:
                jj = j - qb0
                nq = seq.tiles[j][1]
                o0 = bank[ob[0]][0:nq, jj * 129:jj * 129 + 128]
                o1 = bank[ob[1]][0:nq, jj * 129:jj * 129 + 128]
                kb = 8 + 4 * (jj % 2)
                bs = smb(kb)
                fw.op("dve", lambda q: q.reciprocal(out=sm[0:nq, kb:kb + 1], in_=bank[ob[0]][0:nq, jj * 129 + 128:jj * 129 + 129]), reads=[b_bank[ob[0]]], writes=[bs])
                fw.op("dve", lambda q: q.reciprocal(out=sm[0:nq, kb + 1:kb + 2], in_=bank[ob[1]][0:nq, jj * 129 + 128:jj * 129 + 129]), reads=[b_bank[ob[1]]], writes=[bs])
                fw.op("dve", lambda q: q.tensor_tensor(out=sm[0:nq, kb + 1:kb + 2], in0=sm[0:nq, kb + 1:kb + 2], in1=nlam[0:nq, :], op=ALU.mult), reads=[bs, b_lamv], writes=[bs])
                ot = osb[0:nq, 1 + (jj % 2), :]
                bo = b_osb[1 + (jj % 2)]
                fw.op("dve", lambda q: q.tensor_scalar(out=ot, in0=o0, scalar1=sm[0:nq, kb:kb + 1], scalar2=None, op0=ALU.mult), reads=[b_bank[ob[0]], bs], writes=[bo])
                fw.op("dve", lambda q: q.scalar_tensor_tensor(out=ot, in0=o1, scalar=sm[0:nq, kb + 1:kb + 2], in1=ot, op0=ALU.mult, op1=ALU.add),
                      reads=[b_bank[ob[1]], bs, bo], writes=[bo])
                fw.op("dve", lambda q: q.scalar_tensor_tensor(out=osb[0:nq, 3, :], in0=ot, scalar=1.0 / 128, in1=ot, op0=ALU.mult, op1=ALU.mult,
                                                              accum_out=sm[0:nq, kb + 2:kb + 3]), reads=[bo], writes=[b_osb[3], bs])
                rsqrt_col(sm[0:nq, kb + 2:kb + 3], nq, bs)
                fw.op("dve", lambda q: q.scalar_tensor_tensor(out=oall[0:nq, j, h * 128:(h + 1) * 128], in0=ot, scalar=sm[0:nq, kb + 2:kb + 3], in1=gsub[0:nq, :],
                                                              op0=ALU.mult, op1=ALU.mult), reads=[bo, bs, b_gsub], writes=[b_o[j]])

        def stage_QA_sample(seq):
            nq = seq.tiles[0][1]
            units = []
            for h in range(4):
                ob = (4, 5) if h % 2 == 0 else (6, 7)
                started = [False, False]
                cache = [k for k in seq.keys if k[3]]
                new = [k for k in seq.keys if not k[3]]
                for grp_keys in (cache, new):
                    nk = grp_keys[0][2]
                    u = dict(nk=nk, qk=[], extra=[], exps=[], pv=[], rd=[b_QT[0]] + [b_KT[k[0]] for k in grp_keys], rdv=[b_V[k[0]] for k in grp_keys])
                    for si_, (vt, kc0, nk_, is_cache) in enumerate(grp_keys):
                        for m in range(2):
                            u["qk"].append((m, si_ * nq, nq, KT[m * 64:(m + 1) * 64, h, kc0:kc0 + nk], QT[m * 64:(m + 1) * 64, h, 0:nq]))
                            u["pv"].append((bank[ob[m]][0:nq, 0:129], m * 256 + si_ * nq, nq, VR[0:nk, vt, h * 129:(h + 1) * 129], not started[m], b_bank[ob[m]]))
                            started[m] = True
                    for m in range(2):
                        u["exps"].append((m, 0, len(grp_keys) * nq, m * 256))
                    units.append(u)
                units[-1]["fin"] = (lambda h=h, ob=ob: fin_A(seq, h, [0], 0, ob))
            run_units(units, sbanks=((0, 1), (2, 3)))

        def stage_QA(seq):
            if seq.kind == "s":
                return stage_QA_sample(seq)
            nt = len(seq.tiles)
            bq = 2 if seq.kind == "p" else 1
            blk = 0
            for h in range(4):
                for qb0 in range(0, nt, bq):
                    qts = list(range(qb0, min(nt, qb0 + bq)))
                    nqs = [seq.tiles[j][1] for j in qts]
                    bqn = sum(nqs)
                    qcol0 = seq.tiles[qb0][0]
                    ob = (4, 5) if blk % 2 == 0 else (6, 7)
                    blk += 1
                    units = []
                    started = [False, False]
                    for ki, (vt, kc0, nk, is_cache) in enumerate(seq.keys):
                        if seq.kind == "p":
                            valid = [j for j in qts if ki <= j]
                        else:
                            valid = qts
                        if not valid:
                            continue
                        j0 = valid[0] - qb0
                        off0 = j0 * 128
                        nv = sum(seq.tiles[j][1] for j in valid)
                        u = dict(nk=nk, qk=[], extra=[], exps=[], pv=[], rd=[b_KT[vt]] + [b_QT[j] for j in valid], rdv=[b_V[vt]])
                        for m in range(2):
                            u["qk"].append((m, off0, nv, KT[m * 64:(m + 1) * 64, h, kc0:kc0 + nk], QT[m * 64:(m + 1) * 64, h, qcol0 + off0:qcol0 + off0 + nv]))
                            for j in valid:
                                if seq.kind == "p" and ki == j:
                                    u["extra"].append((m, (j - qb0) * 128, seq.tiles[j][1], negq[0:nk, 0:seq.tiles[j][1]]))
                            u["exps"].append((m, off0, nv, m * bqn + off0))
                        for m in range(2):
                            for j in valid:
                                jj = j - qb0
                                nq = seq.tiles[j][1]
                                u["pv"].append((bank[ob[m]][0:nq, jj * 129:(jj + 1) * 129], m * bqn + jj * 128, nq,
                                                VR[0:nk, vt, h * 129:(h + 1) * 129], not started[m], b_bank[ob[m]]))
                                started[m] = True
                        units.append(u)

                    def fin(h=h, qts=qts, qb0=qb0, ob=ob):
                        for j in qts:
                            jj = j - qb0
                            nq = seq.tiles[j][1]
                            o0 = bank[ob[0]][0:nq, jj * 129:jj * 129 + 128]
                            o1 = bank[ob[1]][0:nq, jj * 129:jj * 129 + 128]
                            kb = 8 + 4 * (jj % 2)
                            bs = smb(kb)
                            fw.op("dve", lambda q: q.reciprocal(out=sm[0:nq, kb:kb + 1], in_=bank[ob[0]][0:nq, jj * 129 + 128:jj * 129 + 129]), reads=[b_bank[ob[0]]], writes=[bs])
                            fw.op("dve", lambda q: q.reciprocal(out=sm[0:nq, kb + 1:kb + 2], in_=bank[ob[1]][0:nq, jj * 129 + 128:jj * 129 + 129]), reads=[b_bank[ob[1]]], writes=[bs])
                            fw.op("dve", lambda q: q.tensor_tensor(out=sm[0:nq, kb + 1:kb + 2], in0=sm[0:nq, kb + 1:kb + 2], in1=nlam[0:nq, :], op=ALU.mult), reads=[bs, b_lamv], writes=[bs])
                            ot = osb[0:nq, 1 + (jj % 2), :]
                            bo = b_osb[1 + (jj % 2)]
                            fw.op("dve", lambda q: q.tensor_scalar(out=ot, in0=o0, scalar1=sm[0:nq, kb:kb + 1], scalar2=None, op0=ALU.mult), reads=[b_bank[ob[0]], bs], writes=[bo])
                            fw.op("dve", lambda q: q.scalar_tensor_tensor(out=ot, in0=o1, scalar=sm[0:nq, kb + 1:kb + 2], in1=ot, op0=ALU.mult, op1=ALU.add),
                                  reads=[b_bank[ob[1]], bs, bo], writes=[bo])
                            fw.op("dve", lambda q: q.scalar_tensor_tensor(out=osb[0:nq, 3, :], in0=ot, scalar=1.0 / 128, in1=ot, op0=ALU.mult, op1=ALU.mult,
                                                                          accum_out=sm[0:nq, kb + 2:kb + 3]), reads=[bo], writes=[b_osb[3], bs])
                            rsqrt_col(sm[0:nq, kb + 2:kb + 3], nq, bs)
                            fw.op("dve", lambda q: q.scalar_tensor_tensor(out=oall[0:nq, j, h * 128:(h + 1) * 128], in0=ot, scalar=sm[0:nq, kb + 2:kb + 3], in1=gsub[0:nq, :],
                                                                          op0=ALU.mult, op1=ALU.mult), reads=[bo, bs, b_gsub], writes=[b_o[j]])
                    if DEV_STOP >= 3.4:
                        units[-1]["fin"] = fin
                    if DEV_STOP < 3.3:
                        units = units[:1]
                    run_units(units, sbanks=((0, 1), (2, 3)))
                    if DEV_STOP < 3.35:
                        return

        def stage_PB(seq, b):
            pre = "p" if seq.kind == "p" else "s"
            set_ones(8, 64)
            if seq.kind == "s":
                for (vt, kc0, nk, is_cache) in seq.keys_b:
                    if not is_cache:
                        continue
                    i = nxt("stg", 3)
                    fw.dma("sp", stg[i][:, :], I["cache_b_k"][b, kc0:kc0 + 128, :], writes=[b_stg[i]])
                    ingest_k(stg[i][:, :], b_stg[i], 128, kc0, vt, eng=cast_eng())()
                    i = nxt("stg", 3)
                    fw.dma("sp", stg[i][:, :], I["cache_b_v"][b, kc0:kc0 + 128, :], writes=[b_stg[i]])
                    ingest_v(stg[i][:, :], b_stg[i], 128, vt, 8, 64, eng=cast_eng())
            knew0 = 0 if seq.kind == "p" else 512
            vnew0 = 0 if seq.kind == "p" else 4

            def cons_q(ti, r0, nr, rt, bk, bbk):
                return [ingest_q(bk[0:nr, :], bbk, nr, r0, ti, 0.125, eng="dve")]

            def cons_k(ti, r0, nr, rt, bk, bbk):
                if seq.kind == "s" or r0 >= T - 512:
                    i = nxt("stg", 3)
                    ecopy("act", stg[i][0:nr, :], bk[0:nr, :], [bbk], [b_stg[i]])
                    orow = r0 - (T - 512) if seq.kind == "p" else r0
                    store_out(O["b_k_" + pre][b, orow:orow + nr, :], stg[i][0:nr, :], b_stg[i])
                return [ingest_k(bk[0:nr, :], bbk, nr, knew0 + r0, vnew0 + ti, eng="dve")]

            def cons_v(ti, r0, nr, rt, bk, bbk):
                if seq.kind == "s" or r0 >= T - 512:
                    i = nxt("stg", 3)
                    ecopy("act", stg[i][0:nr, :], bk[0:nr, :], [bbk], [b_stg[i]])
                    orow = r0 - (T - 512) if seq.kind == "p" else r0
                    store_out(O["b_v_" + pre][b, orow:orow + nr, :], stg[i][0:nr, :], b_stg[i])
                ingest_v(bk[0:nr, :], bbk, nr, vnew0 + ti, 8, 64, eng="dve")
                return []

            proj_block(seq, S_w0, 1536, 512, cons_q)
            proj_block(seq, S_w0, 2048, 512, cons_k)
            proj_block(seq, S_w0, 2560, 512, cons_v)

        def stage_QB(seq):
            blk = 0
            for j, (r0, nq, rt) in enumerate(seq.tiles):
                ob = (2, 3) if j % 2 == 0 else (4, 5)
                pairs = []
                for h in (0, 2, 4, 6, 1, 3, 5, 7):
                    if seq.kind == "p":
                        for d in (4, 3, 2, 1, 0):
                            ki = j - d
                            if ki < 0:
                                continue
                            ex = None
                            if d == 0:
                                ex = Tb[:, h * 2 + 0, :]
                            elif d == 1:
                                ex = Tb[:, h * 2 + 1, :]
                            elif d == 4:
                                ex = negq4[:, :]
                            pairs.append((h, ki, ex))
                    else:
                        for ki in range(5):
                            ex = None
                            if ki == 3:
                                ex = Tb[:, h * 2 + 1, :]
                            elif ki == 4:
                                ex = Tb[:, h * 2 + 0, :]
                            pairs.append((h, ki, ex))
                units = []
                started = [False, False]
                slotw = 128 if seq.kind == "p" else nq
                for p0 in range(0, len(pairs), 4):
                    grp = pairs[p0:p0 + 4]
                    nkmax = max(seq.keys_b[ki][2] for (_, ki, _) in grp)
                    u = dict(nk=nkmax, qk=[], extra=[], exps=[], pv=[], rd=[b_QT[j]], rdv=[])
                    same = all(seq.keys_b[ki][2] == nkmax for (_, ki, _) in grp)
                    for s, (h, ki, ex) in enumerate(grp):
                        vt, kc0, nk, _c = seq.keys_b[ki]
                        u["qk"].append((s * slotw, nq, KT[(h % 2) * 64:(h % 2 + 1) * 64, h // 2, kc0:kc0 + nk], QT[(h % 2) * 64:(h % 2 + 1) * 64, h // 2, r0:r0 + nq], nk))
                        if ex is not None:
                            u["extra"].append((s * slotw, nq, ex[0:nk, 0:nq], nk))
                        u["pv"].append((bank[ob[h // 4]][0:nq, (h % 4) * 65:(h % 4 + 1) * 65], s * slotw, nq, VR[0:nk, vt, h * 65:(h + 1) * 65],
                                        not started[h // 4], b_bank[ob[h // 4]], nk))
                        started[h // 4] = True
                        u["rd"].append(b_KT[vt])
                        u["rdv"].append(b_V[vt])
                        u["exps"].append((s * slotw, nq, nk))
                    if same:
                        u["exps"] = [(0, (len(grp) - 1) * slotw + nq, nkmax)]
                    units.append(u)

                def fin(j=j, nq=nq, ob=ob):
                    for g in range(2):
                        kb = 16 + 4 * g
                        bs = smb(kb)
                        fw.op("dve", lambda q: q.reciprocal(out=sm[0:nq, kb:kb + 4], in_=bank[ob[g]][0:nq, 0:260].rearrange("p (h e) -> p h e", e=65)[:, :, 64]),
                              reads=[b_bank[ob[g]]], writes=[bs])
                        for hh in range(4):
                            h = g * 4 + hh
                            osl = oall[0:nq, j, 512 + h * 64:512 + (h + 1) * 64]
                            fw.op("dve", lambda q, osl=osl, hh=hh: q.scalar_tensor_tensor(out=osl, in0=bank[ob[g]][0:nq, hh * 65:hh * 65 + 64], scalar=sm[0:nq, kb + hh:kb + hh + 1],
                                                                                          in1=osl, op0=ALU.mult, op1=ALU.mult), reads=[b_bank[ob[g]], bs, b_o[j]], writes=[b_o[j]])
                units[-1]["fin"] = fin
                run_units_nk(units)

        def run_units_nk(units):
            pend = None
            for ui, u in enumerate(units):
                sbk = ui % 2
                fns = []
                exd = {c0: (n, rhs_t, nk) for (c0, n, rhs_t, nk) in u["extra"]}
                for (c0, n, lhsT, rhs, nk) in u["qk"]:
                    fns.append(lambda q, c0=c0, n=n, lhsT=lhsT, rhs=rhs, nk=nk: q.matmul(bank[sbk][0:nk, c0:c0 + n], lhsT=lhsT, rhs=rhs, start=True, stop=False,
                                                                                         skip_group_check=True))
                    if c0 in exd:
                        n2, rhs_t, nk2 = exd[c0]
                        fns.append(lambda q, c0=c0, n2=n2, rhs_t=rhs_t, nk2=nk2: q.matmul(bank[sbk][0:nk2, c0:c0 + n2], lhsT=ident[0:nk2, 0:nk2], rhs=rhs_t, start=False, stop=True,
                                                                                          skip_group_check=True))
                fw.op("pe", fns, reads=u["rd"] + [b_ident, b_negq4, b_Tb], writes=[b_bank[sbk]])
                pi = ui % 3
                for (c0, n, nk) in u["exps"]:
                    fw.op("act", lambda q, c0=c0, n=n, nk=nk: q.activation(out=PT[pi][0:nk, c0:c0 + n], in_=bank[sbk][0:nk, c0:c0 + n], func=AF.Exp),
                          reads=[b_bank[sbk]], writes=[b_PT[pi]])
                if pend is not None:
                    pend()
                if u.get("hook"):
                    u["hook"]()

                def pvs(u=u, pi=pi):
                    obufs = []
                    fns = []
                    for (O_ap, pc0, nq, V_ap, start, bO, nk) in u["pv"]:
                        fns.append(lambda q, O_ap=O_ap, pc0=pc0, nq=nq, V_ap=V_ap, start=start, nk=nk: q.matmul(
                            O_ap, lhsT=PT[pi][0:nk, pc0:pc0 + nq], rhs=V_ap, start=start, stop=True, skip_group_check=True))
                        if bO not in obufs:
                            obufs.append(bO)
                    fw.op("pe", fns, reads=[b_PT[pi]] + u["rdv"], writes=obufs)
                    if u.get("fin"):
                        u["fin"]()
                pend = pvs
            if pend is not None:
                pend()

        def stage_gate(seq, scr_w, gate_c0, half, premul):
            def cons(ti, r0, nr, rt, bk, bbk):
                k = nxt("tb16", 3)
                fw.op("act", lambda q: q.activation(out=tb16[k][0:nr, :], in_=bk[0:nr, :], func=AF.Tanh, scale=0.5), reads=[bbk], writes=[b_tb16[k]])
                fw.op("dve", lambda q: q.scalar_tensor_tensor(out=tb16[k][0:nr, :], in0=tb16[k][0:nr, :], scalar=1.0, in1=bk[0:nr, :], op0=ALU.add, op1=ALU.mult),
                      reads=[b_tb16[k], bbk], writes=[b_tb16[k]])
                osl = oall[0:nr, ti, half * 512:(half + 1) * 512]
                if premul:
                    fw.op("dve", lambda q: q.scalar_tensor_tensor(out=osl, in0=tb16[k][0:nr, :], scalar=0.5, in1=osl, op0=ALU.mult, op1=ALU.mult),
                          reads=[b_o[ti], b_tb16[k]], writes=[b_o[ti]])
                else:
                    fw.op("dve", lambda q: q.tensor_scalar(out=osl, in0=tb16[k][0:nr, :], scalar1=0.5, scalar2=None, op0=ALU.mult), reads=[b_tb16[k]], writes=[b_o[ti]])
                return []
            proj_block(seq, scr_w, gate_c0 + half * 512, 512, cons)

        def stage_GY(seq, b, resid_src, dst, dst_is_output, src_bufs=()):
            xsel = {}

            def tphase(j):
                r0, nr, rt = seq.tiles[j]
                i = nxt("xt", 2)
                xsel[j] = i
                fw.dma("sp", xt[i][0:nr, :], resid_src[r0:r0 + nr, :], reads=list(src_bufs), writes=[b_xt[i]], sembuf=b_xt[i])
                if DEV_DBG and seq.idx == 0:
                    fw.op("dve", lambda q: q.tensor_copy(out=ytmp[0:nr, :], in_=oall[0:nr, j, :]), reads=[b_o[j]], writes=[b_ytmp])
                    fw.dma("pool", O["dbg"][r0:r0 + nr, :], ytmp[0:nr, :], reads=[b_ytmp], is_output=True, sembuf=b_ytmp)
                k = j % 2
                fw.op("pe", [lambda q, c=c: q.transpose(out=tbk[k][:, c * 128:c * 128 + nr], in_=oall[0:nr, j, c * 128:(c + 1) * 128], identity=ident[0:nr, 0:nr])
                             for c in range(8)], reads=[b_o[j], b_ident], writes=[b_tbk[k]])
                g = j % 2
                ecopy("act", ogT[g][:, 0:4, 0:nr], tbk[k][:, 0:512].rearrange("p (c t) -> p c t", c=4)[:, :, 0:nr], [b_tbk[k]], [b_ogT[g]])
                ecopy("dve", ogT[g][:, 4:8, 0:nr], tbk[k][:, 512:1024].rearrange("p (c t) -> p c t", c=4)[:, :, 0:nr], [b_tbk[k]], [b_ogT[g]])

            def yphase(j):
                r0, nr, rt = seq.tiles[j]
                i = xsel[j]
                g = j % 2
                yb = ((4, 5), (2, 3))[j % 2]
                for half in range(2):
                    fw.op("pe", [lambda q, c=c: q.matmul(bank[yb[half]][0:nr, :], lhsT=ogT[g][:, c, 0:nr], rhs=wout[:, c, half * 512:(half + 1) * 512],
                                                         start=(c == 0), stop=(c == 7)) for c in range(8)],
                          reads=[b_ogT[g], b_wout], writes=[b_bank[yb[half]]])
                rs, bs = rms_scale([bank[yb[0]][0:nr, :], bank[yb[1]][0:nr, :]], nr, DM, [b_bank[yb[0]], b_bank[yb[1]]], 24 + 2 * (j % 2))
                for half in range(2):
                    fw.op("dve", lambda q: q.scalar_tensor_tensor(out=ytmp[0:nr, half * 512:(half + 1) * 512], in0=bank[yb[half]][0:nr, :], scalar=rs,
                                                                  in1=gpost[0:nr, half * 512:(half + 1) * 512], op0=ALU.mult, op1=ALU.mult),
                          reads=[b_bank[yb[half]], bs, b_gpost], writes=[b_ytmp])
                fw.op("dve", lambda q: q.tensor_tensor(out=xt[i][0:nr, 0:512], in0=ytmp[0:nr, 0:512], in1=xt[i][0:nr, 0:512], op=ALU.add), reads=[b_ytmp, b_xt[i]], writes=[b_xt[i]])
                fw.op("pool", lambda q: q.tensor_tensor(out=xt[i][0:nr, 512:1024], in0=ytmp[0:nr, 512:1024], in1=xt[i][0:nr, 512:1024], op=ALU.add), reads=[b_ytmp, b_xt[i]], writes=[b_xt[i]])
                if dst_is_output:
                    fw.dma("pool", dst[r0:r0 + nr, :], xt[i][0:nr, :], reads=[b_xt[i]], is_output=True, sembuf=b_xout[i])
                else:
                    h1_evs.setdefault((seq.kind, b), []).append(
                        fw.dma("pool", dst[r0:r0 + nr, :], xt[i][0:nr, :], reads=[b_xt[i]], writes=[dbuf(("h1", seq.kind, b))], sembuf=b_xout[i]))

            n = len(seq.tiles)
            tphase(0)
            for j in range(n):
                if j + 1 < n:
                    tphase(j + 1)
                yphase(j)

        def stage_PC(seq, b):
            pre = "p" if seq.kind == "p" else "s"
            SC = (64 + 32) ** -0.5
            set_ones(4, 64)
            knew0 = 0 if seq.kind == "p" else 2048
            vnew0 = 0 if seq.kind == "p" else 16
            spill_ev = []
            slot_ctr = {"i": 0}

            def next_slot():
                i = slot_ctr["i"] % 13
                slot_ctr["i"] += 1
                return i

            wq1 = oall[:, 13:16, :].rearrange("p a b -> p (a b)")[:, 0:2304].rearrange("p (c n) -> p c n", c=6)
            bwq1 = [b_o[13], b_o[14], b_o[15]]
            wsrc = S_wuq.rearrange("(c p) n -> p c n", p=128)
            wait_prep(S_wuq.name)
            fw.dma("sp", wqb[:, :, :], wsrc[:, :, 0:384], reads=[dbuf(S_wuq.name)], writes=[b_wqb])
            fw.dma("sp", wq1, wsrc[:, :, 384:768], reads=[dbuf(S_wuq.name)], writes=bwq1, sembuf=b_o[13])

            def lat_ingest(src_lat, b_lat, src_kr, b_kr, nr, kcol0, vt, eng):
                j = nxt("tb16", 3)
                ecopy(eng, tb16[j][0:nr, 0:256], src_lat, [b_lat], [b_tb16[j]])
                ecopy(eng, tb16[j][0:nr, 256:288], src_kr, [b_kr], [b_tb16[j]])
                k = nxt("tbk", 2)
                fw.op("pe", [lambda q, c=c: q.transpose(out=tbk[k][:, c * 128:c * 128 + nr], in_=tb16[j][0:nr, c * 128:(c + 1) * 128], identity=ident[0:nr, 0:nr]) for c in range(2)]
                      + [lambda q: q.transpose(out=tbk[k][0:32, 256:256 + nr], in_=tb16[j][0:nr, 256:288], identity=ident[0:nr, 0:nr])],
                      reads=[b_tb16[j], b_ident], writes=[b_tbk[k]])
                ecopy("dve", latT[:, :, 0:nr], tbk[k][:, 0:256].rearrange("p (c t) -> p c t", c=2)[:, :, 0:nr], [b_tbk[k]], [b_latT])
                ecopy("act", KT[64:96, 0:4, kcol0:kcol0 + nr], tbk[k][0:32, 256:256 + nr].unsqueeze(1).to_broadcast([32, 4, nr]), [b_tbk[k]], [b_KT[vt]])
                for g in range(2):
                    fns = []
                    for hh in range(4):
                        h = g * 4 + hh
                        for c in range(2):
                            fns.append(lambda q, hh=hh, h=h, c=c: q.matmul(bank[g][0:64, hh * 128:hh * 128 + nr], lhsT=wuk[:, c, h * 64:(h + 1) * 64], rhs=latT[:, c, 0:nr],
                                                                           start=(c == 0), stop=(c == 1), skip_group_check=True))
                    fw.op("pe", fns, reads=[b_latT, b_wuk], writes=[b_bank[g]])
                ecopy("dve", KT[0:64, 0:4, kcol0:kcol0 + nr], bank[0][0:64, :].rearrange("p (h t) -> p h t", h=4)[:, :, 0:nr], [b_bank[0]], [b_KT[vt]])
                sk = next_slot()
                kst = oall[0:64, sk, 0:512].rearrange("p (h t) -> p h t", h=4)[:, :, 0:nr]
                ecopy("act", kst, bank[1][0:64, :].rearrange("p (h t) -> p h t", h=4)[:, :, 0:nr], [b_bank[1]], [b_o[sk]])
                spill_ev.append(fw.dma("pool", S_kt1[:, :, kcol0:kcol0 + nr], kst, reads=[b_o[sk]], writes=[dbuf("kt1")], sembuf=b_o[sk]))
                fw.op("pe", [lambda q, c=c: q.matmul(bank[2][0:nr, 0:512], lhsT=latT[:, c, 0:nr], rhs=wuv[:, c, :], start=(c == 0), stop=(c == 1))
                             for c in range(2)], reads=[b_latT, b_wuv], writes=[b_bank[2]])
                ingest_v(bank[2][0:nr, 0:256], b_bank[2], nr, vt, 4, 64, eng="act")
                sv = next_slot()
                ecopy("dve", oall[0:nr, sv, 0:256], bank[2][0:nr, 256:512], [b_bank[2]], [b_o[sv]])
                spill_ev.append(fw.dma("pool", S_v1[vt, 0:nr, :], oall[0:nr, sv, 0:256], reads=[b_o[sv]], writes=[dbuf("v1")], sembuf=b_o[sv]))

            if seq.kind == "s":
                for (vt, kc0, nk, is_cache) in seq.keys:
                    if not is_cache:
                        continue
                    i = nxt("stg", 3)
                    fw.dma("sp", stg[i][:, 0:256], I["cache_c_latent"][b, kc0:kc0 + 128, :], writes=[b_stg[i]])
                    fw.dma("sp", stg[i][:, 256:288], I["cache_c_krope"][b, kc0:kc0 + 128, :], writes=[b_stg[i]])
                    lat_ingest(stg[i][:, 0:256], b_stg[i], stg[i][:, 256:288], b_stg[i], 128, kc0, vt, cast_eng())

            def cons_ckv(ti, r0, nr, rt, bk, bbk):
                i = nxt("stg", 3)
                rs, bs = rms_scale(bk[0:nr, 0:256], nr, 256, [bbk], 28)
                ecopy("act", stg[i][0:nr, 256:288], bk[0:nr, 256:288], [bbk], [b_stg[i]])
                fw.op("dve", lambda q: q.scalar_tensor_tensor(out=stg[i][0:nr, 0:256], in0=bk[0:nr, 0:256], scalar=rs, in1=gckv[0:nr, :], op0=ALU.mult, op1=ALU.mult),
                      reads=[bbk, bs, b_gckv], writes=[b_stg[i]])
                rope_ops(None, stg[i][0:nr, 256:288].rearrange("p (b d) -> p b d", b=1), nr, rt, ropeC, 16, 1, b_stg[i], b_stg[i],
                         stg[i][0:nr, 256:288].rearrange("p (b d) -> p b d", b=1))
                store_out(O["c_lat_" + pre][b, r0:r0 + nr, :], stg[i][0:nr, 0:256], b_stg[i])
                store_out(O["c_krope_" + pre][b, r0:r0 + nr, :], stg[i][0:nr, 256:288], b_stg[i])
                return [lambda: lat_ingest(stg[i][0:nr, 0:256], b_stg[i], stg[i][0:nr, 256:288], b_stg[i], nr, knew0 + r0, vnew0 + ti, "dve")]

            proj_block(seq, S_w1, 768, 288, cons_ckv)

            w1a = load_wblock(S_w1, DM, 0, 512)
            w1b = load_wblock(S_w1, DM, 512, 256)
            def s1(ti):
                r0, nr, rt = seq.tiles[ti]
                ba, bb = ((4, 5), (0, 1))[ti % 2]
                fw.op("pe", [lambda q, c=c: q.matmul(bank[ba][0:nr, :], lhsT=uT[:, c, r0:r0 + nr], rhs=w1a[0][:, c, :], start=(c == 0), stop=(c == 7)) for c in range(8)],
                      reads=[b_uT[ti], w1a[1]], writes=[b_bank[ba]])
                fw.op("pe", [lambda q, c=c: q.matmul(bank[bb][0:nr, 0:256], lhsT=uT[:, c, r0:r0 + nr], rhs=w1b[0][:, c, 0:256], start=(c == 0), stop=(c == 7)) for c in range(8)],
                      reads=[b_uT[ti], w1b[1]], writes=[b_bank[bb]])

            stgsel = {}

            def s2(ti):
                r0, nr, rt = seq.tiles[ti]
                ba, bb = ((4, 5), (0, 1))[ti % 2]
                rs, bs = rms_scale([bank[ba][0:nr, :], bank[bb][0:nr, 0:256]], nr, 768, [b_bank[ba], b_bank[bb]], 32 + 2 * (ti % 2))
                g = ti % 2
                cq16 = ogT[g][:, :, :].rearrange("p c t -> p (c t)")
                fw.op("act", lambda q: q.activation(out=cq16[0:nr, 0:512], in_=bank[ba][0:nr, :], func=AF.Copy, scale=rs), reads=[b_bank[ba], bs], writes=[b_ogT[g]])
                fw.op("act", lambda q: q.activation(out=cq16[0:nr, 512:768], in_=bank[bb][0:nr, 0:256], func=AF.Copy, scale=rs), reads=[b_bank[bb], bs], writes=[b_ogT[g]])
                k = nxt("tbk", 2)
                fw.op("pe", [lambda q, c=c: q.transpose(out=tbk[k][:, c * 128:c * 128 + nr], in_=cq16[0:nr, c * 128:(c + 1) * 128], identity=ident[0:nr, 0:nr]) for c in range(6)],
                      reads=[b_ogT[g], b_ident], writes=[b_tbk[k]])
                cqT = Ef[g][:, :].bitcast(BF16).rearrange("p (c t) -> p c t", c=8)
                ecopy("dve", cqT[:, 0:6, 0:nr], tbk[k][:, 0:768].rearrange("p (c t) -> p c t", c=6)[:, :, 0:nr], [b_tbk[k]], [b_Ef[g]])
                for grp in range(2):
                    qbk = (3, 2)[grp]
                    wq_t, wq_b = (wqb, [b_wqb]) if grp == 0 else (wq1, bwq1)
                    fw.op("pe", [lambda q, c=c: q.matmul(bank[qbk][0:nr, 0:384], lhsT=cqT[:, c, 0:nr], rhs=wq_t[:, c, 0:384], start=(c == 0), stop=(c == 5)) for c in range(6)],
                          reads=[b_Ef[g]] + wq_b, writes=[b_bank[qbk]])
                    i = nxt("stg", 3)
                    stgsel[(ti, grp)] = i
                    ecopy("act", stg[i][0:nr, 0:384], bank[qbk][0:nr, 0:384], [b_bank[qbk]], [b_stg[i]])

            def s3(ti):
                r0, nr, rt = seq.tiles[ti]
                for grp in range(2):
                    i = stgsel[(ti, grp)]
                    dst3 = stg[i][0:nr, 0:384].rearrange("p (h d) -> p h d", h=4)[:, :, 64:96]
                    rope_ops(None, dst3, nr, rt, ropeC, 16, 4, b_stg[i], b_stg[i], dst3)
                    if grp == 0:
                        ingest_q(stg[i][0:nr, 0:384], b_stg[i], nr, r0, ti, SC, ncol=384, part=96, eng="dve")()
                    else:
                        j = nxt("tb16", 3)
                        ecopy("dve", tb16[j][0:nr, 0:384], stg[i][0:nr, 0:384], [b_stg[i]], [b_tb16[j]], SC)
                        k2 = nxt("tbk", 2)
                        fw.op("pe", [lambda q, c=c: q.transpose(out=tbk[k2][0:96, c * 128:c * 128 + nr], in_=tb16[j][0:nr, c * 96:(c + 1) * 96], identity=ident[0:nr, 0:nr]) for c in range(4)],
                              reads=[b_tb16[j], b_ident], writes=[b_tbk[k2]])
                        sq = next_slot()
                        qst = oall[0:96, sq, 0:512].rearrange("p (h t) -> p h t", h=4)[:, :, 0:nr]
                        ecopy(copy_eng(), qst, tbk[k2][0:96, 0:512].rearrange("p (c t) -> p c t", c=4)[:, :, 0:nr], [b_tbk[k2]], [b_o[sq]])
                        spill_ev.append(fw.dma("pool", S_qt1[:, :, r0:r0 + nr], qst, reads=[b_o[sq]], writes=[dbuf("qt1")], sembuf=b_o[sq]))

            ntl = len(seq.tiles)
            s1(0)
            for ti in range(ntl):
                if ti + 1 < ntl:
                    s1(ti + 1)
                if ti >= 1:
                    s3(ti - 1)
                s2(ti)
            s3(ntl - 1)
            return spill_ev

        def reload_C(seq, spill_ev):
            fw.wait_events("sp", spill_ev)
            nq = seq.ntok
            nkc = seq.keys[-1][1] + seq.keys[-1][2]
            fw.dma("sp", QT[0:96, 0:4, 0:nq], S_qt1[:, :, 0:nq], reads=[dbuf("qt1")], writes=b_QT, sembuf=b_QT[0])
            fw.dma("sp", KT[0:64, 0:4, 0:nkc], S_kt1[:, :, 0:nkc], reads=[dbuf("kt1")], writes=b_KT, sembuf=b_KT[0])
            for (vt, kc0, nk, is_cache) in seq.keys:
                fw.dma("sp", VR[0:nk, vt, 0:260].rearrange("p (h e) -> p h e", e=65)[:, :, 0:64], S_v1[vt, 0:nk, :].rearrange("p (h d) -> p h d", h=4),
                       reads=[dbuf("v1")], writes=[b_V[vt]], sembuf=b_V[vt])

        def stage_QC_sample(seq, grp):
            nq = seq.tiles[0][1]
            units = []
            for hh in range(4):
                h = grp * 4 + hh
                ob = 2 + (hh % 2)
                u = dict(qk=[], extra=[], exps=[], pv=[], rd=[b_QT[0]], rdv=[])
                for ki, (vt, kc0, nk, is_cache) in enumerate(seq.keys):
                    c0 = ki * nq
                    u["qk"].append((c0, nq, KT[0:96, hh, kc0:kc0 + nk], QT[0:96, hh, 0:nq], nk))
                    u["pv"].append((bank[ob][0:nq, 0:65], c0, nq, VR[0:nk, vt, hh * 65:(hh + 1) * 65], ki == 0, b_bank[ob], nk))
                    u["rd"].append(b_KT[vt])
                    u["rdv"].append(b_V[vt])
                ncache = sum(1 for k in seq.keys if k[3])
                u["exps"] = [(0, ncache * nq, 128), (ncache * nq, nq, seq.keys[-1][2])]

                def fin(h=h, ob=ob):
                    kb = 36
                    bs = smb(kb)
                    fw.op("dve", lambda q: q.reciprocal(out=sm[0:nq, kb:kb + 1], in_=bank[ob][0:nq, 64:65]), reads=[b_bank[ob]], writes=[bs])
                    ecopy("dve", oall[0:nq, 0, h * 64:(h + 1) * 64], bank[ob][0:nq, 0:64], [b_bank[ob], bs], [b_o[0]], scale=sm[0:nq, kb:kb + 1])
                u["fin"] = fin
                units.append(u)
            run_units_nk(units)

        def stage_QC(seq, grp):
            if seq.kind == "s":
                return stage_QC_sample(seq, grp)
            nt = len(seq.tiles)
            bq = 4 if seq.kind == "p" else 1
            blk = 0
            for hh in range(4):
                h = grp * 4 + hh
                for qb0 in range(0, nt, bq):
                    qts = list(range(qb0, min(nt, qb0 + bq)))
                    qcol0 = seq.tiles[qb0][0]
                    ob = 2 + (blk % 2)
                    blk += 1
                    units = []
                    started = False
                    for ki, (vt, kc0, nk, is_cache) in enumerate(seq.keys):
                        valid = [j for j in qts if ki <= j] if seq.kind == "p" else qts
                        if not valid:
                            continue
                        j0 = valid[0] - qb0
                        off0 = j0 * 128
                        nv = sum(seq.tiles[j][1] for j in valid)
                        u = dict(nk=nk, qk=[], extra=[], exps=[(0, off0, nv, off0)], pv=[], rd=[b_KT[vt]] + [b_QT[j] for j in valid], rdv=[b_V[vt]])
                        u["qk"].append((0, off0, nv, KT[0:96, hh, kc0:kc0 + nk], QT[0:96, hh, qcol0 + off0:qcol0 + off0 + nv]))
                        for j in valid:
                            if seq.kind == "p" and ki == j:
                                u["extra"].append((0, (j - qb0) * 128, seq.tiles[j][1], negq[0:nk, 0:seq.tiles[j][1]]))
                        for j in valid:
                            jj = j - qb0
                            nq = seq.tiles[j][1]
                            u["pv"].append((bank[ob][0:nq, jj * 65:(jj + 1) * 65], jj * 128, nq, VR[0:nk, vt, hh * 65:(hh + 1) * 65], not started, b_bank[ob]))
                            started = True
                        units.append(u)

                    def fin(h=h, qts=qts, qb0=qb0, ob=ob):
                        nq = seq.tiles[qts[0]][1]
                        kb = 36
                        bs = smb(kb)
                        nj = len(qts)
                        fw.op("dve", lambda q: q.reciprocal(out=sm[0:nq, kb:kb + nj], in_=bank[ob][0:nq, 0:nj * 65].rearrange("p (j e) -> p j e", e=65)[:, :, 64]),
                              reads=[b_bank[ob]], writes=[bs])
                        for j in qts:
                            jj = j - qb0
                            ecopy("dve", oall[0:nq, j, h * 64:(h + 1) * 64], bank[ob][0:nq, jj * 65:jj * 65 + 64], [b_bank[ob], bs], [b_o[j]], scale=sm[0:nq, kb + jj:kb + jj + 1])
                    units[-1]["fin"] = fin
                    run_units(units)

        def stage_PD(seq, b):
            pre = "p" if seq.kind == "p" else "s"
            set_ones(8, 64)
            if seq.kind == "s":
                for (vt, kc0, nk, is_cache) in seq.keys:
                    if not is_cache:
                        continue
                    i = nxt("stg", 3)
                    fw.dma("sp", stg[i][:, :], I["cache_d_k"][b, kc0:kc0 + 128, :], writes=[b_stg[i]])
                    ingest_k(stg[i][:, :], b_stg[i], 128, kc0, vt, eng=cast_eng())()
                    i = nxt("stg", 3)
                    fw.dma("sp", stg[i][:, :], I["cache_d_v"][b, kc0:kc0 + 128, :], writes=[b_stg[i]])
                    ingest_v(stg[i][:, :], b_stg[i], 128, vt, 8, 64, eng=cast_eng())
            knew0 = 0 if seq.kind == "p" else 2048
            vnew0 = 0 if seq.kind == "p" else 16

            def cons_q(ti, r0, nr, rt, bk, bbk):
                return [ingest_q(bk[0:nr, :], bbk, nr, r0, ti, 0.125, eng="dve")]

            def cons_k(ti, r0, nr, rt, bk, bbk):
                i = nxt("stg", 3)
                ecopy("act", stg[i][0:nr, :], bk[0:nr, :], [bbk], [b_stg[i]])
                store_out(O["d_k_" + pre][b, r0:r0 + nr, :], stg[i][0:nr, :], b_stg[i])
                return [ingest_k(bk[0:nr, :], bbk, nr, knew0 + r0, vnew0 + ti, eng="dve")]

            def cons_v(ti, r0, nr, rt, bk, bbk):
                i = nxt("stg", 3)
                ecopy("act", stg[i][0:nr, :], bk[0:nr, :], [bbk], [b_stg[i]])
                store_out(O["d_v_" + pre][b, r0:r0 + nr, :], stg[i][0:nr, :], b_stg[i])
                ingest_v(bk[0:nr, :], bbk, nr, vnew0 + ti, 8, 64, eng="dve")
                return []

            proj_block(seq, S_w1, 1056, 512, cons_q)
            proj_block(seq, S_w1, 1568, 512, cons_k)
            proj_block(seq, S_w1, 2080, 512, cons_v)

        def stage_QD(seq):
            nt = len(seq.tiles)
            bq = 4 if seq.kind == "p" else 1
            xbanks = (0, 1, 3)
            st_ = {"u": 0, "cs": 0}
            prev_tiles = []
            for qb0 in range(0, nt, bq):
                qts = list(range(qb0, min(nt, qb0 + bq)))
                qcol0 = seq.tiles[qb0][0]
                for h in range(8):
                    hp = (h % 2) * 64
                    fw.op("pool", lambda q: q.memset(osb[:, :, 0:64], 0.0), writes=b_osb)
                    pend_a = None
                    pend_b = None
                    for ki, (vt, kc0, nk, is_cache) in enumerate(seq.keys):
                        valid = [j for j in qts if ki <= j] if seq.kind == "p" else qts
                        if not valid:
                            continue
                        j0 = valid[0] - qb0
                        off0 = j0 * 128
                        nv = sum(seq.tiles[j][1] for j in valid)
                        ucount = st_["u"]
                        st_["u"] += 1
                        xb = xbanks[ucount % 3]
                        eb = ucount % 2
                        pi = ucount % 3
                        diag = [j for j in valid if (seq.kind == "p" and ki == j) or (seq.kind == "s" and not is_cache)]
                        fw.op("pe", lambda q: q.matmul(bank[xb][0:nk, off0:off0 + nv], lhsT=KT[hp:hp + 64, h // 2, kc0:kc0 + nk], rhs=QT[hp:hp + 64, h // 2, qcol0 + off0:qcol0 + off0 + nv],
                                                       start=True, stop=False, skip_group_check=True), reads=[b_KT[vt]] + [b_QT[j] for j in valid], writes=[b_bank[xb]])
                        fw.op("act", lambda q: q.activation(out=Ef[eb][0:nk, off0:off0 + nv], in_=bank[xb][0:nk, off0:off0 + nv], func=AF.Exp), reads=[b_bank[xb]], writes=[b_Ef[eb]])
                        fw.op("act", lambda q: q.activation(out=SPb[eb][0:nk, off0:off0 + nv], in_=Ef[eb][0:nk, off0:off0 + nv], func=AF.Ln, bias=1.0), reads=[b_Ef[eb]], writes=[b_SPb[eb]])
                        for j in diag:
                            c = (j - qb0) * 128
                            nq = seq.tiles[j][1]
                            fw.op("pool", lambda q: q.tensor_tensor(out=SPb[eb][0:nk, c:c + nq], in0=SPb[eb][0:nk, c:c + nq], in1=m01d[0:nk, 0:nq], op=ALU.mult),
                                  reads=[b_SPb[eb], b_m01d], writes=[b_SPb[eb]])

                        def stage2a(vt=vt, nk=nk, valid=valid, off0=off0, nv=nv, xb=xb, eb=eb, pi=pi, diag=diag, qts=qts, qb0=qb0):
                            fns = [lambda q: q.matmul(bank[xb][0:nk, off0:off0 + nv], lhsT=nut[0:nk, 0:nk], rhs=SPb[eb][0:nk, off0:off0 + nv], start=False, stop=False, skip_group_check=True)]
                            for j in diag:
                                c = (j - qb0) * 128
                                nq = seq.tiles[j][1]
                                fns.append(lambda q, c=c, nq=nq: q.matmul(bank[xb][0:nk, c:c + nq], lhsT=ident[0:nk, 0:nk], rhs=negd[0:nk, 0:nq], start=False, stop=True, skip_group_check=True))
                            fw.op("pe", fns, reads=[b_SPb[eb], b_nut, b_negd, b_ident], writes=[b_bank[xb]])
                            fw.op("act", lambda q: q.activation(out=PT[pi][0:nk, off0:off0 + nv], in_=bank[xb][0:nk, off0:off0 + nv], func=AF.Exp), reads=[b_bank[xb]], writes=[b_PT[pi]])

                        def stage2b(vt=vt, nk=nk, valid=valid, pi=pi, qb0=qb0, h=h, qts=qts, ucount=ucount):
                            dk = 40 + 4 * pi
                            bs = smb(dk)
                            wbk = (2, 4)[ucount % 2]
                            fns = []
                            for j in valid:
                                jj = j - qb0
                                nq = seq.tiles[j][1]
                                fns.append(lambda q, jj=jj, nq=nq: q.matmul(bank[wbk][0:nq, jj * 65:(jj + 1) * 65], lhsT=PT[pi][0:nk, jj * 128:jj * 128 + nq], rhs=VR[0:nk, vt, h * 65:(h + 1) * 65],
                                                                            start=True, stop=True, skip_group_check=True))
                            fw.op("pe", fns, reads=[b_PT[pi], b_V[vt]], writes=[b_bank[wbk]])
                            jlo = valid[0] - qb0
                            nqm = seq.tiles[valid[0]][1]
                            nj = len(qts)
                            w3 = bank[wbk][0:nqm, 0:nj * 65].rearrange("p (j e) -> p j e", e=65)
                            fw.op("dve", lambda q: q.tensor_scalar(out=sm[0:nqm, dk + jlo:dk + nj], in0=w3[:, jlo:nj, 64], scalar1=-1.0, scalar2=1.0, op0=ALU.mult, op1=ALU.add),
                                  reads=[b_bank[wbk]], writes=[bs])
                            nv_ = nj - jlo
                            dec_b = sm[0:nqm, dk + jlo:dk + nj].unsqueeze(2).to_broadcast([nqm, nv_, 64])
                            fw.op("dve", lambda q: q.tensor_tensor(out=osb[0:nqm, jlo:nj, 0:64], in0=osb[0:nqm, jlo:nj, 0:64], in1=dec_b, op=ALU.mult),
                                  reads=b_osb + [bs], writes=b_osb)
                            fw.op("dve", lambda q: q.tensor_tensor(out=osb[0:nqm, jlo:nj, 0:64], in0=w3[:, jlo:nj, 0:64], in1=osb[0:nqm, jlo:nj, 0:64], op=ALU.add),
                                  reads=b_osb + [b_bank[wbk]], writes=b_osb)

                        if pend_a is not None:
                            pend_a()
                        if pend_b is not None:
                            pend_b()
                        pend_b = None
                        if pend_a is not None:
                            pend_b = pend_a.b
                        stage2a.b = stage2b
                        pend_a = stage2a
                    if pend_a is not None:
                        pend_a()
                    if pend_b is not None:
                        pend_b()
                    if pend_a is not None:
                        pend_a.b()
                    for j in qts:
                        jj = j - qb0
                        nq = seq.tiles[j][1]
                        osl = oall[0:nq, j, 512 + h * 64:512 + (h + 1) * 64]
                        fw.op("dve", lambda q, osl=osl, jj=jj: q.tensor_tensor(out=osl, in0=osb[0:nq, jj, 0:64], in1=osl, op=ALU.mult), reads=[b_osb[jj], b_o[j]], writes=[b_o[j]])

        def load_wukv():
            for bb in (b_wuk, b_wuv, b_latT, b_wqb):
                bb.lw = b_Tb.lw
                bb.rd = list(b_Tb.rd)
            for (wsrc, wdst, bw) in ((I["w_uk"], wuk, b_wuk), (I["w_uv"], wuv, b_wuv)):
                for c in range(2):
                    i = nxt("stg", 3)
                    fw.dma("sp", stg[i][:, :], wsrc[c * 128:(c + 1) * 128, :], writes=[b_stg[i]])
                    fw.op("dve", lambda q: q.tensor_copy(out=wdst[:, c, :], in_=stg[i][:, :]), reads=[b_stg[i]], writes=[bw])

        seqs = [Seq("p", i) for i in range(NP)] + [Seq("s", i) for i in range(NS)]
        h1_evs = {}

        def load_layer_consts(layer):
            fw.dma("sp", gpost[:], bcast_ap(I["g_post0"] if layer == 0 else I["g_post1"], DM), writes=[b_gpost])
            wait_prep((S_wo0 if layer == 0 else S_wo1).name)
            fw.dma("sp", wout[:], (S_wo0 if layer == 0 else S_wo1).rearrange("(c p) n -> p c n", p=128), reads=[dbuf((S_wo0 if layer == 0 else S_wo1).name)], writes=[b_wout])

        u_done = set()

        def do_U(seq, layer):
            key = (layer, seq.kind, seq.idx)
            if key in u_done:
                return
            u_done.add(key)
            b = seq.idx
            if layer == 0 or 0 not in layers:
                stage_U(seq, I["x_prompt"][b] if seq.kind == "p" else I["x_sample"][b])
            else:
                fw.wait_events("sp", h1_evs.get((seq.kind, b), []))
                stage_U(seq, S_h1p[b] if seq.kind == "p" else S_h1s[b], [dbuf(("h1", seq.kind, b))])

        if 0 in layers and DEV_STOP >= 2:
            consts0_loaded = [False]
            for si, seq in enumerate(seqs):
                b = seq.idx
                src = I["x_prompt"][b] if seq.kind == "p" else I["x_sample"][b]
                if 1 in layers:
                    dst = S_h1p[b] if seq.kind == "p" else S_h1s[b]
                    is_out = False
                else:
                    dst = O["y_prompt"][b] if seq.kind == "p" else O["y_sample"][b]
                    is_out = True
                do_U(seq, 0)
                if si == 0 and seq.kind == "p":
                    prep_state["active"] = True
                if DEV_STOP >= 2.1:
                    stage_PA(seq, b)
                prep_flush()
                if not consts0_loaded[0]:
                    load_layer_consts(0)
                    consts0_loaded[0] = True
                if DEV_STOP >= 3.1:
                    stage_QA(seq)
                if DEV_STOP >= 5:
                    stage_PB(seq, b)
                if DEV_STOP >= 6:
                    stage_gate(seq, S_w0, 3072, 0, True)
                    stage_gate(seq, S_w0, 3072, 1, False)
                    if si + 1 < len(seqs):
                        do_U(seqs[si + 1], 0)
                    stage_QB(seq)
                    stage_GY(seq, b, src, dst, is_out)
        prep_flush()

        if 1 in layers:
            load_layer_consts(1)
            load_wukv()
            for si, seq in enumerate(seqs):
                b = seq.idx
                if 0 in layers:
                    src = S_h1p[b] if seq.kind == "p" else S_h1s[b]
                else:
                    src = I["x_prompt"][b] if seq.kind == "p" else I["x_sample"][b]
                dst = O["y_prompt"][b] if seq.kind == "p" else O["y_sample"][b]
                hb = [dbuf(("h1", seq.kind, b))] if 0 in layers else []
                fw.wait_events("sp", h1_evs.get((seq.kind, b), []))
                do_U(seq, 1)
                sp_ev = stage_PC(seq, b)
                stage_QC(seq, 0)
                reload_C(seq, sp_ev)
                stage_QC(seq, 1)
                stage_gate(seq, S_w1, 2592, 0, True)
                stage_PD(seq, b)
                stage_gate(seq, S_w1, 2592, 1, False)
                if si + 1 < len(seqs):
                    do_U(seqs[si + 1], 1)
                stage_QD(seq)
                stage_GY(seq, b, src, dst, True, hb)

        fw.finish("sp")
        print("ninst", fw.ninst, {e: fw.cnt[e] for e in fw.cnt})
    return nc


def _consts():
    c = {}
    c["c_ident"] = np.eye(128, dtype=np.float32)
    m = np.zeros((128, 128), np.float32); m[64:128, 0:64] = NEG; c["c_negq"] = m
    m = np.zeros((128, 128), np.float32); m[0:64, 64:128] = NEG; c["c_negq4"] = m
    s = np.arange(128)[:, None]; t = np.arange(128)[None, :]
    c["c_negd"] = np.where(s >= t, NEG, 0.0).astype(np.float32)
    c["c_m01d"] = (s < t).astype(np.float32)
    c["c_nut"] = np.where(s >= t, -1.0, 0.0).astype(np.float32)
    pos = np.zeros((17, 128), np.float32)
    for tt in range(16):
        pos[tt] = tt * 128 + np.arange(128)
    pos[16] = PAST + np.arange(128)
    for nm, half in (("c_ropeA", 8), ("c_ropeC", 16)):
        inv = (np.float32(THETA) ** (-np.arange(half, dtype=np.float32) / np.float32(half))).astype(np.float32)
        ang = (pos[:, :, None] * inv[None, None, :]).astype(np.float32)
        tab = np.stack([np.cos(ang), np.sin(ang)], 0).astype(np.float32)
        c[nm] = np.ascontiguousarray(tab.transpose(2, 0, 1, 3).reshape(128, 2 * 17 * half))
    return c


_CACHE = {}
_IN_SHARDED = ["x_prompt", "x_sample", "cache_a_k", "cache_a_v", "cache_b_k", "cache_b_v", "cache_c_latent", "cache_c_krope", "cache_d_k", "cache_d_v"]
_OUT_NAMES = ["y_prompt", "y_sample", "a_k_p", "a_v_p", "b_k_p", "b_v_p", "c_lat_p", "c_krope_p", "d_k_p", "d_v_p",
              "a_k_s", "a_v_s", "b_k_s", "b_v_s", "c_lat_s", "c_krope_s", "d_k_s", "d_v_s"]


def _out_shapes(nb_p, nb_s):
    return [(nb_p, T, DM), (nb_s, TS, DM), (nb_p, T, 4, 2, 64), (nb_p, T, 4, 128), (nb_p, 512, 8, 64), (nb_p, 512, 8, 64),
            (nb_p, T, 256), (nb_p, T, 32), (nb_p, T, 8, 64), (nb_p, T, 8, 64),
            (nb_s, TS, 4, 2, 64), (nb_s, TS, 4, 128), (nb_s, TS, 8, 64), (nb_s, TS, 8, 64), (nb_s, TS, 256), (nb_s, TS, 32),
            (nb_s, TS, 8, 64), (nb_s, TS, 8, 64)]


def _core_inputs(inputs, lo_p, hi_p, lo_s, hi_s):
    m = {}
    for k, v in inputs.items():
        a = np.asarray(v)
        if k in _IN_SHARDED:
            lo, hi = (lo_p, hi_p) if k == "x_prompt" else (lo_s, hi_s)
            a = a[lo:hi]
            a = a.reshape(a.shape[0], a.shape[1], -1)
        elif k in ("w_uk", "w_uv"):
            a = a.reshape(256, 512)
        m[k] = np.ascontiguousarray(a, dtype=np.float32)
    m.update(_consts())
    rb = np.asarray(inputs["rel_bias_b"], dtype=np.float32)
    s_ = np.arange(128)[:, None]; t_ = np.arange(128)[None, :]
    idx = np.stack([np.clip(t_ - s_, -128, 128) + 128, np.clip(128 + t_ - s_, -128, 128) + 128], 0)
    m["c_rbT"] = np.ascontiguousarray(rb[:, idx], dtype=np.float32)
    return m


def kernel(**inputs):
    nb = np.asarray(inputs["x_prompt"]).shape[0]
    per = nb // NCORES
    key = ("full", per)
    if key not in _CACHE:
        _CACHE[key] = build(per, per)
    nc = _CACHE[key]
    in_maps = [_core_inputs(inputs, i * per, (i + 1) * per, i * per, (i + 1) * per) for i in range(NCORES)]
    res = run_bass_kernel_spmd(nc, in_maps, core_ids=list(range(NCORES)))
    outs = []
    shapes = _out_shapes(nb, nb)
    for nm, shp in zip(_OUT_NAMES, shapes):
        full = np.concatenate([np.asarray(r[nm]) for r in res.results], axis=0)
        outs.append(np.ascontiguousarray(full.reshape(shp), dtype=np.float32))
    return tuple(outs)
```

```python
import math
import itertools
import numpy as np
from contextlib import ExitStack
import concourse.bass as bass
import concourse.mybir as mybir
from concourse.bass_utils import run_bass_kernel_spmd

F32 = mybir.dt.float32
BF16 = mybir.dt.bfloat16
AF = mybir.ActivationFunctionType
ALU = mybir.AluOpType
AX = mybir.AxisListType

NCORES = 8
DM = 1024
T = 2048
TS = 16
PAST = 2048
EPS = 1e-6
NEG = -30000.0
THETA = 500000.0
LAM_INIT0 = 0.8 - 0.6 * math.exp(-0.3 * 0)
W0N = 4096
W1N = 3616
VST = 520


class Buf:
    __slots__ = ("name", "lw", "rd", "dsem", "dcnt", "excl")

    def __init__(self, name):
        self.name = name
        self.excl = False
        self.lw = None
        self.rd = []
        self.dsem = None
        self.dcnt = 0


class FW:
    def __init__(self, nc, stack):
        self.nc = nc
        self.stack = stack
        self.q = {"pe": nc.tensor, "act": nc.scalar, "dve": nc.vector, "pool": nc.gpsimd, "sp": nc.sync}
        self.sem = {}
        self.cnt = {}
        self.waited = {}
        for e in self.q:
            self.sem[e] = stack.enter_context(nc.semaphore("s_" + e))
            self.cnt[e] = 0
            self.waited[e] = {}
        self.out_events = []
        self.ninst = 0
        self.nbuf = 0

    def buf(self, name=None):
        self.nbuf += 1
        return Buf(name or ("b%d" % self.nbuf))

    def sb(self, name, shape, dtype):
        return self.stack.enter_context(self.nc.sbuf_tensor(name, list(shape), dtype))

    def ps(self, name, shape, dtype):
        return self.stack.enter_context(self.nc.psum_tensor(name, list(shape), dtype))

    def _dsem(self, b):
        if b.dsem is None:
            b.dsem = self.stack.enter_context(self.nc.semaphore("d_" + b.name))
        return b.dsem

    def _deps(self, eng, reads, writes):
        best = {}

        def add(ev):
            s, v, en = ev
            if en == "pe" and eng == "pe":
                return
            k = id(s)
            if k not in best or best[k][1] < v:
                best[k] = (s, v)

        for b in reads:
            if b.lw is not None:
                add(b.lw)
            if b.excl:
                for ev in b.rd:
                    if ev[2] != eng:
                        add(ev)
        for b in writes:
            if b.lw is not None:
                add(b.lw)
            for ev in b.rd:
                add(ev)
        w = self.waited[eng]
        out = []
        for k, (s, v) in best.items():
            if w.get(k, 0) >= v:
                continue
            w[k] = v
            out.append((s, v))
        return out

    def _record(self, ev, reads, writes):
        for b in reads:
            b.rd.append(ev)
            if len(b.rd) > 16:
                best = {}
                for e in b.rd:
                    k = id(e[0])
                    if k not in best or best[k][1] < e[1]:
                        best[k] = e
                b.rd = list(best.values())
        for b in writes:
            b.lw = ev
            b.rd = []

    def op(self, eng, fns, reads=(), writes=()):
        q = self.q[eng]
        if not isinstance(fns, (list, tuple)):
            fns = [fns]
        for (s, v) in self._deps(eng, reads, writes):
            q.wait_ge(s, v)
        ins = None
        for f in fns:
            ins = f(q)
            self.ninst += 1
        self.cnt[eng] += 1
        ins.then_inc(self.sem[eng], 1)
        ev = (self.sem[eng], self.cnt[eng], eng)
        self._record(ev, reads, writes)
        return ev

    def dma(self, eng, out, in_, reads=(), writes=(), sembuf=None, is_output=False, **kw):
        q = self.q[eng]
        if sembuf is None:
            sembuf = writes[0] if writes else reads[0]
        s = self._dsem(sembuf)
        for (ws, v) in self._deps(eng, reads, writes):
            q.wait_ge(ws, v)
        q.dma_start(out=out, in_=in_, **kw).then_inc(s, 16)
        self.ninst += 1
        sembuf.dcnt += 16
        ev = (s, sembuf.dcnt, "dma")
        self._record(ev, reads, writes)
        if is_output:
            self.out_events.append(ev)
        return ev

    def wait_events(self, eng, events):
        q = self.q[eng]
        best = {}
        for (s, v, en) in events:
            k = id(s)
            if k not in best or best[k][1] < v:
                best[k] = (s, v)
        w = self.waited[eng]
        for k, (s, v) in best.items():
            if w.get(k, 0) >= v:
                continue
            w[k] = v
            q.wait_ge(s, v)

    def finish(self, eng="sp"):
        q = self.q[eng]
        best = {}
        for (s, v, en) in self.out_events:
            k = id(s)
            if k not in best or best[k][1] < v:
                best[k] = (s, v)
        for k, (s, v) in best.items():
            q.wait_ge(s, v)
        for e in ("pe", "act", "dve", "pool"):
            if self.cnt[e] > 0:
                q.wait_ge(self.sem[e], self.cnt[e])


class Seq:
    def __init__(self, kind, idx):
        self.kind = kind
        self.idx = idx
        if kind == "p":
            self.ntok = T
            self.tiles = [(i * 128, 128, i) for i in range(16)]
            self.keys = [(i, i * 128, 128, False) for i in range(16)]
            self.keys_b = self.keys
        else:
            self.ntok = TS
            self.tiles = [(0, TS, 16)]
            self.keys = [(i, i * 128, 128, True) for i in range(16)] + [(16, 2048, TS, False)]
            self.keys_b = [(i, i * 128, 128, True) for i in range(4)] + [(4, 512, TS, False)]


DEV_STOP = 99
DEV_DBG = False


def build(NP, NS, layers=(0, 1)):
    nc = bass.Bass("TRN2", target_bir_lowering=False)

    def din(name, shape):
        return nc.dram_tensor(name, list(shape), F32, kind="ExternalInput").ap()

    def dout(name, shape):
        return nc.dram_tensor(name, list(shape), F32, kind="ExternalOutput").ap()

    def dscr(name, shape, dtype):
        return nc.dram_tensor(name, list(shape), dtype).ap()

    NPa, NSa = max(NP, 1), max(NS, 1)
    I = {}
    I["x_prompt"] = din("x_prompt", [NPa, T, DM])
    I["x_sample"] = din("x_sample", [NSa, TS, DM])
    I["cache_a_k"] = din("cache_a_k", [NSa, PAST, 512])
    I["cache_a_v"] = din("cache_a_v", [NSa, PAST, 512])
    I["cache_b_k"] = din("cache_b_k", [NSa, 512, 512])
    I["cache_b_v"] = din("cache_b_v", [NSa, 512, 512])
    I["cache_c_latent"] = din("cache_c_latent", [NSa, PAST, 256])
    I["cache_c_krope"] = din("cache_c_krope", [NSa, PAST, 32])
    I["cache_d_k"] = din("cache_d_k", [NSa, PAST, 512])
    I["cache_d_v"] = din("cache_d_v", [NSa, PAST, 512])
    for nm, shp in [("g_pre0", [DM]), ("w_in0", [DM, W0N]), ("lam_q1", [64]), ("lam_k1", [64]), ("lam_q2", [64]),
                    ("lam_k2", [64]), ("g_sub_a", [128]), ("rel_bias_b", [8, 257]), ("w_out0", [DM, DM]),
                    ("g_post0", [DM]), ("g_pre1", [DM]), ("w_in1", [DM, W1N]), ("g_cq", [768]), ("w_uq", [768, 768]),
                    ("g_ckv", [256]), ("w_uk", [256, 512]), ("w_uv", [256, 512]), ("w_out1", [DM, DM]),
                    ("g_post1", [DM])]:
        I[nm] = din(nm, shp)
    I["c_ident"] = din("c_ident", [128, 128])
    I["c_negq"] = din("c_negq", [128, 128])
    I["c_negq4"] = din("c_negq4", [128, 128])
    I["c_negd"] = din("c_negd", [128, 128])
    I["c_m01d"] = din("c_m01d", [128, 128])
    I["c_nut"] = din("c_nut", [128, 128])
    I["c_rbT"] = din("c_rbT", [8, 2, 128, 128])
    I["c_ropeA"] = din("c_ropeA", [128, 2 * 17 * 8])
    I["c_ropeC"] = din("c_ropeC", [128, 2 * 17 * 16])

    O = {}
    O["y_prompt"] = dout("y_prompt", [NPa, T, DM])
    O["y_sample"] = dout("y_sample", [NSa, TS, DM])
    O["a_k_p"] = dout("a_k_p", [NPa, T, 512])
    O["a_v_p"] = dout("a_v_p", [NPa, T, 512])
    O["b_k_p"] = dout("b_k_p", [NPa, 512, 512])
    O["b_v_p"] = dout("b_v_p", [NPa, 512, 512])
    O["c_lat_p"] = dout("c_lat_p", [NPa, T, 256])
    O["c_krope_p"] = dout("c_krope_p", [NPa, T, 32])
    O["d_k_p"] = dout("d_k_p", [NPa, T, 512])
    O["d_v_p"] = dout("d_v_p", [NPa, T, 512])
    O["a_k_s"] = dout("a_k_s", [NSa, TS, 512])
    O["a_v_s"] = dout("a_v_s", [NSa, TS, 512])
    O["b_k_s"] = dout("b_k_s", [NSa, TS, 512])
    O["b_v_s"] = dout("b_v_s", [NSa, TS, 512])
    O["c_lat_s"] = dout("c_lat_s", [NSa, TS, 256])
    O["c_krope_s"] = dout("c_krope_s", [NSa, TS, 32])
    O["d_k_s"] = dout("d_k_s", [NSa, TS, 512])
    O["d_v_s"] = dout("d_v_s", [NSa, TS, 512])
    if DEV_DBG:
        O["dbg"] = dout("dbg", [T, DM])

    S_w0 = dscr("s_w0", [DM, W0N], BF16)
    S_wo0 = dscr("s_wo0", [DM, DM], BF16)
    S_w1 = dscr("s_w1", [DM, W1N], BF16)
    S_wo1 = dscr("s_wo1", [DM, DM], BF16)
    S_wuq = dscr("s_wuq", [768, 768], BF16)
    S_h1p = dscr("s_h1p", [NPa, T, DM], F32)
    S_h1s = dscr("s_h1s", [NSa, TS, DM], F32)
    S_rbp = dscr("s_rbp", [8, 512], F32)
    S_qt1 = dscr("s_qt1", [96, 4, T], BF16)
    S_kt1 = dscr("s_kt1", [64, 4, 2064], BF16)
    S_v1 = dscr("s_v1", [17, 128, 256], BF16)

    with ExitStack() as st:
        fw = FW(nc, st)
        B = fw.buf
        uT = fw.sb("uT", [128, 8, T], BF16)
        b_uT = [B("uT%d" % i) for i in range(16)]
        R = fw.sb("R", [128, 8192 + 8256 + 17 * VST], BF16)
        QT = R[:, 0:8192].rearrange("p (c t) -> p c t", c=4)
        KT = R[:, 8192:8192 + 8256].rearrange("p (c t) -> p c t", c=4)
        VR = R[:, 16448:16448 + 17 * VST].rearrange("p (k e) -> p k e", k=17)
        b_QT = [B("QT%d" % i) for i in range(16)]
        b_KT = [B("KT%d" % i) for i in range(17)]
        b_V = [B("V%d" % i) for i in range(17)]
        oall = fw.sb("oall", [128, 16, DM], BF16)
        b_o = [B("o%d" % i) for i in range(16)]
        wblk = [fw.sb("wblk%d" % i, [128, 8, 512], BF16) for i in range(2)]
        b_wblk = [B("wblk%d" % i) for i in range(2)]
        wout = fw.sb("wout", [128, 8, DM], BF16)
        b_wout = B("wout")
        xt = [fw.sb("xt%d" % i, [128, DM], F32) for i in range(2)]
        b_xt = [B("xt%d" % i) for i in range(2)]
        b_xout = [B("xout%d" % i) for i in range(2)]
        xn = fw.sb("xn", [128, DM], BF16)
        b_xn = B("xn")
        junk = fw.sb("junk", [128, DM], BF16)
        b_junk = B("junk")
        stg = [fw.sb("stg%d" % i, [128, 512], F32) for i in range(3)]
        b_stg = [B("stg%d" % i) for i in range(3)]
        tb16 = [fw.sb("tb16_%d" % i, [128, 512], BF16) for i in range(3)]
        b_tb16 = [B("tb16_%d" % i) for i in range(3)]
        PT = [fw.sb("PT%d" % i, [128, 512], BF16) for i in range(3)]
        b_PT = [B("PT%d" % i) for i in range(3)]
        Ef = [fw.sb("Ef%d" % i, [128, 512], F32) for i in range(2)]
        b_Ef = [B("Ef%d" % i) for i in range(2)]
        SPb = [fw.sb("SPb%d" % i, [128, 512], BF16) for i in range(2)]
        b_SPb = [B("SPb%d" % i) for i in range(2)]
        ogT = [fw.sb("ogT%d" % i, [128, 8, 128], BF16) for i in range(2)]
        b_ogT = [B("ogT%d" % i) for i in range(2)]
        ytmp = fw.sb("ytmp", [128, DM], F32)
        b_ytmp = B("ytmp")
        sm = fw.sb("sm", [128, 64], F32)
        b_sm = {}

        def smb(k):
            if k not in b_sm:
                b_sm[k] = B("sm%d" % k)
            return b_sm[k]

        osb = fw.sb("osb", [128, 4, 128], F32)
        b_osb = [B("osb%d" % i) for i in range(4)]
        ident = fw.sb("ident", [128, 128], BF16); b_ident = B("ident")
        negq = fw.sb("negq", [128, 128], BF16); b_negq = B("negq")
        negq4 = fw.sb("negq4", [128, 128], BF16); b_negq4 = B("negq4")
        negd = fw.sb("negd", [128, 128], BF16); b_negd = B("negd")
        m01d = fw.sb("m01d", [128, 128], BF16); b_m01d = B("m01d")
        nut = fw.sb("nut", [128, 128], BF16); b_nut = B("nut")
        onec = fw.sb("onec", [128, 2], BF16); b_onec = B("onec")
        ropeA = fw.sb("ropeA", [128, 2, 17, 8], F32); b_ropeA = B("ropeA")
        ropeC = fw.sb("ropeC", [128, 2, 17, 16], F32); b_ropeC = B("ropeC")
        gcol = fw.sb("gcol", [128, 24], F32); b_gcol = B("gcol")
        gpost = fw.sb("gpost", [128, DM], F32); b_gpost = B("gpost")
        gsub = fw.sb("gsub", [128, 128], F32); b_gsub = B("gsub")
        gckv = fw.sb("gckv", [128, 256], F32); b_gckv = B("gckv")
        lamt = ytmp[:, 520:776].rearrange("p (a b) -> p a b", a=4); b_lamt = b_ytmp
        lamv = fw.sb("lamv", [128, 8], F32); b_lamv = B("lamv")
        LR = fw.sb("LR", [128, 4608], BF16)
        Tb = LR[:, 0:2048].rearrange("p (a b) -> p a b", a=16); b_Tb = B("Tb")
        wuk = LR[:, 0:1024].rearrange("p (a b) -> p a b", a=2); b_wuk = B("wuk")
        wuv = LR[:, 1024:2048].rearrange("p (a b) -> p a b", a=2); b_wuv = B("wuv")
        latT = LR[:, 2048:2304].rearrange("p (a b) -> p a b", a=2); b_latT = B("latT")
        wqb = LR[:, 2304:4608].rearrange("p (a b) -> p a b", a=6); b_wqb = B("wqb")
        latT2 = fw.sb("latT2", [128, 2, 128], BF16); b_latT2 = B("latT2")
        bank = [fw.ps("bank%d" % i, [128, 512], F32) for i in range(8)]
        b_bank = [B("bank%d" % i) for i in range(8)]
        tbk = [bank[6][:, :].bitcast(BF16), bank[7][:, :].bitcast(BF16)]
        b_tbk = [b_bank[6], b_bank[7]]
        for bb in b_bank:
            bb.excl = True
        pexp = fw.sb("pexp", [128, 1], F32); b_pexp = B("pexp")
        fw.op("pool", lambda q: q.memset(pexp[:], -0.5), writes=[b_pexp])
        b_dram = {}

        def dbuf(k):
            if k not in b_dram:
                b_dram[k] = B("dram_" + str(k))
            return b_dram[k]

        rr_ctr = {"stg": 0, "tb16": 0, "tbk": 0, "xt": 0, "wblk": 0, "pbank": 0, "cp": 0}

        def nxt(k, n):
            v = rr_ctr[k] % n
            rr_ctr[k] += 1
            return v

        def load_const_bf16(dst, b_dst, src):
            i = nxt("stg", 3)
            fw.dma("sp", stg[i][:, 0:128], src, writes=[b_stg[i]])
            fw.op("dve", lambda q: q.tensor_copy(out=dst[:], in_=stg[i][:, 0:128]), reads=[b_stg[i]], writes=[b_dst])

        load_const_bf16(ident, b_ident, I["c_ident"])
        load_const_bf16(negq, b_negq, I["c_negq"])
        load_const_bf16(negq4, b_negq4, I["c_negq4"])
        load_const_bf16(negd, b_negd, I["c_negd"])
        load_const_bf16(m01d, b_m01d, I["c_m01d"])
        load_const_bf16(nut, b_nut, I["c_nut"])
        fw.op("pool", lambda q: q.memset(onec[:], 1.0), writes=[b_onec])
        fw.dma("sp", ropeA[:].rearrange("p a b c -> p (a b c)"), I["c_ropeA"], writes=[b_ropeA])
        fw.dma("sp", ropeC[:].rearrange("p a b c -> p (a b c)"), I["c_ropeC"], writes=[b_ropeC])
        with nc.allow_non_contiguous_dma(reason="tiny gain vectors"):
            for (gsrc, o0, n) in ((I["g_pre0"], 0, 8), (I["g_pre1"], 8, 8), (I["g_cq"], 16, 6)):
                for c in range(n):
                    fw.dma("sp", gcol[:, o0 + c:o0 + c + 1], bass.AP(gsrc.tensor, c * 128, [[1, 128], [1, 1]]), writes=[b_gcol])

        def bcast_ap(src, n):
            return bass.AP(src.tensor, 0, [[0, 128], [1, n]])

        fw.dma("sp", gsub[:], bcast_ap(I["g_sub_a"], 128), writes=[b_gsub])
        fw.op("dve", lambda q: q.tensor_scalar(out=gsub[:], in0=gsub[:], scalar1=1.0 - LAM_INIT0, scalar2=None, op0=ALU.mult),
              reads=[b_gsub], writes=[b_gsub])
        fw.dma("sp", gckv[:], bcast_ap(I["g_ckv"], 256), writes=[b_gckv])
        for i, nm in enumerate(["lam_q1", "lam_k1", "lam_q2", "lam_k2"]):
            fw.dma("sp", lamt[:, i, :], bcast_ap(I[nm], 64), writes=[b_lamt])
        fw.op("dve", lambda q: q.tensor_tensor(out=lamt[:, 0, :], in0=lamt[:, 0, :], in1=lamt[:, 1, :], op=ALU.mult), reads=[b_lamt], writes=[b_lamt])
        fw.op("dve", lambda q: q.tensor_tensor(out=lamt[:, 2, :], in0=lamt[:, 2, :], in1=lamt[:, 3, :], op=ALU.mult), reads=[b_lamt], writes=[b_lamt])
        fw.op("dve", lambda q: q.reduce_sum(out=lamv[:, 0:1], in_=lamt[:, 0, :], axis=AX.X), reads=[b_lamt], writes=[b_lamv])
        fw.op("dve", lambda q: q.reduce_sum(out=lamv[:, 1:2], in_=lamt[:, 2, :], axis=AX.X), reads=[b_lamt], writes=[b_lamv])
        fw.op("act", lambda q: q.activation(out=lamv[:, 0:2], in_=lamv[:, 0:2], func=AF.Exp), reads=[b_lamv], writes=[b_lamv])
        fw.op("dve", lambda q: q.tensor_tensor(out=lamv[:, 2:3], in0=lamv[:, 1:2], in1=lamv[:, 0:1], op=ALU.subtract), reads=[b_lamv], writes=[b_lamv])
        fw.op("dve", lambda q: q.tensor_scalar(out=lamv[:, 3:4], in0=lamv[:, 2:3], scalar1=-LAM_INIT0, scalar2=None, op0=ALU.add), reads=[b_lamv], writes=[b_lamv])
        nlam = lamv[:, 3:4]
        cb = lamv[:, 4:8]
        cbt = fw.sb("cbt", [128, 8], F32); b_cbt = B("cbt")
        with nc.allow_non_contiguous_dma(reason="tiny"):
            for h in range(8):
                fw.dma("sp", cbt[:, h:h + 1], bass.AP(I["rel_bias_b"].tensor, 256 + 257 * h, [[0, 128], [1, 1]]), writes=[b_cbt])
        jn = nxt("stg", 3)
        fw.dma("sp", stg[jn][:, 0:128], I["c_negq"], writes=[b_stg[jn]])
        for h in range(8):
            for d in range(2):
                i = nxt("stg", 3)
                if i == jn:
                    i = nxt("stg", 3)
                fw.dma("sp", stg[i][:, 0:128], I["c_rbT"][h, d], writes=[b_stg[i]])
                fw.op("dve", lambda q: q.tensor_scalar(out=stg[i][:, 128:256], in0=stg[i][:, 0:128], scalar1=cbt[:, h:h + 1], scalar2=None, op0=ALU.subtract),
                      reads=[b_stg[i], b_cbt], writes=[b_stg[i]])
                if d == 0:
                    fw.op("dve", lambda q: q.tensor_tensor(out=Tb[:, h * 2 + d, :], in0=stg[i][:, 128:256], in1=stg[jn][:, 0:128], op=ALU.add),
                          reads=[b_stg[i], b_stg[jn]], writes=[b_Tb])
                else:
                    fw.op("dve", lambda q: q.tensor_copy(out=Tb[:, h * 2 + d, :], in_=stg[i][:, 128:256]), reads=[b_stg[i]], writes=[b_Tb])

        def set_ones(H, dv):
            v = VR[:, :, 0:H * (dv + 1)].rearrange("p k (h e) -> p k h e", h=H)[:, :, :, dv:dv + 1]
            fw.op("pool", lambda q: q.memset(v, 1.0), writes=b_V)

        def load_wblock(scr, K, c0, ncol):
            i = nxt("wblk", 2)
            kc = K // 128
            wait_prep(scr.name)
            fw.dma("sp", wblk[i][:, 0:kc, 0:ncol], scr.rearrange("(c p) n -> p c n", p=128)[:, :, c0:c0 + ncol],
                   reads=[dbuf(scr.name)], writes=[b_wblk[i]])
            return wblk[i], b_wblk[i]

        def rsqrt_col(col_ap, nr, bs):
            fw.op("pool", lambda q: q.tensor_scalar(out=col_ap, in0=col_ap, scalar1=EPS, scalar2=0.0, op0=ALU.add, op1=ALU.add), reads=[bs], writes=[bs])
            fw.op("pool", lambda q: q.tensor_tensor(out=col_ap, in0=col_ap, in1=pexp[0:nr, :], op=ALU.pow), reads=[bs, b_pexp], writes=[bs])

        def rms_scale(src_ap, nr, n, b_src, key, engines="act"):
            bs = smb(key)
            aps = src_ap if isinstance(src_ap, (list, tuple)) else [src_ap]
            for k, a in enumerate(aps):
                w = a.shape[-1]
                fw.op("act", lambda q: q.activation(out=junk[0:nr, 0:w], in_=a, func=AF.Square, scale=float(n) ** -0.5, accum_out=sm[0:nr, key + k:key + k + 1]),
                      reads=b_src, writes=[b_junk, bs])
            if len(aps) == 2:
                fw.op("pool", lambda q: q.tensor_tensor(out=sm[0:nr, key:key + 1], in0=sm[0:nr, key:key + 1], in1=sm[0:nr, key + 1:key + 2], op=ALU.add),
                      reads=[bs], writes=[bs])
            rsqrt_col(sm[0:nr, key:key + 1], nr, bs)
            return sm[0:nr, key:key + 1], bs

        def transposes(src_aps, nr, widths):
            i = nxt("tbk", 2)
            offs = []
            fns = []
            o = 0
            for a, w in zip(src_aps, widths):
                offs.append(o)
                fns.append(lambda q, a=a, w=w, o=o: q.transpose(out=tbk[i][0:w, o:o + nr], in_=a, identity=ident[0:nr, 0:nr]))
                o += 128
            return i, fns, offs

        def copy_eng():
            return ("dve", "act")[nxt("cp", 2)]

        rr_ctr["ce"] = 0

        def cast_eng():
            return ("dve", "pool", "act")[nxt("ce", 3)]

        def ecopy(eng, out, in_, reads, writes, scale=None):
            if eng == "act":
                if scale is None:
                    fw.op("act", lambda q: q.activation(out=out, in_=in_, func=AF.Copy), reads=reads, writes=writes)
                else:
                    fw.op("act", lambda q: q.activation(out=out, in_=in_, func=AF.Copy, scale=scale), reads=reads, writes=writes)
            else:
                if scale is None:
                    fw.op(eng, lambda q: q.tensor_copy(out=out, in_=in_), reads=reads, writes=writes)
                else:
                    fw.op(eng, lambda q: q.tensor_scalar(out=out, in0=in_, scalar1=scale, scalar2=0.0, op0=ALU.mult, op1=ALU.add), reads=reads, writes=writes)

        rr_ctr["pin"] = 0
        rr_ctr["pout"] = 0
        pin_bufs = [(xt[0], b_xt[0]), (xt[1], b_xt[1]), (ytmp, b_ytmp)]

        def prep_weight(src, dst, K, N, gofs, c_lo=0):
            for kc in range(K // 128):
                for c0 in range(c_lo, N, 1024):
                    ncol = min(1024, N - c0)
                    it, ib = pin_bufs[nxt("pin", 3)]
                    so = nxt("pout", 16)
                    ot = oall[:, so, :]
                    fw.dma("sp", it[:, 0:ncol], src[kc * 128:(kc + 1) * 128, c0:c0 + ncol], writes=[ib], sembuf=ib)
                    eng = ("dve", "act")[(kc + c0 // 1024) % 2]
                    if gofs is None:
                        ecopy(eng, ot[:, 0:ncol], it[:, 0:ncol], [ib], [b_o[so]])
                    else:
                        ecopy(eng, ot[:, 0:ncol], it[:, 0:ncol], [ib, b_gcol], [b_o[so]], scale=gcol[:, gofs + kc:gofs + kc + 1])
                    prep_evs.setdefault(dst.name, []).append(
                        fw.dma("pool", dst[kc * 128:(kc + 1) * 128, c0:c0 + ncol], ot[:, 0:ncol], reads=[b_o[so]], writes=[dbuf(dst.name)], sembuf=b_o[so]))
                    yield

        prep_evs = {}
        prep_late = []
        late_gens = []
        if 0 in layers and DEV_STOP >= 1:
            for _ in prep_weight(I["w_in0"], S_w0, DM, 1536, 0):
                pass
            late_gens += [prep_weight(I["w_in0"], S_w0, DM, W0N, 0, c_lo=1536), prep_weight(I["w_out0"], S_wo0, DM, DM, None)]
        if 1 in layers and DEV_STOP >= 1:
            late_gens += [prep_weight(I["w_in1"], S_w1, DM, W1N, 8), prep_weight(I["w_uq"], S_wuq, 768, 768, 16),
                          prep_weight(I["w_out1"], S_wo1, DM, DM, None)]
        prep_late = itertools.chain(*late_gens)
        if 0 not in layers or NP == 0:
            for _ in prep_late:
                pass
            prep_late = []
        prep_state = {"it": iter(prep_late), "active": False}

        def prep_step(n=2):
            if prep_state["active"]:
                for _ in range(n):
                    if next(prep_state["it"], "done") == "done":
                        prep_state["active"] = False
                        break

        def prep_flush():
            for _ in prep_state["it"]:
                pass
            prep_state["active"] = False

        def wait_prep(name):
            fw.wait_events("sp", prep_evs.get(name, []))

        def stage_U(seq, src, src_bufs=()):
            sel = {}

            def pa(ti):
                r0, nr, rt = seq.tiles[ti]
                i = nxt("xt", 2)
                fw.dma("sp", xt[i][0:nr, :], src[r0:r0 + nr, :], reads=list(src_bufs), writes=[b_xt[i]], sembuf=b_xt[i])
                rs, bs = rms_scale(xt[i][0:nr, :], nr, DM, [b_xt[i]], 2 * (ti % 2))
                sel[ti] = (i, rs, bs)

            def pb(ti):
                r0, nr, rt = seq.tiles[ti]
                i, rs, bs = sel[ti]
                fw.op("dve", lambda q: q.tensor_scalar(out=xn[0:nr, :], in0=xt[i][0:nr, :], scalar1=rs, scalar2=None, op0=ALU.mult), reads=[b_xt[i], bs], writes=[b_xn])
                k = nxt("tbk", 2)
                fw.op("pe", [lambda q, c=c: q.transpose(out=tbk[k][:, c * 128:c * 128 + nr], in_=xn[0:nr, c * 128:(c + 1) * 128], identity=ident[0:nr, 0:nr])
                             for c in range(8)], reads=[b_xn, b_ident], writes=[b_tbk[k]])
                ecopy("act", uT[:, 0:4, r0:r0 + nr], tbk[k][:, 0:512].rearrange("p (c t) -> p c t", c=4)[:, :, 0:nr], [b_tbk[k]], [b_uT[ti]])
                ecopy("dve", uT[:, 4:8, r0:r0 + nr], tbk[k][:, 512:1024].rearrange("p (c t) -> p c t", c=4)[:, :, 0:nr], [b_tbk[k]], [b_uT[ti]])

            n = len(seq.tiles)
            pa(0)
            for ti in range(n):
                if ti + 1 < n:
                    pa(ti + 1)
                pb(ti)

        def proj_block(seq, scr, c0, ncol, consume, K=DM, lhs=None):
            wt, bw = load_wblock(scr, K, c0, ncol)
            kcn = K // 128
            later = []
            later2 = []
            for ti, (r0, nr, rt) in enumerate(seq.tiles):
                pb = 4 + nxt("pbank", 2)
                if lhs is None:
                    lf = lambda c: uT[:, c, r0:r0 + nr]
                    lb = [b_uT[ti]]
                else:
                    lf, lb = lhs(ti)
                fw.op("pe", [lambda q, c=c: q.matmul(bank[pb][0:nr, 0:ncol], lhsT=lf(c), rhs=wt[:, c, 0:ncol], start=(c == 0), stop=(c == kcn - 1))
                             for c in range(kcn)], reads=lb + [bw], writes=[b_bank[pb]])
                for f in later2:
                    f()
                later2 = later
                later = consume(ti, r0, nr, rt, bank[pb], b_bank[pb]) or []
                prep_step()
            for f in later2 + later:
                f()

        def store_out(dst, src_ap, b_src):
            fw.dma("pool", dst, src_ap, reads=[b_src], is_output=True, sembuf=b_src)

        def ingest_k(src_ap, b_src, nr, kcol0, ktile, scale=None, ncol=512, part=128, rows=None, eng="dve"):
            j = nxt("tb16", 3)
            ecopy(eng, tb16[j][0:nr, 0:ncol], src_ap, [b_src], [b_tb16[j]], scale)

            def later():
                nchunk = ncol // part
                k = nxt("tbk", 2)
                fw.op("pe", [lambda q, c=c: q.transpose(out=tbk[k][0:part, c * 128:c * 128 + nr], in_=tb16[j][0:nr, c * part:(c + 1) * part],
                                                         identity=ident[0:nr, 0:nr]) for c in range(nchunk)],
                      reads=[b_tb16[j], b_ident], writes=[b_tbk[k]])
                p0, p1 = rows if rows is not None else (0, part)
                ecopy(copy_eng(), KT[p0:p1, 0:nchunk, kcol0:kcol0 + nr] if rows is None else KT[p0:p1, 0:nchunk, kcol0:kcol0 + nr],
                      tbk[k][0:part, 0:nchunk * 128].rearrange("p (c t) -> p c t", c=nchunk)[:, :, 0:nr], [b_tbk[k]], [b_KT[ktile]])
            return later

        def ingest_q(src_ap, b_src, nr, qcol0, qtile, scale, ncol=512, part=128, eng="dve"):
            j = nxt("tb16", 3)
            ecopy(eng, tb16[j][0:nr, 0:ncol], src_ap, [b_src], [b_tb16[j]], scale)

            def later():
                nchunk = ncol // part
                k = nxt("tbk", 2)
                fw.op("pe", [lambda q, c=c: q.transpose(out=tbk[k][0:part, c * 128:c * 128 + nr], in_=tb16[j][0:nr, c * part:(c + 1) * part],
                                                         identity=ident[0:nr, 0:nr]) for c in range(nchunk)],
                      reads=[b_tb16[j], b_ident], writes=[b_tbk[k]])
                ecopy(copy_eng(), QT[0:part, 0:nchunk, qcol0:qcol0 + nr],
                      tbk[k][0:part, 0:nchunk * 128].rearrange("p (c t) -> p c t", c=nchunk)[:, :, 0:nr], [b_tbk[k]], [b_QT[qtile]])
            return later

        def ingest_v(src_ap, b_src, nr, vtile, H, dv, eng="dve", hofs=0, Hsrc=None):
            Hs = Hsrc or H
            src3 = src_ap.rearrange("p (h d) -> p h d", h=Hs)[:, hofs:hofs + H, :]
            dst3 = VR[0:nr, vtile, 0:H * (dv + 1)].rearrange("p (h e) -> p h e", h=H)[:, :, 0:dv] if dv != 64 or True else None
            ecopy(eng, dst3, src3, [b_src], [b_V[vtile]])

        def rope_ops(dst, src3, nr, rt, tab, half, nblk, b_src, b_dst, blk_stride_dst3):
            cosb = tab[0:nr, 0, rt, :].unsqueeze(1).to_broadcast([nr, nblk, half])
            sinb = tab[0:nr, 1, rt, :].unsqueeze(1).to_broadcast([nr, nblk, half])
            x1 = src3[:, :, 0:half]
            x2 = src3[:, :, half:2 * half]
            d3 = blk_stride_dst3
            t = osb[0:nr, :, :].rearrange("p a b -> p (a b)")
            n = nblk * half
            tt = [t[:, k * n:(k + 1) * n].rearrange("p (b d) -> p b d", b=nblk) for k in range(4)]
            btab = b_ropeA if tab is ropeA else b_ropeC
            fw.op("dve", lambda q: q.tensor_tensor(out=tt[0], in0=x1, in1=cosb, op=ALU.mult), reads=[b_src, btab], writes=b_osb)
            fw.op("dve", lambda q: q.tensor_tensor(out=tt[1], in0=x2, in1=sinb, op=ALU.mult), reads=[b_src, btab], writes=b_osb)
            fw.op("dve", lambda q: q.tensor_tensor(out=tt[2], in0=x2, in1=cosb, op=ALU.mult), reads=[b_src, btab], writes=b_osb)
            fw.op("dve", lambda q: q.tensor_tensor(out=tt[3], in0=x1, in1=sinb, op=ALU.mult), reads=[b_src, btab], writes=b_osb)
            fw.op("dve", lambda q: q.tensor_tensor(out=d3[:, :, 0:half], in0=tt[0], in1=tt[1], op=ALU.subtract), reads=b_osb, writes=[b_dst])
            fw.op("dve", lambda q: q.tensor_tensor(out=d3[:, :, half:2 * half], in0=tt[2], in1=tt[3], op=ALU.add), reads=b_osb, writes=[b_dst])

        def run_units(units, sbanks=((0,), (1,))):
            pend = None
            for ui, u in enumerate(units):
                sb_ = sbanks[ui % 2]
                nk = u["nk"]
                used = sorted(set(x[0] for x in u["qk"]))
                for bsel in used:
                    bk_ = sb_[bsel]
                    fns = []
                    for (bs_, c0, n, lhsT, rhs) in u["qk"]:
                        if bs_ == bsel:
                            fns.append(lambda q, c0=c0, n=n, lhsT=lhsT, rhs=rhs: q.matmul(bank[bk_][0:nk, c0:c0 + n], lhsT=lhsT, rhs=rhs, start=True, stop=False,
                                                                                          skip_group_check=True))
                    for (bs_, c0, n, rhs_t) in u["extra"]:
                        if bs_ == bsel:
                            fns.append(lambda q, c0=c0, n=n, rhs_t=rhs_t: q.matmul(bank[bk_][0:nk, c0:c0 + n], lhsT=ident[0:nk, 0:nk], rhs=rhs_t, start=False, stop=True,
                                                                                   skip_group_check=True))
                    fw.op("pe", fns, reads=u["rd"] + [b_ident, b_negq], writes=[b_bank[bk_]])
                pi = ui % 3
                for (bsel, c0, n, ptc0) in u["exps"]:
                    bk_ = sb_[bsel]
                    fw.op("act", lambda q, c0=c0, n=n, ptc0=ptc0: q.activation(out=PT[pi][0:nk, ptc0:ptc0 + n], in_=bank[bk_][0:nk, c0:c0 + n], func=AF.Exp),
                          reads=[b_bank[bk_]], writes=[b_PT[pi]])
                if pend is not None:
                    pend()

                def pvs(u=u, pi=pi, nk=nk):
                    obufs = []
                    fns = []
                    for (O_ap, pc0, nq, V_ap, start, bO) in u["pv"]:
                        fns.append(lambda q, O_ap=O_ap, pc0=pc0, nq=nq, V_ap=V_ap, start=start: q.matmul(
                            O_ap, lhsT=PT[pi][0:nk, pc0:pc0 + nq], rhs=V_ap, start=start, stop=True, skip_group_check=True))
                        if bO not in obufs:
                            obufs.append(bO)
                    if DEV_STOP >= 3.2:
                        fw.op("pe", fns, reads=[b_PT[pi]] + u["rdv"], writes=obufs)
                    if u.get("fin"):
                        u["fin"]()
                pend = pvs
            if pend is not None:
                pend()

        def stage_PA(seq, b):
            pre = "p" if seq.kind == "p" else "s"
            if DEV_STOP >= 2.1:
                set_ones(4, 128)
            if seq.kind == "s" and DEV_STOP >= 2.2:
                for (vt, kc0, nk, is_cache) in seq.keys:
                    if not is_cache:
                        continue
                    i = nxt("stg", 3)
                    fw.dma("sp", stg[i][:, :], I["cache_a_k"][b, kc0:kc0 + 128, :], writes=[b_stg[i]])
                    ingest_k(stg[i][:, :], b_stg[i], 128, kc0, vt, eng=cast_eng())()
                    i = nxt("stg", 3)
                    fw.dma("sp", stg[i][:, :], I["cache_a_v"][b, kc0:kc0 + 128, :], writes=[b_stg[i]])
                    ingest_v(stg[i][:, :], b_stg[i], 128, vt, 4, 128, eng=cast_eng())
            knew0 = 0 if seq.kind == "p" else 2048
            vnew0 = 0 if seq.kind == "p" else 16

            def cons_q(ti, r0, nr, rt, bk, bbk):
                if DEV_STOP < 2.31:
                    return []
                i = nxt("stg", 3)
                ecopy("act", stg[i][0:nr, :], bk[0:nr, :], [bbk], [b_stg[i]])
                if DEV_STOP < 2.32:
                    return []
                rope_ops(None, stg[i][0:nr, :].rearrange("p (b d) -> p b d", b=8), nr, rt, ropeA, 8, 8, b_stg[i], b_stg[i],
                         stg[i][0:nr, :].rearrange("p (b d) -> p b d", b=8))
                if DEV_STOP < 2.33:
                    return []
                return [ingest_q(stg[i][0:nr, :], b_stg[i], nr, r0, ti, 0.125, eng="dve")]

            def cons_k(ti, r0, nr, rt, bk, bbk):
                i = nxt("stg", 3)
                ecopy("act", stg[i][0:nr, :], bk[0:nr, :], [bbk], [b_stg[i]])
                rope_ops(None, stg[i][0:nr, :].rearrange("p (b d) -> p b d", b=8), nr, rt, ropeA, 8, 8, b_stg[i], b_stg[i],
                         stg[i][0:nr, :].rearrange("p (b d) -> p b d", b=8))
                store_out(O["a_k_" + pre][b, r0:r0 + nr, :], stg[i][0:nr, :], b_stg[i])
                return [ingest_k(stg[i][0:nr, :], b_stg[i], nr, knew0 + r0, vnew0 + ti, eng="dve")]

            def cons_v(ti, r0, nr, rt, bk, bbk):
                i = nxt("stg", 3)
                ecopy("act", stg[i][0:nr, :], bk[0:nr, :], [bbk], [b_stg[i]])
                store_out(O["a_v_" + pre][b, r0:r0 + nr, :], stg[i][0:nr, :], b_stg[i])
                ingest_v(stg[i][0:nr, :], b_stg[i], nr, vnew0 + ti, 4, 128, eng="dve")
                return []

            if DEV_STOP >= 2.3:
                proj_block(seq, S_w0, 0, 512, cons_q)
            if DEV_STOP >= 2.4:
                proj_block(seq, S_w0, 512, 512, cons_k)
            if DEV_STOP >= 2.5:
                proj_block(seq, S_w0, 1024, 512, cons_v)

        def fin_A(seq, h, qts, qb0, ob):
            for j in qts:
                jj = j - qb0
                nq = seq.tiles[j][1]
                o0 = bank[ob[0]][0:nq, jj * 129:jj * 129 + 128]
                o1 = bank[ob[1]][0:nq, jj * 129:jj * 129 + 128]
                kb = 8 + 4 * (jj % 2)
                bs = smb(kb)
                fw.op("dve", lambda q: q.reciprocal(out=sm[0:nq, kb:kb + 1], in_=bank[ob[0]][0:nq, jj * 129 + 128:jj * 129 + 129]), reads=[b_bank[ob[0]]], writes=[bs])
                fw.op("dve", lambda q: q.reciprocal(out=sm[0:nq, kb + 1:kb + 2], in_=bank[ob[1]][0:nq, jj * 129 + 128:jj * 129 + 129]), reads=[b_bank[ob[1]]], writes=[bs])
                fw.op("dve", lambda q: q.tensor_tensor(out=sm[0:nq, kb + 1:kb + 2], in0=sm[0:nq, kb + 1:kb + 2], in1=nlam[0:nq, :], op=ALU.mult), reads=[bs, b_lamv], writes=[bs])
                ot = osb[0:nq, 1 + (jj % 2), :]
                bo = b_osb[1 + (jj % 2)]
                fw.op("dve", lambda q: q.tensor_scalar(out=ot, in0=o0, scalar1=sm[0:nq, kb:kb + 1], scalar2=None, op0=ALU.mult), reads=[b_bank[ob[0]], bs], writes=[bo])
                fw.op("dve", lambda q: q.scalar_tensor_tensor(out=ot, in0=o1, scalar=sm[0:nq, kb + 1:kb + 2], in1=ot, op0=ALU.mult, op1=ALU.add),
                      reads=[b_bank[ob[1]], bs, bo], writes=[bo])
                fw.op("dve", lambda q: q.scalar_tensor_tensor(out=osb[0:nq, 3, :], in0=ot, scalar=1.0 / 128, in1=ot, op0=ALU.mult, op1=ALU.mult,
                                                              accum_out=sm[0:nq, kb + 2:kb + 3]), reads=[bo], writes=[b_osb[3], bs])
                rsqrt_col(sm[0:nq, kb + 2:kb + 3], nq, bs)
                fw.op("dve", lambda q: q.scalar_tensor_tensor(out=oall[0:nq, j, h * 128:(h + 1) * 128], in0=ot, scalar=sm[0:nq, kb + 2:kb + 3], in1=gsub[0:nq, :],
                                                              op0=ALU.mult, op1=ALU.mult), reads=[bo, bs, b_gsub], writes=[b_o[j]])

        def stage_QA_sample(seq):
            nq = seq.tiles[0][1]
            units = []
            for h in range(4):
                ob = (4, 5) if h % 2 == 0 else (6, 7)
                started = [False, False]
                cache = [k for k in seq.keys if k[3]]
                new = [k for k in seq.keys if not k[3]]
                for grp_keys in (cache, new):
                    nk = grp_keys[0][2]
                    u = dict(nk=nk, qk=[], extra=[], exps=[], pv=[], rd=[b_QT[0]] + [b_KT[k[0]] for k in grp_keys], rdv=[b_V[k[0]] for k in grp_keys])
                    for si_, (vt, kc0, nk_, is_cache) in enumerate(grp_keys):
                        for m in range(2):
                            u["qk"].append((m, si_ * nq, nq, KT[m * 64:(m + 1) * 64, h, kc0:kc0 + nk], QT[m * 64:(m + 1) * 64, h, 0:nq]))
                            u["pv"].append((bank[ob[m]][0:nq, 0:129], m * 256 + si_ * nq, nq, VR[0:nk, vt, h * 129:(h + 1) * 129], not started[m], b_bank[ob[m]]))
                            started[m] = True
                    for m in range(2):
                        u["exps"].append((m, 0, len(grp_keys) * nq, m * 256))
                    units.append(u)
                units[-1]["fin"] = (lambda h=h, ob=ob: fin_A(seq, h, [0], 0, ob))
            run_units(units, sbanks=((0, 1), (2, 3)))

        def stage_QA(seq):
            if seq.kind == "s":
                return stage_QA_sample(seq)
            nt = len(seq.tiles)
            bq = 2 if seq.kind == "p" else 1
            blk = 0
            for h in range(4):
                for qb0 in range(0, nt, bq):
                    qts = list(range(qb0, min(nt, qb0 + bq)))
                    nqs = [seq.tiles[j][1] for j in qts]
                    bqn = sum(nqs)
                    qcol0 = seq.tiles[qb0][0]
                    ob = (4, 5) if blk % 2 == 0 else (6, 7)
                    blk += 1
                    units = []
                    started = [False, False]
                    for ki, (vt, kc0, nk, is_cache) in enumerate(seq.keys):
                        if seq.kind == "p":
                            valid = [j for j in qts if ki <= j]
                        else:
                            valid = qts
                        if not valid:
                            continue
                        j0 = valid[0] - qb0
                        off0 = j0 * 128
                        nv = sum(seq.tiles[j][1] for j in valid)
                        u = dict(nk=nk, qk=[], extra=[], exps=[], pv=[], rd=[b_KT[vt]] + [b_QT[j] for j in valid], rdv=[b_V[vt]])
                        for m in range(2):
                            u["qk"].append((m, off0, nv, KT[m * 64:(m + 1) * 64, h, kc0:kc0 + nk], QT[m * 64:(m + 1) * 64, h, qcol0 + off0:qcol0 + off0 + nv]))
                            for j in valid:
                                if seq.kind == "p" and ki == j:
                                    u["extra"].append((m, (j - qb0) * 128, seq.tiles[j][1], negq[0:nk, 0:seq.tiles[j][1]]))
                            u["exps"].append((m, off0, nv, m * bqn + off0))
                        for m in range(2):
                            for j in valid:
                                jj = j - qb0
                                nq = seq.tiles[j][1]
                                u["pv"].append((bank[ob[m]][0:nq, jj * 129:(jj + 1) * 129], m * bqn + jj * 128, nq,
                                                VR[0:nk, vt, h * 129:(h + 1) * 129], not started[m], b_bank[ob[m]]))
                                started[m] = True
                        units.append(u)

                    def fin(h=h, qts=qts, qb0=qb0, ob=ob):
                        for j in qts:
                            jj = j - qb0
                            nq = seq.tiles[j][1]
                            o0 = bank[ob[0]][0:nq, jj * 129:jj * 129 + 128]
                            o1 = bank[ob[1]][0:nq, jj * 129:jj * 129 + 128]
                            kb = 8 + 4 * (jj % 2)
                            bs = smb(kb)
                            fw.op("dve", lambda q: q.reciprocal(out=sm[0:nq, kb:kb + 1], in_=bank[ob[0]][0:nq, jj * 129 + 128:jj * 129 + 129]), reads=[b_bank[ob[0]]], writes=[bs])
                            fw.op("dve", lambda q: q.reciprocal(out=sm[0:nq, kb + 1:kb + 2], in_=bank[ob[1]][0:nq, jj * 129 + 128:jj * 129 + 129]), reads=[b_bank[ob[1]]], writes=[bs])
                            fw.op("dve", lambda q: q.tensor_tensor(out=sm[0:nq, kb + 1:kb + 2], in0=sm[0:nq, kb + 1:kb + 2], in1=nlam[0:nq, :], op=ALU.mult), reads=[bs, b_lamv], writes=[bs])
                            ot = osb[0:nq, 1 + (jj % 2), :]
                            bo = b_osb[1 + (jj % 2)]
                            fw.op("dve", lambda q: q.tensor_scalar(out=ot, in0=o0, scalar1=sm[0:nq, kb:kb + 1], scalar2=None, op0=ALU.mult), reads=[b_bank[ob[0]], bs], writes=[bo])
                            fw.op("dve", lambda q: q.scalar_tensor_tensor(out=ot, in0=o1, scalar=sm[0:nq, kb + 1:kb + 2], in1=ot, op0=ALU.mult, op1=ALU.add),
                                  reads=[b_bank[ob[1]], bs, bo], writes=[bo])
                            fw.op("dve", lambda q: q.scalar_tensor_tensor(out=osb[0:nq, 3, :], in0=ot, scalar=1.0 / 128, in1=ot, op0=ALU.mult, op1=ALU.mult,
                                                                          accum_out=sm[0:nq, kb + 2:kb + 3]), reads=[bo], writes=[b_osb[3], bs])
                            rsqrt_col(sm[0:nq, kb + 2:kb + 3], nq, bs)
                            fw.op("dve", lambda q: q.scalar_tensor_tensor(out=oall[0:nq, j, h * 128:(h + 1) * 128], in0=ot, scalar=sm[0:nq, kb + 2:kb + 3], in1=gsub[0:nq, :],
                                                                          op0=ALU.mult, op1=ALU.mult), reads=[bo, bs, b_gsub], writes=[b_o[j]])
                    if DEV_STOP >= 3.4:
                        units[-1]["fin"] = fin
                    if DEV_STOP < 3.3:
                        units = units[:1]
                    run_units(units, sbanks=((0, 1), (2, 3)))
                    if DEV_STOP < 3.35:
                        return

        def stage_PB(seq, b):
            pre = "p" if seq.kind == "p" else "s"
            set_ones(8, 64)
            if seq.kind == "s":
                for (vt, kc0, nk, is_cache) in seq.keys_b:
                    if not is_cache:
                        continue
                    i = nxt("stg", 3)
                    fw.dma("sp", stg[i][:, :], I["cache_b_k"][b, kc0:kc0 + 128, :], writes=[b_stg[i]])
                    ingest_k(stg[i][:, :], b_stg[i], 128, kc0, vt, eng=cast_eng())()
                    i = nxt("stg", 3)
                    fw.dma("sp", stg[i][:, :], I["cache_b_v"][b, kc0:kc0 + 128, :], writes=[b_stg[i]])
                    ingest_v(stg[i][:, :], b_stg[i], 128, vt, 8, 64, eng=cast_eng())
            knew0 = 0 if seq.kind == "p" else 512
            vnew0 = 0 if seq.kind == "p" else 4

            def cons_q(ti, r0, nr, rt, bk, bbk):
                return [ingest_q(bk[0:nr, :], bbk, nr, r0, ti, 0.125, eng="dve")]

            def cons_k(ti, r0, nr, rt, bk, bbk):
                if seq.kind == "s" or r0 >= T - 512:
                    i = nxt("stg", 3)
                    ecopy("act", stg[i][0:nr, :], bk[0:nr, :], [bbk], [b_stg[i]])
                    orow = r0 - (T - 512) if seq.kind == "p" else r0
                    store_out(O["b_k_" + pre][b, orow:orow + nr, :], stg[i][0:nr, :], b_stg[i])
                return [ingest_k(bk[0:nr, :], bbk, nr, knew0 + r0, vnew0 + ti, eng="dve")]

            def cons_v(ti, r0, nr, rt, bk, bbk):
                if seq.kind == "s" or r0 >= T - 512:
                    i = nxt("stg", 3)
                    ecopy("act", stg[i][0:nr, :], bk[0:nr, :], [bbk], [b_stg[i]])
                    orow = r0 - (T - 512) if seq.kind == "p" else r0
                    store_out(O["b_v_" + pre][b, orow:orow + nr, :], stg[i][0:nr, :], b_stg[i])
                ingest_v(bk[0:nr, :], bbk, nr, vnew0 + ti, 8, 64, eng="dve")
                return []

            proj_block(seq, S_w0, 1536, 512, cons_q)
            proj_block(seq, S_w0, 2048, 512, cons_k)
            proj_block(seq, S_w0, 2560, 512, cons_v)

        def stage_QB(seq):
            blk = 0
            for j, (r0, nq, rt) in enumerate(seq.tiles):
                ob = (2, 3) if j % 2 == 0 else (4, 5)
                pairs = []
                for h in (0, 2, 4, 6, 1, 3, 5, 7):
                    if seq.kind == "p":
                        for d in (4, 3, 2, 1, 0):
                            ki = j - d
                            if ki < 0:
                                continue
                            ex = None
                            if d == 0:
                                ex = Tb[:, h * 2 + 0, :]
                            elif d == 1:
                                ex = Tb[:, h * 2 + 1, :]
                            elif d == 4:
                                ex = negq4[:, :]
                            pairs.append((h, ki, ex))
                    else:
                        for ki in range(5):
                            ex = None
                            if ki == 3:
                                ex = Tb[:, h * 2 + 1, :]
                            elif ki == 4:
                                ex = Tb[:, h * 2 + 0, :]
                            pairs.append((h, ki, ex))
                units = []
                started = [False, False]
                slotw = 128 if seq.kind == "p" else nq
                for p0 in range(0, len(pairs), 4):
                    grp = pairs[p0:p0 + 4]
                    nkmax = max(seq.keys_b[ki][2] for (_, ki, _) in grp)
                    u = dict(nk=nkmax, qk=[], extra=[], exps=[], pv=[], rd=[b_QT[j]], rdv=[])
                    same = all(seq.keys_b[ki][2] == nkmax for (_, ki, _) in grp)
                    for s, (h, ki, ex) in enumerate(grp):
                        vt, kc0, nk, _c = seq.keys_b[ki]
                        u["qk"].append((s * slotw, nq, KT[(h % 2) * 64:(h % 2 + 1) * 64, h // 2, kc0:kc0 + nk], QT[(h % 2) * 64:(h % 2 + 1) * 64, h // 2, r0:r0 + nq], nk))
                        if ex is not None:
                            u["extra"].append((s * slotw, nq, ex[0:nk, 0:nq], nk))
                        u["pv"].append((bank[ob[h // 4]][0:nq, (h % 4) * 65:(h % 4 + 1) * 65], s * slotw, nq, VR[0:nk, vt, h * 65:(h + 1) * 65],
                                        not started[h // 4], b_bank[ob[h // 4]], nk))
                        started[h // 4] = True
                        u["rd"].append(b_KT[vt])
                        u["rdv"].append(b_V[vt])
                        u["exps"].append((s * slotw, nq, nk))
                    if same:
                        u["exps"] = [(0, (len(grp) - 1) * slotw + nq, nkmax)]
                    units.append(u)

                def fin(j=j, nq=nq, ob=ob):
                    for g in range(2):
                        kb = 16 + 4 * g
                        bs = smb(kb)
                        fw.op("dve", lambda q: q.reciprocal(out=sm[0:nq, kb:kb + 4], in_=bank[ob[g]][0:nq, 0:260].rearrange("p (h e) -> p h e", e=65)[:, :, 64]),
                              reads=[b_bank[ob[g]]], writes=[bs])
                        for hh in range(4):
                            h = g * 4 + hh
                            osl = oall[0:nq, j, 512 + h * 64:512 + (h + 1) * 64]
                            fw.op("dve", lambda q, osl=osl, hh=hh: q.scalar_tensor_tensor(out=osl, in0=bank[ob[g]][0:nq, hh * 65:hh * 65 + 64], scalar=sm[0:nq, kb + hh:kb + hh + 1],
                                                                                          in1=osl, op0=ALU.mult, op1=ALU.mult), reads=[b_bank[ob[g]], bs, b_o[j]], writes=[b_o[j]])
                units[-1]["fin"] = fin
                run_units_nk(units)

        def run_units_nk(units):
            pend = None
            for ui, u in enumerate(units):
                sbk = ui % 2
                fns = []
                exd = {c0: (n, rhs_t, nk) for (c0, n, rhs_t, nk) in u["extra"]}
                for (c0, n, lhsT, rhs, nk) in u["qk"]:
                    fns.append(lambda q, c0=c0, n=n, lhsT=lhsT, rhs=rhs, nk=nk: q.matmul(bank[sbk][0:nk, c0:c0 + n], lhsT=lhsT, rhs=rhs, start=True, stop=False,
                                                                                         skip_group_check=True))
                    if c0 in exd:
                        n2, rhs_t, nk2 = exd[c0]
                        fns.append(lambda q, c0=c0, n2=n2, rhs_t=rhs_t, nk2=nk2: q.matmul(bank[sbk][0:nk2, c0:c0 + n2], lhsT=ident[0:nk2, 0:nk2], rhs=rhs_t, start=False, stop=True,
                                                                                          skip_group_check=True))
                fw.op("pe", fns, reads=u["rd"] + [b_ident, b_negq4, b_Tb], writes=[b_bank[sbk]])
                pi = ui % 3
                for (c0, n, nk) in u["exps"]:
                    fw.op("act", lambda q, c0=c0, n=n, nk=nk: q.activation(out=PT[pi][0:nk, c0:c0 + n], in_=bank[sbk][0:nk, c0:c0 + n], func=AF.Exp),
                          reads=[b_bank[sbk]], writes=[b_PT[pi]])
                if pend is not None:
                    pend()
                if u.get("hook"):
                    u["hook"]()

                def pvs(u=u, pi=pi):
                    obufs = []
                    fns = []
                    for (O_ap, pc0, nq, V_ap, start, bO, nk) in u["pv"]:
                        fns.append(lambda q, O_ap=O_ap, pc0=pc0, nq=nq, V_ap=V_ap, start=start, nk=nk: q.matmul(
                            O_ap, lhsT=PT[pi][0:nk, pc0:pc0 + nq], rhs=V_ap, start=start, stop=True, skip_group_check=True))
                        if bO not in obufs:
                            obufs.append(bO)
                    fw.op("pe", fns, reads=[b_PT[pi]] + u["rdv"], writes=obufs)
                    if u.get("fin"):
                        u["fin"]()
                pend = pvs
            if pend is not None:
                pend()

        def stage_gate(seq, scr_w, gate_c0, half, premul):
            def cons(ti, r0, nr, rt, bk, bbk):
                k = nxt("tb16", 3)
                fw.op("act", lambda q: q.activation(out=tb16[k][0:nr, :], in_=bk[0:nr, :], func=AF.Tanh, scale=0.5), reads=[bbk], writes=[b_tb16[k]])
                fw.op("dve", lambda q: q.scalar_tensor_tensor(out=tb16[k][0:nr, :], in0=tb16[k][0:nr, :], scalar=1.0, in1=bk[0:nr, :], op0=ALU.add, op1=ALU.mult),
                      reads=[b_tb16[k], bbk], writes=[b_tb16[k]])
                osl = oall[0:nr, ti, half * 512:(half + 1) * 512]
                if premul:
                    fw.op("dve", lambda q: q.scalar_tensor_tensor(out=osl, in0=tb16[k][0:nr, :], scalar=0.5, in1=osl, op0=ALU.mult, op1=ALU.mult),
                          reads=[b_o[ti], b_tb16[k]], writes=[b_o[ti]])
                else:
                    fw.op("dve", lambda q: q.tensor_scalar(out=osl, in0=tb16[k][0:nr, :], scalar1=0.5, scalar2=None, op0=ALU.mult), reads=[b_tb16[k]], writes=[b_o[ti]])
                return []
            proj_block(seq, scr_w, gate_c0 + half * 512, 512, cons)

        def stage_GY(seq, b, resid_src, dst, dst_is_output, src_bufs=()):
            xsel = {}

            def tphase(j):
                r0, nr, rt = seq.tiles[j]
                i = nxt("xt", 2)
                xsel[j] = i
                fw.dma("sp", xt[i][0:nr, :], resid_src[r0:r0 + nr, :], reads=list(src_bufs), writes=[b_xt[i]], sembuf=b_xt[i])
                if DEV_DBG and seq.idx == 0:
                    fw.op("dve", lambda q: q.tensor_copy(out=ytmp[0:nr, :], in_=oall[0:nr, j, :]), reads=[b_o[j]], writes=[b_ytmp])
                    fw.dma("pool", O["dbg"][r0:r0 + nr, :], ytmp[0:nr, :], reads=[b_ytmp], is_output=True, sembuf=b_ytmp)
                k = j % 2
                fw.op("pe", [lambda q, c=c: q.transpose(out=tbk[k][:, c * 128:c * 128 + nr], in_=oall[0:nr, j, c * 128:(c + 1) * 128], identity=ident[0:nr, 0:nr])
                             for c in range(8)], reads=[b_o[j], b_ident], writes=[b_tbk[k]])
                g = j % 2
                ecopy("act", ogT[g][:, 0:4, 0:nr], tbk[k][:, 0:512].rearrange("p (c t) -> p c t", c=4)[:, :, 0:nr], [b_tbk[k]], [b_ogT[g]])
                ecopy("dve", ogT[g][:, 4:8, 0:nr], tbk[k][:, 512:1024].rearrange("p (c t) -> p c t", c=4)[:, :, 0:nr], [b_tbk[k]], [b_ogT[g]])

            def yphase(j):
                r0, nr, rt = seq.tiles[j]
                i = xsel[j]
                g = j % 2
                yb = ((4, 5), (2, 3))[j % 2]
                for half in range(2):
                    fw.op("pe", [lambda q, c=c: q.matmul(bank[yb[half]][0:nr, :], lhsT=ogT[g][:, c, 0:nr], rhs=wout[:, c, half * 512:(half + 1) * 512],
                                                         start=(c == 0), stop=(c == 7)) for c in range(8)],
                          reads=[b_ogT[g], b_wout], writes=[b_bank[yb[half]]])
                rs, bs = rms_scale([bank[yb[0]][0:nr, :], bank[yb[1]][0:nr, :]], nr, DM, [b_bank[yb[0]], b_bank[yb[1]]], 24 + 2 * (j % 2))
                for half in range(2):
                    fw.op("dve", lambda q: q.scalar_tensor_tensor(out=ytmp[0:nr, half * 512:(half + 1) * 512], in0=bank[yb[half]][0:nr, :], scalar=rs,
                                                                  in1=gpost[0:nr, half * 512:(half + 1) * 512], op0=ALU.mult, op1=ALU.mult),
                          reads=[b_bank[yb[half]], bs, b_gpost], writes=[b_ytmp])
                fw.op("dve", lambda q: q.tensor_tensor(out=xt[i][0:nr, 0:512], in0=ytmp[0:nr, 0:512], in1=xt[i][0:nr, 0:512], op=ALU.add), reads=[b_ytmp, b_xt[i]], writes=[b_xt[i]])
                fw.op("pool", lambda q: q.tensor_tensor(out=xt[i][0:nr, 512:1024], in0=ytmp[0:nr, 512:1024], in1=xt[i][0:nr, 512:1024], op=ALU.add), reads=[b_ytmp, b_xt[i]], writes=[b_xt[i]])
                if dst_is_output:
                    fw.dma("pool", dst[r0:r0 + nr, :], xt[i][0:nr, :], reads=[b_xt[i]], is_output=True, sembuf=b_xout[i])
                else:
                    h1_evs.setdefault((seq.kind, b), []).append(
                        fw.dma("pool", dst[r0:r0 + nr, :], xt[i][0:nr, :], reads=[b_xt[i]], writes=[dbuf(("h1", seq.kind, b))], sembuf=b_xout[i]))

            n = len(seq.tiles)
            tphase(0)
            for j in range(n):
                if j + 1 < n:
                    tphase(j + 1)
                yphase(j)

        def stage_PC(seq, b):
            pre = "p" if seq.kind == "p" else "s"
            SC = (64 + 32) ** -0.5
            set_ones(4, 64)
            knew0 = 0 if seq.kind == "p" else 2048
            vnew0 = 0 if seq.kind == "p" else 16
            spill_ev = []
            slot_ctr = {"i": 0}

            def next_slot():
                i = slot_ctr["i"] % 13
                slot_ctr["i"] += 1
                return i

            wq1 = oall[:, 13:16, :].rearrange("p a b -> p (a b)")[:, 0:2304].rearrange("p (c n) -> p c n", c=6)
            bwq1 = [b_o[13], b_o[14], b_o[15]]
            wsrc = S_wuq.rearrange("(c p) n -> p c n", p=128)
            wait_prep(S_wuq.name)
            fw.dma("sp", wqb[:, :, :], wsrc[:, :, 0:384], reads=[dbuf(S_wuq.name)], writes=[b_wqb])
            fw.dma("sp", wq1, wsrc[:, :, 384:768], reads=[dbuf(S_wuq.name)], writes=bwq1, sembuf=b_o[13])

            lat_state = {"n": 0, "pend": None}

            def lat_ingest(src_lat, b_lat, src_kr, b_kr, nr, kcol0, vt, eng):
                lt, blt = ((latT, b_latT), (latT2, b_latT2))[lat_state["n"] % 2]
                lat_state["n"] += 1
                j = nxt("tb16", 3)
                ecopy(eng, tb16[j][0:nr, 0:256], src_lat, [b_lat], [b_tb16[j]])
                ecopy(eng, tb16[j][0:nr, 256:288], src_kr, [b_kr], [b_tb16[j]])
                k = nxt("tbk", 2)
                fw.op("pe", [lambda q, c=c: q.transpose(out=tbk[k][:, c * 128:c * 128 + nr], in_=tb16[j][0:nr, c * 128:(c + 1) * 128], identity=ident[0:nr, 0:nr]) for c in range(2)]
                      + [lambda q: q.transpose(out=tbk[k][0:32, 256:256 + nr], in_=tb16[j][0:nr, 256:288], identity=ident[0:nr, 0:nr])],
                      reads=[b_tb16[j], b_ident], writes=[b_tbk[k]])
                ecopy("dve", lt[:, :, 0:nr], tbk[k][:, 0:256].rearrange("p (c t) -> p c t", c=2)[:, :, 0:nr], [b_tbk[k]], [blt])
                ecopy("act", KT[64:96, 0:4, kcol0:kcol0 + nr], tbk[k][0:32, 256:256 + nr].unsqueeze(1).to_broadcast([32, 4, nr]), [b_tbk[k]], [b_KT[vt]])

                def l2():
                    for g in range(2):
                        fns = []
                        for hh in range(4):
                            h = g * 4 + hh
                            for c in range(2):
                                fns.append(lambda q, hh=hh, h=h, c=c: q.matmul(bank[g][0:64, hh * 128:hh * 128 + nr], lhsT=wuk[:, c, h * 64:(h + 1) * 64], rhs=lt[:, c, 0:nr],
                                                                               start=(c == 0), stop=(c == 1), skip_group_check=True))
                        fw.op("pe", fns, reads=[blt, b_wuk], writes=[b_bank[g]])
                    ecopy("dve", KT[0:64, 0:4, kcol0:kcol0 + nr], bank[0][0:64, :].rearrange("p (h t) -> p h t", h=4)[:, :, 0:nr], [b_bank[0]], [b_KT[vt]])
                    sk = next_slot()
                    kst = oall[0:64, sk, 0:512].rearrange("p (h t) -> p h t", h=4)[:, :, 0:nr]
                    ecopy("act", kst, bank[1][0:64, :].rearrange("p (h t) -> p h t", h=4)[:, :, 0:nr], [b_bank[1]], [b_o[sk]])
                    spill_ev.append(fw.dma("pool", S_kt1[:, :, kcol0:kcol0 + nr], kst, reads=[b_o[sk]], writes=[dbuf("kt1")], sembuf=b_o[sk]))
                    fw.op("pe", [lambda q, c=c: q.matmul(bank[2][0:nr, 0:512], lhsT=lt[:, c, 0:nr], rhs=wuv[:, c, :], start=(c == 0), stop=(c == 1))
                                 for c in range(2)], reads=[blt, b_wuv], writes=[b_bank[2]])
                    ingest_v(bank[2][0:nr, 0:256], b_bank[2], nr, vt, 4, 64, eng="act")
                    sv = next_slot()
                    ecopy("dve", oall[0:nr, sv, 0:256], bank[2][0:nr, 256:512], [b_bank[2]], [b_o[sv]])
                    spill_ev.append(fw.dma("pool", S_v1[vt, 0:nr, :], oall[0:nr, sv, 0:256], reads=[b_o[sv]], writes=[dbuf("v1")], sembuf=b_o[sv]))

                prev = lat_state["pend"]
                lat_state["pend"] = l2
                if prev is not None:
                    prev()

            def lat_flush():
                if lat_state["pend"] is not None:
                    lat_state["pend"]()
                    lat_state["pend"] = None

            if seq.kind == "s":
                for (vt, kc0, nk, is_cache) in seq.keys:
                    if not is_cache:
                        continue
                    i = nxt("stg", 3)
                    fw.dma("sp", stg[i][:, 0:256], I["cache_c_latent"][b, kc0:kc0 + 128, :], writes=[b_stg[i]])
                    fw.dma("sp", stg[i][:, 256:288], I["cache_c_krope"][b, kc0:kc0 + 128, :], writes=[b_stg[i]])
                    lat_ingest(stg[i][:, 0:256], b_stg[i], stg[i][:, 256:288], b_stg[i], 128, kc0, vt, cast_eng())
                lat_flush()

            def cons_ckv(ti, r0, nr, rt, bk, bbk):
                i = nxt("stg", 3)
                rs, bs = rms_scale(bk[0:nr, 0:256], nr, 256, [bbk], 28)
                ecopy("act", stg[i][0:nr, 256:288], bk[0:nr, 256:288], [bbk], [b_stg[i]])
                fw.op("dve", lambda q: q.scalar_tensor_tensor(out=stg[i][0:nr, 0:256], in0=bk[0:nr, 0:256], scalar=rs, in1=gckv[0:nr, :], op0=ALU.mult, op1=ALU.mult),
                      reads=[bbk, bs, b_gckv], writes=[b_stg[i]])
                rope_ops(None, stg[i][0:nr, 256:288].rearrange("p (b d) -> p b d", b=1), nr, rt, ropeC, 16, 1, b_stg[i], b_stg[i],
                         stg[i][0:nr, 256:288].rearrange("p (b d) -> p b d", b=1))
                store_out(O["c_lat_" + pre][b, r0:r0 + nr, :], stg[i][0:nr, 0:256], b_stg[i])
                store_out(O["c_krope_" + pre][b, r0:r0 + nr, :], stg[i][0:nr, 256:288], b_stg[i])
                return [lambda: lat_ingest(stg[i][0:nr, 0:256], b_stg[i], stg[i][0:nr, 256:288], b_stg[i], nr, knew0 + r0, vnew0 + ti, "dve")]

            proj_block(seq, S_w1, 768, 288, cons_ckv)
            lat_flush()

            w1a = load_wblock(S_w1, DM, 0, 512)
            w1b = load_wblock(S_w1, DM, 512, 256)
            def s1(ti):
                r0, nr, rt = seq.tiles[ti]
                ba, bb = ((4, 5), (0, 1))[ti % 2]
                fw.op("pe", [lambda q, c=c: q.matmul(bank[ba][0:nr, :], lhsT=uT[:, c, r0:r0 + nr], rhs=w1a[0][:, c, :], start=(c == 0), stop=(c == 7)) for c in range(8)],
                      reads=[b_uT[ti], w1a[1]], writes=[b_bank[ba]])
                fw.op("pe", [lambda q, c=c: q.matmul(bank[bb][0:nr, 0:256], lhsT=uT[:, c, r0:r0 + nr], rhs=w1b[0][:, c, 0:256], start=(c == 0), stop=(c == 7)) for c in range(8)],
                      reads=[b_uT[ti], w1b[1]], writes=[b_bank[bb]])

            stgsel = {}

            def s2(ti):
                r0, nr, rt = seq.tiles[ti]
                ba, bb = ((4, 5), (0, 1))[ti % 2]
                rs, bs = rms_scale([bank[ba][0:nr, :], bank[bb][0:nr, 0:256]], nr, 768, [b_bank[ba], b_bank[bb]], 32 + 2 * (ti % 2))
                g = ti % 2
                cq16 = ogT[g][:, :, :].rearrange("p c t -> p (c t)")
                fw.op("act", lambda q: q.activation(out=cq16[0:nr, 0:512], in_=bank[ba][0:nr, :], func=AF.Copy, scale=rs), reads=[b_bank[ba], bs], writes=[b_ogT[g]])
                fw.op("act", lambda q: q.activation(out=cq16[0:nr, 512:768], in_=bank[bb][0:nr, 0:256], func=AF.Copy, scale=rs), reads=[b_bank[bb], bs], writes=[b_ogT[g]])
                k = nxt("tbk", 2)
                fw.op("pe", [lambda q, c=c: q.transpose(out=tbk[k][:, c * 128:c * 128 + nr], in_=cq16[0:nr, c * 128:(c + 1) * 128], identity=ident[0:nr, 0:nr]) for c in range(6)],
                      reads=[b_ogT[g], b_ident], writes=[b_tbk[k]])
                cqT = Ef[g][:, :].bitcast(BF16).rearrange("p (c t) -> p c t", c=8)
                ecopy("dve", cqT[:, 0:6, 0:nr], tbk[k][:, 0:768].rearrange("p (c t) -> p c t", c=6)[:, :, 0:nr], [b_tbk[k]], [b_Ef[g]])
                for grp in range(2):
                    qbk = (3, 2)[grp]
                    wq_t, wq_b = (wqb, [b_wqb]) if grp == 0 else (wq1, bwq1)
                    fw.op("pe", [lambda q, c=c: q.matmul(bank[qbk][0:nr, 0:384], lhsT=cqT[:, c, 0:nr], rhs=wq_t[:, c, 0:384], start=(c == 0), stop=(c == 5)) for c in range(6)],
                          reads=[b_Ef[g]] + wq_b, writes=[b_bank[qbk]])
                    i = nxt("stg", 3)
                    stgsel[(ti, grp)] = i
                    ecopy("act", stg[i][0:nr, 0:384], bank[qbk][0:nr, 0:384], [b_bank[qbk]], [b_stg[i]])

            def s3(ti):
                r0, nr, rt = seq.tiles[ti]
                for grp in range(2):
                    i = stgsel[(ti, grp)]
                    dst3 = stg[i][0:nr, 0:384].rearrange("p (h d) -> p h d", h=4)[:, :, 64:96]
                    rope_ops(None, dst3, nr, rt, ropeC, 16, 4, b_stg[i], b_stg[i], dst3)
                    if grp == 0:
                        ingest_q(stg[i][0:nr, 0:384], b_stg[i], nr, r0, ti, SC, ncol=384, part=96, eng="dve")()
                    else:
                        j = nxt("tb16", 3)
                        ecopy("dve", tb16[j][0:nr, 0:384], stg[i][0:nr, 0:384], [b_stg[i]], [b_tb16[j]], SC)
                        k2 = nxt("tbk", 2)
                        fw.op("pe", [lambda q, c=c: q.transpose(out=tbk[k2][0:96, c * 128:c * 128 + nr], in_=tb16[j][0:nr, c * 96:(c + 1) * 96], identity=ident[0:nr, 0:nr]) for c in range(4)],
                              reads=[b_tb16[j], b_ident], writes=[b_tbk[k2]])
                        sq = next_slot()
                        qst = oall[0:96, sq, 0:512].rearrange("p (h t) -> p h t", h=4)[:, :, 0:nr]
                        ecopy(copy_eng(), qst, tbk[k2][0:96, 0:512].rearrange("p (c t) -> p c t", c=4)[:, :, 0:nr], [b_tbk[k2]], [b_o[sq]])
                        spill_ev.append(fw.dma("pool", S_qt1[:, :, r0:r0 + nr], qst, reads=[b_o[sq]], writes=[dbuf("qt1")], sembuf=b_o[sq]))

            ntl = len(seq.tiles)
            s1(0)
            for ti in range(ntl):
                if ti + 1 < ntl:
                    s1(ti + 1)
                if ti >= 1:
                    s3(ti - 1)
                s2(ti)
            s3(ntl - 1)
            return spill_ev

        def reload_C(seq, spill_ev):
            fw.wait_events("sp", spill_ev)
            nq = seq.ntok
            nkc = seq.keys[-1][1] + seq.keys[-1][2]
            fw.dma("sp", QT[0:96, 0:4, 0:nq], S_qt1[:, :, 0:nq], reads=[dbuf("qt1")], writes=b_QT, sembuf=b_QT[0])
            fw.dma("sp", KT[0:64, 0:4, 0:nkc], S_kt1[:, :, 0:nkc], reads=[dbuf("kt1")], writes=b_KT, sembuf=b_KT[0])
            for (vt, kc0, nk, is_cache) in seq.keys:
                fw.dma("sp", VR[0:nk, vt, 0:260].rearrange("p (h e) -> p h e", e=65)[:, :, 0:64], S_v1[vt, 0:nk, :].rearrange("p (h d) -> p h d", h=4),
                       reads=[dbuf("v1")], writes=[b_V[vt]], sembuf=b_V[vt])

        def stage_QC_sample(seq, grp):
            nq = seq.tiles[0][1]
            units = []
            for hh in range(4):
                h = grp * 4 + hh
                ob = 2 + (hh % 2)
                u = dict(qk=[], extra=[], exps=[], pv=[], rd=[b_QT[0]], rdv=[])
                for ki, (vt, kc0, nk, is_cache) in enumerate(seq.keys):
                    c0 = ki * nq
                    u["qk"].append((c0, nq, KT[0:96, hh, kc0:kc0 + nk], QT[0:96, hh, 0:nq], nk))
                    u["pv"].append((bank[ob][0:nq, 0:65], c0, nq, VR[0:nk, vt, hh * 65:(hh + 1) * 65], ki == 0, b_bank[ob], nk))
                    u["rd"].append(b_KT[vt])
                    u["rdv"].append(b_V[vt])
                ncache = sum(1 for k in seq.keys if k[3])
                u["exps"] = [(0, ncache * nq, 128), (ncache * nq, nq, seq.keys[-1][2])]

                def fin(h=h, ob=ob):
                    kb = 36
                    bs = smb(kb)
                    fw.op("dve", lambda q: q.reciprocal(out=sm[0:nq, kb:kb + 1], in_=bank[ob][0:nq, 64:65]), reads=[b_bank[ob]], writes=[bs])
                    ecopy("dve", oall[0:nq, 0, h * 64:(h + 1) * 64], bank[ob][0:nq, 0:64], [b_bank[ob], bs], [b_o[0]], scale=sm[0:nq, kb:kb + 1])
                u["fin"] = fin
                units.append(u)
            run_units_nk(units)

        def stage_QC(seq, grp):
            if seq.kind == "s":
                return stage_QC_sample(seq, grp)
            nt = len(seq.tiles)
            bq = 4 if seq.kind == "p" else 1
            blk = 0
            for hh in range(4):
                h = grp * 4 + hh
                for qb0 in range(0, nt, bq):
                    qts = list(range(qb0, min(nt, qb0 + bq)))
                    qcol0 = seq.tiles[qb0][0]
                    ob = 2 + (blk % 2)
                    blk += 1
                    units = []
                    started = False
                    for ki, (vt, kc0, nk, is_cache) in enumerate(seq.keys):
                        valid = [j for j in qts if ki <= j] if seq.kind == "p" else qts
                        if not valid:
                            continue
                        j0 = valid[0] - qb0
                        off0 = j0 * 128
                        nv = sum(seq.tiles[j][1] for j in valid)
                        u = dict(nk=nk, qk=[], extra=[], exps=[(0, off0, nv, off0)], pv=[], rd=[b_KT[vt]] + [b_QT[j] for j in valid], rdv=[b_V[vt]])
                        u["qk"].append((0, off0, nv, KT[0:96, hh, kc0:kc0 + nk], QT[0:96, hh, qcol0 + off0:qcol0 + off0 + nv]))
                        for j in valid:
                            if seq.kind == "p" and ki == j:
                                u["extra"].append((0, (j - qb0) * 128, seq.tiles[j][1], negq[0:nk, 0:seq.tiles[j][1]]))
                        for j in valid:
                            jj = j - qb0
                            nq = seq.tiles[j][1]
                            u["pv"].append((bank[ob][0:nq, jj * 65:(jj + 1) * 65], jj * 128, nq, VR[0:nk, vt, hh * 65:(hh + 1) * 65], not started, b_bank[ob]))
                            started = True
                        units.append(u)

                    def fin(h=h, qts=qts, qb0=qb0, ob=ob):
                        nq = seq.tiles[qts[0]][1]
                        kb = 36
                        bs = smb(kb)
                        nj = len(qts)
                        fw.op("dve", lambda q: q.reciprocal(out=sm[0:nq, kb:kb + nj], in_=bank[ob][0:nq, 0:nj * 65].rearrange("p (j e) -> p j e", e=65)[:, :, 64]),
                              reads=[b_bank[ob]], writes=[bs])
                        for j in qts:
                            jj = j - qb0
                            ecopy("dve", oall[0:nq, j, h * 64:(h + 1) * 64], bank[ob][0:nq, jj * 65:jj * 65 + 64], [b_bank[ob], bs], [b_o[j]], scale=sm[0:nq, kb + jj:kb + jj + 1])
                    units[-1]["fin"] = fin
                    run_units(units)

        def stage_PD(seq, b):
            pre = "p" if seq.kind == "p" else "s"
            set_ones(8, 64)
            if seq.kind == "s":
                for (vt, kc0, nk, is_cache) in seq.keys:
                    if not is_cache:
                        continue
                    i = nxt("stg", 3)
                    fw.dma("sp", stg[i][:, :], I["cache_d_k"][b, kc0:kc0 + 128, :], writes=[b_stg[i]])
                    ingest_k(stg[i][:, :], b_stg[i], 128, kc0, vt, eng=cast_eng())()
                    i = nxt("stg", 3)
                    fw.dma("sp", stg[i][:, :], I["cache_d_v"][b, kc0:kc0 + 128, :], writes=[b_stg[i]])
                    ingest_v(stg[i][:, :], b_stg[i], 128, vt, 8, 64, eng=cast_eng())
            knew0 = 0 if seq.kind == "p" else 2048
            vnew0 = 0 if seq.kind == "p" else 16

            def cons_q(ti, r0, nr, rt, bk, bbk):
                return [ingest_q(bk[0:nr, :], bbk, nr, r0, ti, 0.125, eng="dve")]

            def cons_k(ti, r0, nr, rt, bk, bbk):
                i = nxt("stg", 3)
                ecopy("act", stg[i][0:nr, :], bk[0:nr, :], [bbk], [b_stg[i]])
                store_out(O["d_k_" + pre][b, r0:r0 + nr, :], stg[i][0:nr, :], b_stg[i])
                return [ingest_k(bk[0:nr, :], bbk, nr, knew0 + r0, vnew0 + ti, eng="dve")]

            def cons_v(ti, r0, nr, rt, bk, bbk):
                i = nxt("stg", 3)
                ecopy("act", stg[i][0:nr, :], bk[0:nr, :], [bbk], [b_stg[i]])
                store_out(O["d_v_" + pre][b, r0:r0 + nr, :], stg[i][0:nr, :], b_stg[i])
                ingest_v(bk[0:nr, :], bbk, nr, vnew0 + ti, 8, 64, eng="dve")
                return []

            proj_block(seq, S_w1, 1056, 512, cons_q)
            proj_block(seq, S_w1, 1568, 512, cons_k)
            proj_block(seq, S_w1, 2080, 512, cons_v)

        def stage_QD(seq):
            nt = len(seq.tiles)
            bq = 4 if seq.kind == "p" else 1
            xbanks = (0, 1, 3)
            st_ = {"u": 0, "cs": 0}
            prev_tiles = []
            for qb0 in range(0, nt, bq):
                qts = list(range(qb0, min(nt, qb0 + bq)))
                qcol0 = seq.tiles[qb0][0]
                for h in range(8):
                    hp = (h % 2) * 64
                    fw.op("pool", lambda q: q.memset(osb[:, :, 0:64], 0.0), writes=b_osb)
                    pend_a = None
                    pend_b = None
                    for ki, (vt, kc0, nk, is_cache) in enumerate(seq.keys):
                        valid = [j for j in qts if ki <= j] if seq.kind == "p" else qts
                        if not valid:
                            continue
                        j0 = valid[0] - qb0
                        off0 = j0 * 128
                        nv = sum(seq.tiles[j][1] for j in valid)
                        ucount = st_["u"]
                        st_["u"] += 1
                        xb = xbanks[ucount % 3]
                        eb = ucount % 2
                        pi = ucount % 3
                        diag = [j for j in valid if (seq.kind == "p" and ki == j) or (seq.kind == "s" and not is_cache)]
                        fw.op("pe", lambda q: q.matmul(bank[xb][0:nk, off0:off0 + nv], lhsT=KT[hp:hp + 64, h // 2, kc0:kc0 + nk], rhs=QT[hp:hp + 64, h // 2, qcol0 + off0:qcol0 + off0 + nv],
                                                       start=True, stop=False, skip_group_check=True), reads=[b_KT[vt]] + [b_QT[j] for j in valid], writes=[b_bank[xb]])
                        fw.op("act", lambda q: q.activation(out=Ef[eb][0:nk, off0:off0 + nv], in_=bank[xb][0:nk, off0:off0 + nv], func=AF.Exp), reads=[b_bank[xb]], writes=[b_Ef[eb]])
                        fw.op("act", lambda q: q.activation(out=SPb[eb][0:nk, off0:off0 + nv], in_=Ef[eb][0:nk, off0:off0 + nv], func=AF.Ln, bias=1.0), reads=[b_Ef[eb]], writes=[b_SPb[eb]])
                        for j in diag:
                            c = (j - qb0) * 128
                            nq = seq.tiles[j][1]
                            fw.op("pool", lambda q: q.tensor_tensor(out=SPb[eb][0:nk, c:c + nq], in0=SPb[eb][0:nk, c:c + nq], in1=m01d[0:nk, 0:nq], op=ALU.mult),
                                  reads=[b_SPb[eb], b_m01d], writes=[b_SPb[eb]])

                        def stage2a(vt=vt, nk=nk, valid=valid, off0=off0, nv=nv, xb=xb, eb=eb, pi=pi, diag=diag, qts=qts, qb0=qb0):
                            fns = [lambda q: q.matmul(bank[xb][0:nk, off0:off0 + nv], lhsT=nut[0:nk, 0:nk], rhs=SPb[eb][0:nk, off0:off0 + nv], start=False, stop=False, skip_group_check=True)]
                            for j in diag:
                                c = (j - qb0) * 128
                                nq = seq.tiles[j][1]
                                fns.append(lambda q, c=c, nq=nq: q.matmul(bank[xb][0:nk, c:c + nq], lhsT=ident[0:nk, 0:nk], rhs=negd[0:nk, 0:nq], start=False, stop=True, skip_group_check=True))
                            fw.op("pe", fns, reads=[b_SPb[eb], b_nut, b_negd, b_ident], writes=[b_bank[xb]])
                            fw.op("act", lambda q: q.activation(out=PT[pi][0:nk, off0:off0 + nv], in_=bank[xb][0:nk, off0:off0 + nv], func=AF.Exp), reads=[b_bank[xb]], writes=[b_PT[pi]])

                        def stage2b(vt=vt, nk=nk, valid=valid, pi=pi, qb0=qb0, h=h, qts=qts, ucount=ucount):
                            dk = 40 + 4 * pi
                            bs = smb(dk)
                            wbk = (2, 4)[ucount % 2]
                            fns = []
                            for j in valid:
                                jj = j - qb0
                                nq = seq.tiles[j][1]
                                fns.append(lambda q, jj=jj, nq=nq: q.matmul(bank[wbk][0:nq, jj * 65:(jj + 1) * 65], lhsT=PT[pi][0:nk, jj * 128:jj * 128 + nq], rhs=VR[0:nk, vt, h * 65:(h + 1) * 65],
                                                                            start=True, stop=True, skip_group_check=True))
                            fw.op("pe", fns, reads=[b_PT[pi], b_V[vt]], writes=[b_bank[wbk]])
                            jlo = valid[0] - qb0
                            nqm = seq.tiles[valid[0]][1]
                            nj = len(qts)
                            w3 = bank[wbk][0:nqm, 0:nj * 65].rearrange("p (j e) -> p j e", e=65)
                            fw.op("dve", lambda q: q.tensor_scalar(out=sm[0:nqm, dk + jlo:dk + nj], in0=w3[:, jlo:nj, 64], scalar1=-1.0, scalar2=1.0, op0=ALU.mult, op1=ALU.add),
                                  reads=[b_bank[wbk]], writes=[bs])
                            nv_ = nj - jlo
                            dec_b = sm[0:nqm, dk + jlo:dk + nj].unsqueeze(2).to_broadcast([nqm, nv_, 64])
                            fw.op("dve", lambda q: q.tensor_tensor(out=osb[0:nqm, jlo:nj, 0:64], in0=osb[0:nqm, jlo:nj, 0:64], in1=dec_b, op=ALU.mult),
                                  reads=b_osb + [bs], writes=b_osb)
                            fw.op("dve", lambda q: q.tensor_tensor(out=osb[0:nqm, jlo:nj, 0:64], in0=w3[:, jlo:nj, 0:64], in1=osb[0:nqm, jlo:nj, 0:64], op=ALU.add),
                                  reads=b_osb + [b_bank[wbk]], writes=b_osb)

                        if pend_a is not None:
                            pend_a()
                        if pend_b is not None:
                            pend_b()
                        pend_b = None
                        if pend_a is not None:
                            pend_b = pend_a.b
                        stage2a.b = stage2b
                        pend_a = stage2a
                    if pend_a is not None:
                        pend_a()
                    if pend_b is not None:
                        pend_b()
                    if pend_a is not None:
                        pend_a.b()
                    for j in qts:
                        jj = j - qb0
                        nq = seq.tiles[j][1]
                        osl = oall[0:nq, j, 512 + h * 64:512 + (h + 1) * 64]
                        fw.op("dve", lambda q, osl=osl, jj=jj: q.tensor_tensor(out=osl, in0=osb[0:nq, jj, 0:64], in1=osl, op=ALU.mult), reads=[b_osb[jj], b_o[j]], writes=[b_o[j]])

        def load_wukv():
            for bb in (b_wuk, b_wuv, b_latT, b_wqb):
                bb.lw = b_Tb.lw
                bb.rd = list(b_Tb.rd)
            for (wsrc, wdst, bw) in ((I["w_uk"], wuk, b_wuk), (I["w_uv"], wuv, b_wuv)):
                for c in range(2):
                    i = nxt("stg", 3)
                    fw.dma("sp", stg[i][:, :], wsrc[c * 128:(c + 1) * 128, :], writes=[b_stg[i]])
                    fw.op("dve", lambda q: q.tensor_copy(out=wdst[:, c, :], in_=stg[i][:, :]), reads=[b_stg[i]], writes=[bw])

        seqs = [Seq("p", i) for i in range(NP)] + [Seq("s", i) for i in range(NS)]
        h1_evs = {}

        def load_layer_consts(layer):
            fw.dma("sp", gpost[:], bcast_ap(I["g_post0"] if layer == 0 else I["g_post1"], DM), writes=[b_gpost])
            wait_prep((S_wo0 if layer == 0 else S_wo1).name)
            fw.dma("sp", wout[:], (S_wo0 if layer == 0 else S_wo1).rearrange("(c p) n -> p c n", p=128), reads=[dbuf((S_wo0 if layer == 0 else S_wo1).name)], writes=[b_wout])

        u_done = set()

        def do_U(seq, layer):
            key = (layer, seq.kind, seq.idx)
            if key in u_done:
                return
            u_done.add(key)
            b = seq.idx
            if layer == 0 or 0 not in layers:
                stage_U(seq, I["x_prompt"][b] if seq.kind == "p" else I["x_sample"][b])
            else:
                fw.wait_events("sp", h1_evs.get((seq.kind, b), []))
                stage_U(seq, S_h1p[b] if seq.kind == "p" else S_h1s[b], [dbuf(("h1", seq.kind, b))])

        if 0 in layers and DEV_STOP >= 2:
            consts0_loaded = [False]
            for si, seq in enumerate(seqs):
                b = seq.idx
                src = I["x_prompt"][b] if seq.kind == "p" else I["x_sample"][b]
                if 1 in layers:
                    dst = S_h1p[b] if seq.kind == "p" else S_h1s[b]
                    is_out = False
                else:
                    dst = O["y_prompt"][b] if seq.kind == "p" else O["y_sample"][b]
                    is_out = True
                do_U(seq, 0)
                if si == 0 and seq.kind == "p":
                    prep_state["active"] = True
                if DEV_STOP >= 2.1:
                    stage_PA(seq, b)
                prep_flush()
                if not consts0_loaded[0]:
                    load_layer_consts(0)
                    consts0_loaded[0] = True
                if DEV_STOP >= 3.1:
                    stage_QA(seq)
                if DEV_STOP >= 5:
                    stage_PB(seq, b)
                if DEV_STOP >= 6:
                    stage_gate(seq, S_w0, 3072, 0, True)
                    stage_gate(seq, S_w0, 3072, 1, False)
                    if si + 1 < len(seqs):
                        do_U(seqs[si + 1], 0)
                    stage_QB(seq)
                    stage_GY(seq, b, src, dst, is_out)
        prep_flush()

        if 1 in layers:
            load_layer_consts(1)
            load_wukv()
            for si, seq in enumerate(seqs):
                b = seq.idx
                if 0 in layers:
                    src = S_h1p[b] if seq.kind == "p" else S_h1s[b]
                else:
                    src = I["x_prompt"][b] if seq.kind == "p" else I["x_sample"][b]
                dst = O["y_prompt"][b] if seq.kind == "p" else O["y_sample"][b]
                hb = [dbuf(("h1", seq.kind, b))] if 0 in layers else []
                fw.wait_events("sp", h1_evs.get((seq.kind, b), []))
                do_U(seq, 1)
                sp_ev = stage_PC(seq, b)
                stage_QC(seq, 0)
                reload_C(seq, sp_ev)
                stage_QC(seq, 1)
                stage_gate(seq, S_w1, 2592, 0, True)
                stage_PD(seq, b)
                stage_gate(seq, S_w1, 2592, 1, False)
                if si + 1 < len(seqs):
                    do_U(seqs[si + 1], 1)
                stage_QD(seq)
                stage_GY(seq, b, src, dst, True, hb)

        fw.finish("sp")
        print("ninst", fw.ninst, {e: fw.cnt[e] for e in fw.cnt})
    return nc


def _consts():
    c = {}
    c["c_ident"] = np.eye(128, dtype=np.float32)
    m = np.zeros((128, 128), np.float32); m[64:128, 0:64] = NEG; c["c_negq"] = m
    m = np.zeros((128, 128), np.float32); m[0:64, 64:128] = NEG; c["c_negq4"] = m
    s = np.arange(128)[:, None]; t = np.arange(128)[None, :]
    c["c_negd"] = np.where(s >= t, NEG, 0.0).astype(np.float32)
    c["c_m01d"] = (s < t).astype(np.float32)
    c["c_nut"] = np.where(s >= t, -1.0, 0.0).astype(np.float32)
    pos = np.zeros((17, 128), np.float32)
    for tt in range(16):
        pos[tt] = tt * 128 + np.arange(128)
    pos[16] = PAST + np.arange(128)
    for nm, half in (("c_ropeA", 8), ("c_ropeC", 16)):
        inv = (np.float32(THETA) ** (-np.arange(half, dtype=np.float32) / np.float32(half))).astype(np.float32)
        ang = (pos[:, :, None] * inv[None, None, :]).astype(np.float32)
        tab = np.stack([np.cos(ang), np.sin(ang)], 0).astype(np.float32)
        c[nm] = np.ascontiguousarray(tab.transpose(2, 0, 1, 3).reshape(128, 2 * 17 * half))
    return c


_CACHE = {}
_IN_SHARDED = ["x_prompt", "x_sample", "cache_a_k", "cache_a_v", "cache_b_k", "cache_b_v", "cache_c_latent", "cache_c_krope", "cache_d_k", "cache_d_v"]
_OUT_NAMES = ["y_prompt", "y_sample", "a_k_p", "a_v_p", "b_k_p", "b_v_p", "c_lat_p", "c_krope_p", "d_k_p", "d_v_p",
              "a_k_s", "a_v_s", "b_k_s", "b_v_s", "c_lat_s", "c_krope_s", "d_k_s", "d_v_s"]


def _out_shapes(nb_p, nb_s):
    return [(nb_p, T, DM), (nb_s, TS, DM), (nb_p, T, 4, 2, 64), (nb_p, T, 4, 128), (nb_p, 512, 8, 64), (nb_p, 512, 8, 64),
            (nb_p, T, 256), (nb_p, T, 32), (nb_p, T, 8, 64), (nb_p, T, 8, 64),
            (nb_s, TS, 4, 2, 64), (nb_s, TS, 4, 128), (nb_s, TS, 8, 64), (nb_s, TS, 8, 64), (nb_s, TS, 256), (nb_s, TS, 32),
            (nb_s, TS, 8, 64), (nb_s, TS, 8, 64)]


def _core_inputs(inputs, lo_p, hi_p, lo_s, hi_s):
    m = {}
    for k, v in inputs.items():
        a = np.asarray(v)
        if k in _IN_SHARDED:
            lo, hi = (lo_p, hi_p) if k == "x_prompt" else (lo_s, hi_s)
            a = a[lo:hi]
            a = a.reshape(a.shape[0], a.shape[1], -1)
        elif k in ("w_uk", "w_uv"):
            a = a.reshape(256, 512)
        m[k] = np.ascontiguousarray(a, dtype=np.float32)
    m.update(_consts())
    rb = np.asarray(inputs["rel_bias_b"], dtype=np.float32)
    s_ = np.arange(128)[:, None]; t_ = np.arange(128)[None, :]
    idx = np.stack([np.clip(t_ - s_, -128, 128) + 128, np.clip(128 + t_ - s_, -128, 128) + 128], 0)
    m["c_rbT"] = np.ascontiguousarray(rb[:, idx], dtype=np.float32)
    return m


def kernel(**inputs):
    nb = np.asarray(inputs["x_prompt"]).shape[0]
    per = nb // NCORES
    key = ("full", per)
    if key not in _CACHE:
        _CACHE[key] = build(per, per)
    nc = _CACHE[key]
    in_maps = [_core_inputs(inputs, i * per, (i + 1) * per, i * per, (i + 1) * per) for i in range(NCORES)]
    res = run_bass_kernel_spmd(nc, in_maps, core_ids=list(range(NCORES)))
    outs = []
    shapes = _out_shapes(nb, nb)
    for nm, shp in zip(_OUT_NAMES, shapes):
        full = np.concatenate([np.asarray(r[nm]) for r in res.results], axis=0)
        outs.append(np.ascontiguousarray(full.reshape(shp), dtype=np.float32))
    return tuple(outs)
```

```python
import math
import itertools
import numpy as np
from contextlib import ExitStack
import concourse.bass as bass
import concourse.mybir as mybir
from concourse.bass_utils import run_bass_kernel_spmd

F32 = mybir.dt.float32
BF16 = mybir.dt.bfloat16
AF = mybir.ActivationFunctionType
ALU = mybir.AluOpType
AX = mybir.AxisListType

NCORES = 8
DM = 1024
T = 2048
TS = 16
PAST = 2048
EPS = 1e-6
NEG = -30000.0
THETA = 500000.0
LAM_INIT0 = 0.8 - 0.6 * math.exp(-0.3 * 0)
W0N = 4096
W1N = 3616
VST = 520


class Buf:
    __slots__ = ("name", "lw", "rd", "dsem", "dcnt", "excl")

    def __init__(self, name):
        self.name = name
        self.excl = False
        self.lw = None
        self.rd = []
        self.dsem = None
        self.dcnt = 0


class FW:
    def __init__(self, nc, stack):
        self.nc = nc
        self.stack = stack
        self.q = {"pe": nc.tensor, "act": nc.scalar, "dve": nc.vector, "pool": nc.gpsimd, "sp": nc.sync}
        self.sem = {}
        self.cnt = {}
        self.waited = {}
        for e in self.q:
            self.sem[e] = stack.enter_context(nc.semaphore("s_" + e))
            self.cnt[e] = 0
            self.waited[e] = {}
        self.out_events = []
        self.ninst = 0
        self.nbuf = 0

    def buf(self, name=None):
        self.nbuf += 1
        return Buf(name or ("b%d" % self.nbuf))

    def sb(self, name, shape, dtype):
        return self.stack.enter_context(self.nc.sbuf_tensor(name, list(shape), dtype))

    def ps(self, name, shape, dtype):
        return self.stack.enter_context(self.nc.psum_tensor(name, list(shape), dtype))

    def _dsem(self, b):
        if b.dsem is None:
            b.dsem = self.stack.enter_context(self.nc.semaphore("d_" + b.name))
        return b.dsem

    def _deps(self, eng, reads, writes):
        best = {}

        def add(ev):
            s, v, en = ev
            if en == "pe" and eng == "pe":
                return
            k = id(s)
            if k not in best or best[k][1] < v:
                best[k] = (s, v)

        for b in reads:
            if b.lw is not None:
                add(b.lw)
            if b.excl:
                for ev in b.rd:
                    if ev[2] != eng:
                        add(ev)
        for b in writes:
            if b.lw is not None:
                add(b.lw)
            for ev in b.rd:
                add(ev)
        w = self.waited[eng]
        out = []
        for k, (s, v) in best.items():
            if w.get(k, 0) >= v:
                continue
            w[k] = v
            out.append((s, v))
        return out

    def _record(self, ev, reads, writes):
        for b in reads:
            b.rd.append(ev)
            if len(b.rd) > 16:
                best = {}
                for e in b.rd:
                    k = id(e[0])
                    if k not in best or best[k][1] < e[1]:
                        best[k] = e
                b.rd = list(best.values())
        for b in writes:
            b.lw = ev
            b.rd = []

    def op(self, eng, fns, reads=(), writes=()):
        q = self.q[eng]
        if not isinstance(fns, (list, tuple)):
            fns = [fns]
        for (s, v) in self._deps(eng, reads, writes):
            q.wait_ge(s, v)
        ins = None
        for f in fns:
            ins = f(q)
            self.ninst += 1
        self.cnt[eng] += 1
        ins.then_inc(self.sem[eng], 1)
        ev = (self.sem[eng], self.cnt[eng], eng)
        self._record(ev, reads, writes)
        return ev

    def dma(self, eng, out, in_, reads=(), writes=(), sembuf=None, is_output=False, **kw):
        q = self.q[eng]
        if sembuf is None:
            sembuf = writes[0] if writes else reads[0]
        s = self._dsem(sembuf)
        for (ws, v) in self._deps(eng, reads, writes):
            q.wait_ge(ws, v)
        q.dma_start(out=out, in_=in_, **kw).then_inc(s, 16)
        self.ninst += 1
        sembuf.dcnt += 16
        ev = (s, sembuf.dcnt, "dma")
        self._record(ev, reads, writes)
        if is_output:
            self.out_events.append(ev)
        return ev

    def wait_events(self, eng, events):
        q = self.q[eng]
        best = {}
        for (s, v, en) in events:
            k = id(s)
            if k not in best or best[k][1] < v:
                best[k] = (s, v)
        w = self.waited[eng]
        for k, (s, v) in best.items():
            if w.get(k, 0) >= v:
                continue
            w[k] = v
            q.wait_ge(s, v)

    def finish(self, eng="sp"):
        q = self.q[eng]
        best = {}
        for (s, v, en) in self.out_events:
            k = id(s)
            if k not in best or best[k][1] < v:
                best[k] = (s, v)
        for k, (s, v) in best.items():
            q.wait_ge(s, v)
        for e in ("pe", "act", "dve", "pool"):
            if self.cnt[e] > 0:
                q.wait_ge(self.sem[e], self.cnt[e])


class Seq:
    def __init__(self, kind, idx):
        self.kind = kind
        self.idx = idx
        if kind == "p":
            self.ntok = T
            self.tiles = [(i * 128, 128, i) for i in range(16)]
            self.keys = [(i, i * 128, 128, False) for i in range(16)]
            self.keys_b = self.keys
        else:
            self.ntok = TS
            self.tiles = [(0, TS, 16)]
            self.keys = [(i, i * 128, 128, True) for i in range(16)] + [(16, 2048, TS, False)]
            self.keys_b = [(i, i * 128, 128, True) for i in range(4)] + [(4, 512, TS, False)]


DEV_STOP = 99
DEV_DBG = False


def build(NP, NS, layers=(0, 1)):
    nc = bass.Bass("TRN2", target_bir_lowering=False)

    def din(name, shape):
        return nc.dram_tensor(name, list(shape), F32, kind="ExternalInput").ap()

    def dout(name, shape):
        return nc.dram_tensor(name, list(shape), F32, kind="ExternalOutput").ap()

    def dscr(name, shape, dtype):
        return nc.dram_tensor(name, list(shape), dtype).ap()

    NPa, NSa = max(NP, 1), max(NS, 1)
    I = {}
    I["x_prompt"] = din("x_prompt", [NPa, T, DM])
    I["x_sample"] = din("x_sample", [NSa, TS, DM])
    I["cache_a_k"] = din("cache_a_k", [NSa, PAST, 512])
    I["cache_a_v"] = din("cache_a_v", [NSa, PAST, 512])
    I["cache_b_k"] = din("cache_b_k", [NSa, 512, 512])
    I["cache_b_v"] = din("cache_b_v", [NSa, 512, 512])
    I["cache_c_latent"] = din("cache_c_latent", [NSa, PAST, 256])
    I["cache_c_krope"] = din("cache_c_krope", [NSa, PAST, 32])
    I["cache_d_k"] = din("cache_d_k", [NSa, PAST, 512])
    I["cache_d_v"] = din("cache_d_v", [NSa, PAST, 512])
    for nm, shp in [("g_pre0", [DM]), ("w_in0", [DM, W0N]), ("lam_q1", [64]), ("lam_k1", [64]), ("lam_q2", [64]),
                    ("lam_k2", [64]), ("g_sub_a", [128]), ("rel_bias_b", [8, 257]), ("w_out0", [DM, DM]),
                    ("g_post0", [DM]), ("g_pre1", [DM]), ("w_in1", [DM, W1N]), ("g_cq", [768]), ("w_uq", [768, 768]),
                    ("g_ckv", [256]), ("w_uk", [256, 512]), ("w_uv", [256, 512]), ("w_out1", [DM, DM]),
                    ("g_post1", [DM])]:
        I[nm] = din(nm, shp)
    I["c_ident"] = din("c_ident", [128, 128])
    I["c_negq"] = din("c_negq", [128, 128])
    I["c_negq4"] = din("c_negq4", [128, 128])
    I["c_negd"] = din("c_negd", [128, 128])
    I["c_m01d"] = din("c_m01d", [128, 128])
    I["c_nut"] = din("c_nut", [128, 128])
    I["c_rbT"] = din("c_rbT", [8, 2, 128, 128])
    I["c_ropeA"] = din("c_ropeA", [128, 2 * 17 * 8])
    I["c_ropeC"] = din("c_ropeC", [128, 2 * 17 * 16])

    O = {}
    O["y_prompt"] = dout("y_prompt", [NPa, T, DM])
    O["y_sample"] = dout("y_sample", [NSa, TS, DM])
    O["a_k_p"] = dout("a_k_p", [NPa, T, 512])
    O["a_v_p"] = dout("a_v_p", [NPa, T, 512])
    O["b_k_p"] = dout("b_k_p", [NPa, 512, 512])
    O["b_v_p"] = dout("b_v_p", [NPa, 512, 512])
    O["c_lat_p"] = dout("c_lat_p", [NPa, T, 256])
    O["c_krope_p"] = dout("c_krope_p", [NPa, T, 32])
    O["d_k_p"] = dout("d_k_p", [NPa, T, 512])
    O["d_v_p"] = dout("d_v_p", [NPa, T, 512])
    O["a_k_s"] = dout("a_k_s", [NSa, TS, 512])
    O["a_v_s"] = dout("a_v_s", [NSa, TS, 512])
    O["b_k_s"] = dout("b_k_s", [NSa, TS, 512])
    O["b_v_s"] = dout("b_v_s", [NSa, TS, 512])
    O["c_lat_s"] = dout("c_lat_s", [NSa, TS, 256])
    O["c_krope_s"] = dout("c_krope_s", [NSa, TS, 32])
    O["d_k_s"] = dout("d_k_s", [NSa, TS, 512])
    O["d_v_s"] = dout("d_v_s", [NSa, TS, 512])
    if DEV_DBG:
        O["dbg"] = dout("dbg", [T, DM])

    S_w0 = dscr("s_w0", [DM, W0N], BF16)
    S_wo0 = dscr("s_wo0", [DM, DM], BF16)
    S_w1 = dscr("s_w1", [DM, W1N], BF16)
    S_wo1 = dscr("s_wo1", [DM, DM], BF16)
    S_wuq = dscr("s_wuq", [768, 768], BF16)
    S_h1p = dscr("s_h1p", [NPa, T, DM], F32)
    S_h1s = dscr("s_h1s", [NSa, TS, DM], F32)
    S_rbp = dscr("s_rbp", [8, 512], F32)
    S_qt1 = dscr("s_qt1", [96, 4, T], BF16)
    S_kt1 = dscr("s_kt1", [64, 4, 2064], BF16)
    S_v1 = dscr("s_v1", [17, 128, 256], BF16)

    with ExitStack() as st:
        fw = FW(nc, st)
        B = fw.buf
        uT = fw.sb("uT", [128, 8, T], BF16)
        b_uT = [B("uT%d" % i) for i in range(16)]
        R = fw.sb("R", [128, 8192 + 8256 + 17 * VST], BF16)
        QT = R[:, 0:8192].rearrange("p (c t) -> p c t", c=4)
        KT = R[:, 8192:8192 + 8256].rearrange("p (c t) -> p c t", c=4)
        VR = R[:, 16448:16448 + 17 * VST].rearrange("p (k e) -> p k e", k=17)
        b_QT = [B("QT%d" % i) for i in range(16)]
        b_KT = [B("KT%d" % i) for i in range(17)]
        b_V = [B("V%d" % i) for i in range(17)]
        oall = fw.sb("oall", [128, 16, DM], BF16)
        b_o = [B("o%d" % i) for i in range(16)]
        wblk = [fw.sb("wblk%d" % i, [128, 8, 512], BF16) for i in range(2)]
        b_wblk = [B("wblk%d" % i) for i in range(2)]
        wout = fw.sb("wout", [128, 8, DM], BF16)
        b_wout = B("wout")
        xt = [fw.sb("xt%d" % i, [128, DM], F32) for i in range(2)]
        b_xt = [B("xt%d" % i) for i in range(2)]
        b_xout = [B("xout%d" % i) for i in range(2)]
        xn = fw.sb("xn", [128, DM], BF16)
        b_xn = B("xn")
        junk = fw.sb("junk", [128, DM], BF16)
        b_junk = B("junk")
        stg = [fw.sb("stg%d" % i, [128, 512], F32) for i in range(3)]
        b_stg = [B("stg%d" % i) for i in range(3)]
        tb16 = [fw.sb("tb16_%d" % i, [128, 512], BF16) for i in range(3)]
        b_tb16 = [B("tb16_%d" % i) for i in range(3)]
        PT = [fw.sb("PT%d" % i, [128, 512], BF16) for i in range(3)]
        b_PT = [B("PT%d" % i) for i in range(3)]
        Ef = [fw.sb("Ef%d" % i, [128, 512], F32) for i in range(2)]
        b_Ef = [B("Ef%d" % i) for i in range(2)]
        SPb = [fw.sb("SPb%d" % i, [128, 512], BF16) for i in range(2)]
        b_SPb = [B("SPb%d" % i) for i in range(2)]
        ogT = [fw.sb("ogT%d" % i, [128, 8, 128], BF16) for i in range(2)]
        b_ogT = [B("ogT%d" % i) for i in range(2)]
        ytmp = fw.sb("ytmp", [128, DM], F32)
        b_ytmp = B("ytmp")
        sm = fw.sb("sm", [128, 64], F32)
        b_sm = {}

        def smb(k):
            if k not in b_sm:
                b_sm[k] = B("sm%d" % k)
            return b_sm[k]

        osb = fw.sb("osb", [128, 4, 128], F32)
        b_osb = [B("osb%d" % i) for i in range(4)]
        ident = fw.sb("ident", [128, 128], BF16); b_ident = B("ident")
        negq = fw.sb("negq", [128, 128], BF16); b_negq = B("negq")
        negq4 = fw.sb("negq4", [128, 128], BF16); b_negq4 = B("negq4")
        negd = fw.sb("negd", [128, 128], BF16); b_negd = B("negd")
        m01d = fw.sb("m01d", [128, 128], BF16); b_m01d = B("m01d")
        nut = fw.sb("nut", [128, 128], BF16); b_nut = B("nut")
        onec = fw.sb("onec", [128, 2], BF16); b_onec = B("onec")
        ropeA = fw.sb("ropeA", [128, 2, 17, 8], F32); b_ropeA = B("ropeA")
        ropeC = fw.sb("ropeC", [128, 2, 17, 16], F32); b_ropeC = B("ropeC")
        gcol = fw.sb("gcol", [128, 24], F32); b_gcol = B("gcol")
        gpost = fw.sb("gpost", [128, DM], F32); b_gpost = B("gpost")
        gsub = fw.sb("gsub", [128, 128], F32); b_gsub = B("gsub")
        gckv = fw.sb("gckv", [128, 256], F32); b_gckv = B("gckv")
        lamt = ytmp[:, 520:776].rearrange("p (a b) -> p a b", a=4); b_lamt = b_ytmp
        lamv = fw.sb("lamv", [128, 8], F32); b_lamv = B("lamv")
        LR = fw.sb("LR", [128, 4608], BF16)
        Tb = LR[:, 0:2048].rearrange("p (a b) -> p a b", a=16); b_Tb = B("Tb")
        wuk = LR[:, 0:1024].rearrange("p (a b) -> p a b", a=2); b_wuk = B("wuk")
        wuv = LR[:, 1024:2048].rearrange("p (a b) -> p a b", a=2); b_wuv = B("wuv")
        latT = LR[:, 2048:2304].rearrange("p (a b) -> p a b", a=2); b_latT = B("latT")
        wqb = LR[:, 2304:4608].rearrange("p (a b) -> p a b", a=6); b_wqb = B("wqb")
        latT2 = fw.sb("latT2", [128, 2, 128], BF16); b_latT2 = B("latT2")
        bank = [fw.ps("bank%d" % i, [128, 512], F32) for i in range(8)]
        b_bank = [B("bank%d" % i) for i in range(8)]
        tbk = [bank[6][:, :].bitcast(BF16), bank[7][:, :].bitcast(BF16)]
        b_tbk = [b_bank[6], b_bank[7]]
        for bb in b_bank:
            bb.excl = True
        pexp = fw.sb("pexp", [128, 1], F32); b_pexp = B("pexp")
        fw.op("pool", lambda q: q.memset(pexp[:], -0.5), writes=[b_pexp])
        b_dram = {}

        def dbuf(k):
            if k not in b_dram:
                b_dram[k] = B("dram_" + str(k))
            return b_dram[k]

        rr_ctr = {"stg": 0, "tb16": 0, "tbk": 0, "xt": 0, "wblk": 0, "pbank": 0, "cp": 0}

        def nxt(k, n):
            v = rr_ctr[k] % n
            rr_ctr[k] += 1
            return v

        def load_const_bf16(dst, b_dst, src):
            i = nxt("stg", 3)
            fw.dma("sp", stg[i][:, 0:128], src, writes=[b_stg[i]])
            fw.op("dve", lambda q: q.tensor_copy(out=dst[:], in_=stg[i][:, 0:128]), reads=[b_stg[i]], writes=[b_dst])

        load_const_bf16(ident, b_ident, I["c_ident"])
        load_const_bf16(negq, b_negq, I["c_negq"])
        load_const_bf16(negq4, b_negq4, I["c_negq4"])
        load_const_bf16(negd, b_negd, I["c_negd"])
        load_const_bf16(m01d, b_m01d, I["c_m01d"])
        load_const_bf16(nut, b_nut, I["c_nut"])
        fw.op("pool", lambda q: q.memset(onec[:], 1.0), writes=[b_onec])
        fw.dma("sp", ropeA[:].rearrange("p a b c -> p (a b c)"), I["c_ropeA"], writes=[b_ropeA])
        fw.dma("sp", ropeC[:].rearrange("p a b c -> p (a b c)"), I["c_ropeC"], writes=[b_ropeC])
        with nc.allow_non_contiguous_dma(reason="tiny gain vectors"):
            for (gsrc, o0, n) in ((I["g_pre0"], 0, 8), (I["g_pre1"], 8, 8), (I["g_cq"], 16, 6)):
                for c in range(n):
                    fw.dma("sp", gcol[:, o0 + c:o0 + c + 1], bass.AP(gsrc.tensor, c * 128, [[1, 128], [1, 1]]), writes=[b_gcol])

        def bcast_ap(src, n):
            return bass.AP(src.tensor, 0, [[0, 128], [1, n]])

        fw.dma("sp", gsub[:], bcast_ap(I["g_sub_a"], 128), writes=[b_gsub])
        fw.op("dve", lambda q: q.tensor_scalar(out=gsub[:], in0=gsub[:], scalar1=1.0 - LAM_INIT0, scalar2=None, op0=ALU.mult),
              reads=[b_gsub], writes=[b_gsub])
        fw.dma("sp", gckv[:], bcast_ap(I["g_ckv"], 256), writes=[b_gckv])
        for i, nm in enumerate(["lam_q1", "lam_k1", "lam_q2", "lam_k2"]):
            fw.dma("sp", lamt[:, i, :], bcast_ap(I[nm], 64), writes=[b_lamt])
        fw.op("dve", lambda q: q.tensor_tensor(out=lamt[:, 0, :], in0=lamt[:, 0, :], in1=lamt[:, 1, :], op=ALU.mult), reads=[b_lamt], writes=[b_lamt])
        fw.op("dve", lambda q: q.tensor_tensor(out=lamt[:, 2, :], in0=lamt[:, 2, :], in1=lamt[:, 3, :], op=ALU.mult), reads=[b_lamt], writes=[b_lamt])
        fw.op("dve", lambda q: q.reduce_sum(out=lamv[:, 0:1], in_=lamt[:, 0, :], axis=AX.X), reads=[b_lamt], writes=[b_lamv])
        fw.op("dve", lambda q: q.reduce_sum(out=lamv[:, 1:2], in_=lamt[:, 2, :], axis=AX.X), reads=[b_lamt], writes=[b_lamv])
        fw.op("act", lambda q: q.activation(out=lamv[:, 0:2], in_=lamv[:, 0:2], func=AF.Exp), reads=[b_lamv], writes=[b_lamv])
        fw.op("dve", lambda q: q.tensor_tensor(out=lamv[:, 2:3], in0=lamv[:, 1:2], in1=lamv[:, 0:1], op=ALU.subtract), reads=[b_lamv], writes=[b_lamv])
        fw.op("dve", lambda q: q.tensor_scalar(out=lamv[:, 3:4], in0=lamv[:, 2:3], scalar1=-LAM_INIT0, scalar2=None, op0=ALU.add), reads=[b_lamv], writes=[b_lamv])
        nlam = lamv[:, 3:4]
        cb = lamv[:, 4:8]
        cbt = fw.sb("cbt", [128, 8], F32); b_cbt = B("cbt")
        with nc.allow_non_contiguous_dma(reason="tiny"):
            for h in range(8):
                fw.dma("sp", cbt[:, h:h + 1], bass.AP(I["rel_bias_b"].tensor, 256 + 257 * h, [[0, 128], [1, 1]]), writes=[b_cbt])
        jn = nxt("stg", 3)
        fw.dma("sp", stg[jn][:, 0:128], I["c_negq"], writes=[b_stg[jn]])
        for h in range(8):
            for d in range(2):
                i = nxt("stg", 3)
                if i == jn:
                    i = nxt("stg", 3)
                fw.dma("sp", stg[i][:, 0:128], I["c_rbT"][h, d], writes=[b_stg[i]])
                fw.op("dve", lambda q: q.tensor_scalar(out=stg[i][:, 128:256], in0=stg[i][:, 0:128], scalar1=cbt[:, h:h + 1], scalar2=None, op0=ALU.subtract),
                      reads=[b_stg[i], b_cbt], writes=[b_stg[i]])
                if d == 0:
                    fw.op("dve", lambda q: q.tensor_tensor(out=Tb[:, h * 2 + d, :], in0=stg[i][:, 128:256], in1=stg[jn][:, 0:128], op=ALU.add),
                          reads=[b_stg[i], b_stg[jn]], writes=[b_Tb])
                else:
                    fw.op("dve", lambda q: q.tensor_copy(out=Tb[:, h * 2 + d, :], in_=stg[i][:, 128:256]), reads=[b_stg[i]], writes=[b_Tb])

        def set_ones(H, dv):
            v = VR[:, :, 0:H * (dv + 1)].rearrange("p k (h e) -> p k h e", h=H)[:, :, :, dv:dv + 1]
            fw.op("pool", lambda q: q.memset(v, 1.0), writes=b_V)

        def load_wblock(scr, K, c0, ncol):
            i = nxt("wblk", 2)
            kc = K // 128
            wait_prep(scr.name)
            fw.dma("sp", wblk[i][:, 0:kc, 0:ncol], scr.rearrange("(c p) n -> p c n", p=128)[:, :, c0:c0 + ncol],
                   reads=[dbuf(scr.name)], writes=[b_wblk[i]])
            return wblk[i], b_wblk[i]

        def rsqrt_col(col_ap, nr, bs):
            fw.op("pool", lambda q: q.tensor_scalar(out=col_ap, in0=col_ap, scalar1=EPS, scalar2=0.0, op0=ALU.add, op1=ALU.add), reads=[bs], writes=[bs])
            fw.op("pool", lambda q: q.tensor_tensor(out=col_ap, in0=col_ap, in1=pexp[0:nr, :], op=ALU.pow), reads=[bs, b_pexp], writes=[bs])

        def rms_scale(src_ap, nr, n, b_src, key, engines="act"):
            bs = smb(key)
            aps = src_ap if isinstance(src_ap, (list, tuple)) else [src_ap]
            for k, a in enumerate(aps):
                w = a.shape[-1]
                fw.op("act", lambda q: q.activation(out=junk[0:nr, 0:w], in_=a, func=AF.Square, scale=float(n) ** -0.5, accum_out=sm[0:nr, key + k:key + k + 1]),
                      reads=b_src, writes=[b_junk, bs])
            if len(aps) == 2:
                fw.op("pool", lambda q: q.tensor_tensor(out=sm[0:nr, key:key + 1], in0=sm[0:nr, key:key + 1], in1=sm[0:nr, key + 1:key + 2], op=ALU.add),
                      reads=[bs], writes=[bs])
            rsqrt_col(sm[0:nr, key:key + 1], nr, bs)
            return sm[0:nr, key:key + 1], bs

        def transposes(src_aps, nr, widths):
            i = nxt("tbk", 2)
            offs = []
            fns = []
            o = 0
            for a, w in zip(src_aps, widths):
                offs.append(o)
                fns.append(lambda q, a=a, w=w, o=o: q.transpose(out=tbk[i][0:w, o:o + nr], in_=a, identity=ident[0:nr, 0:nr]))
                o += 128
            return i, fns, offs

        def copy_eng():
            return ("dve", "act")[nxt("cp", 2)]

        rr_ctr["ce"] = 0

        def cast_eng():
            return ("dve", "pool", "act")[nxt("ce", 3)]

        def ecopy(eng, out, in_, reads, writes, scale=None):
            if eng == "act":
                if scale is None:
                    fw.op("act", lambda q: q.activation(out=out, in_=in_, func=AF.Copy), reads=reads, writes=writes)
                else:
                    fw.op("act", lambda q: q.activation(out=out, in_=in_, func=AF.Copy, scale=scale), reads=reads, writes=writes)
            else:
                if scale is None:
                    fw.op(eng, lambda q: q.tensor_copy(out=out, in_=in_), reads=reads, writes=writes)
                else:
                    fw.op(eng, lambda q: q.tensor_scalar(out=out, in0=in_, scalar1=scale, scalar2=0.0, op0=ALU.mult, op1=ALU.add), reads=reads, writes=writes)

        rr_ctr["pin"] = 0
        rr_ctr["pout"] = 0
        pin_bufs = [(xt[0], b_xt[0]), (xt[1], b_xt[1]), (ytmp, b_ytmp)]

        def prep_weight(src, dst, K, N, gofs, c_lo=0):
            for kc in range(K // 128):
                for c0 in range(c_lo, N, 1024):
                    ncol = min(1024, N - c0)
                    it, ib = pin_bufs[nxt("pin", 3)]
                    so = nxt("pout", 16)
                    ot = oall[:, so, :]
                    fw.dma("sp", it[:, 0:ncol], src[kc * 128:(kc + 1) * 128, c0:c0 + ncol], writes=[ib], sembuf=ib)
                    eng = ("dve", "act")[(kc + c0 // 1024) % 2]
                    if gofs is None:
                        ecopy(eng, ot[:, 0:ncol], it[:, 0:ncol], [ib], [b_o[so]])
                    else:
                        ecopy(eng, ot[:, 0:ncol], it[:, 0:ncol], [ib, b_gcol], [b_o[so]], scale=gcol[:, gofs + kc:gofs + kc + 1])
                    prep_evs.setdefault(dst.name, []).append(
                        fw.dma("pool", dst[kc * 128:(kc + 1) * 128, c0:c0 + ncol], ot[:, 0:ncol], reads=[b_o[so]], writes=[dbuf(dst.name)], sembuf=b_o[so]))
                    yield

        prep_evs = {}
        prep_late = []
        late_gens = []
        if 0 in layers and DEV_STOP >= 1:
            for _ in prep_weight(I["w_in0"], S_w0, DM, 1536, 0):
                pass
            late_gens += [prep_weight(I["w_in0"], S_w0, DM, W0N, 0, c_lo=1536), prep_weight(I["w_out0"], S_wo0, DM, DM, None)]
        if 1 in layers and DEV_STOP >= 1:
            late_gens += [prep_weight(I["w_in1"], S_w1, DM, W1N, 8), prep_weight(I["w_uq"], S_wuq, 768, 768, 16),
                          prep_weight(I["w_out1"], S_wo1, DM, DM, None)]
        prep_late = itertools.chain(*late_gens)
        if 0 not in layers or NP == 0:
            for _ in prep_late:
                pass
            prep_late = []
        prep_state = {"it": iter(prep_late), "active": False}

        def prep_step(n=2):
            if prep_state["active"]:
                for _ in range(n):
                    if next(prep_state["it"], "done") == "done":
                        prep_state["active"] = False
                        break

        def prep_flush():
            for _ in prep_state["it"]:
                pass
            prep_state["active"] = False

        def wait_prep(name):
            fw.wait_events("sp", prep_evs.get(name, []))

        def stage_U(seq, src, src_bufs=()):
            sel = {}

            def pa(ti):
                r0, nr, rt = seq.tiles[ti]
                i = nxt("xt", 2)
                fw.dma("sp", xt[i][0:nr, :], src[r0:r0 + nr, :], reads=list(src_bufs), writes=[b_xt[i]], sembuf=b_xt[i])
                rs, bs = rms_scale(xt[i][0:nr, :], nr, DM, [b_xt[i]], 2 * (ti % 2))
                sel[ti] = (i, rs, bs)

            def pb(ti):
                r0, nr, rt = seq.tiles[ti]
                i, rs, bs = sel[ti]
                fw.op("dve", lambda q: q.tensor_scalar(out=xn[0:nr, :], in0=xt[i][0:nr, :], scalar1=rs, scalar2=None, op0=ALU.mult), reads=[b_xt[i], bs], writes=[b_xn])
                k = nxt("tbk", 2)
                fw.op("pe", [lambda q, c=c: q.transpose(out=tbk[k][:, c * 128:c * 128 + nr], in_=xn[0:nr, c * 128:(c + 1) * 128], identity=ident[0:nr, 0:nr])
                             for c in range(8)], reads=[b_xn, b_ident], writes=[b_tbk[k]])
                ecopy("act", uT[:, 0:4, r0:r0 + nr], tbk[k][:, 0:512].rearrange("p (c t) -> p c t", c=4)[:, :, 0:nr], [b_tbk[k]], [b_uT[ti]])
                ecopy("dve", uT[:, 4:8, r0:r0 + nr], tbk[k][:, 512:1024].rearrange("p (c t) -> p c t", c=4)[:, :, 0:nr], [b_tbk[k]], [b_uT[ti]])

            n = len(seq.tiles)
            pa(0)
            for ti in range(n):
                if ti + 1 < n:
                    pa(ti + 1)
                pb(ti)

        def proj_block(seq, scr, c0, ncol, consume, K=DM, lhs=None):
            wt, bw = load_wblock(scr, K, c0, ncol)
            kcn = K // 128
            later = []
            later2 = []
            for ti, (r0, nr, rt) in enumerate(seq.tiles):
                pb = 4 + nxt("pbank", 2)
                if lhs is None:
                    lf = lambda c: uT[:, c, r0:r0 + nr]
                    lb = [b_uT[ti]]
                else:
                    lf, lb = lhs(ti)
                fw.op("pe", [lambda q, c=c: q.matmul(bank[pb][0:nr, 0:ncol], lhsT=lf(c), rhs=wt[:, c, 0:ncol], start=(c == 0), stop=(c == kcn - 1))
                             for c in range(kcn)], reads=lb + [bw], writes=[b_bank[pb]])
                for f in later2:
                    f()
                later2 = later
                later = consume(ti, r0, nr, rt, bank[pb], b_bank[pb]) or []
                prep_step()
            for f in later2 + later:
                f()

        def store_out(dst, src_ap, b_src):
            fw.dma("pool", dst, src_ap, reads=[b_src], is_output=True, sembuf=b_src)

        def ingest_k(src_ap, b_src, nr, kcol0, ktile, scale=None, ncol=512, part=128, rows=None, eng="dve"):
            j = nxt("tb16", 3)
            ecopy(eng, tb16[j][0:nr, 0:ncol], src_ap, [b_src], [b_tb16[j]], scale)

            def later():
                nchunk = ncol // part
                k = nxt("tbk", 2)
                fw.op("pe", [lambda q, c=c: q.transpose(out=tbk[k][0:part, c * 128:c * 128 + nr], in_=tb16[j][0:nr, c * part:(c + 1) * part],
                                                         identity=ident[0:nr, 0:nr]) for c in range(nchunk)],
                      reads=[b_tb16[j], b_ident], writes=[b_tbk[k]])
                p0, p1 = rows if rows is not None else (0, part)
                ecopy(copy_eng(), KT[p0:p1, 0:nchunk, kcol0:kcol0 + nr] if rows is None else KT[p0:p1, 0:nchunk, kcol0:kcol0 + nr],
                      tbk[k][0:part, 0:nchunk * 128].rearrange("p (c t) -> p c t", c=nchunk)[:, :, 0:nr], [b_tbk[k]], [b_KT[ktile]])
            return later

        def ingest_q(src_ap, b_src, nr, qcol0, qtile, scale, ncol=512, part=128, eng="dve"):
            j = nxt("tb16", 3)
            ecopy(eng, tb16[j][0:nr, 0:ncol], src_ap, [b_src], [b_tb16[j]], scale)

            def later():
                nchunk = ncol // part
                k = nxt("tbk", 2)
                fw.op("pe", [lambda q, c=c: q.transpose(out=tbk[k][0:part, c * 128:c * 128 + nr], in_=tb16[j][0:nr, c * part:(c + 1) * part],
                                                         identity=ident[0:nr, 0:nr]) for c in range(nchunk)],
                      reads=[b_tb16[j], b_ident], writes=[b_tbk[k]])
                ecopy(copy_eng(), QT[0:part, 0:nchunk, qcol0:qcol0 + nr],
                      tbk[k][0:part, 0:nchunk * 128].rearrange("p (c t) -> p c t", c=nchunk)[:, :, 0:nr], [b_tbk[k]], [b_QT[qtile]])
            return later

        def ingest_v(src_ap, b_src, nr, vtile, H, dv, eng="dve", hofs=0, Hsrc=None):
            Hs = Hsrc or H
            src3 = src_ap.rearrange("p (h d) -> p h d", h=Hs)[:, hofs:hofs + H, :]
            dst3 = VR[0:nr, vtile, 0:H * (dv + 1)].rearrange("p (h e) -> p h e", h=H)[:, :, 0:dv] if dv != 64 or True else None
            ecopy(eng, dst3, src3, [b_src], [b_V[vtile]])

        def rope_ops(dst, src3, nr, rt, tab, half, nblk, b_src, b_dst, blk_stride_dst3):
            cosb = tab[0:nr, 0, rt, :].unsqueeze(1).to_broadcast([nr, nblk, half])
            sinb = tab[0:nr, 1, rt, :].unsqueeze(1).to_broadcast([nr, nblk, half])
            x1 = src3[:, :, 0:half]
            x2 = src3[:, :, half:2 * half]
            d3 = blk_stride_dst3
            t = osb[0:nr, :, :].rearrange("p a b -> p (a b)")
            n = nblk * half
            tt = [t[:, k * n:(k + 1) * n].rearrange("p (b d) -> p b d", b=nblk) for k in range(4)]
            btab = b_ropeA if tab is ropeA else b_ropeC
            fw.op("dve", lambda q: q.tensor_tensor(out=tt[0], in0=x1, in1=cosb, op=ALU.mult), reads=[b_src, btab], writes=b_osb)
            fw.op("dve", lambda q: q.tensor_tensor(out=tt[1], in0=x2, in1=sinb, op=ALU.mult), reads=[b_src, btab], writes=b_osb)
            fw.op("dve", lambda q: q.tensor_tensor(out=tt[2], in0=x2, in1=cosb, op=ALU.mult), reads=[b_src, btab], writes=b_osb)
            fw.op("dve", lambda q: q.tensor_tensor(out=tt[3], in0=x1, in1=sinb, op=ALU.mult), reads=[b_src, btab], writes=b_osb)
            fw.op("dve", lambda q: q.tensor_tensor(out=d3[:, :, 0:half], in0=tt[0], in1=tt[1], op=ALU.subtract), reads=b_osb, writes=[b_dst])
            fw.op("dve", lambda q: q.tensor_tensor(out=d3[:, :, half:2 * half], in0=tt[2], in1=tt[3], op=ALU.add), reads=b_osb, writes=[b_dst])

        def run_units(units, sbanks=((0,), (1,))):
            pend = None
            for ui, u in enumerate(units):
                sb_ = sbanks[ui % 2]
                nk = u["nk"]
                used = sorted(set(x[0] for x in u["qk"]))
                for bsel in used:
                    bk_ = sb_[bsel]
                    fns = []
                    for (bs_, c0, n, lhsT, rhs) in u["qk"]:
                        if bs_ == bsel:
                            fns.append(lambda q, c0=c0, n=n, lhsT=lhsT, rhs=rhs: q.matmul(bank[bk_][0:nk, c0:c0 + n], lhsT=lhsT, rhs=rhs, start=True, stop=False,
                                                                                          skip_group_check=True))
                    for (bs_, c0, n, rhs_t) in u["extra"]:
                        if bs_ == bsel:
                            fns.append(lambda q, c0=c0, n=n, rhs_t=rhs_t: q.matmul(bank[bk_][0:nk, c0:c0 + n], lhsT=ident[0:nk, 0:nk], rhs=rhs_t, start=False, stop=True,
                                                                                   skip_group_check=True))
                    fw.op("pe", fns, reads=u["rd"] + [b_ident, b_negq], writes=[b_bank[bk_]])
                pi = ui % 3
                for (bsel, c0, n, ptc0) in u["exps"]:
                    bk_ = sb_[bsel]
                    fw.op("act", lambda q, c0=c0, n=n, ptc0=ptc0: q.activation(out=PT[pi][0:nk, ptc0:ptc0 + n], in_=bank[bk_][0:nk, c0:c0 + n], func=AF.Exp),
                          reads=[b_bank[bk_]], writes=[b_PT[pi]])
                if pend is not None:
                    pend()

                def pvs(u=u, pi=pi, nk=nk):
                    obufs = []
                    fns = []
                    for (O_ap, pc0, nq, V_ap, start, bO) in u["pv"]:
                        fns.append(lambda q, O_ap=O_ap, pc0=pc0, nq=nq, V_ap=V_ap, start=start: q.matmul(
                            O_ap, lhsT=PT[pi][0:nk, pc0:pc0 + nq], rhs=V_ap, start=start, stop=True, skip_group_check=True))
                        if bO not in obufs:
                            obufs.append(bO)
                    if DEV_STOP >= 3.2:
                        fw.op("pe", fns, reads=[b_PT[pi]] + u["rdv"], writes=obufs)
                    if u.get("fin"):
                        u["fin"]()
                pend = pvs
            if pend is not None:
                pend()

        def stage_PA(seq, b):
            pre = "p" if seq.kind == "p" else "s"
            if DEV_STOP >= 2.1:
                set_ones(4, 128)
            if seq.kind == "s" and DEV_STOP >= 2.2:
                for (vt, kc0, nk, is_cache) in seq.keys:
                    if not is_cache:
                        continue
                    i = nxt("stg", 3)
                    fw.dma("sp", stg[i][:, :], I["cache_a_k"][b, kc0:kc0 + 128, :], writes=[b_stg[i]])
                    ingest_k(stg[i][:, :], b_stg[i], 128, kc0, vt, eng=cast_eng())()
                    i = nxt("stg", 3)
                    fw.dma("sp", stg[i][:, :], I["cache_a_v"][b, kc0:kc0 + 128, :], writes=[b_stg[i]])
                    ingest_v(stg[i][:, :], b_stg[i], 128, vt, 4, 128, eng=cast_eng())
            knew0 = 0 if seq.kind == "p" else 2048
            vnew0 = 0 if seq.kind == "p" else 16

            def cons_q(ti, r0, nr, rt, bk, bbk):
                if DEV_STOP < 2.31:
                    return []
                i = nxt("stg", 3)
                ecopy("act", stg[i][0:nr, :], bk[0:nr, :], [bbk], [b_stg[i]])
                if DEV_STOP < 2.32:
                    return []
                rope_ops(None, stg[i][0:nr, :].rearrange("p (b d) -> p b d", b=8), nr, rt, ropeA, 8, 8, b_stg[i], b_stg[i],
                         stg[i][0:nr, :].rearrange("p (b d) -> p b d", b=8))
                if DEV_STOP < 2.33:
                    return []
                return [ingest_q(stg[i][0:nr, :], b_stg[i], nr, r0, ti, 0.125, eng="dve")]

            def cons_k(ti, r0, nr, rt, bk, bbk):
                i = nxt("stg", 3)
                ecopy("act", stg[i][0:nr, :], bk[0:nr, :], [bbk], [b_stg[i]])
                rope_ops(None, stg[i][0:nr, :].rearrange("p (b d) -> p b d", b=8), nr, rt, ropeA, 8, 8, b_stg[i], b_stg[i],
                         stg[i][0:nr, :].rearrange("p (b d) -> p b d", b=8))
                store_out(O["a_k_" + pre][b, r0:r0 + nr, :], stg[i][0:nr, :], b_stg[i])
                return [ingest_k(stg[i][0:nr, :], b_stg[i], nr, knew0 + r0, vnew0 + ti, eng="dve")]

            def cons_v(ti, r0, nr, rt, bk, bbk):
                i = nxt("stg", 3)
                ecopy("act", stg[i][0:nr, :], bk[0:nr, :], [bbk], [b_stg[i]])
                store_out(O["a_v_" + pre][b, r0:r0 + nr, :], stg[i][0:nr, :], b_stg[i])
                ingest_v(stg[i][0:nr, :], b_stg[i], nr, vnew0 + ti, 4, 128, eng="dve")
                return []

            if DEV_STOP >= 2.3:
                proj_block(seq, S_w0, 0, 512, cons_q)
            if DEV_STOP >= 2.4:
                proj_block(seq, S_w0, 512, 512, cons_k)
            if DEV_STOP >= 2.5:
                proj_block(seq, S_w0, 1024, 512, cons_v)

        def fin_A(seq, h, qts, qb0, ob):
            for j in qts:
                jj = j - qb0
                nq = seq.tiles[j][1]
                o0 = bank[ob[0]][0:nq, jj * 129:jj * 129 + 128]
                o1 = bank[ob[1]][0:nq, jj * 129:jj * 129 + 128]
                kb = 8 + 4 * (jj % 2)
                bs = smb(kb)
                fw.op("dve", lambda q: q.reciprocal(out=sm[0:nq, kb:kb + 1], in_=bank[ob[0]][0:nq, jj * 129 + 128:jj * 129 + 129]), reads=[b_bank[ob[0]]], writes=[bs])
                fw.op("dve", lambda q: q.reciprocal(out=sm[0:nq, kb + 1:kb + 2], in_=bank[ob[1]][0:nq, jj * 129 + 128:jj * 129 + 129]), reads=[b_bank[ob[1]]], writes=[bs])
                fw.op("dve", lambda q: q.tensor_tensor(out=sm[0:nq, kb + 1:kb + 2], in0=sm[0:nq, kb + 1:kb + 2], in1=nlam[0:nq, :], op=ALU.mult), reads=[bs, b_lamv], writes=[bs])
                ot = osb[0:nq, 1 + (jj % 2), :]
                bo = b_osb[1 + (jj % 2)]
                fw.op("dve", lambda q: q.tensor_scalar(out=ot, in0=o0, scalar1=sm[0:nq, kb:kb + 1], scalar2=None, op0=ALU.mult), reads=[b_bank[ob[0]], bs], writes=[bo])
                fw.op("dve", lambda q: q.scalar_tensor_tensor(out=ot, in0=o1, scalar=sm[0:nq, kb + 1:kb + 2], in1=ot, op0=ALU.mult, op1=ALU.add),
                      reads=[b_bank[ob[1]], bs, bo], writes=[bo])
                fw.op("dve", lambda q: q.scalar_tensor_tensor(out=osb[0:nq, 3, :], in0=ot, scalar=1.0 / 128, in1=ot, op0=ALU.mult, op1=ALU.mult,
                                                              accum_out=sm[0:nq, kb + 2:kb + 3]), reads=[bo], writes=[b_osb[3], bs])
                rsqrt_col(sm[0:nq, kb + 2:kb + 3], nq, bs)
                fw.op("dve", lambda q: q.scalar_tensor_tensor(out=oall[0:nq, j, h * 128:(h + 1) * 128], in0=ot, scalar=sm[0:nq, kb + 2:kb + 3], in1=gsub[0:nq, :],
                                                              op0=ALU.mult, op1=ALU.mult), reads=[bo, bs, b_gsub], writes=[b_o[j]])

        def stage_QA_sample(seq):
            nq = seq.tiles[0][1]
            units = []
            for h in range(4):
                ob = (4, 5) if h % 2 == 0 else (6, 7)
                started = [False, False]
                cache = [k for k in seq.keys if k[3]]
                new = [k for k in seq.keys if not k[3]]
                for grp_keys in (cache, new):
                    nk = grp_keys[0][2]
                    u = dict(nk=nk, qk=[], extra=[], exps=[], pv=[], rd=[b_QT[0]] + [b_KT[k[0]] for k in grp_keys], rdv=[b_V[k[0]] for k in grp_keys])
                    for si_, (vt, kc0, nk_, is_cache) in enumerate(grp_keys):
                        for m in range(2):
                            u["qk"].append((m, si_ * nq, nq, KT[m * 64:(m + 1) * 64, h, kc0:kc0 + nk], QT[m * 64:(m + 1) * 64, h, 0:nq]))
                            u["pv"].append((bank[ob[m]][0:nq, 0:129], m * 256 + si_ * nq, nq, VR[0:nk, vt, h * 129:(h + 1) * 129], not started[m], b_bank[ob[m]]))
                            started[m] = True
                    for m in range(2):
                        u["exps"].append((m, 0, len(grp_keys) * nq, m * 256))
                    units.append(u)
                units[-1]["fin"] = (lambda h=h, ob=ob: fin_A(seq, h, [0], 0, ob))
            run_units(units, sbanks=((0, 1), (2, 3)))

        def stage_QA(seq):
            if seq.kind == "s":
                return stage_QA_sample(seq)
            nt = len(seq.tiles)
            bq = 2 if seq.kind == "p" else 1
            blk = 0
            for h in range(4):
                for qb0 in range(0, nt, bq):
                    qts = list(range(qb0, min(nt, qb0 + bq)))
                    nqs = [seq.tiles[j][1] for j in qts]
                    bqn = sum(nqs)
                    qcol0 = seq.tiles[qb0][0]
                    ob = (4, 5) if blk % 2 == 0 else (6, 7)
                    blk += 1
                    units = []
                    started = [False, False]
                    for ki, (vt, kc0, nk, is_cache) in enumerate(seq.keys):
                        if seq.kind == "p":
                            valid = [j for j in qts if ki <= j]
                        else:
                            valid = qts
                        if not valid:
                            continue
                        j0 = valid[0] - qb0
                        off0 = j0 * 128
                        nv = sum(seq.tiles[j][1] for j in valid)
                        u = dict(nk=nk, qk=[], extra=[], exps=[], pv=[], rd=[b_KT[vt]] + [b_QT[j] for j in valid], rdv=[b_V[vt]])
                        for m in range(2):
                            u["qk"].append((m, off0, nv, KT[m * 64:(m + 1) * 64, h, kc0:kc0 + nk], QT[m * 64:(m + 1) * 64, h, qcol0 + off0:qcol0 + off0 + nv]))
                            for j in valid:
                                if seq.kind == "p" and ki == j:
                                    u["extra"].append((m, (j - qb0) * 128, seq.tiles[j][1], negq[0:nk, 0:seq.tiles[j][1]]))
                            u["exps"].append((m, off0, nv, m * bqn + off0))
                        for m in range(2):
                            for j in valid:
                                jj = j - qb0
                                nq = seq.tiles[j][1]
                                u["pv"].append((bank[ob[m]][0:nq, jj * 129:(jj + 1) * 129], m * bqn + jj * 128, nq,
                                                VR[0:nk, vt, h * 129:(h + 1) * 129], not started[m], b_bank[ob[m]]))
                                started[m] = True
                        units.append(u)

                    def fin(h=h, qts=qts, qb0=qb0, ob=ob):
                        for j in qts:
                            jj = j - qb0
                            nq = seq.tiles[j][1]
                            o0 = bank[ob[0]][0:nq, jj * 129:jj * 129 + 128]
                            o1 = bank[ob[1]][0:nq, jj * 129:jj * 129 + 128]
                            kb = 8 + 4 * (jj % 2)
                            bs = smb(kb)
                            fw.op("dve", lambda q: q.reciprocal(out=sm[0:nq, kb:kb + 1], in_=bank[ob[0]][0:nq, jj * 129 + 128:jj * 129 + 129]), reads=[b_bank[ob[0]]], writes=[bs])
                            fw.op("dve", lambda q: q.reciprocal(out=sm[0:nq, kb + 1:kb + 2], in_=bank[ob[1]][0:nq, jj * 129 + 128:jj * 129 + 129]), reads=[b_bank[ob[1]]], writes=[bs])
                            fw.op("dve", lambda q: q.tensor_tensor(out=sm[0:nq, kb + 1:kb + 2], in0=sm[0:nq, kb + 1:kb + 2], in1=nlam[0:nq, :], op=ALU.mult), reads=[bs, b_lamv], writes=[bs])
                            ot = osb[0:nq, 1 + (jj % 2), :]
                            bo = b_osb[1 + (jj % 2)]
                            fw.op("dve", lambda q: q.tensor_scalar(out=ot, in0=o0, scalar1=sm[0:nq, kb:kb + 1], scalar2=None, op0=ALU.mult), reads=[b_bank[ob[0]], bs], writes=[bo])
                            fw.op("dve", lambda q: q.scalar_tensor_tensor(out=ot, in0=o1, scalar=sm[0:nq, kb + 1:kb + 2], in1=ot, op0=ALU.mult, op1=ALU.add),
                                  reads=[b_bank[ob[1]], bs, bo], writes=[bo])
                            fw.op("dve", lambda q: q.scalar_tensor_tensor(out=osb[0:nq, 3, :], in0=ot, scalar=1.0 / 128, in1=ot, op0=ALU.mult, op1=ALU.mult,
                                                                          accum_out=sm[0:nq, kb + 2:kb + 3]), reads=[bo], writes=[b_osb[3], bs])
                            rsqrt_col(sm[0:nq, kb + 2:kb + 3], nq, bs)
                            fw.op("dve", lambda q: q.scalar_tensor_tensor(out=oall[0:nq, j, h * 128:(h + 1) * 128], in0=ot, scalar=sm[0:nq, kb + 2:kb + 3], in1=gsub[0:nq, :],
                                                                          op0=ALU.mult, op1=ALU.mult), reads=[bo, bs, b_gsub], writes=[b_o[j]])
                    if DEV_STOP >= 3.4:
                        units[-1]["fin"] = fin
                    if DEV_STOP < 3.3:
                        units = units[:1]
                    run_units(units, sbanks=((0, 1), (2, 3)))
                    if DEV_STOP < 3.35:
                        return

        def stage_PB(seq, b):
            pre = "p" if seq.kind == "p" else "s"
            set_ones(8, 64)
            if seq.kind == "s":
                for (vt, kc0, nk, is_cache) in seq.keys_b:
                    if not is_cache:
                        continue
                    i = nxt("stg", 3)
                    fw.dma("sp", stg[i][:, :], I["cache_b_k"][b, kc0:kc0 + 128, :], writes=[b_stg[i]])
                    ingest_k(stg[i][:, :], b_stg[i], 128, kc0, vt, eng=cast_eng())()
                    i = nxt("stg", 3)
                    fw.dma("sp", stg[i][:, :], I["cache_b_v"][b, kc0:kc0 + 128, :], writes=[b_stg[i]])
                    ingest_v(stg[i][:, :], b_stg[i], 128, vt, 8, 64, eng=cast_eng())
            knew0 = 0 if seq.kind == "p" else 512
            vnew0 = 0 if seq.kind == "p" else 4

            def cons_q(ti, r0, nr, rt, bk, bbk):
                return [ingest_q(bk[0:nr, :], bbk, nr, r0, ti, 0.125, eng="dve")]

            def cons_k(ti, r0, nr, rt, bk, bbk):
                if seq.kind == "s" or r0 >= T - 512:
                    i = nxt("stg", 3)
                    ecopy("act", stg[i][0:nr, :], bk[0:nr, :], [bbk], [b_stg[i]])
                    orow = r0 - (T - 512) if seq.kind == "p" else r0
                    store_out(O["b_k_" + pre][b, orow:orow + nr, :], stg[i][0:nr, :], b_stg[i])
                return [ingest_k(bk[0:nr, :], bbk, nr, knew0 + r0, vnew0 + ti, eng="dve")]

            def cons_v(ti, r0, nr, rt, bk, bbk):
                if seq.kind == "s" or r0 >= T - 512:
                    i = nxt("stg", 3)
                    ecopy("act", stg[i][0:nr, :], bk[0:nr, :], [bbk], [b_stg[i]])
                    orow = r0 - (T - 512) if seq.kind == "p" else r0
                    store_out(O["b_v_" + pre][b, orow:orow + nr, :], stg[i][0:nr, :], b_stg[i])
                ingest_v(bk[0:nr, :], bbk, nr, vnew0 + ti, 8, 64, eng="dve")
                return []

            proj_block(seq, S_w0, 1536, 512, cons_q)
            proj_block(seq, S_w0, 2048, 512, cons_k)
            proj_block(seq, S_w0, 2560, 512, cons_v)

        def stage_QB(seq):
            blk = 0
            for j, (r0, nq, rt) in enumerate(seq.tiles):
                ob = (2, 3) if j % 2 == 0 else (4, 5)
                pairs = []
                for h in (0, 2, 4, 6, 1, 3, 5, 7):
                    if seq.kind == "p":
                        for d in (4, 3, 2, 1, 0):
                            ki = j - d
                            if ki < 0:
                                continue
                            ex = None
                            if d == 0:
                                ex = Tb[:, h * 2 + 0, :]
                            elif d == 1:
                                ex = Tb[:, h * 2 + 1, :]
                            elif d == 4:
                                ex = negq4[:, :]
                            pairs.append((h, ki, ex))
                    else:
                        for ki in range(5):
                            ex = None
                            if ki == 3:
                                ex = Tb[:, h * 2 + 1, :]
                            elif ki == 4:
                                ex = Tb[:, h * 2 + 0, :]
                            pairs.append((h, ki, ex))
                units = []
                started = [False, False]
                slotw = 128 if seq.kind == "p" else nq
                for p0 in range(0, len(pairs), 4):
                    grp = pairs[p0:p0 + 4]
                    nkmax = max(seq.keys_b[ki][2] for (_, ki, _) in grp)
                    u = dict(nk=nkmax, qk=[], extra=[], exps=[], pv=[], rd=[b_QT[j]], rdv=[])
                    same = all(seq.keys_b[ki][2] == nkmax for (_, ki, _) in grp)
                    for s, (h, ki, ex) in enumerate(grp):
                        vt, kc0, nk, _c = seq.keys_b[ki]
                        u["qk"].append((s * slotw, nq, KT[(h % 2) * 64:(h % 2 + 1) * 64, h // 2, kc0:kc0 + nk], QT[(h % 2) * 64:(h % 2 + 1) * 64, h // 2, r0:r0 + nq], nk))
                        if ex is not None:
                            u["extra"].append((s * slotw, nq, ex[0:nk, 0:nq], nk))
                        u["pv"].append((bank[ob[h // 4]][0:nq, (h % 4) * 65:(h % 4 + 1) * 65], s * slotw, nq, VR[0:nk, vt, h * 65:(h + 1) * 65],
                                        not started[h // 4], b_bank[ob[h // 4]], nk))
                        started[h // 4] = True
                        u["rd"].append(b_KT[vt])
                        u["rdv"].append(b_V[vt])
                        u["exps"].append((s * slotw, nq, nk))
                    if same:
                        u["exps"] = [(0, (len(grp) - 1) * slotw + nq, nkmax)]
                    units.append(u)

                def fin(j=j, nq=nq, ob=ob):
                    for g in range(2):
                        kb = 16 + 4 * g
                        bs = smb(kb)
                        fw.op("dve", lambda q: q.reciprocal(out=sm[0:nq, kb:kb + 4], in_=bank[ob[g]][0:nq, 0:260].rearrange("p (h e) -> p h e", e=65)[:, :, 64]),
                              reads=[b_bank[ob[g]]], writes=[bs])
                        for hh in range(4):
                            h = g * 4 + hh
                            osl = oall[0:nq, j, 512 + h * 64:512 + (h + 1) * 64]
                            fw.op("dve", lambda q, osl=osl, hh=hh: q.scalar_tensor_tensor(out=osl, in0=bank[ob[g]][0:nq, hh * 65:hh * 65 + 64], scalar=sm[0:nq, kb + hh:kb + hh + 1],
                                                                                          in1=osl, op0=ALU.mult, op1=ALU.mult), reads=[b_bank[ob[g]], bs, b_o[j]], writes=[b_o[j]])
                units[-1]["fin"] = fin
                run_units_nk(units)

        def run_units_nk(units):
            pend = None
            for ui, u in enumerate(units):
                sbk = ui % 2
                fns = []
                exd = {c0: (n, rhs_t, nk) for (c0, n, rhs_t, nk) in u["extra"]}
                for (c0, n, lhsT, rhs, nk) in u["qk"]:
                    fns.append(lambda q, c0=c0, n=n, lhsT=lhsT, rhs=rhs, nk=nk: q.matmul(bank[sbk][0:nk, c0:c0 + n], lhsT=lhsT, rhs=rhs, start=True, stop=False,
                                                                                         skip_group_check=True))
                    if c0 in exd:
                        n2, rhs_t, nk2 = exd[c0]
                        fns.append(lambda q, c0=c0, n2=n2, rhs_t=rhs_t, nk2=nk2: q.matmul(bank[sbk][0:nk2, c0:c0 + n2], lhsT=ident[0:nk2, 0:nk2], rhs=rhs_t, start=False, stop=True,
                                                                                          skip_group_check=True))
                fw.op("pe", fns, reads=u["rd"] + [b_ident, b_negq4, b_Tb], writes=[b_bank[sbk]])
                pi = ui % 3
                for (c0, n, nk) in u["exps"]:
                    fw.op("act", lambda q, c0=c0, n=n, nk=nk: q.activation(out=PT[pi][0:nk, c0:c0 + n], in_=bank[sbk][0:nk, c0:c0 + n], func=AF.Exp),
                          reads=[b_bank[sbk]], writes=[b_PT[pi]])
                if pend is not None:
                    pend()
                if u.get("hook"):
                    u["hook"]()

                def pvs(u=u, pi=pi):
                    obufs = []
                    fns = []
                    for (O_ap, pc0, nq, V_ap, start, bO, nk) in u["pv"]:
                        fns.append(lambda q, O_ap=O_ap, pc0=pc0, nq=nq, V_ap=V_ap, start=start, nk=nk: q.matmul(
                            O_ap, lhsT=PT[pi][0:nk, pc0:pc0 + nq], rhs=V_ap, start=start, stop=True, skip_group_check=True))
                        if bO not in obufs:
                            obufs.append(bO)
                    fw.op("pe", fns, reads=[b_PT[pi]] + u["rdv"], writes=obufs)
                    if u.get("fin"):
                        u["fin"]()
                pend = pvs
            if pend is not None:
                pend()

        def stage_gate(seq, scr_w, gate_c0, half, premul):
            def cons(ti, r0, nr, rt, bk, bbk):
                k = nxt("tb16", 3)
                fw.op("act", lambda q: q.activation(out=tb16[k][0:nr, :], in_=bk[0:nr, :], func=AF.Tanh, scale=0.5), reads=[bbk], writes=[b_tb16[k]])
                fw.op("dve", lambda q: q.scalar_tensor_tensor(out=tb16[k][0:nr, :], in0=tb16[k][0:nr, :], scalar=1.0, in1=bk[0:nr, :], op0=ALU.add, op1=ALU.mult),
                      reads=[b_tb16[k], bbk], writes=[b_tb16[k]])
                osl = oall[0:nr, ti, half * 512:(half + 1) * 512]
                if premul:
                    fw.op("dve", lambda q: q.scalar_tensor_tensor(out=osl, in0=tb16[k][0:nr, :], scalar=0.5, in1=osl, op0=ALU.mult, op1=ALU.mult),
                          reads=[b_o[ti], b_tb16[k]], writes=[b_o[ti]])
                else:
                    fw.op("dve", lambda q: q.tensor_scalar(out=osl, in0=tb16[k][0:nr, :], scalar1=0.5, scalar2=None, op0=ALU.mult), reads=[b_tb16[k]], writes=[b_o[ti]])
                return []
            proj_block(seq, scr_w, gate_c0 + half * 512, 512, cons)

        def stage_GY(seq, b, resid_src, dst, dst_is_output, src_bufs=()):
            xsel = {}

            def tphase(j):
                r0, nr, rt = seq.tiles[j]
                i = nxt("xt", 2)
                xsel[j] = i
                fw.dma("sp", xt[i][0:nr, :], resid_src[r0:r0 + nr, :], reads=list(src_bufs), writes=[b_xt[i]], sembuf=b_xt[i])
                if DEV_DBG and seq.idx == 0:
                    fw.op("dve", lambda q: q.tensor_copy(out=ytmp[0:nr, :], in_=oall[0:nr, j, :]), reads=[b_o[j]], writes=[b_ytmp])
                    fw.dma("pool", O["dbg"][r0:r0 + nr, :], ytmp[0:nr, :], reads=[b_ytmp], is_output=True, sembuf=b_ytmp)
                k = j % 2
                fw.op("pe", [lambda q, c=c: q.transpose(out=tbk[k][:, c * 128:c * 128 + nr], in_=oall[0:nr, j, c * 128:(c + 1) * 128], identity=ident[0:nr, 0:nr])
                             for c in range(8)], reads=[b_o[j], b_ident], writes=[b_tbk[k]])
                g = j % 2
                ecopy("act", ogT[g][:, 0:4, 0:nr], tbk[k][:, 0:512].rearrange("p (c t) -> p c t", c=4)[:, :, 0:nr], [b_tbk[k]], [b_ogT[g]])
                ecopy("dve", ogT[g][:, 4:8, 0:nr], tbk[k][:, 512:1024].rearrange("p (c t) -> p c t", c=4)[:, :, 0:nr], [b_tbk[k]], [b_ogT[g]])

            def yphase(j):
                r0, nr, rt = seq.tiles[j]
                i = xsel[j]
                g = j % 2
                yb = ((4, 5), (2, 3))[j % 2]
                for half in range(2):
                    fw.op("pe", [lambda q, c=c: q.matmul(bank[yb[half]][0:nr, :], lhsT=ogT[g][:, c, 0:nr], rhs=wout[:, c, half * 512:(half + 1) * 512],
                                                         start=(c == 0), stop=(c == 7)) for c in range(8)],
                          reads=[b_ogT[g], b_wout], writes=[b_bank[yb[half]]])
                rs, bs = rms_scale([bank[yb[0]][0:nr, :], bank[yb[1]][0:nr, :]], nr, DM, [b_bank[yb[0]], b_bank[yb[1]]], 24 + 2 * (j % 2))
                for half in range(2):
                    fw.op("dve", lambda q: q.scalar_tensor_tensor(out=ytmp[0:nr, half * 512:(half + 1) * 512], in0=bank[yb[half]][0:nr, :], scalar=rs,
                                                                  in1=gpost[0:nr, half * 512:(half + 1) * 512], op0=ALU.mult, op1=ALU.mult),
                          reads=[b_bank[yb[half]], bs, b_gpost], writes=[b_ytmp])
                fw.op("dve", lambda q: q.tensor_tensor(out=xt[i][0:nr, 0:512], in0=ytmp[0:nr, 0:512], in1=xt[i][0:nr, 0:512], op=ALU.add), reads=[b_ytmp, b_xt[i]], writes=[b_xt[i]])
                fw.op("pool", lambda q: q.tensor_tensor(out=xt[i][0:nr, 512:1024], in0=ytmp[0:nr, 512:1024], in1=xt[i][0:nr, 512:1024], op=ALU.add), reads=[b_ytmp, b_xt[i]], writes=[b_xt[i]])
                if dst_is_output:
                    fw.dma("pool", dst[r0:r0 + nr, :], xt[i][0:nr, :], reads=[b_xt[i]], is_output=True, sembuf=b_xout[i])
                else:
                    h1_evs.setdefault((seq.kind, b), []).append(
                        fw.dma("pool", dst[r0:r0 + nr, :], xt[i][0:nr, :], reads=[b_xt[i]], writes=[dbuf(("h1", seq.kind, b))], sembuf=b_xout[i]))

            n = len(seq.tiles)
            tphase(0)
            for j in range(n):
                if j + 1 < n:
                    tphase(j + 1)
                yphase(j)

        def stage_PC(seq, b):
            pre = "p" if seq.kind == "p" else "s"
            SC = (64 + 32) ** -0.5
            set_ones(4, 64)
            knew0 = 0 if seq.kind == "p" else 2048
            vnew0 = 0 if seq.kind == "p" else 16
            spill_ev = []
            slot_ctr = {"i": 0}

            def next_slot():
                i = slot_ctr["i"] % 13
                slot_ctr["i"] += 1
                return i

            wq1 = oall[:, 13:16, :].rearrange("p a b -> p (a b)")[:, 0:2304].rearrange("p (c n) -> p c n", c=6)
            bwq1 = [b_o[13], b_o[14], b_o[15]]
            wsrc = S_wuq.rearrange("(c p) n -> p c n", p=128)
            wait_prep(S_wuq.name)
            fw.dma("sp", wqb[:, :, :], wsrc[:, :, 0:384], reads=[dbuf(S_wuq.name)], writes=[b_wqb])
            fw.dma("sp", wq1, wsrc[:, :, 384:768], reads=[dbuf(S_wuq.name)], writes=bwq1, sembuf=b_o[13])

            lat_state = {"n": 0, "pend": None}

            def lat_ingest(src_lat, b_lat, src_kr, b_kr, nr, kcol0, vt, eng):
                lt, blt = ((latT, b_latT), (latT2, b_latT2))[lat_state["n"] % 2]
                lat_state["n"] += 1
                j = nxt("tb16", 3)
                ecopy(eng, tb16[j][0:nr, 0:256], src_lat, [b_lat], [b_tb16[j]])
                ecopy(eng, tb16[j][0:nr, 256:288], src_kr, [b_kr], [b_tb16[j]])
                k = nxt("tbk", 2)
                fw.op("pe", [lambda q, c=c: q.transpose(out=tbk[k][:, c * 128:c * 128 + nr], in_=tb16[j][0:nr, c * 128:(c + 1) * 128], identity=ident[0:nr, 0:nr]) for c in range(2)]
                      + [lambda q: q.transpose(out=tbk[k][0:32, 256:256 + nr], in_=tb16[j][0:nr, 256:288], identity=ident[0:nr, 0:nr])],
                      reads=[b_tb16[j], b_ident], writes=[b_tbk[k]])
                ecopy("dve", lt[:, :, 0:nr], tbk[k][:, 0:256].rearrange("p (c t) -> p c t", c=2)[:, :, 0:nr], [b_tbk[k]], [blt])
                ecopy("act", KT[64:96, 0:4, kcol0:kcol0 + nr], tbk[k][0:32, 256:256 + nr].unsqueeze(1).to_broadcast([32, 4, nr]), [b_tbk[k]], [b_KT[vt]])

                def l2():
                    for g in range(2):
                        fns = []
                        for hh in range(4):
                            h = g * 4 + hh
                            for c in range(2):
                                fns.append(lambda q, hh=hh, h=h, c=c: q.matmul(bank[g][0:64, hh * 128:hh * 128 + nr], lhsT=wuk[:, c, h * 64:(h + 1) * 64], rhs=lt[:, c, 0:nr],
                                                                               start=(c == 0), stop=(c == 1), skip_group_check=True))
                        fw.op("pe", fns, reads=[blt, b_wuk], writes=[b_bank[g]])
                    ecopy("dve", KT[0:64, 0:4, kcol0:kcol0 + nr], bank[0][0:64, :].rearrange("p (h t) -> p h t", h=4)[:, :, 0:nr], [b_bank[0]], [b_KT[vt]])
                    sk = next_slot()
                    kst = oall[0:64, sk, 0:512].rearrange("p (h t) -> p h t", h=4)[:, :, 0:nr]
                    ecopy("act", kst, bank[1][0:64, :].rearrange("p (h t) -> p h t", h=4)[:, :, 0:nr], [b_bank[1]], [b_o[sk]])
                    spill_ev.append(fw.dma("pool", S_kt1[:, :, kcol0:kcol0 + nr], kst, reads=[b_o[sk]], writes=[dbuf("kt1")], sembuf=b_o[sk]))
                    fw.op("pe", [lambda q, c=c: q.matmul(bank[2][0:nr, 0:512], lhsT=lt[:, c, 0:nr], rhs=wuv[:, c, :], start=(c == 0), stop=(c == 1))
                                 for c in range(2)], reads=[blt, b_wuv], writes=[b_bank[2]])
                    ingest_v(bank[2][0:nr, 0:256], b_bank[2], nr, vt, 4, 64, eng="act")
                    sv = next_slot()
                    ecopy("dve", oall[0:nr, sv, 0:256], bank[2][0:nr, 256:512], [b_bank[2]], [b_o[sv]])
                    spill_ev.append(fw.dma("pool", S_v1[vt, 0:nr, :], oall[0:nr, sv, 0:256], reads=[b_o[sv]], writes=[dbuf("v1")], sembuf=b_o[sv]))

                prev = lat_state["pend"]
                lat_state["pend"] = l2
                if prev is not None:
                    prev()

            def lat_flush():
                if lat_state["pend"] is not None:
                    lat_state["pend"]()
                    lat_state["pend"] = None

            if seq.kind == "s":
                for (vt, kc0, nk, is_cache) in seq.keys:
                    if not is_cache:
                        continue
                    i = nxt("stg", 3)
                    fw.dma("sp", stg[i][:, 0:256], I["cache_c_latent"][b, kc0:kc0 + 128, :], writes=[b_stg[i]])
                    fw.dma("sp", stg[i][:, 256:288], I["cache_c_krope"][b, kc0:kc0 + 128, :], writes=[b_stg[i]])
                    lat_ingest(stg[i][:, 0:256], b_stg[i], stg[i][:, 256:288], b_stg[i], 128, kc0, vt, cast_eng())
                lat_flush()

            def cons_ckv(ti, r0, nr, rt, bk, bbk):
                i = nxt("stg", 3)
                rs, bs = rms_scale(bk[0:nr, 0:256], nr, 256, [bbk], 28)
                ecopy("act", stg[i][0:nr, 256:288], bk[0:nr, 256:288], [bbk], [b_stg[i]])
                fw.op("dve", lambda q: q.scalar_tensor_tensor(out=stg[i][0:nr, 0:256], in0=bk[0:nr, 0:256], scalar=rs, in1=gckv[0:nr, :], op0=ALU.mult, op1=ALU.mult),
                      reads=[bbk, bs, b_gckv], writes=[b_stg[i]])
                rope_ops(None, stg[i][0:nr, 256:288].rearrange("p (b d) -> p b d", b=1), nr, rt, ropeC, 16, 1, b_stg[i], b_stg[i],
                         stg[i][0:nr, 256:288].rearrange("p (b d) -> p b d", b=1))
                store_out(O["c_lat_" + pre][b, r0:r0 + nr, :], stg[i][0:nr, 0:256], b_stg[i])
                store_out(O["c_krope_" + pre][b, r0:r0 + nr, :], stg[i][0:nr, 256:288], b_stg[i])
                return [lambda: lat_ingest(stg[i][0:nr, 0:256], b_stg[i], stg[i][0:nr, 256:288], b_stg[i], nr, knew0 + r0, vnew0 + ti, "dve")]

            proj_block(seq, S_w1, 768, 288, cons_ckv)
            lat_flush()

            w1a = load_wblock(S_w1, DM, 0, 512)
            w1b = load_wblock(S_w1, DM, 512, 256)
            def s1(ti):
                r0, nr, rt = seq.tiles[ti]
                ba, bb = ((4, 5), (0, 1))[ti % 2]
                fw.op("pe", [lambda q, c=c: q.matmul(bank[ba][0:nr, :], lhsT=uT[:, c, r0:r0 + nr], rhs=w1a[0][:, c, :], start=(c == 0), stop=(c == 7)) for c in range(8)],
                      reads=[b_uT[ti], w1a[1]], writes=[b_bank[ba]])
                fw.op("pe", [lambda q, c=c: q.matmul(bank[bb][0:nr, 0:256], lhsT=uT[:, c, r0:r0 + nr], rhs=w1b[0][:, c, 0:256], start=(c == 0), stop=(c == 7)) for c in range(8)],
                      reads=[b_uT[ti], w1b[1]], writes=[b_bank[bb]])

            stgsel = {}

            def s2(ti):
                r0, nr, rt = seq.tiles[ti]
                ba, bb = ((4, 5), (0, 1))[ti % 2]
                rs, bs = rms_scale([bank[ba][0:nr, :], bank[bb][0:nr, 0:256]], nr, 768, [b_bank[ba], b_bank[bb]], 32 + 2 * (ti % 2))
                g = ti % 2
                cq16 = ogT[g][:, :, :].rearrange("p c t -> p (c t)")
                fw.op("act", lambda q: q.activation(out=cq16[0:nr, 0:512], in_=bank[ba][0:nr, :], func=AF.Copy, scale=rs), reads=[b_bank[ba], bs], writes=[b_ogT[g]])
                fw.op("act", lambda q: q.activation(out=cq16[0:nr, 512:768], in_=bank[bb][0:nr, 0:256], func=AF.Copy, scale=rs), reads=[b_bank[bb], bs], writes=[b_ogT[g]])
                k = nxt("tbk", 2)
                fw.op("pe", [lambda q, c=c: q.transpose(out=tbk[k][:, c * 128:c * 128 + nr], in_=cq16[0:nr, c * 128:(c + 1) * 128], identity=ident[0:nr, 0:nr]) for c in range(6)],
                      reads=[b_ogT[g], b_ident], writes=[b_tbk[k]])
                cqT = Ef[g][:, :].bitcast(BF16).rearrange("p (c t) -> p c t", c=8)
                ecopy("dve", cqT[:, 0:6, 0:nr], tbk[k][:, 0:768].rearrange("p (c t) -> p c t", c=6)[:, :, 0:nr], [b_tbk[k]], [b_Ef[g]])
                for grp in range(2):
                    qbk = (3, 2)[grp]
                    wq_t, wq_b = (wqb, [b_wqb]) if grp == 0 else (wq1, bwq1)
                    fw.op("pe", [lambda q, c=c: q.matmul(bank[qbk][0:nr, 0:384], lhsT=cqT[:, c, 0:nr], rhs=wq_t[:, c, 0:384], start=(c == 0), stop=(c == 5)) for c in range(6)],
                          reads=[b_Ef[g]] + wq_b, writes=[b_bank[qbk]])
                    i = nxt("stg", 3)
                    stgsel[(ti, grp)] = i
                    ecopy("act", stg[i][0:nr, 0:384], bank[qbk][0:nr, 0:384], [b_bank[qbk]], [b_stg[i]])

            def s3(ti):
                r0, nr, rt = seq.tiles[ti]
                for grp in range(2):
                    i = stgsel[(ti, grp)]
                    dst3 = stg[i][0:nr, 0:384].rearrange("p (h d) -> p h d", h=4)[:, :, 64:96]
                    rope_ops(None, dst3, nr, rt, ropeC, 16, 4, b_stg[i], b_stg[i], dst3)
                    if grp == 0:
                        ingest_q(stg[i][0:nr, 0:384], b_stg[i], nr, r0, ti, SC, ncol=384, part=96, eng="dve")()
                    else:
                        j = nxt("tb16", 3)
                        ecopy("dve", tb16[j][0:nr, 0:384], stg[i][0:nr, 0:384], [b_stg[i]], [b_tb16[j]], SC)
                        k2 = nxt("tbk", 2)
                        fw.op("pe", [lambda q, c=c: q.transpose(out=tbk[k2][0:96, c * 128:c * 128 + nr], in_=tb16[j][0:nr, c * 96:(c + 1) * 96], identity=ident[0:nr, 0:nr]) for c in range(4)],
                              reads=[b_tb16[j], b_ident], writes=[b_tbk[k2]])
                        sq = next_slot()
                        qst = oall[0:96, sq, 0:512].rearrange("p (h t) -> p h t", h=4)[:, :, 0:nr]
                        ecopy(copy_eng(), qst, tbk[k2][0:96, 0:512].rearrange("p (c t) -> p c t", c=4)[:, :, 0:nr], [b_tbk[k2]], [b_o[sq]])
                        spill_ev.append(fw.dma("pool", S_qt1[:, :, r0:r0 + nr], qst, reads=[b_o[sq]], writes=[dbuf("qt1")], sembuf=b_o[sq]))

            ntl = len(seq.tiles)
            s1(0)
            for ti in range(ntl):
                if ti + 1 < ntl:
                    s1(ti + 1)
                if ti >= 1:
                    s3(ti - 1)
                s2(ti)
            s3(ntl - 1)
            return spill_ev

        def reload_C(seq, spill_ev):
            fw.wait_events("sp", spill_ev)
            nq = seq.ntok
            nkc = seq.keys[-1][1] + seq.keys[-1][2]
            fw.dma("sp", QT[0:96, 0:4, 0:nq], S_qt1[:, :, 0:nq], reads=[dbuf("qt1")], writes=b_QT, sembuf=b_QT[0])
            fw.dma("sp", KT[0:64, 0:4, 0:nkc], S_kt1[:, :, 0:nkc], reads=[dbuf("kt1")], writes=b_KT, sembuf=b_KT[0])
            for (vt, kc0, nk, is_cache) in seq.keys:
                fw.dma("sp", VR[0:nk, vt, 0:260].rearrange("p (h e) -> p h e", e=65)[:, :, 0:64], S_v1[vt, 0:nk, :].rearrange("p (h d) -> p h d", h=4),
                       reads=[dbuf("v1")], writes=[b_V[vt]], sembuf=b_V[vt])

        def stage_QC_sample(seq, grp):
            nq = seq.tiles[0][1]
            units = []
            for hh in range(4):
                h = grp * 4 + hh
                ob = 2 + (hh % 2)
                u = dict(qk=[], extra=[], exps=[], pv=[], rd=[b_QT[0]], rdv=[])
                for ki, (vt, kc0, nk, is_cache) in enumerate(seq.keys):
                    c0 = ki * nq
                    u["qk"].append((c0, nq, KT[0:96, hh, kc0:kc0 + nk], QT[0:96, hh, 0:nq], nk))
                    u["pv"].append((bank[ob][0:nq, 0:65], c0, nq, VR[0:nk, vt, hh * 65:(hh + 1) * 65], ki == 0, b_bank[ob], nk))
                    u["rd"].append(b_KT[vt])
                    u["rdv"].append(b_V[vt])
                ncache = sum(1 for k in seq.keys if k[3])
                u["exps"] = [(0, ncache * nq, 128), (ncache * nq, nq, seq.keys[-1][2])]

                def fin(h=h, ob=ob):
                    kb = 36
                    bs = smb(kb)
                    fw.op("dve", lambda q: q.reciprocal(out=sm[0:nq, kb:kb + 1], in_=bank[ob][0:nq, 64:65]), reads=[b_bank[ob]], writes=[bs])
                    ecopy("dve", oall[0:nq, 0, h * 64:(h + 1) * 64], bank[ob][0:nq, 0:64], [b_bank[ob], bs], [b_o[0]], scale=sm[0:nq, kb:kb + 1])
                u["fin"] = fin
                units.append(u)
            run_units_nk(units)

        def stage_QC(seq, grp):
            if seq.kind == "s":
                return stage_QC_sample(seq, grp)
            nt = len(seq.tiles)
            bq = 4 if seq.kind == "p" else 1
            blk = 0
            for hh in range(4):
                h = grp * 4 + hh
                for qb0 in range(0, nt, bq):
                    qts = list(range(qb0, min(nt, qb0 + bq)))
                    qcol0 = seq.tiles[qb0][0]
                    ob = 2 + (blk % 2)
                    blk += 1
                    units = []
                    started = False
                    for ki, (vt, kc0, nk, is_cache) in enumerate(seq.keys):
                        valid = [j for j in qts if ki <= j] if seq.kind == "p" else qts
                        if not valid:
                            continue
                        j0 = valid[0] - qb0
                        off0 = j0 * 128
                        nv = sum(seq.tiles[j][1] for j in valid)
                        u = dict(nk=nk, qk=[], extra=[], exps=[(0, off0, nv, off0)], pv=[], rd=[b_KT[vt]] + [b_QT[j] for j in valid], rdv=[b_V[vt]])
                        u["qk"].append((0, off0, nv, KT[0:96, hh, kc0:kc0 + nk], QT[0:96, hh, qcol0 + off0:qcol0 + off0 + nv]))
                        for j in valid:
                            if seq.kind == "p" and ki == j:
                                u["extra"].append((0, (j - qb0) * 128, seq.tiles[j][1], negq[0:nk, 0:seq.tiles[j][1]]))
                        for j in valid:
                            jj = j - qb0
                            nq = seq.tiles[j][1]
                            u["pv"].append((bank[ob][0:nq, jj * 65:(jj + 1) * 65], jj * 128, nq, VR[0:nk, vt, hh * 65:(hh + 1) * 65], not started, b_bank[ob]))
                            started = True
                        units.append(u)

                    def fin(h=h, qts=qts, qb0=qb0, ob=ob):
                        nq = seq.tiles[qts[0]][1]
                        kb = 36
                        bs = smb(kb)
                        nj = len(qts)
                        fw.op("dve", lambda q: q.reciprocal(out=sm[0:nq, kb:kb + nj], in_=bank[ob][0:nq, 0:nj * 65].rearrange("p (j e) -> p j e", e=65)[:, :, 64]),
                              reads=[b_bank[ob]], writes=[bs])
                        for j in qts:
                            jj = j - qb0
                            ecopy("dve", oall[0:nq, j, h * 64:(h + 1) * 64], bank[ob][0:nq, jj * 65:jj * 65 + 64], [b_bank[ob], bs], [b_o[j]], scale=sm[0:nq, kb + jj:kb + jj + 1])
                    units[-1]["fin"] = fin
                    run_units(units)

        def stage_PD(seq, b):
            pre = "p" if seq.kind == "p" else "s"
            set_ones(8, 64)
            if seq.kind == "s":
                for (vt, kc0, nk, is_cache) in seq.keys:
                    if not is_cache:
                        continue
                    i = nxt("stg", 3)
                    fw.dma("sp", stg[i][:, :], I["cache_d_k"][b, kc0:kc0 + 128, :], writes=[b_stg[i]])
                    ingest_k(stg[i][:, :], b_stg[i], 128, kc0, vt, eng=cast_eng())()
                    i = nxt("stg", 3)
                    fw.dma("sp", stg[i][:, :], I["cache_d_v"][b, kc0:kc0 + 128, :], writes=[b_stg[i]])
                    ingest_v(stg[i][:, :], b_stg[i], 128, vt, 8, 64, eng=cast_eng())
            knew0 = 0 if seq.kind == "p" else 2048
            vnew0 = 0 if seq.kind == "p" else 16

            def cons_q(ti, r0, nr, rt, bk, bbk):
                return [ingest_q(bk[0:nr, :], bbk, nr, r0, ti, 0.125, eng="dve")]

            def cons_k(ti, r0, nr, rt, bk, bbk):
                i = nxt("stg", 3)
                ecopy("act", stg[i][0:nr, :], bk[0:nr, :], [bbk], [b_stg[i]])
                store_out(O["d_k_" + pre][b, r0:r0 + nr, :], stg[i][0:nr, :], b_stg[i])
                return [ingest_k(bk[0:nr, :], bbk, nr, knew0 + r0, vnew0 + ti, eng="dve")]

            def cons_v(ti, r0, nr, rt, bk, bbk):
                i = nxt("stg", 3)
                ecopy("act", stg[i][0:nr, :], bk[0:nr, :], [bbk], [b_stg[i]])
                store_out(O["d_v_" + pre][b, r0:r0 + nr, :], stg[i][0:nr, :], b_stg[i])
                ingest_v(bk[0:nr, :], bbk, nr, vnew0 + ti, 8, 64, eng="dve")
                return []

            proj_block(seq, S_w1, 1056, 512, cons_q)
            proj_block(seq, S_w1, 1568, 512, cons_k)
            proj_block(seq, S_w1, 2080, 512, cons_v)

        def stage_QD_sample(seq):
            nq = seq.tiles[0][1]
            keys = seq.keys
            ncache = sum(1 for k in keys if k[3])
            wc = ncache * nq
            nk_new = keys[-1][2]
            pend = None
            for h in range(8):
                hp = (h % 2) * 64
                xb = (0, 1, 3)[h % 3]
                eb = h % 2
                pi = h % 3
                fns = []
                for ki, (vt, kc0, nk, is_cache) in enumerate(keys):
                    fns.append(lambda q, ki=ki, kc0=kc0, nk=nk: q.matmul(bank[xb][0:nk, ki * nq:(ki + 1) * nq], lhsT=KT[hp:hp + 64, h // 2, kc0:kc0 + nk], rhs=QT[hp:hp + 64, h // 2, 0:nq],
                                                                         start=True, stop=False, skip_group_check=True))
                fw.op("pe", fns, reads=b_KT + [b_QT[0]], writes=[b_bank[xb]])
                for (r_, c0, c1) in ((128, 0, wc), (nk_new, wc, wc + nq)):
                    fw.op("act", lambda q, r_=r_, c0=c0, c1=c1: q.activation(out=Ef[eb][0:r_, c0:c1], in_=bank[xb][0:r_, c0:c1], func=AF.Exp), reads=[b_bank[xb]], writes=[b_Ef[eb]])
                    fw.op("act", lambda q, r_=r_, c0=c0, c1=c1: q.activation(out=SPb[eb][0:r_, c0:c1], in_=Ef[eb][0:r_, c0:c1], func=AF.Ln, bias=1.0), reads=[b_Ef[eb]], writes=[b_SPb[eb]])
                fw.op("pool", lambda q: q.tensor_tensor(out=SPb[eb][0:nk_new, wc:wc + nq], in0=SPb[eb][0:nk_new, wc:wc + nq], in1=m01d[0:nk_new, 0:nq], op=ALU.mult),
                      reads=[b_SPb[eb], b_m01d], writes=[b_SPb[eb]])
                if pend is not None:
                    pend()

                def stage2(h=h, xb=xb, eb=eb, pi=pi):
                    fw.op("pe", [lambda q: q.matmul(bank[xb][0:128, 0:wc], lhsT=nut[:, :], rhs=SPb[eb][0:128, 0:wc], start=False, stop=False, skip_group_check=True),
                                 lambda q: q.matmul(bank[xb][0:nk_new, wc:wc + nq], lhsT=nut[0:nk_new, 0:nk_new], rhs=SPb[eb][0:nk_new, wc:wc + nq], start=False, stop=False, skip_group_check=True),
                                 lambda q: q.matmul(bank[xb][0:nk_new, wc:wc + nq], lhsT=ident[0:nk_new, 0:nk_new], rhs=negd[0:nk_new, 0:nq], start=False, stop=True, skip_group_check=True)],
                          reads=[b_SPb[eb], b_nut, b_negd, b_ident], writes=[b_bank[xb]])
                    fw.op("act", lambda q: q.activation(out=PT[pi][0:128, 0:wc], in_=bank[xb][0:128, 0:wc], func=AF.Exp), reads=[b_bank[xb]], writes=[b_PT[pi]])
                    fw.op("act", lambda q: q.activation(out=PT[pi][0:nk_new, wc:wc + nq], in_=bank[xb][0:nk_new, wc:wc + nq], func=AF.Exp), reads=[b_bank[xb]], writes=[b_PT[pi]])
                    wbanks = (2, 4, 5)
                    for g in range(3):
                        kis = [ki for ki in range(len(keys)) if ki // 7 == g]
                        if not kis:
                            continue
                        fw.op("pe", [lambda q, ki=ki: q.matmul(bank[wbanks[g]][0:nq, (ki % 7) * 65:(ki % 7 + 1) * 65], lhsT=PT[pi][0:keys[ki][2], ki * nq:(ki + 1) * nq],
                                                               rhs=VR[0:keys[ki][2], keys[ki][0], h * 65:(h + 1) * 65], start=True, stop=True, skip_group_check=True) for ki in kis],
                              reads=[b_PT[pi]] + b_V, writes=[b_bank[wbanks[g]]])
                    dk = 40
                    bs = smb(dk)
                    acc = osb[0:nq, 0, 0:64]
                    for ki in range(len(keys)):
                        wb_ = bank[wbanks[ki // 7]]
                        c = (ki % 7) * 65
                        if ki == 0:
                            fw.op("dve", lambda q, wb_=wb_, c=c: q.tensor_copy(out=acc, in_=wb_[0:nq, c:c + 64]), reads=[b_bank[wbanks[0]]], writes=[b_osb[0]])
                            continue
                        fw.op("dve", lambda q, wb_=wb_, c=c: q.tensor_scalar(out=sm[0:nq, dk:dk + 1], in0=wb_[0:nq, c + 64:c + 65], scalar1=-1.0, scalar2=1.0, op0=ALU.mult, op1=ALU.add),
                              reads=[b_bank[wbanks[ki // 7]]], writes=[bs])
                        fw.op("dve", lambda q, wb_=wb_, c=c: q.scalar_tensor_tensor(out=acc, in0=acc, scalar=sm[0:nq, dk:dk + 1], in1=wb_[0:nq, c:c + 64], op0=ALU.mult, op1=ALU.add),
                              reads=[b_osb[0], bs, b_bank[wbanks[ki // 7]]], writes=[b_osb[0]])
                    osl = oall[0:nq, 0, 512 + h * 64:512 + (h + 1) * 64]
                    fw.op("dve", lambda q: q.tensor_tensor(out=osl, in0=acc, in1=osl, op=ALU.mult), reads=[b_osb[0], b_o[0]], writes=[b_o[0]])
                pend = stage2
            if pend is not None:
                pend()

        def stage_QD(seq):
            if seq.kind == "s":
                return stage_QD_sample(seq)
            nt = len(seq.tiles)
            bq = 4 if seq.kind == "p" else 1
            xbanks = (0, 1, 3)
            st_ = {"u": 0, "cs": 0}
            prev_tiles = []
            for qb0 in range(0, nt, bq):
                qts = list(range(qb0, min(nt, qb0 + bq)))
                qcol0 = seq.tiles[qb0][0]
                for h in range(8):
                    hp = (h % 2) * 64
                    fw.op("pool", lambda q: q.memset(osb[:, :, 0:64], 0.0), writes=b_osb)
                    pend_a = None
                    pend_b = None
                    for ki, (vt, kc0, nk, is_cache) in enumerate(seq.keys):
                        valid = [j for j in qts if ki <= j] if seq.kind == "p" else qts
                        if not valid:
                            continue
                        j0 = valid[0] - qb0
                        off0 = j0 * 128
                        nv = sum(seq.tiles[j][1] for j in valid)
                        ucount = st_["u"]
                        st_["u"] += 1
                        xb = xbanks[ucount % 3]
                        eb = ucount % 2
                        pi = ucount % 3
                        diag = [j for j in valid if (seq.kind == "p" and ki == j) or (seq.kind == "s" and not is_cache)]
                        fw.op("pe", lambda q: q.matmul(bank[xb][0:nk, off0:off0 + nv], lhsT=KT[hp:hp + 64, h // 2, kc0:kc0 + nk], rhs=QT[hp:hp + 64, h // 2, qcol0 + off0:qcol0 + off0 + nv],
                                                       start=True, stop=False, skip_group_check=True), reads=[b_KT[vt]] + [b_QT[j] for j in valid], writes=[b_bank[xb]])
                        fw.op("act", lambda q: q.activation(out=Ef[eb][0:nk, off0:off0 + nv], in_=bank[xb][0:nk, off0:off0 + nv], func=AF.Exp), reads=[b_bank[xb]], writes=[b_Ef[eb]])
                        fw.op("act", lambda q: q.activation(out=SPb[eb][0:nk, off0:off0 + nv], in_=Ef[eb][0:nk, off0:off0 + nv], func=AF.Ln, bias=1.0), reads=[b_Ef[eb]], writes=[b_SPb[eb]])
                        for j in diag:
                            c = (j - qb0) * 128
                            nq = seq.tiles[j][1]
                            fw.op("pool", lambda q: q.tensor_tensor(out=SPb[eb][0:nk, c:c + nq], in0=SPb[eb][0:nk, c:c + nq], in1=m01d[0:nk, 0:nq], op=ALU.mult),
                                  reads=[b_SPb[eb], b_m01d], writes=[b_SPb[eb]])

                        def stage2a(vt=vt, nk=nk, valid=valid, off0=off0, nv=nv, xb=xb, eb=eb, pi=pi, diag=diag, qts=qts, qb0=qb0):
                            fns = [lambda q: q.matmul(bank[xb][0:nk, off0:off0 + nv], lhsT=nut[0:nk, 0:nk], rhs=SPb[eb][0:nk, off0:off0 + nv], start=False, stop=False, skip_group_check=True)]
                            for j in diag:
                                c = (j - qb0) * 128
                                nq = seq.tiles[j][1]
                                fns.append(lambda q, c=c, nq=nq: q.matmul(bank[xb][0:nk, c:c + nq], lhsT=ident[0:nk, 0:nk], rhs=negd[0:nk, 0:nq], start=False, stop=True, skip_group_check=True))
                            fw.op("pe", fns, reads=[b_SPb[eb], b_nut, b_negd, b_ident], writes=[b_bank[xb]])
                            fw.op("act", lambda q: q.activation(out=PT[pi][0:nk, off0:off0 + nv], in_=bank[xb][0:nk, off0:off0 + nv], func=AF.Exp), reads=[b_bank[xb]], writes=[b_PT[pi]])

                        def stage2b(vt=vt, nk=nk, valid=valid, pi=pi, qb0=qb0, h=h, qts=qts, ucount=ucount):
                            dk = 40 + 4 * pi
                            bs = smb(dk)
                            wbk = (2, 4)[ucount % 2]
                            fns = []
                            for j in valid:
                                jj = j - qb0
                                nq = seq.tiles[j][1]
                                fns.append(lambda q, jj=jj, nq=nq: q.matmul(bank[wbk][0:nq, jj * 65:(jj + 1) * 65], lhsT=PT[pi][0:nk, jj * 128:jj * 128 + nq], rhs=VR[0:nk, vt, h * 65:(h + 1) * 65],
                                                                            start=True, stop=True, skip_group_check=True))
                            fw.op("pe", fns, reads=[b_PT[pi], b_V[vt]], writes=[b_bank[wbk]])
                            jlo = valid[0] - qb0
                            nqm = seq.tiles[valid[0]][1]
                            nj = len(qts)
                            w3 = bank[wbk][0:nqm, 0:nj * 65].rearrange("p (j e) -> p j e", e=65)
                            fw.op("dve", lambda q: q.tensor_scalar(out=sm[0:nqm, dk + jlo:dk + nj], in0=w3[:, jlo:nj, 64], scalar1=-1.0, scalar2=1.0, op0=ALU.mult, op1=ALU.add),
                                  reads=[b_bank[wbk]], writes=[bs])
                            nv_ = nj - jlo
                            dec_b = sm[0:nqm, dk + jlo:dk + nj].unsqueeze(2).to_broadcast([nqm, nv_, 64])
                            fw.op("dve", lambda q: q.tensor_tensor(out=osb[0:nqm, jlo:nj, 0:64], in0=osb[0:nqm, jlo:nj, 0:64], in1=dec_b, op=ALU.mult),
                                  reads=b_osb + [bs], writes=b_osb)
                            fw.op("dve", lambda q: q.tensor_tensor(out=osb[0:nqm, jlo:nj, 0:64], in0=w3[:, jlo:nj, 0:64], in1=osb[0:nqm, jlo:nj, 0:64], op=ALU.add),
                                  reads=b_osb + [b_bank[wbk]], writes=b_osb)

                        if pend_a is not None:
                            pend_a()
                        if pend_b is not None:
                            pend_b()
                        pend_b = None
                        if pend_a is not None:
                            pend_b = pend_a.b
                        stage2a.b = stage2b
                        pend_a = stage2a
                    if pend_a is not None:
                        pend_a()
                    if pend_b is not None:
                        pend_b()
                    if pend_a is not None:
                        pend_a.b()
                    for j in qts:
                        jj = j - qb0
                        nq = seq.tiles[j][1]
                        osl = oall[0:nq, j, 512 + h * 64:512 + (h + 1) * 64]
                        fw.op("dve", lambda q, osl=osl, jj=jj: q.tensor_tensor(out=osl, in0=osb[0:nq, jj, 0:64], in1=osl, op=ALU.mult), reads=[b_osb[jj], b_o[j]], writes=[b_o[j]])

        def load_wukv():
            for bb in (b_wuk, b_wuv, b_latT, b_wqb):
                bb.lw = b_Tb.lw
                bb.rd = list(b_Tb.rd)
            for (wsrc, wdst, bw) in ((I["w_uk"], wuk, b_wuk), (I["w_uv"], wuv, b_wuv)):
                for c in range(2):
                    i = nxt("stg", 3)
                    fw.dma("sp", stg[i][:, :], wsrc[c * 128:(c + 1) * 128, :], writes=[b_stg[i]])
                    fw.op("dve", lambda q: q.tensor_copy(out=wdst[:, c, :], in_=stg[i][:, :]), reads=[b_stg[i]], writes=[bw])

        seqs = [Seq("p", i) for i in range(NP)] + [Seq("s", i) for i in range(NS)]
        h1_evs = {}

        def load_layer_consts(layer):
            fw.dma("sp", gpost[:], bcast_ap(I["g_post0"] if layer == 0 else I["g_post1"], DM), writes=[b_gpost])
            wait_prep((S_wo0 if layer == 0 else S_wo1).name)
            fw.dma("sp", wout[:], (S_wo0 if layer == 0 else S_wo1).rearrange("(c p) n -> p c n", p=128), reads=[dbuf((S_wo0 if layer == 0 else S_wo1).name)], writes=[b_wout])

        u_done = set()

        def do_U(seq, layer):
            key = (layer, seq.kind, seq.idx)
            if key in u_done:
                return
            u_done.add(key)
            b = seq.idx
            if layer == 0 or 0 not in layers:
                stage_U(seq, I["x_prompt"][b] if seq.kind == "p" else I["x_sample"][b])
            else:
                fw.wait_events("sp", h1_evs.get((seq.kind, b), []))
                stage_U(seq, S_h1p[b] if seq.kind == "p" else S_h1s[b], [dbuf(("h1", seq.kind, b))])

        if 0 in layers and DEV_STOP >= 2:
            consts0_loaded = [False]
            for si, seq in enumerate(seqs):
                b = seq.idx
                src = I["x_prompt"][b] if seq.kind == "p" else I["x_sample"][b]
                if 1 in layers:
                    dst = S_h1p[b] if seq.kind == "p" else S_h1s[b]
                    is_out = False
                else:
                    dst = O["y_prompt"][b] if seq.kind == "p" else O["y_sample"][b]
                    is_out = True
                do_U(seq, 0)
                if si == 0 and seq.kind == "p":
                    prep_state["active"] = True
                if DEV_STOP >= 2.1:
                    stage_PA(seq, b)
                prep_flush()
                if not consts0_loaded[0]:
                    load_layer_consts(0)
                    consts0_loaded[0] = True
                if DEV_STOP >= 3.1:
                    stage_QA(seq)
                if DEV_STOP >= 5:
                    stage_PB(seq, b)
                if DEV_STOP >= 6:
                    stage_gate(seq, S_w0, 3072, 0, True)
                    stage_gate(seq, S_w0, 3072, 1, False)
                    if si + 1 < len(seqs):
                        do_U(seqs[si + 1], 0)
                    stage_QB(seq)
                    stage_GY(seq, b, src, dst, is_out)
        prep_flush()

        if 1 in layers:
            load_layer_consts(1)
            load_wukv()
            for si, seq in enumerate(seqs):
                b = seq.idx
                if 0 in layers:
                    src = S_h1p[b] if seq.kind == "p" else S_h1s[b]
                else:
                    src = I["x_prompt"][b] if seq.kind == "p" else I["x_sample"][b]
                dst = O["y_prompt"][b] if seq.kind == "p" else O["y_sample"][b]
                hb = [dbuf(("h1", seq.kind, b))] if 0 in layers else []
                fw.wait_events("sp", h1_evs.get((seq.kind, b), []))
                do_U(seq, 1)
                sp_ev = stage_PC(seq, b)
                stage_QC(seq, 0)
                reload_C(seq, sp_ev)
                stage_QC(seq, 1)
                stage_gate(seq, S_w1, 2592, 0, True)
                stage_PD(seq, b)
                stage_gate(seq, S_w1, 2592, 1, False)
                if si + 1 < len(seqs):
                    do_U(seqs[si + 1], 1)
                stage_QD(seq)
                stage_GY(seq, b, src, dst, True, hb)

        fw.finish("sp")
        print("ninst", fw.ninst, {e: fw.cnt[e] for e in fw.cnt})
    return nc


def _consts():
    c = {}
    c["c_ident"] = np.eye(128, dtype=np.float32)
    m = np.zeros((128, 128), np.float32); m[64:128, 0:64] = NEG; c["c_negq"] = m
    m = np.zeros((128, 128), np.float32); m[0:64, 64:128] = NEG; c["c_negq4"] = m
    s = np.arange(128)[:, None]; t = np.arange(128)[None, :]
    c["c_negd"] = np.where(s >= t, NEG, 0.0).astype(np.float32)
    c["c_m01d"] = (s < t).astype(np.float32)
    c["c_nut"] = np.where(s >= t, -1.0, 0.0).astype(np.float32)
    pos = np.zeros((17, 128), np.float32)
    for tt in range(16):
        pos[tt] = tt * 128 + np.arange(128)
    pos[16] = PAST + np.arange(128)
    for nm, half in (("c_ropeA", 8), ("c_ropeC", 16)):
        inv = (np.float32(THETA) ** (-np.arange(half, dtype=np.float32) / np.float32(half))).astype(np.float32)
        ang = (pos[:, :, None] * inv[None, None, :]).astype(np.float32)
        tab = np.stack([np.cos(ang), np.sin(ang)], 0).astype(np.float32)
        c[nm] = np.ascontiguousarray(tab.transpose(2, 0, 1, 3).reshape(128, 2 * 17 * half))
    return c


_CACHE = {}
_IN_SHARDED = ["x_prompt", "x_sample", "cache_a_k", "cache_a_v", "cache_b_k", "cache_b_v", "cache_c_latent", "cache_c_krope", "cache_d_k", "cache_d_v"]
_OUT_NAMES = ["y_prompt", "y_sample", "a_k_p", "a_v_p", "b_k_p", "b_v_p", "c_lat_p", "c_krope_p", "d_k_p", "d_v_p",
              "a_k_s", "a_v_s", "b_k_s", "b_v_s", "c_lat_s", "c_krope_s", "d_k_s", "d_v_s"]


def _out_shapes(nb_p, nb_s):
    return [(nb_p, T, DM), (nb_s, TS, DM), (nb_p, T, 4, 2, 64), (nb_p, T, 4, 128), (nb_p, 512, 8, 64), (nb_p, 512, 8, 64),
            (nb_p, T, 256), (nb_p, T, 32), (nb_p, T, 8, 64), (nb_p, T, 8, 64),
            (nb_s, TS, 4, 2, 64), (nb_s, TS, 4, 128), (nb_s, TS, 8, 64), (nb_s, TS, 8, 64), (nb_s, TS, 256), (nb_s, TS, 32),
            (nb_s, TS, 8, 64), (nb_s, TS, 8, 64)]


def _core_inputs(inputs, lo_p, hi_p, lo_s, hi_s):
    m = {}
    for k, v in inputs.items():
        a = np.asarray(v)
        if k in _IN_SHARDED:
            lo, hi = (lo_p, hi_p) if k == "x_prompt" else (lo_s, hi_s)
            a = a[lo:hi]
            a = a.reshape(a.shape[0], a.shape[1], -1)
        elif k in ("w_uk", "w_uv"):
            a = a.reshape(256, 512)
        m[k] = np.ascontiguousarray(a, dtype=np.float32)
    m.update(_consts())
    rb = np.asarray(inputs["rel_bias_b"], dtype=np.float32)
    s_ = np.arange(128)[:, None]; t_ = np.arange(128)[None, :]
    idx = np.stack([np.clip(t_ - s_, -128, 128) + 128, np.clip(128 + t_ - s_, -128, 128) + 128], 0)
    m["c_rbT"] = np.ascontiguousarray(rb[:, idx], dtype=np.float32)
    return m


def kernel(**inputs):
    nb = np.asarray(inputs["x_prompt"]).shape[0]
    per = nb // NCORES
    key = ("full", per)
    if key not in _CACHE:
        _CACHE[key] = build(per, per)
    nc = _CACHE[key]
    in_maps = [_core_inputs(inputs, i * per, (i + 1) * per, i * per, (i + 1) * per) for i in range(NCORES)]
    res = run_bass_kernel_spmd(nc, in_maps, core_ids=list(range(NCORES)))
    outs = []
    shapes = _out_shapes(nb, nb)
    for nm, shp in zip(_OUT_NAMES, shapes):
        full = np.concatenate([np.asarray(r[nm]) for r in res.results], axis=0)
        outs.append(np.ascontiguousarray(full.reshape(shp), dtype=np.float32))
    return tuple(outs)
```
